# Optimizing a Trainium2 kernel written in Bass

```python
import jax, jax.numpy as jnp
from jax import lax
import numpy as np

D_MODEL = 1024
BATCH = 2
SEQ = 16384
DEPTH = 1
DEC_BATCH = 32
DEC_SEQ = 16
PAST_LEN = 4096

CHUNK = 64
HEAD_K = 128
HEAD_V = 128
N_HEADS = D_MODEL // HEAD_V
KEY_DIM = N_HEADS * HEAD_K
VAL_DIM = N_HEADS * HEAD_V
QKV_DIM = 2 * KEY_DIM + VAL_DIM
GDN_CONV = 4
SC_WIDTH = D_MODEL
SC_CONV = 3
D_FF = 2816
FFN_CONV = 3
IN_DIM = QKV_DIM + VAL_DIM + 2 * N_HEADS + 3 * SC_WIDTH + 2 * D_MODEL
ALPHA = (2 * DEPTH) ** 0.25
BETA_INIT = (8 * DEPTH) ** -0.25
LN_EPS = 1e-5
NORM_EPS = 1e-6

kernel_name = "hybrid_gdn_shortconv_convglu_stream_step"


def causal_dwconv(x, w, hist):
    K = w.shape[0]
    T = x.shape[1]
    xp = jnp.concatenate([hist.astype(x.dtype), x], axis=1)
    y = xp[:, :T] * w[0]
    for j in range(1, K):
        y = y + xp[:, j:j + T] * w[j]
    return y, xp[:, xp.shape[1] - (K - 1):]


def layer_norm(x, g, b):
    xf = x.astype(jnp.float32)
    mu = jnp.mean(xf, axis=-1, keepdims=True)
    var = jnp.mean(jnp.square(xf - mu), axis=-1, keepdims=True)
    return ((xf - mu) * lax.rsqrt(var + LN_EPS) * g.astype(jnp.float32) + b.astype(jnp.float32)).astype(x.dtype)


def l2norm(x):
    return x * lax.rsqrt(jnp.sum(x * x, axis=-1, keepdims=True) + NORM_EPS)


def gated_delta_chunked(q, k, v, g, beta, S0, block):
    Bsz, T, H, dk = q.shape
    dv = v.shape[-1]
    N = T // block

    def to_blocks(a):
        return a.reshape((Bsz, N, block, H) + a.shape[3:]).swapaxes(2, 3)

    qc, kc, vc = to_blocks(q), to_blocks(k), to_blocks(v)
    gc, bc = to_blocks(g), to_blocks(beta)
    G = jnp.cumsum(gc, axis=-1)
    causal = jnp.tril(jnp.ones((block, block), dtype=bool))
    strict = jnp.tril(jnp.ones((block, block), dtype=bool), -1)
    decay = jnp.exp(jnp.where(causal, G[..., :, None] - G[..., None, :], -jnp.inf))
    kk = jnp.einsum('bnhid,bnhjd->bnhij', kc, kc)
    A = jnp.where(strict, bc[..., :, None] * kk * decay, 0.0)
    eye = jnp.eye(block, dtype=jnp.float32)
    rhs = jnp.concatenate([vc * bc[..., None], kc * (bc * jnp.exp(G))[..., None]], axis=-1)
    sol = lax.linalg.triangular_solve(A + eye, rhs, left_side=True, lower=True, unit_diagonal=True)
    u, w = sol[..., :dv], sol[..., dv:]
    qk = jnp.einsum('bnhid,bnhjd->bnhij', qc, kc) * decay
    q_dec = qc * jnp.exp(G)[..., None]
    G_last = G[..., -1]
    k_tail = kc * jnp.exp(G_last[..., None] - G)[..., None]

    def step(S, xs):
        q_d, w_b, u_b, qk_b, k_t, g_l = xs
        v_new = u_b - jnp.einsum('bhcd,bhde->bhce', w_b, S)
        o = jnp.einsum('bhcd,bhde->bhce', q_d, S) + jnp.einsum('bhij,bhje->bhie', qk_b, v_new)
        S = S * jnp.exp(g_l)[..., None, None] + jnp.einsum('bhcd,bhce->bhde', k_t, v_new)
        return S, o

    xs = tuple(a.swapaxes(0, 1) for a in (q_dec, w, u, qk, k_tail, G_last))
    S, o = lax.scan(step, S0, xs)
    o = o.transpose(1, 0, 3, 2, 4).reshape(Bsz, T, H, dv)
    return o, S


def hybrid_layer(x, gdn_hist, S0, sc_hist, ffn_hist, w_in, gdn_conv_w, a_log, dt_bias, o_norm_g,
                 sc_conv_w, w_o, ln1_g, ln1_b, w_up, ffn_conv_w, w_down, ln2_g, ln2_b):
    f32 = jnp.float32
    Bsz, T, _ = x.shape
    cuts = np.cumsum([QKV_DIM, VAL_DIM, N_HEADS, N_HEADS, SC_WIDTH, SC_WIDTH, SC_WIDTH, D_MODEL]).tolist()
    qkv, z, b_lin, a_lin, sB, sC, sH, gA, gB = jnp.split(x @ w_in, cuts, axis=-1)

    qkv, gdn_new = causal_dwconv(qkv, gdn_conv_w, gdn_hist)
    qkv = jax.nn.silu(qkv).astype(f32)
    q, k, v = jnp.split(qkv, [KEY_DIM, 2 * KEY_DIM], axis=-1)
    q = l2norm(q.reshape(Bsz, T, N_HEADS, HEAD_K)) * (HEAD_K ** -0.5)
    k = l2norm(k.reshape(Bsz, T, N_HEADS, HEAD_K))
    v = v.reshape(Bsz, T, N_HEADS, HEAD_V)
    beta = jax.nn.sigmoid(b_lin.astype(f32))
    g = -jnp.exp(a_log.astype(f32)) * jax.nn.softplus(a_lin.astype(f32) + dt_bias.astype(f32))
    o, S_new = gated_delta_chunked(q, k, v, g, beta, S0.astype(f32), min(CHUNK, T))
    o = o * lax.rsqrt(jnp.mean(o * o, axis=-1, keepdims=True) + NORM_EPS) * o_norm_g.astype(f32)
    o = o * jax.nn.silu(z.astype(f32).reshape(Bsz, T, N_HEADS, HEAD_V))
    y_gdn = o.reshape(Bsz, T, VAL_DIM).astype(x.dtype)

    c, sc_new = causal_dwconv(sC * sH, sc_conv_w, sc_hist)
    y_sc = sB * c

    mixed = jax.nn.sigmoid(gA) * y_gdn + jax.nn.sigmoid(gB) * y_sc
    x = layer_norm(ALPHA * x + mixed @ w_o, ln1_g, ln1_b)

    a, vff = jnp.split(x @ w_up, [D_FF], axis=-1)
    a, ffn_new = causal_dwconv(a, ffn_conv_w, ffn_hist)
    x = layer_norm(ALPHA * x + (jax.nn.gelu(a, approximate=False) * vff) @ w_down, ln2_g, ln2_b)
    return x, gdn_new, S_new, sc_new, ffn_new


def setup_inputs(seed: int = 0) -> dict:
    key = jax.random.key(seed)
    ks = jax.random.split(key, 24)
    nrm = lambda k, s: jax.random.normal(k, s, jnp.float32)
    L = DEPTH
    u = jax.random.uniform(ks[9], (L, N_HEADS), jnp.float32)
    dt = jnp.exp(u * (np.log(0.1) - np.log(0.001)) + np.log(0.001)).astype(jnp.float32)
    return {
        "x_prompt": nrm(ks[0], (BATCH, SEQ, D_MODEL)),
        "x_sample": nrm(ks[1], (DEC_BATCH, DEC_SEQ, D_MODEL)),
        "state_gdn_conv": nrm(ks[2], (L, DEC_BATCH, GDN_CONV - 1, QKV_DIM)),
        "state_gdn_S": 0.1 * nrm(ks[3], (L, DEC_BATCH, N_HEADS, HEAD_K, HEAD_V)),
        "state_sc_conv": nrm(ks[4], (L, DEC_BATCH, SC_CONV - 1, SC_WIDTH)),
        "state_ffn_conv": nrm(ks[5], (L, DEC_BATCH, FFN_CONV - 1, D_FF)),
        "w_in": nrm(ks[6], (L, D_MODEL, IN_DIM)) * D_MODEL ** -0.5,
        "gdn_conv_w": nrm(ks[7], (L, GDN_CONV, QKV_DIM)) * GDN_CONV ** -0.5,
        "a_log": jnp.log(jax.random.uniform(ks[8], (L, N_HEADS), jnp.float32, 1.0, 16.0)),
        "dt_bias": dt + jnp.log(-jnp.expm1(-dt)),
        "o_norm_g": 1.0 + 0.02 * nrm(ks[10], (L, HEAD_V)),
        "sc_conv_w": nrm(ks[11], (L, SC_CONV, SC_WIDTH)) * SC_CONV ** -0.5,
        "w_o": nrm(ks[12], (L, D_MODEL, D_MODEL)) * (D_MODEL ** -0.5 * BETA_INIT),
        "ln1_g": 1.0 + 0.02 * nrm(ks[13], (L, D_MODEL)),
        "ln1_b": 0.02 * nrm(ks[14], (L, D_MODEL)),
        "w_up": nrm(ks[15], (L, D_MODEL, 2 * D_FF)) * D_MODEL ** -0.5,
        "ffn_conv_w": nrm(ks[16], (L, FFN_CONV, D_FF)) * FFN_CONV ** -0.5,
        "w_down": nrm(ks[17], (L, D_FF, D_MODEL)) * (D_FF ** -0.5 * BETA_INIT),
        "ln2_g": 1.0 + 0.02 * nrm(ks[18], (L, D_MODEL)),
        "ln2_b": 0.02 * nrm(ks[19], (L, D_MODEL)),
    }


def reference(x_prompt, x_sample, state_gdn_conv, state_gdn_S, state_sc_conv, state_ffn_conv,
              w_in, gdn_conv_w, a_log, dt_bias, o_norm_g, sc_conv_w, w_o, ln1_g, ln1_b,
              w_up, ffn_conv_w, w_down, ln2_g, ln2_b):
    hp, hs = x_prompt, x_sample
    p_gc, p_S, p_sc, p_ff = [], [], [], []
    s_gc, s_S, s_sc, s_ff = [], [], [], []
    for l in range(DEPTH):
        params = (w_in[l], gdn_conv_w[l], a_log[l], dt_bias[l], o_norm_g[l], sc_conv_w[l], w_o[l],
                  ln1_g[l], ln1_b[l], w_up[l], ffn_conv_w[l], w_down[l], ln2_g[l], ln2_b[l])
        z_gc = jnp.zeros((BATCH, GDN_CONV - 1, QKV_DIM), x_prompt.dtype)
        z_S = jnp.zeros((BATCH, N_HEADS, HEAD_K, HEAD_V), jnp.float32)
        z_sc = jnp.zeros((BATCH, SC_CONV - 1, SC_WIDTH), x_prompt.dtype)
        z_ff = jnp.zeros((BATCH, FFN_CONV - 1, D_FF), x_prompt.dtype)
        hp, a1, a2, a3, a4 = hybrid_layer(hp, z_gc, z_S, z_sc, z_ff, *params)
        p_gc.append(a1); p_S.append(a2); p_sc.append(a3); p_ff.append(a4)
        hs, b1, b2, b3, b4 = hybrid_layer(hs, state_gdn_conv[l], state_gdn_S[l], state_sc_conv[l],
                                          state_ffn_conv[l], *params)
        s_gc.append(b1); s_S.append(b2); s_sc.append(b3); s_ff.append(b4)
    return (hp, hs,
            jnp.stack(p_gc), jnp.stack(p_S), jnp.stack(p_sc), jnp.stack(p_ff),
            jnp.stack(s_gc), jnp.stack(s_S), jnp.stack(s_sc), jnp.stack(s_ff))
```

```python
import numpy as np
from contextlib import ExitStack
import concourse.bass as bass
import concourse.mybir as mybir
from concourse.bass_utils import run_bass_kernel_spmd

F32 = mybir.dt.float32
BF16 = mybir.dt.bfloat16
ALU = mybir.AluOpType
AF = mybir.ActivationFunctionType
AX = mybir.AxisListType

D = 1024
NT = 256
DFF = 2816
NFC = 22
ALPHA = 2.0 ** 0.25
LN_EPS = 1e-5
NORM_EPS = 1e-6
ENGS = ("pe", "act", "dve", "pool", "sp")
NROT = 4
NDMASEM = 40
STOP = None
STOP2 = None
PEND_MODE = None


BANK_GEN = {}


class BankRes(str):
    def __new__(cls, s, gen):
        o = str.__new__(cls, s)
        o.gen = gen
        return o


class Op:
    __slots__ = ("eng", "fn", "waits", "inc", "epoch", "dma", "cnt")


class Prog:
    def __init__(self):
        self.ops = {e: [] for e in ENGS}
        self.lastw = {}
        self.rd = {}
        self.children = {}
        self.waited = {}
        self.dwaited = {}
        self.epoch = 0
        self.dma_vals = [0] * NDMASEM
        self.dma_rr = 0
        self.pool_dmas = 0

    @staticmethod
    def _norm(reads, writes):
        ps = [r for r in reads if r.startswith("ps")]
        if ps:
            reads = [r for r in reads if not r.startswith("ps")]
            writes = list(writes) + [p for p in ps if p not in writes]
        return reads, writes

    def _rel(self, r):
        if "/" in r:
            par = r.split("/")[0]
            self.children.setdefault(par, set()).add(r)
            return (r, par)
        ch = self.children.get(r)
        return (r,) + tuple(ch) if ch else (r,)

    def _mk(self, eng, fn, reads, writes, extra=()):
        deps = list(extra)
        for r in reads:
            for q in self._rel(r):
                t = self.lastw.get(q)
                if t is not None:
                    deps.append(t)
        for w in writes:
            for q in self._rel(w):
                t = self.lastw.get(q)
                if t is not None:
                    deps.append(t)
                deps.extend(self.rd.get(q, ()))
        op = Op()
        op.eng, op.fn, op.inc, op.epoch, op.dma, op.cnt = eng, fn, False, self.epoch, None, None
        waits = []
        best = {}
        for t in deps:
            if t[0] == "op":
                _, e2, i2 = t
                if e2 == eng and eng == "pe":
                    continue
                if i2 > best.get(e2, -1):
                    best[e2] = i2
            else:
                _, k, v = t
                if self.dwaited.get((eng, k), 0) < v:
                    self.dwaited[(eng, k)] = v
                    waits.append(("dma", k, v))
        for e2, i2 in best.items():
            if self.waited.get((eng, e2), -1) < i2:
                self.waited[(eng, e2)] = i2
                self.ops[e2][i2].inc = True
                waits.append(("op", e2, i2))
        op.waits = waits
        return op

    def _commit(self, tok, reads, writes):
        for r in reads:
            self.rd.setdefault(r, []).append(tok)
        for w in writes:
            self.lastw[w] = tok
            self.rd[w] = []

    def add(self, eng, fn, reads=(), writes=()):
        for r in list(reads) + list(writes):
            if isinstance(r, BankRes):
                assert BANK_GEN[str(r)] == r.gen, f"stale PSUM bank {r} gen {r.gen} != {BANK_GEN[str(r)]}"
        reads, writes = self._norm(reads, writes)
        op = self._mk(eng, fn, reads, writes)
        idx = len(self.ops[eng])
        self.ops[eng].append(op)
        self._commit(("op", eng, idx), reads, writes)

    def dma(self, eng, out, in_, reads=(), writes=()):
        if eng == "pool":
            op = self._mk(eng, lambda e: e.dma_start(out=out, in_=in_), reads, writes)
            self.pool_dmas += 1
            op.dma = ("p", 0)
            self.ops[eng].append(op)
            self._commit(("dma", "p", 1), reads, writes)
            return
        k = self.dma_rr
        self.dma_rr = (k + 1) % NDMASEM
        prev = self.dma_vals[k]
        extra = [("dma", k, prev)] if prev > 0 else []
        op = self._mk(eng, lambda e: e.dma_start(out=out, in_=in_), reads, writes, extra)
        self.dma_vals[k] = prev + 16
        op.dma = (k, prev + 16)
        self.ops[eng].append(op)
        self._commit(("dma", k, prev + 16), reads, writes)

    def emit(self, nc, es):
        sems = {e: [es.enter_context(nc.semaphore(f"s_{e}{r}")) for r in range(NROT)] for e in ENGS}
        dsems = [es.enter_context(nc.semaphore(f"d{k}")) for k in range(NDMASEM)]
        psem = es.enter_context(nc.semaphore("pdma"))
        ptotal = 16 * self.pool_dmas
        for e in ENGS:
            cnt = [0] * NROT
            for op in self.ops[e]:
                if op.inc:
                    r = op.epoch % NROT
                    cnt[r] += 1
                    op.cnt = (r, cnt[r])
        final_vals = list(self.dma_vals)

        def run(e, eo):
            for op in self.ops[e]:
                for w in op.waits:
                    if w[0] == "op":
                        d = self.ops[w[1]][w[2]]
                        eo.wait_ge(sems[w[1]][d.cnt[0]], d.cnt[1])
                    elif w[1] == "p":
                        eo.wait_ge(psem, ptotal)
                    else:
                        eo.wait_ge(dsems[w[1]], w[2])
                ins = op.fn(eo)
                if op.dma is not None and op.dma[0] == "p":
                    ins.then_inc(psem, 16)
                elif op.dma is not None:
                    ins.then_inc(dsems[op.dma[0]], 16)
                elif op.inc:
                    ins.then_inc(sems[e][op.cnt[0]], 1)
            if e == "sp":
                for k, v in enumerate(final_vals):
                    if v > 0:
                        eo.wait_ge(dsems[k], v)
                if ptotal:
                    eo.wait_ge(psem, ptotal)

        block = es.enter_context(nc.Block())

        @block.tensor
        def _(eo):
            run("pe", eo)

        @block.scalar
        def _(eo):
            run("act", eo)

        @block.vector
        def _(eo):
            run("dve", eo)

        @block.gpsimd
        def _(eo):
            run("pool", eo)

        @block.sync
        def _(eo):
            run("sp", eo)


def build(NPRE, NFULL, nsamp=4):
    BANK_GEN.clear()
    nc = bass.Bass("TRN2", target_bir_lowering=False)
    P = Prog()
    es = ExitStack()
    ROWS = (NPRE + NFULL) * NT
    NOWN = (NFULL - 1) * NT

    def din(name, shape):
        return nc.dram_tensor(name, list(shape), F32, kind="ExternalInput").ap()

    def dout(name, shape):
        return nc.dram_tensor(name, list(shape), F32, kind="ExternalOutput").ap()

    xp = din("xp", [ROWS, D])
    xs = din("xs", [nsamp * 16, D])
    st_gc = din("st_gc", [nsamp * 3, 3072])
    st_S = din("st_S", [nsamp, 8, 128, 128])
    st_sc = din("st_sc", [nsamp * 2, D])
    st_ff = din("st_ff", [nsamp * 2, DFF])
    w_in = din("w_in", [D, 9232])
    gdn_conv_w = din("gdn_conv_w", [4, 3072])
    a_log = din("a_log", [1, 8])
    dt_bias = din("dt_bias", [1, 8])
    o_norm_g = din("o_norm_g", [1, 128])
    sc_conv_w = din("sc_conv_w", [3, D])
    w_o = din("w_o", [D, D])
    lnp_d = din("lnp", [4, D])
    w_up = din("w_up", [D, 2 * DFF])
    ffn_conv_w = din("ffn_conv_w", [3, DFF])
    w_down = din("w_down", [DFF, D])
    consts = din("consts", [128, 384])

    yp = dout("yp", [NOWN, D])
    ys = dout("ys", [nsamp * 16, D])
    p_gc = dout("p_gc", [3, 3072])
    p_S = dout("p_S", [8, 128, 128])
    p_sc = dout("p_sc", [2, D])
    p_ff = dout("p_ff", [2, DFF])
    s_gc = dout("s_gc", [nsamp * 3, 3072])
    s_S = dout("s_S", [nsamp, 8, 128, 128])
    s_sc = dout("s_sc", [nsamp * 2, D])
    s_ff = dout("s_ff", [nsamp * 2, DFF])

    win_g = nc.dram_tensor("win_g", [36, 128, 2048], BF16).ap()
    wo_g = nc.dram_tensor("wo_g", [4, 128, 2048], BF16).ap()
    wup_g = nc.dram_tensor("wup_g", [NFC, 128, 2048], BF16).ap()
    wdn_g = nc.dram_tensor("wdn_g", [12, 128, 2048], BF16).ap()

    def sb(name, shape, dt=F32):
        return es.enter_context(nc.sbuf_tensor(name, list(shape), dt))

    banks = [es.enter_context(nc.psum_tensor(f"bank{i}", [128, 512], F32)) for i in range(8)]
    POOLS = {"all": list(range(8)), "front": [4, 5, 6, 7], "pend": [0, 1, 2, 3]}
    bank_rr = {"all": 0, "front": 0, "pend": 0}
    cur_pool = ["all"]

    def bank():
        pn = cur_pool[0]
        pool = POOLS[pn]
        i = pool[bank_rr[pn] % len(pool)]
        bank_rr[pn] += 1
        BANK_GEN[f"ps{i}"] = BANK_GEN.get(f"ps{i}", 0) + 1
        return banks[i], BankRes(f"ps{i}", BANK_GEN[f"ps{i}"])

    rot_state = {}

    def rot(name, n):
        i = rot_state.get(name, 0)
        rot_state[name] = (i + 1) % n
        return i

    cst = sb("cst", [128, 384])
    ident = cst[:, 0:128]
    tri = cst[:64, 128:192]
    sgt = cst[:64, 192:256]
    ones = cst[:, 256:384]
    ident_bf = sb("ident_bf", [128, 128], BF16)
    ones_bf = sb("ones_bf", [128, 128], BF16)
    trisg_bf = sb("trisg_bf", [64, 128], BF16)
    tri_bf = trisg_bf[:, 0:64]
    sgt_bf = trisg_bf[:, 64:128]
    onesm = sb("onesm", [128, 128])
    prm = sb("prm", [128, 224])
    epsn = sb("epsn", [128, 8])
    dtb = sb("dtb", [128, 8])
    nA = sb("nA", [128, 8])
    hso = sb("hso", [12, 512])

    P.dma("sp", cst[:], consts[:, :], writes=["cst"])
    P.dma("sp", hso[0:1, 0:8], dt_bias[:, :], writes=["hso"])
    P.dma("sp", hso[0:1, 8:16], a_log[:, :], writes=["hso"])
    bk, bkr = bank()
    P.add("pe", lambda e, bk=bk: e.matmul(bk[:, 0:16], ones[0:1, :], hso[0:1, 0:16], start=True, stop=True),
          reads=["hso", "cst"], writes=[bkr])
    P.add("dve", lambda e, bk=bk: e.tensor_copy(dtb[:], bk[:, 0:8]), reads=[bkr], writes=["dtb"])
    P.add("dve", lambda e, bk=bk: e.tensor_copy(nA[:], bk[:, 8:16]), reads=[bkr], writes=["nA"])

    WIN_RES = [f"win_sA{r}" for r in range(8)] + [f"win_sB{r}" for r in range(8)]
    WIN_KV = [f"win_kv{r}" for r in range(8)]
    WO_RES = [f"wo_s{r}" for r in range(8)]
    WUP_RES = [f"wup_sA{r}" for r in range(8)] + [f"wup_sB{r}" for r in range(8)]
    WDN_RES = [f"wdn_s{r}" for r in range(NFC)]
    wba = sb("wba", [128, 8, 16], BF16)

    P.add("pool", lambda e: e.memset(epsn[:, 0:1], NORM_EPS), writes=["epsc"])
    P.add("pool", lambda e: e.memset(epsn[:, 1:2], LN_EPS), writes=["epsc"])
    P.add("pool", lambda e: e.memset(epsn[:, 2:3], 1.0), writes=["epsc"])
    P.add("pool", lambda e: e.memset(epsn[:, 3:4], -1.0), writes=["epsc"])
    P.add("pool", lambda e: e.memset(epsn[:, 4:5], ALPHA), writes=["epsc"])
    P.add("dve", lambda e: e.tensor_copy(ident_bf[:], ident), reads=["cst"], writes=["ident_bf"])
    P.add("dve", lambda e: e.tensor_copy(ones_bf[:], ones), reads=["cst"], writes=["ones_bf"])
    P.add("dve", lambda e: e.tensor_copy(trisg_bf[:], cst[:64, 128:256]), reads=["cst"], writes=["ones_bf"])
    P.add("dve", lambda e: e.tensor_scalar(onesm[:], ones, 1.0 / D, None, ALU.mult), reads=["cst"], writes=["onesm"])
    P.add("dve", lambda e: e.tensor_copy(onesm_bf[:], onesm[:]), reads=["onesm"], writes=["onesm_bf"])
    P.add("act", lambda e: e.activation(nA[:], nA[:], AF.Exp), reads=["nA"], writes=["nA"])
    P.add("dve", lambda e: e.tensor_scalar(nA[:], nA[:], -1.0, None, ALU.mult), reads=["nA"], writes=["nA"])

    bk, bkr = bank()
    col = 0
    plist = [(gdn_conv_w, 4, 3072), (sc_conv_w, 3, D), (ffn_conv_w, 3, DFF), (lnp_d, 4, D), (o_norm_g, 1, 128)]
    for (src_d, r, width) in plist:
        for c0 in range(0, width, 512):
            wd = min(512, width - c0)
            P.dma("sp", hso[:r, 0:wd], src_d[:, c0:c0 + wd], writes=["hso"])
            for ch in range(wd // 128):
                o_ap = bk[:, col:col + r]
                i_ap = hso[:r, ch * 128:(ch + 1) * 128]
                P.add("pe", lambda e, o_ap=o_ap, i_ap=i_ap, r=r: e.transpose(o_ap, i_ap, ident[:r, :r]),
                      reads=["hso", "cst"], writes=[bkr])
                col += r
    P.add("dve", lambda e: e.tensor_copy(prm[:, 0:219], bk[:, 0:219]), reads=[bkr], writes=["prm"])
    cwg = prm[:, 0:96].rearrange("p (c k) -> p c k", k=4)
    cws = prm[:, 96:120].rearrange("p (c k) -> p c k", k=3)
    cwf = prm[:, 120:186].rearrange("p (c k) -> p c k", k=3)
    lnp = prm[:, 186:218].rearrange("p (c k) -> p c k", k=4)
    ong = prm[:, 218:219]

    xa = sb("xa", [128, 2, D])
    xT = sb("xT", [128, 8, NT], BF16)
    x_f = sb("x_f", [128, 8, NT])
    preqkv = sb("preqkv", [128, 24, NT + 3], BF16)
    post = sb("post", [128, 24, NT], BF16)
    NCACC = 4
    cacc = [sb(f"cacc{i}", [128, NT]) for i in range(NCACC)]
    sqb = [sb(f"sqb{i}", [128, NT], BF16) for i in range(2)]
    rsb = [sb(f"rsb{i}", [128, NT]) for i in range(2)]
    wst = [sb(f"wst{i}", [128, 2048], BF16) for i in range(4)]
    szg = sb("szg", [128, 8, NT], BF16)
    ppre = sb("ppre", [128, 8, NT + 2], BF16)
    mixg = sb("mixg", [128, 8, NT], BF16)
    sCs = mixg
    mixed = sb("mixed", [128, 8, NT], BF16)
    r1 = sb("r1", [128, 8, NT])
    cfl = r1
    onesm_bf = sb("onesm_bf", [128, 128], BF16)
    mean_sb = sb("mean_sb", [128, NT])
    m2 = sb("m2", [128, NT])
    rstd = sb("rstd", [128, NT])
    x1f = r1
    x1b = mixed
    apre = [sb(f"apre{i}", [128, NT + 2]) for i in range(4)]
    ahalo = sb("ahalo", [128, NFC, 4, 2])
    gab = [sb(f"gab{i}", [128, NT], BF16) for i in range(4)]
    hb = sb("hb", [128, NFC, NT], BF16)
    yout = xa
    hs = sb("hs", [128, 24, 12])
    ba_beta = sb("ba_beta", [64, 4, 8])
    ba_t = sb("ba_t", [64, 4, 8])
    ba_g = sb("ba_g", [64, 4, 8])
    ba_gh = sb("ba_gh", [64, 4, 8], BF16)
    ba_gl = sb("ba_gl", [64, 4, 8], BF16)
    NB = 4
    eGt = [sb(f"eGt{i}", [64, 16]) for i in range(NB)]
    eGl = [sb(f"eGl{i}", [128, 8]) for i in range(NB)]
    bg = [sb(f"bg{i}", [64, 8]) for i in range(NB)]
    gtri = [sb(f"gtri{i}", [64, 2, 8, 64], BF16) for i in range(NB)]
    Gam = [sb(f"Gam{i}", [64, 8, 64], BF16) for i in range(NB)]
    GamT = [sb(f"GamT{i}", [64, 8, 64], BF16) for i in range(NB)]
    kbg = [sb(f"kbg{i}", [64, 8, 128], BF16) for i in range(NB)]
    ktail = [sb(f"ktail{i}", [64, 8, 128], BF16) for i in range(NB)]
    vb = [sb(f"vb{i}", [64, 8, 128], BF16) for i in range(NB)]
    Am = [sb(f"Am{i}", [64, 8, 64], BF16) for i in range(NB)]
    Xm = [[sb(f"Xm{i}_{k}", [64, 8, 64], BF16) for k in range(2)] for i in range(NB)]
    Ym = [[sb(f"Ym{i}_{k}", [64, 8, 64], BF16) for k in range(2)] for i in range(NB)]
    Rm = [[sb(f"Rm{i}_0", [64, 8, 64], BF16)] * 2 for i in range(NB)]
    nwT = [sb(f"nwT{i}", [128, 8, 64], BF16) for i in range(NB)]
    vnew = [sb("vnew0", [64, 8, 128], BF16)] * NB
    qkT = [sb("qkT0", [64, 8, 64], BF16)] * NB
    o1s = [sb("o1s0", [64, 8, 128])] * NB
    osq = [sb("osq0", [64, 8, 128], BF16)] * NB
    oss = [sb(f"oss{i}", [64, 8]) for i in range(NB)]
    onb = [sb(f"onb{i}", [64, 8, 128], BF16) for i in range(2)] * 2
    Sp = sb("Sp", [128, 8, 128])
    Spb = sb("Spb", [128, 8, 128], BF16)
    Ss, Ssb = Sp, Spb

    stg_f = [(r1, "r1"), (x_f, "x_f")]
    stg_b = [(mixed, "mixed"), (mixg, "mixg")]
    pp = [0]

    def cast_block(src_ap, dst_ap, ncol, res, nk=None):
        i = pp[0] % 2
        pp[0] += 1
        (sf, sfr), (sbf, sbr) = stg_f[i], stg_b[i]
        fv = sf[:].rearrange("p a b -> p (a b)")[:, 0:ncol]
        bv = sbf[:].rearrange("p a b -> p (a b)")[:, 0:ncol]
        if nk is not None:
            fv = fv.rearrange("p (c k) -> p c k", k=nk)
            bv = bv.rearrange("p (c k) -> p c k", k=nk)
        P.dma("sp", fv, src_ap, writes=[sfr])
        eng = ("act", "dve", "pool")[pp[0] % 3]
        if eng == "act":
            P.add("act", lambda e: e.activation(bv, fv, AF.Copy), reads=[sfr], writes=[sbr])
        else:
            P.add(eng, lambda e: e.tensor_copy(bv, fv), reads=[sfr], writes=[sbr])
        P.dma("sp", dst_ap, bv, reads=[sbr], writes=[res])

    cast_jobs = []
    for r in range(8):
        rs = slice(r * 128, (r + 1) * 128)
        cast_block(w_in[rs, 1024:3072].rearrange("r (g c) -> r g c", c=256), win_g[4:12, :, r * 256:(r + 1) * 256].rearrange("g p c -> p g c"), 2048, f"win_kv{r}", nk=256)
    for r in range(8):
        rs = slice(r * 128, (r + 1) * 128)
        cast_jobs.append(lambda rs=rs, r=r: cast_block(w_in[rs, 0:1024].rearrange("r (g c) -> r g c", c=256), win_g[0:4, :, r * 256:(r + 1) * 256].rearrange("g p c -> p g c"), 1024, f"win_sA{r}", nk=256))
        cast_jobs.append(lambda rs=rs, r=r: cast_block(w_in[rs, 3072:4096].rearrange("r (g c) -> r g c", c=256), win_g[12:16, :, r * 256:(r + 1) * 256].rearrange("g p c -> p g c"), 1024, f"win_sA{r}", nk=256))
        cast_jobs.append(lambda rs=rs, r=r: cast_block(w_in[rs, 4112:6160].rearrange("r (g c) -> r g c", c=256), win_g[16:24, :, r * 256:(r + 1) * 256].rearrange("g p c -> p g c"), 2048, f"win_sB{r}", nk=256))
        cast_jobs.append(lambda rs=rs, r=r: cast_block(w_in[rs, 6160:8208].rearrange("r (g c) -> r g c", c=256), win_g[24:32, :, r * 256:(r + 1) * 256].rearrange("g p c -> p g c"), 2048, f"win_sB{r}", nk=256))
        cast_jobs.append(lambda rs=rs, r=r: cast_block(w_in[rs, 8208:9232].rearrange("r (g c) -> r g c", c=256), win_g[32:36, :, r * 256:(r + 1) * 256].rearrange("g p c -> p g c"), 1024, f"win_sB{r}", nk=256))
        cast_jobs.append(lambda rs=rs, r=r: cast_block(w_o[rs, :].rearrange("r (g c) -> r g c", c=256), wo_g[:, :, r * 256:(r + 1) * 256].rearrange("g p c -> p g c"), 1024, f"wo_s{r}", nk=256))
        for (c0, c1) in ((0, 16), (16, NFC)):
            for half in range(2):
                cast_jobs.append(lambda rs=rs, r=r, c0=c0, c1=c1, half=half: cast_block(
                    w_up[rs, half * DFF + c0 * 128:half * DFF + c1 * 128].rearrange("r (c k) -> r c k", k=128),
                    wup_g[c0:c1, :, r * 256 + half * 128:r * 256 + half * 128 + 128].rearrange("g p c -> p g c"),
                    (c1 - c0) * 128, f"wup_s{'AB'[half]}{r}", nk=128))
    for r in range(NFC):
        rs = slice(r * 128, (r + 1) * 128)
        kg, k8 = r // 8, r % 8
        cast_jobs.append(lambda rs=rs, r=r, kg=kg, k8=k8: cast_block(
            w_down[rs, :].rearrange("r (g c) -> r g c", c=256),
            wdn_g.rearrange("(cg kg) p c -> cg kg p c", kg=3)[:, kg, :, k8 * 256:(k8 + 1) * 256].rearrange("g p c -> p g c"),
            1024, f"wdn_s{r}", nk=256))
    fv = r1[:].rearrange("p a b -> p (a b)")[:, 0:128].rearrange("p (k c) -> p k c", c=16)
    P.dma("sp", fv, w_in[:, 4096:4112].rearrange("(kc p) c -> p kc c", p=128), writes=["r1"])
    P.add("dve", lambda e: e.tensor_copy(wba[:], fv), reads=["r1"], writes=["wba"])

    P.add("pool", lambda e: e.memset(Sp[:], 0.0), writes=["Sp"])
    P.add("pool", lambda e: e.memset(Spb[:], 0.0), writes=["Spb"])
    P.add("pool", lambda e: e.memset(preqkv[:], 0.0), writes=["preqkv"])
    P.add("pool", lambda e: e.memset(ppre[:], 0.0), writes=["ppre"])
    P.add("pool", lambda e: e.memset(ahalo[:], 0.0), writes=["ahalo"])

    wst_rr = [0]
    xa_loaded = [False]

    def wload(src_ap, nel, shape_str, **kw):
        i = wst_rr[0]
        wst_rr[0] = (i + 1) % 4
        view = wst[i][:, 0:nel].rearrange(shape_str, **kw)
        return i, view

    def proj(xin, xres, wview, wres, ncc, N, consume, cc0=0):
        for cc in range(ncc):
            bk, bkr = bank()
            for kc in range(8):
                P.add("pe", lambda e, bk=bk, kc=kc, cc=cc: e.matmul(
                    bk[:, 0:N], wview[:, kc, cc * 128:(cc + 1) * 128], xin[:, kc, 0:N],
                    start=(kc == 0), stop=(kc == 7)), reads=[xres, wres], writes=[bkr])
            consume(cc0 + cc, bk[:, 0:N], bkr)

    def tile(x_src, nseq, L, C, full, S, Sb, Sres, y_dst, hist_src=None, state_dst=None, write_y=True, S_src=None, x_next=None):
        N = nseq * L
        nch = N // C
        ntb = (N + 127) // 128
        TB = min(N, 128)
        H = 3

        def v3(ap):
            return ap.rearrange("p (s l) -> p s l", s=nseq)

        if not xa_loaded[0]:
            P.dma("sp", xa[:TB, 0:ntb, :], x_src.rearrange("(tb p) d -> p tb d", p=TB), writes=["xa"])
        xa_loaded[0] = False
        for tb in range(ntb):
            for kq in range(2):
                bk, bkr = bank()
                for k4 in range(4):
                    kc = kq * 4 + k4
                    P.add("pe", lambda e, bk=bk, k4=k4, kc=kc, tb=tb: e.transpose(
                        bk[:, k4 * 128:k4 * 128 + TB], xa[:TB, tb, kc * 128:(kc + 1) * 128], ident[:TB, :TB]),
                        reads=["xa", "cst"], writes=[bkr])
                src = bk[:, :].rearrange("p (k t) -> p k t", k=4)[:, :, 0:TB]
                P.add("dve", lambda e, src=src, kq=kq, tb=tb: e.tensor_copy(
                    xT[:, kq * 4:kq * 4 + 4, tb * 128:tb * 128 + TB], src), reads=[bkr], writes=["xT"])
                if full:
                    P.add("act", lambda e, src=src, kq=kq, tb=tb: e.activation(
                        x_f[:, kq * 4:kq * 4 + 4, tb * 128:tb * 128 + TB], src, AF.Copy),
                        reads=[bkr], writes=["x_f"])

        if x_next is not None:
            xn, tbn, ntbn = x_next
            P.dma("sp", xa[:tbn, 0:ntbn, :], xn.rearrange("(tb p) d -> p tb d", p=tbn), reads=["xT"], writes=["xa"])
            xa_loaded[0] = True
        yield "front"
        pv = preqkv[:, :, 0:nseq * (L + H)].rearrange("p c (s l) -> p c s l", s=nseq)
        ppv = ppre[:, :, 0:nseq * (L + 2)].rearrange("p c (s l) -> p c s l", s=nseq)
        ahv = ahalo[:, :, 0:nseq, :]

        def load_hist(src_d, nrows, r, width, dst, dstres):
            for c0 in range(0, width, 512):
                wd = min(512, width - c0)
                ncc = wd // 128
                P.dma("sp", hso[:nrows, 0:wd], src_d[:, c0:c0 + wd], writes=["hso"])
                bk, bkr = bank()
                for k in range(ncc):
                    P.add("pe", lambda e, bk=bk, k=k: e.transpose(
                        bk[:, k * 12:k * 12 + nrows], hso[:nrows, k * 128:(k + 1) * 128],
                        ident[:nrows, :nrows]), reads=["hso", "cst"], writes=[bkr])
                srcv = bk[:, 0:ncc * 12].rearrange("p (c x) -> p c x", x=12)[:, :, 0:nrows].rearrange(
                    "p c (s r) -> p c s r", r=r)
                P.add("dve", lambda e, srcv=srcv, c0=c0, ncc=ncc: e.tensor_copy(
                    dst[:, c0 // 128:c0 // 128 + ncc, :, :], srcv), reads=[bkr], writes=[dstres])

        if hist_src is not None:
            (h_gc, h_sc, h_ff) = hist_src
            load_hist(h_gc, nseq * 3, 3, 3072, pv[:, :, :, 0:3], "preqkv")
            load_hist(h_sc, nseq * 2, 2, D, ppv[:, :, :, 0:2], "ppre")
            load_hist(h_ff, nseq * 2, 2, DFF, ahv, "ahalo")
        else:
            P.add("pool", lambda e: e.tensor_copy(pv[:, :, :, 0:3], pv[:, :, :, L:L + 3]),
                  reads=["preqkv"], writes=["preqkv"])
            if full:
                P.add("pool", lambda e: e.tensor_copy(ppv[:, :, :, 0:2], ppv[:, :, :, L:L + 2]),
                      reads=["ppre"], writes=["ppre"])

        qpend = []
        qstage2 = []

        def qkv_flush():
            grp = list(qpend)
            del qpend[:]
            accs = []
            for (ch, ps, psr) in grp:
                P.add("act", lambda e, ch=ch, ps=ps: e.activation(pv[:, ch, :, 3:3 + L], v3(ps), AF.Copy),
                      reads=[psr], writes=[f"preqkv/{ch}"])
                accs.append(rot("cacc", NCACC))
            for (ch, ps, psr), i in zip(grp, accs):
                a3 = v3(cacc[i][:, 0:N])
                P.add("pool", lambda e, ch=ch, a3=a3: e.tensor_tensor(
                    a3, pv[:, ch, :, 0:L], cwg[:, ch, 0:1].unsqueeze(1).to_broadcast([128, nseq, L]), ALU.mult),
                    reads=[f"preqkv/{ch}", "prm"], writes=[f"cacc{i}"])
            for j in range(1, 4):
                for (ch, ps, psr), i in zip(grp, accs):
                    a3 = v3(cacc[i][:, 0:N])
                    P.add("dve", lambda e, j=j, ch=ch, a3=a3: e.scalar_tensor_tensor(
                        a3, pv[:, ch, :, j:j + L], cwg[:, ch, j:j + 1], a3, ALU.mult, ALU.add),
                        reads=[f"preqkv/{ch}", "prm", f"cacc{i}"], writes=[f"cacc{i}"])
            prev = list(qstage2)
            del qstage2[:]
            for th in prev:
                th()
            for (ch, ps, psr), i in zip(grp, accs):
                qstage2.append(lambda ch=ch, i=i: P.add(
                    "act", lambda e: e.activation(post[:, ch, 0:N], cacc[i][:, 0:N], AF.Silu),
                    reads=[f"cacc{i}"], writes=[f"post{ch}"]))

        def qkv_finish():
            if qpend:
                qkv_flush()
            prev = list(qstage2)
            del qstage2[:]
            for th in prev:
                th()

        def qkv_consume(ch, ps, psr):
            qpend.append((ch, ps, psr))
            if len(qpend) == 2:
                qkv_flush()

        l2pendB = []

        def l2_finish(keep=0):
            while len(l2pendB) > keep:
                l2pendB.pop(0)()

        def l2norm2(chs):
            for ch in chs:
                i = rot("sqb", 2)
                k = rot("cacc", NCACC)
                rs_ = cacc[k]
                P.add("pool", lambda e, ch=ch, i=i: e.tensor_tensor(
                    sqb[i][:, 0:N], post[:, ch, 0:N], post[:, ch, 0:N], ALU.mult),
                    reads=[f"post{ch}"], writes=[f"sqb{i}"])
                bk, bkr = bank()
                P.add("pe", lambda e, bk=bk, i=i: e.matmul(bk[:, 0:N], ones_bf[:], sqb[i][:, 0:N], start=True, stop=True),
                      reads=[f"sqb{i}", "ones_bf"], writes=[bkr])
                P.add("act", lambda e, bk=bk, rs_=rs_: e.activation(rs_[:, 0:N], bk[:, 0:N], AF.Ln, bias=epsn[:, 0:1]),
                      reads=[bkr, "epsc"], writes=[f"cacc{k}"])
                l2_finish(keep=2)

                def stageB(ch=ch, k=k, rs_=rs_):
                    P.add("act", lambda e: e.activation(rs_[:, 0:N], rs_[:, 0:N], AF.Exp, scale=-0.5),
                          reads=[f"cacc{k}"], writes=[f"cacc{k}"])
                    P.add("pool", lambda e: e.tensor_tensor(
                        post[:, ch, 0:N], post[:, ch, 0:N], rs_[:, 0:N], ALU.mult),
                        reads=[f"post{ch}", f"cacc{k}"], writes=[f"post{ch}"])
                l2pendB.append(stageB)

        for g in range(8):
            i, wv_ = wload(None, 2048, "p (k c) -> p k c", k=8)
            P.dma("sp", wst[i][:, 0:2048], win_g[4 + g], reads=WIN_KV, writes=[f"wst{i}"])
            proj(xT, "xT", wv_, f"wst{i}", 2, N, qkv_consume, cc0=8 + g * 2)
            yield "front"
        if full:
            for g in range(4):
                i, wv_ = wload(None, 2048, "p (k c) -> p k c", k=8)
                P.dma("sp", wst[i][:, 0:2048], win_g[g], reads=WIN_RES, writes=[f"wst{i}"])
                proj(xT, "xT", wv_, f"wst{i}", 2, N, qkv_consume, cc0=g * 2)
        qkv_finish()
        l2todo = []
        if full:
            l2todo = [[ch, ch + 1] for ch in range(0, 16, 2)]
        else:
            for ch in range(8, 16, 2):
                l2norm2([ch, ch + 1])
                yield "front"
            l2_finish()

        bk, bkr = bank()
        bav = bk[:C, 0:nch * 16].rearrange("p (c k) -> p c k", k=16)
        for c in range(nch):
            for kc in range(8):
                P.add("pe", lambda e, c=c, kc=kc: e.matmul(
                    bav[:, c, :], xT[:, kc, c * C:(c + 1) * C], wba[:, kc, :], start=(kc == 0), stop=(kc == 7)),
                    reads=["xT", "wba"], writes=[bkr])
        P.add("act", lambda e: e.activation(ba_beta[:C, 0:nch, :], bav[:, :, 0:8], AF.Sigmoid),
              reads=[bkr], writes=["ba_beta"])
        P.add("dve", lambda e: e.tensor_tensor(
            ba_t[:C, 0:nch, :], bav[:, :, 8:16], dtb[:C, :].unsqueeze(1).to_broadcast([C, nch, 8]), ALU.add),
            reads=[bkr, "dtb"], writes=["ba_t"])
        P.add("act", lambda e: e.activation(ba_t[:C, 0:nch, :], ba_t[:C, 0:nch, :], AF.Exp),
              reads=["ba_t"], writes=["ba_t"])
        P.add("act", lambda e: e.activation(ba_t[:C, 0:nch, :], ba_t[:C, 0:nch, :], AF.Ln, bias=epsn[:C, 2:3]),
              reads=["ba_t", "epsc"], writes=["ba_t"])
        P.add("dve", lambda e: e.tensor_tensor(
            ba_g[:C, 0:nch, :], ba_t[:C, 0:nch, :], nA[:C, :].unsqueeze(1).to_broadcast([C, nch, 8]), ALU.mult),
            reads=["ba_t", "nA"], writes=["ba_g"])
        P.add("dve", lambda e: e.tensor_copy(ba_gh[:C, 0:nch, :], ba_g[:C, 0:nch, :]), reads=["ba_g"], writes=["ba_gh"])
        P.add("dve", lambda e: e.tensor_tensor(ba_gl[:C, 0:nch, :], ba_g[:C, 0:nch, :], ba_gh[:C, 0:nch, :],
                                               ALU.subtract), reads=["ba_g", "ba_gh"], writes=["ba_gh"])

        if full:
            def mk_consume(kind):
                def f(ch, ps, psr):
                    if kind == "z":
                        P.add("act", lambda e: e.activation(szg[:, ch, 0:N], ps, AF.Silu),
                              reads=[psr], writes=[f"szg/{ch}"])
                    elif kind == "sB":
                        P.add("dve", lambda e: e.tensor_tensor(cfl[:, ch, 0:N], ps, cfl[:, ch, 0:N], ALU.mult),
                              reads=[psr, f"r1/{ch}"], writes=[f"r1/{ch}"])
                    elif kind == "sC":
                        P.add("act", lambda e: e.activation(sCs[:, ch, 0:N], ps, AF.Copy),
                              reads=[psr], writes=[f"mixg/{ch}"])
                    elif kind == "sH":
                        P.add("dve", lambda e: e.tensor_tensor(
                            ppv[:, ch, :, 2:2 + L], v3(ps), v3(sCs[:, ch, 0:N]), ALU.mult),
                            reads=[psr, f"mixg/{ch}"], writes=[f"ppre/{ch}"])
                        c3 = v3(cfl[:, ch, 0:N])
                        P.add("pool", lambda e: e.tensor_tensor(
                            c3, ppv[:, ch, :, 0:L], cws[:, ch, 0:1].unsqueeze(1).to_broadcast([128, nseq, L]), ALU.mult),
                              reads=[f"ppre/{ch}", "prm"], writes=[f"r1/{ch}"])
                        for j in range(1, 3):
                            P.add("dve", lambda e, j=j: e.scalar_tensor_tensor(
                                c3, ppv[:, ch, :, j:j + L], cws[:, ch, j:j + 1], c3, ALU.mult, ALU.add),
                                reads=[f"ppre/{ch}", "prm", f"r1/{ch}"], writes=[f"r1/{ch}"])
                    elif kind == "gA":
                        i = rot("cacc", NCACC)
                        P.add("act", lambda e: e.activation(cacc[i][:, 0:N], ps, AF.Sigmoid),
                              reads=[psr], writes=[f"cacc{i}"])
                        P.add("pool", lambda e: e.tensor_tensor(
                            szg[:, ch, 0:N], szg[:, ch, 0:N], cacc[i][:, 0:N], ALU.mult),
                            reads=[f"szg/{ch}", f"cacc{i}"], writes=[f"szg/{ch}"])
                    elif kind == "gB":
                        i = rot("cacc", NCACC)
                        P.add("act", lambda e: e.activation(cacc[i][:, 0:N], ps, AF.Sigmoid),
                              reads=[psr], writes=[f"cacc{i}"])
                        P.add("pool", lambda e: e.tensor_tensor(
                            cfl[:, ch, 0:N], cfl[:, ch, 0:N], cacc[i][:, 0:N], ALU.mult),
                            reads=[f"r1/{ch}", f"cacc{i}"], writes=[f"r1/{ch}"])
                return f

            for kind, base in [("z", 3072), ("sC", 5120), ("sH", 6144), ("sB", 4096), ("gA", 7168), ("gB", 8192)]:
                for g in range(4):
                    i, wv_ = wload(None, 2048, "p (k c) -> p k c", k=8)
                    P.dma("sp", wst[i][:, 0:2048], win_g[base // 256 + g], reads=WIN_RES, writes=[f"wst{i}"])
                    proj(xT, "xT", wv_, f"wst{i}", 2, N, mk_consume(kind), cc0=g * 2)
                    if l2todo and kind in ("sC", "sH"):
                        l2norm2(l2todo.pop(0))
                    if kind == "sH" and g == 3:
                        while l2todo:
                            l2norm2(l2todo.pop(0))
                        l2_finish()
            while l2todo:
                l2norm2(l2todo.pop(0))
            l2_finish()

        yield "front_done"
        nst = {64: 5, 16: 3}[C]
        def chunk(c):
            b = c % NB
            cs = slice(c * C, (c + 1) * C)
            Sc, Scb, Scr = S, Sb, Sres
            g_c = ba_g[:C, c, :]
            be_c = ba_beta[:C, c, :]
            bk, bkr = bank()
            gh_c = ba_gh[:C, c, :]
            gl_c = ba_gl[:C, c, :]
            for (oap, lt) in ((bk[:C, 0:8], tri_bf[:C, :C]), (bk[:C, 8:16], sgt_bf[:C, :C]), (bk[:, 16:24], ones_bf[:C, :])):
                P.add("pe", lambda e, oap=oap, lt=lt: e.matmul(oap, lt, gh_c, start=True, stop=False),
                      reads=["ones_bf", "ba_gh"], writes=[bkr])
                P.add("pe", lambda e, oap=oap, lt=lt: e.matmul(oap, lt, gl_c, start=False, stop=True),
                      reads=["ones_bf", "ba_gh"], writes=[bkr])
            yield
            P.add("act", lambda e, bk=bk: e.activation(eGt[b][:C, :], bk[:C, 0:16], AF.Exp),
                  reads=[bkr], writes=[f"eGt{b}"])
            P.add("act", lambda e, bk=bk: e.activation(eGl[b][:, :], bk[:, 16:24], AF.Exp),
                  reads=[bkr], writes=[f"eGl{b}"])
            P.add("pool", lambda e: e.tensor_tensor(bg[b][:C, :], be_c, eGt[b][:C, 0:8], ALU.mult),
                  reads=["ba_beta", f"eGt{b}"], writes=[f"bg{b}"])
            P.add("dve", lambda e: e.tensor_tensor(
                gtri[b][:C, 0, :, :C], tri[:C, :C].unsqueeze(1).to_broadcast([C, 8, C]),
                gh_c.unsqueeze(2).to_broadcast([C, 8, C]), ALU.mult),
                reads=["cst", "ba_gh"], writes=[f"gtri{b}"])
            P.add("dve", lambda e: e.tensor_tensor(
                gtri[b][:C, 1, :, :C], tri[:C, :C].unsqueeze(1).to_broadcast([C, 8, C]),
                gl_c.unsqueeze(2).to_broadcast([C, 8, C]), ALU.mult),
                reads=["cst", "ba_gh"], writes=[f"gtri{b}"])
            yield
            bkD, bkDr = bank()
            Dv = bkD[:C, 0:8 * C].rearrange("p (h c) -> p h c", h=8)
            for h in range(8):
                for hl in range(2):
                    P.add("pe", lambda e, h=h, Dv=Dv, hl=hl: e.matmul(Dv[:, h, :], gtri[b][:C, hl, h, :C], sgt_bf[:C, :C],
                                                                     start=(hl == 0), stop=(hl == 1)),
                          reads=[f"gtri{b}", "ones_bf"], writes=[bkDr])
            yield
            P.add("act", lambda e, Dv=Dv: e.activation(Gam[b][:C, :, :C], Dv, AF.Exp),
                  reads=[bkDr], writes=[f"Gam{b}"])
            P.add("pool", lambda e: e.tensor_tensor(
                Gam[b][:C, :, :C], Gam[b][:C, :, :C], sgt[:C, :C].unsqueeze(1).to_broadcast([C, 8, C]), ALU.mult),
                reads=[f"Gam{b}", "cst"], writes=[f"Gam{b}"])
            P.add("pool", lambda e: e.tensor_tensor(
                Gam[b][:C, :, :C], Gam[b][:C, :, :C], be_c.unsqueeze(2).to_broadcast([C, 8, C]), ALU.mult),
                reads=[f"Gam{b}", "ba_beta"], writes=[f"Gam{b}"])
            if full:
                bkT, bkTr = bank()
                DTv = bkT[:C, 0:8 * C].rearrange("p (h c) -> p h c", h=8)
                for h in range(8):
                    for hl in range(2):
                        P.add("pe", lambda e, h=h, DTv=DTv, hl=hl: e.matmul(
                            DTv[:, h, :], sgt_bf[:C, :C], gtri[b][:C, hl, h, :C], start=(hl == 0), stop=(hl == 1)),
                            reads=[f"gtri{b}", "ones_bf"], writes=[bkTr])
                P.add("act", lambda e, DTv=DTv: e.activation(GamT[b][:C, :, :C], DTv, AF.Exp),
                      reads=[bkTr], writes=[f"GamT{b}"])
                P.add("pool", lambda e: e.tensor_tensor(
                    GamT[b][:C, :, :C], GamT[b][:C, :, :C], tri[:C, :C].unsqueeze(1).to_broadcast([C, 8, C]),
                    ALU.mult), reads=[f"GamT{b}", "cst"], writes=[f"GamT{b}"])
            if STOP2 == "c_a":
                return
            yield
            bkk, bkkr = bank()
            kt = bkk[:, :].bitcast(BF16)[:C, :].rearrange("p (h d) -> p h d", h=8)
            for h in range(8):
                P.add("pe", lambda e, h=h, kt=kt: e.transpose(kt[:, h, :], post[:, 8 + h, cs], ident_bf[:]),
                      reads=[f"post{8 + h}", "ident_bf"], writes=[bkkr])
            yield
            P.add("dve", lambda e, kt=kt: e.tensor_tensor(
                kbg[b][:C], kt, bg[b][:C, :].unsqueeze(2).to_broadcast([C, 8, 128]), ALU.mult),
                reads=[bkkr, f"bg{b}"], writes=[f"kbg{b}"])
            P.add("dve", lambda e, kt=kt: e.tensor_tensor(
                ktail[b][:C], kt, eGt[b][:C, 8:16].unsqueeze(2).to_broadcast([C, 8, 128]), ALU.mult),
                reads=[bkkr, f"eGt{b}"], writes=[f"ktail{b}"])
            yield
            bkv, bkvr = bank()
            vt = bkv[:, :].bitcast(BF16)[:C, :].rearrange("p (h d) -> p h d", h=8)
            for h in range(8):
                P.add("pe", lambda e, h=h, vt=vt: e.transpose(vt[:, h, :], post[:, 16 + h, cs], ident_bf[:]),
                      reads=[f"post{16 + h}", "ident_bf"], writes=[bkvr])
            yield
            P.add("dve", lambda e, vt=vt: e.tensor_tensor(
                vb[b][:C], vt, be_c.unsqueeze(2).to_broadcast([C, 8, 128]), ALU.mult),
                reads=[bkvr, "ba_beta"], writes=[f"vb{b}"])
            if STOP2 == "c_f":
                return
            yield
            bka, bkar = bank()
            kkv = bka[:C, 0:8 * C].rearrange("p (h c) -> p h c", h=8)
            for h in range(8):
                P.add("pe", lambda e, h=h, kkv=kkv: e.matmul(kkv[:, h, :], post[:, 8 + h, cs], post[:, 8 + h, cs],
                                                            start=True, stop=True),
                      reads=[f"post{8 + h}"], writes=[bkar])
            yield
            P.add("dve", lambda e, kkv=kkv: e.tensor_tensor(Am[b][:C, :, :C], kkv, Gam[b][:C, :, :C], ALU.mult),
                  reads=[bkar, f"Gam{b}"], writes=[f"Am{b}"])
            if STOP2 == "c_g1":
                return
            yield
            bkb, bkbr = bank()
            atv = bkb[:C, 0:8 * C].rearrange("p (h c) -> p h c", h=8)
            for h in range(8):
                P.add("pe", lambda e, h=h, atv=atv: e.matmul(atv[:, h, :], Am[b][:C, h, :C], ident_bf[:C, :C],
                                                            start=True, stop=True),
                      reads=[f"Am{b}", "ident_bf"], writes=[bkbr])
            yield
            X0, Y0, R0 = Xm[b][0], Am[b], Rm[b][0]
            P.add("act", lambda e, atv=atv: e.activation(X0[:C, :, :C], atv, AF.Copy),
                  reads=[bkbr], writes=[f"Xm{b}_0"])
            P.add("dve", lambda e, atv=atv: e.scalar_tensor_tensor(
                R0[:C, :, :C], atv, epsn[:C, 3:4], ident[:C, :C].unsqueeze(1).to_broadcast([C, 8, C]), ALU.mult, ALU.add),
                reads=[bkbr, "cst", "epsc"], writes=[f"Rm{b}_0"])
            Xc, Xr, Yc, Yr, Rc, Rr = X0, f"Xm{b}_0", Y0, f"Am{b}", R0, f"Rm{b}_0"
            yield "presolve_done"
            for n in range(1, nst + 1):
                yield
                k = n % 2
                Yn, Ynr = Ym[b][k], f"Ym{b}_{k}"
                bky, bkyr = bank()
                yv = bky[:C, 0:8 * C].rearrange("p (h c) -> p h c", h=8)
                for h in range(8):
                    P.add("pe", lambda e, h=h, yv=yv, Xc=Xc, Yc=Yc: e.matmul(
                        yv[:, h, :], Xc[:C, h, :C], Yc[:C, h, :C], start=True, stop=True),
                        reads=[Xr, Yr], writes=[bkyr])
                yield
                P.add("act", lambda e, yv=yv, Yn=Yn: e.activation(Yn[:C, :, :C], yv, AF.Copy),
                      reads=[bkyr], writes=[Ynr])
                if n < nst:
                    Xn, Xnr = Xm[b][k], f"Xm{b}_{k}"
                    bkx, bkxr = bank()
                    xv = bkx[:C, 0:8 * C].rearrange("p (h c) -> p h c", h=8)
                    for h in range(8):
                        P.add("pe", lambda e, h=h, xv=xv, Xc=Xc, Yc=Yc: e.matmul(
                            xv[:, h, :], Yc[:C, h, :C], Xc[:C, h, :C], start=True, stop=True),
                            reads=[Xr, Yr], writes=[bkxr])
                    yield
                    P.add("dve", lambda e, xv=xv, Xn=Xn: e.tensor_copy(Xn[:C, :, :C], xv),
                          reads=[bkxr], writes=[Xnr])
                yield
                Rn, Rnr = Rm[b][0], f"Rm{b}_0"
                bkq, bkqr = bank()
                rv = bkq[:C, 0:8 * C].rearrange("p (h c) -> p h c", h=8)
                for h in range(8):
                    P.add("pe", lambda e, h=h, rv=rv, Rc=Rc, Yn=Yn: e.matmul(
                        rv[:, h, :], Yn[:C, h, :C], Rc[:C, h, :C], start=True, stop=True),
                        reads=[Ynr, Rr], writes=[bkqr])
                yield
                P.add("dve", lambda e, rv=rv, Rn=Rn, Rc=Rc: e.tensor_tensor(Rn[:C, :, :C], rv, Rc[:C, :, :C], ALU.add),
                      reads=[bkqr, Rr], writes=[Rnr])
                if n < nst:
                    Xc, Xr = Xn, Xnr
                Yc, Yr, Rc, Rr = Yn, Ynr, Rn, Rnr
            TT, TTr = Rc, Rr
            if STOP2 == "c_h":
                return
            yield
            bkw, bkwr = bank()
            wv2 = bkw[:, 0:8 * C].rearrange("p (h c) -> p h c", h=8)
            for h in range(8):
                P.add("pe", lambda e, h=h, wv2=wv2, TT=TT: e.matmul(
                    wv2[:, h, :], kbg[b][:C, h, :], TT[:C, h, :C], start=True, stop=True),
                    reads=[f"kbg{b}", TTr], writes=[bkwr])
            yield
            P.add("act", lambda e, wv2=wv2: e.activation(nwT[b][:, :, :C], wv2, AF.Copy, scale=-1.0),
                  reads=[bkwr], writes=[f"nwT{b}"])
            yield "chain"
            if S_src is not None:
                P.dma("sp", Sc[:], S_src[c].rearrange("h k v -> k h v"), writes=[Scr])
                P.add("act", lambda e, Sc=Sc, Scb=Scb: e.activation(Scb[:], Sc[:], AF.Copy),
                      reads=[Scr], writes=[Scr + "b"])
            vbanks = []
            for hh in range(2):
                bkn, bknr = bank()
                vn = bkn[:C, :].rearrange("p (h d) -> p h d", h=4)
                for h4 in range(4):
                    h = hh * 4 + h4
                    P.add("pe", lambda e, h=h, h4=h4, vn=vn, TT=TT: e.matmul(
                        vn[:, h4, :], TT[:C, h, :C], vb[b][:C, h, :], start=True, stop=False),
                        reads=[TTr, f"vb{b}"], writes=[bknr])
                    P.add("pe", lambda e, h=h, h4=h4, vn=vn, Scb=Scb: e.matmul(
                        vn[:, h4, :], nwT[b][:, h, :C], Scb[:, h, :], start=False, stop=True),
                        reads=[f"nwT{b}", f"{Scr}b/{h // 4}"], writes=[bknr])
                eng = "act" if hh == 0 else "dve"
                if eng == "act":
                    P.add("act", lambda e, vn=vn, hh=hh: e.activation(vnew[b][:C, hh * 4:hh * 4 + 4, :], vn, AF.Copy),
                          reads=[bknr], writes=["vnew0"])
                else:
                    P.add("dve", lambda e, vn=vn, hh=hh: e.tensor_copy(vnew[b][:C, hh * 4:hh * 4 + 4, :], vn),
                          reads=[bknr], writes=["vnew0"])
            o1b = []
            if full:
                for hh in range(2):
                    bko, bkor = bank()
                    ov = bko[:C, :].rearrange("p (h d) -> p h d", h=4)
                    for h4 in range(4):
                        h = hh * 4 + h4
                        P.add("pe", lambda e, h=h, h4=h4, ov=ov, Scb=Scb: e.matmul(
                            ov[:, h4, :], post[:, h, cs], Scb[:, h, :], start=True, stop=True),
                            reads=[f"post{h}", f"{Scr}b/{h // 4}"], writes=[bkor])
                    o1b.append((ov, bkor))
            for hh in range(2):
                bkd, bkdr = bank()
                dv = bkd[:, :].rearrange("p (h d) -> p h d", h=4)
                for h4 in range(4):
                    h = hh * 4 + h4
                    P.add("pe", lambda e, h=h, h4=h4, dv=dv: e.matmul(
                        dv[:, h4, :], ktail[b][:C, h, :], vnew[b][:C, h, :], start=True, stop=True),
                        reads=[f"ktail{b}", "vnew0"], writes=[bkdr])
                for h4 in range(4):
                    h = hh * 4 + h4
                    P.add("dve", lambda e, h=h, h4=h4, dv=dv, Sc=Sc: e.scalar_tensor_tensor(
                        Sc[:, h, :], Sc[:, h, :], eGl[b][:, h:h + 1], dv[:, h4, :], ALU.mult, ALU.add),
                        reads=[bkdr, f"{Scr}/{h}", f"eGl{b}"], writes=[f"{Scr}/{h}"])
                P.add("act", lambda e, Sc=Sc, Scb=Scb, hh=hh: e.activation(
                    Scb[:, hh * 4:hh * 4 + 4, :], Sc[:, hh * 4:hh * 4 + 4, :], AF.Copy),
                    reads=[f"{Scr}/{h_}" for h_ in range(hh * 4, hh * 4 + 4)], writes=[f"{Scr}b/{hh}"])
            if nseq > 1 and state_dst is not None:
                P.dma("sp", state_dst[1][c].rearrange("h k v -> k h v"), Sc[:], reads=[Scr], writes=["dram_out"])
            if full:
                for hh, (ov, bkor) in enumerate(o1b):
                    P.add("dve", lambda e, ov=ov, hh=hh: e.tensor_tensor(
                        o1s[b][:C, hh * 4:hh * 4 + 4, :], ov,
                        eGt[b][:C, hh * 4:hh * 4 + 4].unsqueeze(2).to_broadcast([C, 4, 128]), ALU.mult),
                        reads=[bkor, f"eGt{b}"], writes=["o1s0"])
                bkq2, bkq2r = bank()
                qv = bkq2[:C, 0:8 * C].rearrange("p (h c) -> p h c", h=8)
                for h in range(8):
                    P.add("pe", lambda e, h=h, qv=qv: e.matmul(qv[:, h, :], post[:, 8 + h, cs], post[:, h, cs],
                                                              start=True, stop=True),
                          reads=[f"post{8 + h}", f"post{h}"], writes=[bkq2r])
                P.add("dve", lambda e, qv=qv: e.tensor_tensor(qkT[b][:C, :, :C], qv, GamT[b][:C, :, :C], ALU.mult),
                      reads=[bkq2r, f"GamT{b}"], writes=["qkT0"])
                for hh in range(2):
                    bko, bkor = bank()
                    ov = bko[:C, :].rearrange("p (h d) -> p h d", h=4)
                    for h4 in range(4):
                        h = hh * 4 + h4
                        P.add("pe", lambda e, h=h, h4=h4, ov=ov: e.matmul(
                            ov[:, h4, :], qkT[b][:C, h, :C], vnew[b][:C, h, :], start=True, stop=True),
                            reads=["qkT0", "vnew0"], writes=[bkor])
                    P.add("dve", lambda e, ov=ov, hh=hh: e.tensor_tensor(
                        o1s[b][:C, hh * 4:hh * 4 + 4, :], ov, o1s[b][:C, hh * 4:hh * 4 + 4, :], ALU.add),
                        reads=[bkor, "o1s0"], writes=["o1s0"])
            if full:
                P.add("act", lambda e: e.activation(osq[b][:C], o1s[b][:C], AF.Square),
                      reads=["o1s0"], writes=["osq0"])
                P.add("dve", lambda e: e.reduce_sum(oss[b][:C, :], osq[b][:C], axis=AX.X),
                      reads=["osq0"], writes=[f"oss{b}"])
                P.add("dve", lambda e: e.tensor_scalar(oss[b][:C, :], oss[b][:C, :], 1.0 / 128, NORM_EPS * 128,
                                                       ALU.mult, ALU.add),
                      reads=[f"oss{b}"], writes=[f"oss{b}"])
                P.add("act", lambda e: e.activation(oss[b][:C, :], oss[b][:C, :], AF.Sqrt),
                      reads=[f"oss{b}"], writes=[f"oss{b}"])
                P.add("dve", lambda e: e.reciprocal(oss[b][:C, :], oss[b][:C, :]),
                      reads=[f"oss{b}"], writes=[f"oss{b}"])
                P.add("dve", lambda e: e.tensor_tensor(
                    onb[b][:C], o1s[b][:C], oss[b][:C, :].unsqueeze(2).to_broadcast([C, 8, 128]), ALU.mult),
                    reads=["o1s0", f"oss{b}"], writes=[f"onb{b % 2}"])
                yield "tail"
                bkt, bktr = bank()
                tv = bkt[:, 0:8 * C].rearrange("p (h c) -> p h c", h=8)
                for h in range(8):
                    P.add("pe", lambda e, h=h, tv=tv: e.matmul(tv[:, h, :], onb[b][:C, h, :], ident_bf[:C, :C],
                                                              start=True, stop=True),
                          reads=[f"onb{b % 2}", "ident_bf"], writes=[bktr])
                P.add("dve", lambda e, tv=tv: e.scalar_tensor_tensor(
                    mixg[:, :, cs], tv, ong, szg[:, :, cs], ALU.mult, ALU.mult),
                    reads=[bktr, "prm", "szg"], writes=["mixg"])

        gens = [chunk(c) for c in range(nch)]
        live = list(gens)
        while live:
            for g_ in list(live):
                if next(g_) == "presolve_done":
                    live.remove(g_)
        yield "presolve_done"
        live = list(gens)
        while live:
            for g_ in list(live):
                if next(g_) == "chain":
                    live.remove(g_)
            yield "solve"
        yield "solve_done"
        prev_tail = None
        for g_ in gens:
            alive = next(g_, None) is not None
            yield "chain"
            if prev_tail is not None:
                for _ in prev_tail:
                    pass
            prev_tail = g_ if alive else None
        if prev_tail is not None:
            for _ in prev_tail:
                pass
        yield "chain_done"

        if not full:
            return

        P.add("dve", lambda e: e.tensor_tensor(mixed[:, :, 0:N], mixg[:, :, 0:N], cfl[:, :, 0:N], ALU.add),
              reads=["mixg"] + ["r1"], writes=["mixed"])

        def layer_norm(rbuf, rres, gcol, bcol, outf, outfres):
            P.add("pool", lambda e: e.tensor_tensor(mixed[:, :, 0:N], rbuf[:, :, 0:N], rbuf[:, :, 0:N], ALU.mult),
                  reads=[rres], writes=["mixed"])
            P.add("act", lambda e: e.activation(mixg[:, :, 0:N], rbuf[:, :, 0:N], AF.Copy),
                  reads=[rres], writes=["mixg"])
            bk, bkr = bank()
            for oc in range(8):
                P.add("pe", lambda e, oc=oc, bk=bk: e.matmul(bk[:, 0:N], onesm_bf[:], mixg[:, oc, 0:N],
                                                            start=(oc == 0), stop=(oc == 7)),
                      reads=["mixg", "onesm_bf"], writes=[bkr])
            for oc in range(8):
                P.add("pe", lambda e, oc=oc, bk=bk: e.matmul(bk[:, 256:256 + N], onesm_bf[:], mixed[:, oc, 0:N],
                                                            start=(oc == 0), stop=(oc == 7)),
                      reads=["mixed", "onesm_bf"], writes=[bkr])
            P.add("act", lambda e, bk=bk: e.activation(mean_sb[:, 0:N], bk[:, 0:N], AF.Copy),
                  reads=[bkr], writes=["mean_sb"])
            P.add("pool", lambda e: e.tensor_tensor(m2[:, 0:N], mean_sb[:, 0:N], mean_sb[:, 0:N], ALU.mult),
                  reads=["mean_sb"], writes=["m2"])
            P.add("dve", lambda e, bk=bk: e.tensor_tensor(rstd[:, 0:N], bk[:, 256:256 + N], m2[:, 0:N], ALU.subtract),
                  reads=[bkr, "m2"], writes=["rstd"])
            P.add("act", lambda e: e.activation(rstd[:, 0:N], rstd[:, 0:N], AF.Sqrt, bias=epsn[:, 1:2]),
                  reads=["rstd", "epsc"], writes=["rstd"])
            P.add("dve", lambda e: e.reciprocal(rstd[:, 0:N], rstd[:, 0:N]),
                  reads=["rstd"], writes=["rstd"])
            P.add("pool", lambda e: e.tensor_tensor(
                rbuf[:, :, 0:N], rbuf[:, :, 0:N], mean_sb[:, 0:N].unsqueeze(1).to_broadcast([128, 8, N]),
                ALU.subtract), reads=[rres, "mean_sb"], writes=[rres])
            P.add("pool", lambda e: e.tensor_tensor(
                rbuf[:, :, 0:N], rbuf[:, :, 0:N], rstd[:, 0:N].unsqueeze(1).to_broadcast([128, 8, N]),
                ALU.mult), reads=[rres, "rstd"], writes=[rres])
            for oc in range(8):
                P.add("dve", lambda e, oc=oc: e.tensor_scalar(
                    outf[:, oc, 0:N], rbuf[:, oc, 0:N], lnp[:, oc, gcol:gcol + 1], lnp[:, oc, bcol:bcol + 1],
                    ALU.mult, ALU.add), reads=[f"{rres}/{oc}", "prm"], writes=[f"{outfres}/{oc}"])

        def wo_consume(oc, ps, psr):
            P.add("dve", lambda e: e.scalar_tensor_tensor(r1[:, oc, 0:N], x_f[:, oc, 0:N], epsn[:, 4:5], ps, ALU.mult, ALU.add),
                  reads=[psr, "x_f"], writes=[f"r1/{oc}"])

        for g in range(4):
            i, wv_ = wload(None, 2048, "p (k c) -> p k c", k=8)
            P.dma("sp", wst[i][:, 0:2048], wo_g[g], reads=WO_RES, writes=[f"wst{i}"])
            proj(mixed, "mixed", wv_, f"wst{i}", 2, N, wo_consume, cc0=g * 2)
        layer_norm(r1, "r1", 0, 1, x1f, "r1")
        P.add("act", lambda e: e.activation(x1b[:, :, 0:N], x1f[:, :, 0:N], AF.Copy), reads=["r1"], writes=["mixed"])

        ffn_pend = []
        for c in range(NFC):
            i, wv_ = wload(None, 2048, "p (k c) -> p k c", k=8)
            P.dma("sp", wst[i][:, 0:2048], wup_g[c], reads=WUP_RES, writes=[f"wst{i}"])
            pbanks = []
            for half in range(2):
                bk, bkr = bank()
                for kc in range(8):
                    P.add("pe", lambda e, bk=bk, kc=kc, wv_=wv_, half=half: e.matmul(
                        bk[:, 0:N], wv_[:, kc, half * 128:(half + 1) * 128], x1b[:, kc, 0:N],
                        start=(kc == 0), stop=(kc == 7)), reads=["mixed", f"wst{i}"], writes=[bkr])
                pbanks.append((bk, bkr))
            (abk, abkr), (vbk, vbkr) = pbanks
            ai = rot("apre", 4)
            k = rot("cacc", NCACC)
            apv = apre[ai][:, 0:nseq * (L + 2)].rearrange("p (s l) -> p s l", s=nseq)
            a3 = v3(cacc[k][:, 0:N])
            P.add("pool", lambda e, c=c, apv=apv: e.tensor_copy(apv[:, :, 0:2], ahv[:, c, :, :]),
                  reads=[f"ahalo/{c}"], writes=[f"apre{ai}"])
            P.add("act", lambda e, apv=apv, abk=abk: e.activation(apv[:, :, 2:2 + L], v3(abk[:, 0:N]), AF.Copy),
                  reads=[abkr], writes=[f"apre{ai}"])
            P.add("pool", lambda e, c=c, apv=apv: e.tensor_copy(ahv[:, c, :, :], apv[:, :, L:L + 2]),
                  reads=[f"apre{ai}"], writes=[f"ahalo/{c}"])
            P.add("pool", lambda e, c=c, apv=apv, a3=a3: e.tensor_tensor(
                a3, apv[:, :, 0:L], cwf[:, c, 0:1].unsqueeze(1).to_broadcast([128, nseq, L]), ALU.mult),
                reads=[f"apre{ai}", "prm"], writes=[f"cacc{k}"])
            for j in range(1, 3):
                P.add("dve", lambda e, j=j, c=c, apv=apv, a3=a3: e.scalar_tensor_tensor(
                    a3, apv[:, :, j:j + L], cwf[:, c, j:j + 1], a3, ALU.mult, ALU.add),
                    reads=[f"apre{ai}", "prm", f"cacc{k}"], writes=[f"cacc{k}"])
            prev = list(ffn_pend)
            del ffn_pend[:]
            for th in prev:
                th()

            def stage2(c=c, ai=ai, k=k, vbk=vbk, vbkr=vbkr):
                P.add("act", lambda e: e.activation(gab[ai][:, 0:N], cacc[k][:, 0:N], AF.Gelu),
                      reads=[f"cacc{k}"], writes=[f"gab{ai}"])
                P.add("dve", lambda e: e.tensor_tensor(hb[:, c, 0:N], vbk[:, 0:N], gab[ai][:, 0:N], ALU.mult),
                      reads=[vbkr, f"gab{ai}"], writes=[f"hb/{c}"])
            ffn_pend.append(stage2)
        for th in ffn_pend:
            th()

        for cg in range(4):
            bk0, bkr0 = bank()
            bk1, bkr1 = bank()
            bks, bkrs = (bk0, bk1), (bkr0, bkr1)
            for kg in range(3):
                nk = 8 if kg < 2 else NFC - 16
                i, wv_ = wload(None, nk * 256, "p (k c) -> p k c", k=nk)
                P.dma("sp", wst[i][:, 0:nk * 256], wdn_g[cg * 3 + kg][:, 0:nk * 256], reads=WDN_RES, writes=[f"wst{i}"])
                for o2 in range(2):
                    for k8 in range(nk):
                        kc = kg * 8 + k8
                        P.add("pe", lambda e, bk=bks[o2], kc=kc, k8=k8, o2=o2, wv_=wv_: e.matmul(
                            bk[:, 0:N], wv_[:, k8, o2 * 128:(o2 + 1) * 128], hb[:, kc, 0:N],
                            start=(kc == 0), stop=(kc == NFC - 1)),
                            reads=[f"hb/{kc}", f"wst{i}"], writes=[bkrs[o2]])
            for o2 in range(2):
                oc = cg * 2 + o2
                P.add("dve", lambda e, bk=bks[o2], oc=oc, o2=o2: e.scalar_tensor_tensor(
                    r1[:, oc, 0:N], r1[:, oc, 0:N], epsn[:, 4:5], bk[:, 0:N], ALU.mult, ALU.add),
                    reads=[bkrs[o2], f"r1/{oc}"], writes=[f"r1/{oc}"])
        layer_norm(r1, "r1", 2, 3, x_f, "x_f")

        if write_y:
            for tb in range(ntb):
                for kq in range(2):
                    bk, bkr = bank()
                    for k4 in range(4):
                        oc = kq * 4 + k4
                        P.add("pe", lambda e, bk=bk, k4=k4, oc=oc, tb=tb: e.transpose(
                            bk[:TB, k4 * 128:(k4 + 1) * 128], x_f[:, oc, tb * 128:tb * 128 + TB], ident[:, :]),
                            reads=["x_f", "cst"], writes=[bkr])
                    P.add("act" if kq == 0 else "dve",
                          (lambda e, bk=bk, kq=kq, tb=tb: e.activation(yout[:TB, tb, kq * 512:(kq + 1) * 512], bk[:TB, :], AF.Copy))
                          if kq == 0 else
                          (lambda e, bk=bk, kq=kq, tb=tb: e.tensor_copy(yout[:TB, tb, kq * 512:(kq + 1) * 512], bk[:TB, :])),
                          reads=[bkr], writes=["xa"])
            P.dma("sp", y_dst.rearrange("(tb p) d -> p tb d", p=TB), yout[:TB, 0:ntb, :], reads=["xa"],
                  writes=["dram_out"])

        if state_dst is not None:
            d_gc, d_S, d_sc, d_ff = state_dst

            def store_state(dst_d, nrows, r, width, srcv, srcres):
                nchk = width // 128
                hv = hs[:, 0:nchk, 0:nrows].rearrange("p c (s r) -> p c s r", r=r)
                P.add("pool", lambda e: e.tensor_copy(hv, srcv), reads=[srcres], writes=["hs"])
                for c0 in range(0, width, 512):
                    wd = min(512, width - c0)
                    for c1 in range(0, wd, 512):
                        w2 = min(512, wd - c1)
                        bk, bkr = bank()
                        for k4 in range(w2 // 128):
                            ch = (c0 + c1) // 128 + k4
                            P.add("pe", lambda e, bk=bk, k4=k4, ch=ch: e.transpose(
                                bk[:nrows, k4 * 128:(k4 + 1) * 128], hs[:, ch, 0:nrows], ident[:, :]),
                                reads=["hs", "cst"], writes=[bkr])
                        P.add("dve", lambda e, bk=bk, c1=c1, w2=w2: e.tensor_copy(
                            hso[:nrows, c1:c1 + w2], bk[:nrows, 0:w2]), reads=[bkr], writes=["hso"])
                    P.dma("sp", dst_d[:, c0:c0 + wd], hso[:nrows, 0:wd], reads=["hso"], writes=["dram_out"])

            store_state(d_gc, nseq * 3, 3, 3072, pv[:, :, :, L:L + 3], "preqkv")
            store_state(d_sc, nseq * 2, 2, D, ppv[:, :, :, L:L + 2], "ppre")
            store_state(d_ff, nseq * 2, 2, DFF, ahv, "ahalo")
            if nseq == 1:
                P.dma("sp", d_S.rearrange("h k v -> k h v"), S[:], reads=[Sres], writes=["dram_out"])

    specs = []
    if STOP != "setup":
        for t in range(NPRE):
            specs.append(dict(full=False, x=xp[t * NT:(t + 1) * NT, :], tb=128, ntb=2, kw=dict(
                nseq=1, L=NT, C=64, full=False, S=Sp, Sb=Spb, Sres="Sp", y_dst=None)))
    if STOP not in ("setup", "pre"):
        for t in range(NFULL):
            r0 = (NPRE + t) * NT
            last = (t == NFULL - 1)
            specs.append(dict(full=True, x=xp[r0:r0 + NT, :], tb=128, ntb=2, kw=dict(
                nseq=1, L=NT, C=64, full=True, S=Sp, Sb=Spb, Sres="Sp",
                y_dst=yp[(t - 1) * NT:t * NT, :] if t > 0 else None,
                state_dst=(p_gc, p_S, p_sc, p_ff) if last else None, write_y=(t > 0))))
    if nsamp and STOP is None:
        specs.append(dict(full=True, x=xs, tb=nsamp * 16, ntb=1, kw=dict(
            nseq=nsamp, L=16, C=16, full=True, S=Ss, Sb=Ssb, Sres="Sp", y_dst=ys, hist_src=(st_gc, st_sc, st_ff),
            state_dst=(s_gc, s_S, s_sc, s_ff), write_y=True, S_src=st_S)))
    pend = None
    PEND_PER_FRONT = 1
    FRONT_PER_PEND = 1
    DEFER_AFTER = "chain_done"
    pend_last = [None]
    for ti, sp_ in enumerate(specs):
        P.epoch = ti
        nxt = specs[ti + 1] if ti + 1 < len(specs) else None
        x_next = (nxt["x"], nxt["tb"], nxt["ntb"]) if (nxt is not None and not sp_["full"]) else None
        njobs = 3 if not sp_["full"] else len(cast_jobs)
        for _ in range(min(njobs, len(cast_jobs))):
            cast_jobs.pop(0)()
        g = tile(sp_["x"], x_next=x_next, **sp_["kw"])
        fstep = 0
        while True:
            if fstep % FRONT_PER_PEND == 0:
                for _ in range(PEND_PER_FRONT):
                    if pend is not None:
                        cur_pool[0] = "pend"
                        if next(pend, None) is None:
                            pend = None
            fstep += 1
            cur_pool[0] = "front" if pend is not None else "all"
            m = next(g)
            if m == "front_done":
                break
        if pend is not None:
            cur_pool[0] = "pend"
            for _ in pend:
                pass
            pend = None
        cur_pool[0] = "all"
        while next(g) != DEFER_AFTER:
            pass
        if sp_["full"]:
            for _ in g:
                pass
        else:
            pend = g
            pend_last[0] = "presolve_done"
    if pend is not None:
        cur_pool[0] = "pend"
        for _ in pend:
            pass

    P.emit(nc, es)
    es.close()
    return nc


def _consts():
    c = np.zeros((128, 384), np.float32)
    c[:, 0:128] = np.eye(128, dtype=np.float32)
    t = np.arange(64)
    c[:64, 128:192] = (t[:, None] <= t[None, :]).astype(np.float32)
    c[:64, 192:256] = (t[:, None] > t[None, :]).astype(np.float32)
    c[:, 256:384] = 1.0
    return c


def make_in_maps(inp, NPRE, NFULL, ncores=8):
    ROWS = (NPRE + NFULL) * NT
    NOWN = (NFULL - 1) * NT
    xp = np.asarray(inp["x_prompt"], np.float32)
    xs = np.asarray(inp["x_sample"], np.float32)
    nseg = xp.shape[1] // NOWN
    lnp = np.stack([np.asarray(inp[k], np.float32)[0] for k in ("ln1_g", "ln1_b", "ln2_g", "ln2_b")])
    shared = {
        "w_in": np.ascontiguousarray(inp["w_in"][0]), "gdn_conv_w": np.ascontiguousarray(inp["gdn_conv_w"][0]),
        "a_log": np.ascontiguousarray(inp["a_log"]), "dt_bias": np.ascontiguousarray(inp["dt_bias"]),
        "o_norm_g": np.ascontiguousarray(inp["o_norm_g"]), "sc_conv_w": np.ascontiguousarray(inp["sc_conv_w"][0]),
        "w_o": np.ascontiguousarray(inp["w_o"][0]), "lnp": lnp, "w_up": np.ascontiguousarray(inp["w_up"][0]),
        "ffn_conv_w": np.ascontiguousarray(inp["ffn_conv_w"][0]), "w_down": np.ascontiguousarray(inp["w_down"][0]),
        "consts": _consts(),
    }
    maps = []
    for core in range(ncores):
        b, j = core // nseg, core % nseg
        end = NOWN * (j + 1)
        x_ext = np.zeros((ROWS, D), np.float32)
        n = min(end, ROWS)
        x_ext[ROWS - n:] = xp[b, end - n:end]
        m = dict(shared)
        m["xp"] = x_ext
        m["xs"] = np.ascontiguousarray(xs[4 * core:4 * core + 4].reshape(64, D))
        m["st_gc"] = np.ascontiguousarray(inp["state_gdn_conv"][0, 4 * core:4 * core + 4].reshape(12, 3072))
        m["st_S"] = np.ascontiguousarray(inp["state_gdn_S"][0, 4 * core:4 * core + 4])
        m["st_sc"] = np.ascontiguousarray(inp["state_sc_conv"][0, 4 * core:4 * core + 4].reshape(8, D))
        m["st_ff"] = np.ascontiguousarray(inp["state_ffn_conv"][0, 4 * core:4 * core + 4].reshape(8, DFF))
        maps.append(m)
    return maps


_NC_CACHE = {}


def kernel(**inp):
    NPRE, NFULL = 47, 17
    key = (NPRE, NFULL)
    if key not in _NC_CACHE:
        _NC_CACHE[key] = build(NPRE, NFULL)
    nc = _NC_CACHE[key]
    maps = make_in_maps(inp, NPRE, NFULL)
    res = run_bass_kernel_spmd(nc, maps, core_ids=list(range(8))).results
    B, SEQ = 2, 16384
    yp = np.zeros((B, SEQ, D), np.float32)
    ys = np.zeros((32, 16, D), np.float32)
    p_gc = np.zeros((1, B, 3, 3072), np.float32)
    p_S = np.zeros((1, B, 8, 128, 128), np.float32)
    p_sc = np.zeros((1, B, 2, D), np.float32)
    p_ff = np.zeros((1, B, 2, DFF), np.float32)
    s_gc = np.zeros((1, 32, 3, 3072), np.float32)
    s_S = np.zeros((1, 32, 8, 128, 128), np.float32)
    s_sc = np.zeros((1, 32, 2, D), np.float32)
    s_ff = np.zeros((1, 32, 2, DFF), np.float32)
    for core in range(8):
        r = res[core]
        b, j = core // 4, core % 4
        yp[b, 4096 * j:4096 * (j + 1)] = r["yp"]
        ys[4 * core:4 * core + 4] = r["ys"].reshape(4, 16, D)
        s_gc[0, 4 * core:4 * core + 4] = r["s_gc"].reshape(4, 3, 3072)
        s_S[0, 4 * core:4 * core + 4] = r["s_S"]
        s_sc[0, 4 * core:4 * core + 4] = r["s_sc"].reshape(4, 2, D)
        s_ff[0, 4 * core:4 * core + 4] = r["s_ff"].reshape(4, 2, DFF)
        if j == 3:
            p_gc[0, b] = r["p_gc"]
            p_S[0, b] = r["p_S"]
            p_sc[0, b] = r["p_sc"]
            p_ff[0, b] = r["p_ff"]
    return (yp, ys, p_gc, p_S, p_sc, p_ff, s_gc, s_S, s_sc, s_ff)
```

```python
import numpy as np
from contextlib import ExitStack
import concourse.bass as bass
import concourse.mybir as mybir
from concourse.bass_utils import run_bass_kernel_spmd

F32 = mybir.dt.float32
BF16 = mybir.dt.bfloat16
ALU = mybir.AluOpType
AF = mybir.ActivationFunctionType
AX = mybir.AxisListType

D = 1024
NT = 256
DFF = 2816
NFC = 22
ALPHA = 2.0 ** 0.25
LN_EPS = 1e-5
NORM_EPS = 1e-6
ENGS = ("pe", "act", "dve", "pool", "sp")
NROT = 4
NDMASEM = 40
STOP = None
STOP2 = None
PEND_MODE = None


BANK_GEN = {}


class BankRes(str):
    def __new__(cls, s, gen):
        o = str.__new__(cls, s)
        o.gen = gen
        return o


class Op:
    __slots__ = ("eng", "fn", "waits", "inc", "epoch", "dma", "cnt")


class Prog:
    def __init__(self):
        self.ops = {e: [] for e in ENGS}
        self.lastw = {}
        self.rd = {}
        self.children = {}
        self.waited = {}
        self.dwaited = {}
        self.epoch = 0
        self.dma_vals = [0] * NDMASEM
        self.dma_rr = 0
        self.pool_dmas = 0

    @staticmethod
    def _norm(reads, writes):
        ps = [r for r in reads if r.startswith("ps")]
        if ps:
            reads = [r for r in reads if not r.startswith("ps")]
            writes = list(writes) + [p for p in ps if p not in writes]
        return reads, writes

    def _rel(self, r):
        if "/" in r:
            par = r.split("/")[0]
            self.children.setdefault(par, set()).add(r)
            return (r, par)
        ch = self.children.get(r)
        return (r,) + tuple(ch) if ch else (r,)

    def _mk(self, eng, fn, reads, writes, extra=()):
        deps = list(extra)
        for r in reads:
            for q in self._rel(r):
                t = self.lastw.get(q)
                if t is not None:
                    deps.append(t)
        for w in writes:
            for q in self._rel(w):
                t = self.lastw.get(q)
                if t is not None:
                    deps.append(t)
                deps.extend(self.rd.get(q, ()))
        op = Op()
        op.eng, op.fn, op.inc, op.epoch, op.dma, op.cnt = eng, fn, False, self.epoch, None, None
        waits = []
        best = {}
        for t in deps:
            if t[0] == "op":
                _, e2, i2 = t
                if e2 == eng and eng == "pe":
                    continue
                if i2 > best.get(e2, -1):
                    best[e2] = i2
            else:
                _, k, v = t
                if self.dwaited.get((eng, k), 0) < v:
                    self.dwaited[(eng, k)] = v
                    waits.append(("dma", k, v))
        for e2, i2 in best.items():
            if self.waited.get((eng, e2), -1) < i2:
                self.waited[(eng, e2)] = i2
                self.ops[e2][i2].inc = True
                waits.append(("op", e2, i2))
        op.waits = waits
        return op

    def _commit(self, tok, reads, writes):
        for r in reads:
            self.rd.setdefault(r, []).append(tok)
        for w in writes:
            self.lastw[w] = tok
            self.rd[w] = []

    def add(self, eng, fn, reads=(), writes=()):
        for r in list(reads) + list(writes):
            if isinstance(r, BankRes):
                assert BANK_GEN[str(r)] == r.gen, f"stale PSUM bank {r} gen {r.gen} != {BANK_GEN[str(r)]}"
        reads, writes = self._norm(reads, writes)
        op = self._mk(eng, fn, reads, writes)
        idx = len(self.ops[eng])
        self.ops[eng].append(op)
        self._commit(("op", eng, idx), reads, writes)

    def dma(self, eng, out, in_, reads=(), writes=()):
        if eng == "pool":
            op = self._mk(eng, lambda e: e.dma_start(out=out, in_=in_), reads, writes)
            self.pool_dmas += 1
            op.dma = ("p", 0)
            self.ops[eng].append(op)
            self._commit(("dma", "p", 1), reads, writes)
            return
        k = self.dma_rr
        self.dma_rr = (k + 1) % NDMASEM
        prev = self.dma_vals[k]
        extra = [("dma", k, prev)] if prev > 0 else []
        op = self._mk(eng, lambda e: e.dma_start(out=out, in_=in_), reads, writes, extra)
        self.dma_vals[k] = prev + 16
        op.dma = (k, prev + 16)
        self.ops[eng].append(op)
        self._commit(("dma", k, prev + 16), reads, writes)

    def emit(self, nc, es):
        sems = {e: [es.enter_context(nc.semaphore(f"s_{e}{r}")) for r in range(NROT)] for e in ENGS}
        dsems = [es.enter_context(nc.semaphore(f"d{k}")) for k in range(NDMASEM)]
        psem = es.enter_context(nc.semaphore("pdma"))
        ptotal = 16 * self.pool_dmas
        for e in ENGS:
            cnt = [0] * NROT
            for op in self.ops[e]:
                if op.inc:
                    r = op.epoch % NROT
                    cnt[r] += 1
                    op.cnt = (r, cnt[r])
        final_vals = list(self.dma_vals)

        def run(e, eo):
            for op in self.ops[e]:
                for w in op.waits:
                    if w[0] == "op":
                        d = self.ops[w[1]][w[2]]
                        eo.wait_ge(sems[w[1]][d.cnt[0]], d.cnt[1])
                    elif w[1] == "p":
                        eo.wait_ge(psem, ptotal)
                    else:
                        eo.wait_ge(dsems[w[1]], w[2])
                ins = op.fn(eo)
                if op.dma is not None and op.dma[0] == "p":
                    ins.then_inc(psem, 16)
                elif op.dma is not None:
                    ins.then_inc(dsems[op.dma[0]], 16)
                elif op.inc:
                    ins.then_inc(sems[e][op.cnt[0]], 1)
            if e == "sp":
                for k, v in enumerate(final_vals):
                    if v > 0:
                        eo.wait_ge(dsems[k], v)
                if ptotal:
                    eo.wait_ge(psem, ptotal)

        block = es.enter_context(nc.Block())

        @block.tensor
        def _(eo):
            run("pe", eo)

        @block.scalar
        def _(eo):
            run("act", eo)

        @block.vector
        def _(eo):
            run("dve", eo)

        @block.gpsimd
        def _(eo):
            run("pool", eo)

        @block.sync
        def _(eo):
            run("sp", eo)


def build(NPRE, NFULL, nsamp=4):
    BANK_GEN.clear()
    nc = bass.Bass("TRN2", target_bir_lowering=False)
    P = Prog()
    es = ExitStack()
    ROWS = (NPRE + NFULL) * NT
    NOWN = (NFULL - 1) * NT

    def din(name, shape):
        return nc.dram_tensor(name, list(shape), F32, kind="ExternalInput").ap()

    def dout(name, shape):
        return nc.dram_tensor(name, list(shape), F32, kind="ExternalOutput").ap()

    xp = din("xp", [ROWS, D])
    xs = din("xs", [nsamp * 16, D])
    st_gc = din("st_gc", [nsamp * 3, 3072])
    st_S = din("st_S", [nsamp, 8, 128, 128])
    st_sc = din("st_sc", [nsamp * 2, D])
    st_ff = din("st_ff", [nsamp * 2, DFF])
    w_in = din("w_in", [D, 9232])
    gdn_conv_w = din("gdn_conv_w", [4, 3072])
    a_log = din("a_log", [1, 8])
    dt_bias = din("dt_bias", [1, 8])
    o_norm_g = din("o_norm_g", [1, 128])
    sc_conv_w = din("sc_conv_w", [3, D])
    w_o = din("w_o", [D, D])
    lnp_d = din("lnp", [4, D])
    w_up = din("w_up", [D, 2 * DFF])
    ffn_conv_w = din("ffn_conv_w", [3, DFF])
    w_down = din("w_down", [DFF, D])
    consts = din("consts", [128, 384])

    yp = dout("yp", [NOWN, D])
    ys = dout("ys", [nsamp * 16, D])
    p_gc = dout("p_gc", [3, 3072])
    p_S = dout("p_S", [8, 128, 128])
    p_sc = dout("p_sc", [2, D])
    p_ff = dout("p_ff", [2, DFF])
    s_gc = dout("s_gc", [nsamp * 3, 3072])
    s_S = dout("s_S", [nsamp, 8, 128, 128])
    s_sc = dout("s_sc", [nsamp * 2, D])
    s_ff = dout("s_ff", [nsamp * 2, DFF])

    win_g = nc.dram_tensor("win_g", [36, 128, 2048], BF16).ap()
    wo_g = nc.dram_tensor("wo_g", [4, 128, 2048], BF16).ap()
    wup_g = nc.dram_tensor("wup_g", [NFC, 128, 2048], BF16).ap()
    wdn_g = nc.dram_tensor("wdn_g", [12, 128, 2048], BF16).ap()

    def sb(name, shape, dt=F32):
        return es.enter_context(nc.sbuf_tensor(name, list(shape), dt))

    banks = [es.enter_context(nc.psum_tensor(f"bank{i}", [128, 512], F32)) for i in range(8)]
    POOLS = {"all": list(range(8)), "front": [4, 5, 6, 7], "pend": [0, 1, 2, 3]}
    bank_rr = {"all": 0, "front": 0, "pend": 0}
    cur_pool = ["all"]

    def bank():
        pn = cur_pool[0]
        pool = POOLS[pn]
        i = pool[bank_rr[pn] % len(pool)]
        bank_rr[pn] += 1
        BANK_GEN[f"ps{i}"] = BANK_GEN.get(f"ps{i}", 0) + 1
        return banks[i], BankRes(f"ps{i}", BANK_GEN[f"ps{i}"])

    rot_state = {}

    def rot(name, n):
        i = rot_state.get(name, 0)
        rot_state[name] = (i + 1) % n
        return i

    cst = sb("cst", [128, 384])
    ident = cst[:, 0:128]
    tri = cst[:64, 128:192]
    sgt = cst[:64, 192:256]
    ones = cst[:, 256:384]
    ident_bf = sb("ident_bf", [128, 128], BF16)
    ones_bf = sb("ones_bf", [128, 128], BF16)
    trisg_bf = sb("trisg_bf", [64, 128], BF16)
    tri_bf = trisg_bf[:, 0:64]
    sgt_bf = trisg_bf[:, 64:128]
    onesm = sb("onesm", [128, 128])
    prm = sb("prm", [128, 224])
    epsn = sb("epsn", [128, 8])
    dtb = sb("dtb", [128, 8])
    nA = sb("nA", [128, 8])
    hso = sb("hso", [12, 512])

    P.dma("sp", cst[:], consts[:, :], writes=["cst"])
    P.dma("sp", hso[0:1, 0:8], dt_bias[:, :], writes=["hso"])
    P.dma("sp", hso[0:1, 8:16], a_log[:, :], writes=["hso"])
    bk, bkr = bank()
    P.add("pe", lambda e, bk=bk: e.matmul(bk[:, 0:16], ones[0:1, :], hso[0:1, 0:16], start=True, stop=True),
          reads=["hso", "cst"], writes=[bkr])
    P.add("dve", lambda e, bk=bk: e.tensor_copy(dtb[:], bk[:, 0:8]), reads=[bkr], writes=["dtb"])
    P.add("dve", lambda e, bk=bk: e.tensor_copy(nA[:], bk[:, 8:16]), reads=[bkr], writes=["nA"])

    WIN_RES = [f"win_sA{r}" for r in range(8)] + [f"win_sB{r}" for r in range(8)]
    WIN_KV = [f"win_kv{r}" for r in range(8)]
    WO_RES = [f"wo_s{r}" for r in range(8)]
    WUP_RES = [f"wup_sA{r}" for r in range(8)] + [f"wup_sB{r}" for r in range(8)]
    WDN_RES = [f"wdn_s{r}" for r in range(NFC)]
    wba = sb("wba", [128, 8, 16], BF16)

    P.add("pool", lambda e: e.memset(epsn[:, 0:1], NORM_EPS), writes=["epsc"])
    P.add("pool", lambda e: e.memset(epsn[:, 1:2], LN_EPS), writes=["epsc"])
    P.add("pool", lambda e: e.memset(epsn[:, 2:3], 1.0), writes=["epsc"])
    P.add("pool", lambda e: e.memset(epsn[:, 3:4], -1.0), writes=["epsc"])
    P.add("pool", lambda e: e.memset(epsn[:, 4:5], ALPHA), writes=["epsc"])
    P.add("dve", lambda e: e.tensor_copy(ident_bf[:], ident), reads=["cst"], writes=["ident_bf"])
    P.add("dve", lambda e: e.tensor_copy(ones_bf[:], ones), reads=["cst"], writes=["ones_bf"])
    P.add("dve", lambda e: e.tensor_copy(trisg_bf[:], cst[:64, 128:256]), reads=["cst"], writes=["ones_bf"])
    P.add("dve", lambda e: e.tensor_scalar(onesm[:], ones, 1.0 / D, None, ALU.mult), reads=["cst"], writes=["onesm"])
    P.add("dve", lambda e: e.tensor_copy(onesm_bf[:], onesm[:]), reads=["onesm"], writes=["onesm_bf"])
    P.add("act", lambda e: e.activation(nA[:], nA[:], AF.Exp), reads=["nA"], writes=["nA"])
    P.add("dve", lambda e: e.tensor_scalar(nA[:], nA[:], -1.0, None, ALU.mult), reads=["nA"], writes=["nA"])

    bk, bkr = bank()
    col = 0
    plist = [(gdn_conv_w, 4, 3072), (sc_conv_w, 3, D), (ffn_conv_w, 3, DFF), (lnp_d, 4, D), (o_norm_g, 1, 128)]
    for (src_d, r, width) in plist:
        for c0 in range(0, width, 512):
            wd = min(512, width - c0)
            P.dma("sp", hso[:r, 0:wd], src_d[:, c0:c0 + wd], writes=["hso"])
            for ch in range(wd // 128):
                o_ap = bk[:, col:col + r]
                i_ap = hso[:r, ch * 128:(ch + 1) * 128]
                P.add("pe", lambda e, o_ap=o_ap, i_ap=i_ap, r=r: e.transpose(o_ap, i_ap, ident[:r, :r]),
                      reads=["hso", "cst"], writes=[bkr])
                col += r
    P.add("dve", lambda e: e.tensor_copy(prm[:, 0:219], bk[:, 0:219]), reads=[bkr], writes=["prm"])
    cwg = prm[:, 0:96].rearrange("p (c k) -> p c k", k=4)
    cws = prm[:, 96:120].rearrange("p (c k) -> p c k", k=3)
    cwf = prm[:, 120:186].rearrange("p (c k) -> p c k", k=3)
    lnp = prm[:, 186:218].rearrange("p (c k) -> p c k", k=4)
    ong = prm[:, 218:219]

    xa = sb("xa", [128, 2, D])
    xT = sb("xT", [128, 8, NT], BF16)
    x_f = sb("x_f", [128, 8, NT])
    preqkv = sb("preqkv", [128, 24, NT + 3], BF16)
    post = sb("post", [128, 24, NT], BF16)
    NCACC = 4
    cacc = [sb(f"cacc{i}", [128, NT]) for i in range(NCACC)]
    sqb = [sb(f"sqb{i}", [128, NT], BF16) for i in range(2)]
    rsb = [sb(f"rsb{i}", [128, NT]) for i in range(2)]
    wst = [sb(f"wst{i}", [128, 2048], BF16) for i in range(4)]
    szg = sb("szg", [128, 8, NT], BF16)
    ppre = sb("ppre", [128, 8, NT + 2], BF16)
    mixg = sb("mixg", [128, 8, NT], BF16)
    sCs = mixg
    mixed = sb("mixed", [128, 8, NT], BF16)
    r1 = sb("r1", [128, 8, NT])
    cfl = r1
    onesm_bf = sb("onesm_bf", [128, 128], BF16)
    mean_sb = sb("mean_sb", [128, NT])
    m2 = sb("m2", [128, NT])
    rstd = sb("rstd", [128, NT])
    x1f = r1
    x1b = mixed
    apre = [sb(f"apre{i}", [128, NT + 2]) for i in range(4)]
    ahalo = sb("ahalo", [128, NFC, 4, 2])
    gab = [sb(f"gab{i}", [128, NT], BF16) for i in range(4)]
    hb = sb("hb", [128, NFC, NT], BF16)
    yout = hb[:].rearrange("p c t -> p (c t)").bitcast(F32)[:, 0:2 * D].rearrange("p (tb d) -> p tb d", tb=2)
    hs = sb("hs", [128, 24, 12])
    ba_beta = sb("ba_beta", [64, 4, 8])
    ba_t = sb("ba_t", [64, 4, 8])
    ba_g = sb("ba_g", [64, 4, 8])
    ba_gh = sb("ba_gh", [64, 4, 8], BF16)
    ba_gl = sb("ba_gl", [64, 4, 8], BF16)
    NB = 4
    eGt = [sb(f"eGt{i}", [64, 16]) for i in range(NB)]
    eGl = [sb(f"eGl{i}", [128, 8]) for i in range(NB)]
    bg = [sb(f"bg{i}", [64, 8]) for i in range(NB)]
    gtri = [sb(f"gtri{i}", [64, 2, 8, 64], BF16) for i in range(NB)]
    Gam = [sb(f"Gam{i}", [64, 8, 64], BF16) for i in range(NB)]
    GamT = [sb(f"GamT{i}", [64, 8, 64], BF16) for i in range(NB)]
    kbg = [sb(f"kbg{i}", [64, 8, 128], BF16) for i in range(NB)]
    ktail = [sb(f"ktail{i}", [64, 8, 128], BF16) for i in range(NB)]
    vb = [sb(f"vb{i}", [64, 8, 128], BF16) for i in range(NB)]
    Am = [sb(f"Am{i}", [64, 8, 64], BF16) for i in range(NB)]
    Xm = [[sb(f"Xm{i}_{k}", [64, 8, 64], BF16) for k in range(2)] for i in range(NB)]
    Ym = [[sb(f"Ym{i}_{k}", [64, 8, 64], BF16) for k in range(2)] for i in range(NB)]
    Rm = [[sb(f"Rm{i}_0", [64, 8, 64], BF16)] * 2 for i in range(NB)]
    nwT = [sb(f"nwT{i}", [128, 8, 64], BF16) for i in range(NB)]
    vnew = [sb("vnew0", [64, 8, 128], BF16)] * NB
    qkT = [sb("qkT0", [64, 8, 64], BF16)] * NB
    o1s = [sb("o1s0", [64, 8, 128])] * NB
    osq = [sb("osq0", [64, 8, 128], BF16)] * NB
    oss = [sb(f"oss{i}", [64, 8]) for i in range(NB)]
    onb = [sb(f"onb{i}", [64, 8, 128], BF16) for i in range(2)] * 2
    Sp = sb("Sp", [128, 8, 128])
    Spb = sb("Spb", [128, 8, 128], BF16)
    Ss, Ssb = Sp, Spb

    stg_f = [(r1, "r1"), (x_f, "x_f")]
    stg_b = [(mixed, "mixed"), (mixg, "mixg")]
    pp = [0]

    def cast_block(src_ap, dst_ap, ncol, res, nk=None):
        i = pp[0] % 2
        pp[0] += 1
        (sf, sfr), (sbf, sbr) = stg_f[i], stg_b[i]
        fv = sf[:].rearrange("p a b -> p (a b)")[:, 0:ncol]
        bv = sbf[:].rearrange("p a b -> p (a b)")[:, 0:ncol]
        if nk is not None:
            fv = fv.rearrange("p (c k) -> p c k", k=nk)
            bv = bv.rearrange("p (c k) -> p c k", k=nk)
        P.dma("sp", fv, src_ap, writes=[sfr])
        eng = ("act", "dve", "pool")[pp[0] % 3]
        if eng == "act":
            P.add("act", lambda e: e.activation(bv, fv, AF.Copy), reads=[sfr], writes=[sbr])
        else:
            P.add(eng, lambda e: e.tensor_copy(bv, fv), reads=[sfr], writes=[sbr])
        P.dma("sp", dst_ap, bv, reads=[sbr], writes=[res])

    cast_jobs = []
    for r in range(8):
        rs = slice(r * 128, (r + 1) * 128)
        cast_block(w_in[rs, 1024:3072].rearrange("r (g c) -> r g c", c=256), win_g[4:12, :, r * 256:(r + 1) * 256].rearrange("g p c -> p g c"), 2048, f"win_kv{r}", nk=256)
    for r in range(8):
        rs = slice(r * 128, (r + 1) * 128)
        cast_jobs.append(lambda rs=rs, r=r: cast_block(w_in[rs, 0:1024].rearrange("r (g c) -> r g c", c=256), win_g[0:4, :, r * 256:(r + 1) * 256].rearrange("g p c -> p g c"), 1024, f"win_sA{r}", nk=256))
        cast_jobs.append(lambda rs=rs, r=r: cast_block(w_in[rs, 3072:4096].rearrange("r (g c) -> r g c", c=256), win_g[12:16, :, r * 256:(r + 1) * 256].rearrange("g p c -> p g c"), 1024, f"win_sA{r}", nk=256))
        cast_jobs.append(lambda rs=rs, r=r: cast_block(w_in[rs, 4112:6160].rearrange("r (g c) -> r g c", c=256), win_g[16:24, :, r * 256:(r + 1) * 256].rearrange("g p c -> p g c"), 2048, f"win_sB{r}", nk=256))
        cast_jobs.append(lambda rs=rs, r=r: cast_block(w_in[rs, 6160:8208].rearrange("r (g c) -> r g c", c=256), win_g[24:32, :, r * 256:(r + 1) * 256].rearrange("g p c -> p g c"), 2048, f"win_sB{r}", nk=256))
        cast_jobs.append(lambda rs=rs, r=r: cast_block(w_in[rs, 8208:9232].rearrange("r (g c) -> r g c", c=256), win_g[32:36, :, r * 256:(r + 1) * 256].rearrange("g p c -> p g c"), 1024, f"win_sB{r}", nk=256))
        cast_jobs.append(lambda rs=rs, r=r: cast_block(w_o[rs, :].rearrange("r (g c) -> r g c", c=256), wo_g[:, :, r * 256:(r + 1) * 256].rearrange("g p c -> p g c"), 1024, f"wo_s{r}", nk=256))
        for (c0, c1) in ((0, 16), (16, NFC)):
            for half in range(2):
                cast_jobs.append(lambda rs=rs, r=r, c0=c0, c1=c1, half=half: cast_block(
                    w_up[rs, half * DFF + c0 * 128:half * DFF + c1 * 128].rearrange("r (c k) -> r c k", k=128),
                    wup_g[c0:c1, :, r * 256 + half * 128:r * 256 + half * 128 + 128].rearrange("g p c -> p g c"),
                    (c1 - c0) * 128, f"wup_s{'AB'[half]}{r}", nk=128))
    for r in range(NFC):
        rs = slice(r * 128, (r + 1) * 128)
        kg, k8 = r // 8, r % 8
        cast_jobs.append(lambda rs=rs, r=r, kg=kg, k8=k8: cast_block(
            w_down[rs, :].rearrange("r (g c) -> r g c", c=256),
            wdn_g.rearrange("(cg kg) p c -> cg kg p c", kg=3)[:, kg, :, k8 * 256:(k8 + 1) * 256].rearrange("g p c -> p g c"),
            1024, f"wdn_s{r}", nk=256))
    fv = r1[:].rearrange("p a b -> p (a b)")[:, 0:128].rearrange("p (k c) -> p k c", c=16)
    P.dma("sp", fv, w_in[:, 4096:4112].rearrange("(kc p) c -> p kc c", p=128), writes=["r1"])
    P.add("dve", lambda e: e.tensor_copy(wba[:], fv), reads=["r1"], writes=["wba"])

    P.add("pool", lambda e: e.memset(Sp[:], 0.0), writes=["Sp"])
    P.add("pool", lambda e: e.memset(Spb[:], 0.0), writes=["Spb"])
    P.add("pool", lambda e: e.memset(preqkv[:], 0.0), writes=["preqkv"])
    P.add("pool", lambda e: e.memset(ppre[:], 0.0), writes=["ppre"])
    P.add("pool", lambda e: e.memset(ahalo[:], 0.0), writes=["ahalo"])

    wst_rr = [0]
    xa_loaded = [False]

    def wload(src_ap, nel, shape_str, **kw):
        i = wst_rr[0]
        wst_rr[0] = (i + 1) % 4
        view = wst[i][:, 0:nel].rearrange(shape_str, **kw)
        return i, view

    def proj(xin, xres, wview, wres, ncc, N, consume, cc0=0):
        for cc in range(ncc):
            bk, bkr = bank()
            for kc in range(8):
                P.add("pe", lambda e, bk=bk, kc=kc, cc=cc: e.matmul(
                    bk[:, 0:N], wview[:, kc, cc * 128:(cc + 1) * 128], xin[:, kc, 0:N],
                    start=(kc == 0), stop=(kc == 7)), reads=[xres, wres], writes=[bkr])
            consume(cc0 + cc, bk[:, 0:N], bkr)

    def tile(x_src, nseq, L, C, full, S, Sb, Sres, y_dst, hist_src=None, state_dst=None, write_y=True, S_src=None, x_next=None):
        N = nseq * L
        nch = N // C
        ntb = (N + 127) // 128
        TB = min(N, 128)
        H = 3

        def v3(ap):
            return ap.rearrange("p (s l) -> p s l", s=nseq)

        if not xa_loaded[0]:
            P.dma("sp", xa[:TB, 0:ntb, :], x_src.rearrange("(tb p) d -> p tb d", p=TB), writes=["xa"])
        xa_loaded[0] = False
        for tb in range(ntb):
            for kq in range(2):
                bk, bkr = bank()
                for k4 in range(4):
                    kc = kq * 4 + k4
                    P.add("pe", lambda e, bk=bk, k4=k4, kc=kc, tb=tb: e.transpose(
                        bk[:, k4 * 128:k4 * 128 + TB], xa[:TB, tb, kc * 128:(kc + 1) * 128], ident[:TB, :TB]),
                        reads=["xa", "cst"], writes=[bkr])
                src = bk[:, :].rearrange("p (k t) -> p k t", k=4)[:, :, 0:TB]
                P.add("dve", lambda e, src=src, kq=kq, tb=tb: e.tensor_copy(
                    xT[:, kq * 4:kq * 4 + 4, tb * 128:tb * 128 + TB], src), reads=[bkr], writes=["xT"])
                if full:
                    P.add("act", lambda e, src=src, kq=kq, tb=tb: e.activation(
                        x_f[:, kq * 4:kq * 4 + 4, tb * 128:tb * 128 + TB], src, AF.Copy),
                        reads=[bkr], writes=["x_f"])

        if x_next is not None:
            xn, tbn, ntbn = x_next
            P.dma("sp", xa[:tbn, 0:ntbn, :], xn.rearrange("(tb p) d -> p tb d", p=tbn), reads=["xT"], writes=["xa"])
            xa_loaded[0] = True
        yield "front"
        pv = preqkv[:, :, 0:nseq * (L + H)].rearrange("p c (s l) -> p c s l", s=nseq)
        ppv = ppre[:, :, 0:nseq * (L + 2)].rearrange("p c (s l) -> p c s l", s=nseq)
        ahv = ahalo[:, :, 0:nseq, :]

        def load_hist(src_d, nrows, r, width, dst, dstres):
            for c0 in range(0, width, 512):
                wd = min(512, width - c0)
                ncc = wd // 128
                P.dma("sp", hso[:nrows, 0:wd], src_d[:, c0:c0 + wd], writes=["hso"])
                bk, bkr = bank()
                for k in range(ncc):
                    P.add("pe", lambda e, bk=bk, k=k: e.transpose(
                        bk[:, k * 12:k * 12 + nrows], hso[:nrows, k * 128:(k + 1) * 128],
                        ident[:nrows, :nrows]), reads=["hso", "cst"], writes=[bkr])
                srcv = bk[:, 0:ncc * 12].rearrange("p (c x) -> p c x", x=12)[:, :, 0:nrows].rearrange(
                    "p c (s r) -> p c s r", r=r)
                P.add("dve", lambda e, srcv=srcv, c0=c0, ncc=ncc: e.tensor_copy(
                    dst[:, c0 // 128:c0 // 128 + ncc, :, :], srcv), reads=[bkr], writes=[dstres])

        if hist_src is not None:
            (h_gc, h_sc, h_ff) = hist_src
            load_hist(h_gc, nseq * 3, 3, 3072, pv[:, :, :, 0:3], "preqkv")
            load_hist(h_sc, nseq * 2, 2, D, ppv[:, :, :, 0:2], "ppre")
            load_hist(h_ff, nseq * 2, 2, DFF, ahv, "ahalo")
        else:
            P.add("pool", lambda e: e.tensor_copy(pv[:, :, :, 0:3], pv[:, :, :, L:L + 3]),
                  reads=["preqkv"], writes=["preqkv"])
            if full:
                P.add("pool", lambda e: e.tensor_copy(ppv[:, :, :, 0:2], ppv[:, :, :, L:L + 2]),
                      reads=["ppre"], writes=["ppre"])

        qpend = []
        qstage2 = []

        def qkv_flush():
            grp = list(qpend)
            del qpend[:]
            accs = []
            for (ch, ps, psr) in grp:
                P.add("act", lambda e, ch=ch, ps=ps: e.activation(pv[:, ch, :, 3:3 + L], v3(ps), AF.Copy),
                      reads=[psr], writes=[f"preqkv/{ch}"])
                accs.append(rot("cacc", NCACC))
            for (ch, ps, psr), i in zip(grp, accs):
                a3 = v3(cacc[i][:, 0:N])
                P.add("pool", lambda e, ch=ch, a3=a3: e.tensor_tensor(
                    a3, pv[:, ch, :, 0:L], cwg[:, ch, 0:1].unsqueeze(1).to_broadcast([128, nseq, L]), ALU.mult),
                    reads=[f"preqkv/{ch}", "prm"], writes=[f"cacc{i}"])
            for j in range(1, 4):
                for (ch, ps, psr), i in zip(grp, accs):
                    a3 = v3(cacc[i][:, 0:N])
                    P.add("dve", lambda e, j=j, ch=ch, a3=a3: e.scalar_tensor_tensor(
                        a3, pv[:, ch, :, j:j + L], cwg[:, ch, j:j + 1], a3, ALU.mult, ALU.add),
                        reads=[f"preqkv/{ch}", "prm", f"cacc{i}"], writes=[f"cacc{i}"])
            prev = list(qstage2)
            del qstage2[:]
            for th in prev:
                th()
            for (ch, ps, psr), i in zip(grp, accs):
                qstage2.append(lambda ch=ch, i=i: P.add(
                    "act", lambda e: e.activation(post[:, ch, 0:N], cacc[i][:, 0:N], AF.Silu),
                    reads=[f"cacc{i}"], writes=[f"post{ch}"]))

        def qkv_finish():
            if qpend:
                qkv_flush()
            prev = list(qstage2)
            del qstage2[:]
            for th in prev:
                th()

        def qkv_consume(ch, ps, psr):
            qpend.append((ch, ps, psr))
            if len(qpend) == 2:
                qkv_flush()

        l2pendB = []

        def l2_finish(keep=0):
            while len(l2pendB) > keep:
                l2pendB.pop(0)()

        def l2norm2(chs):
            for ch in chs:
                i = rot("sqb", 2)
                k = rot("cacc", NCACC)
                rs_ = cacc[k]
                P.add("pool", lambda e, ch=ch, i=i: e.tensor_tensor(
                    sqb[i][:, 0:N], post[:, ch, 0:N], post[:, ch, 0:N], ALU.mult),
                    reads=[f"post{ch}"], writes=[f"sqb{i}"])
                bk, bkr = bank()
                P.add("pe", lambda e, bk=bk, i=i: e.matmul(bk[:, 0:N], ones_bf[:], sqb[i][:, 0:N], start=True, stop=True),
                      reads=[f"sqb{i}", "ones_bf"], writes=[bkr])
                P.add("act", lambda e, bk=bk, rs_=rs_: e.activation(rs_[:, 0:N], bk[:, 0:N], AF.Ln, bias=epsn[:, 0:1]),
                      reads=[bkr, "epsc"], writes=[f"cacc{k}"])
                l2_finish(keep=2)

                def stageB(ch=ch, k=k, rs_=rs_):
                    P.add("act", lambda e: e.activation(rs_[:, 0:N], rs_[:, 0:N], AF.Exp, scale=-0.5),
                          reads=[f"cacc{k}"], writes=[f"cacc{k}"])
                    P.add("pool", lambda e: e.tensor_tensor(
                        post[:, ch, 0:N], post[:, ch, 0:N], rs_[:, 0:N], ALU.mult),
                        reads=[f"post{ch}", f"cacc{k}"], writes=[f"post{ch}"])
                l2pendB.append(stageB)

        for g in range(8):
            i, wv_ = wload(None, 2048, "p (k c) -> p k c", k=8)
            P.dma("sp", wst[i][:, 0:2048], win_g[4 + g], reads=WIN_KV, writes=[f"wst{i}"])
            proj(xT, "xT", wv_, f"wst{i}", 2, N, qkv_consume, cc0=8 + g * 2)
            yield "front"
        if full:
            for g in range(4):
                i, wv_ = wload(None, 2048, "p (k c) -> p k c", k=8)
                P.dma("sp", wst[i][:, 0:2048], win_g[g], reads=WIN_RES, writes=[f"wst{i}"])
                proj(xT, "xT", wv_, f"wst{i}", 2, N, qkv_consume, cc0=g * 2)
        qkv_finish()
        l2todo = []
        if full:
            l2todo = [[ch, ch + 1] for ch in range(0, 16, 2)]
        else:
            for ch in range(8, 16, 2):
                l2norm2([ch, ch + 1])
                yield "front"
            l2_finish()

        bk, bkr = bank()
        bav = bk[:C, 0:nch * 16].rearrange("p (c k) -> p c k", k=16)
        for c in range(nch):
            for kc in range(8):
                P.add("pe", lambda e, c=c, kc=kc: e.matmul(
                    bav[:, c, :], xT[:, kc, c * C:(c + 1) * C], wba[:, kc, :], start=(kc == 0), stop=(kc == 7)),
                    reads=["xT", "wba"], writes=[bkr])
        P.add("act", lambda e: e.activation(ba_beta[:C, 0:nch, :], bav[:, :, 0:8], AF.Sigmoid),
              reads=[bkr], writes=["ba_beta"])
        P.add("dve", lambda e: e.tensor_tensor(
            ba_t[:C, 0:nch, :], bav[:, :, 8:16], dtb[:C, :].unsqueeze(1).to_broadcast([C, nch, 8]), ALU.add),
            reads=[bkr, "dtb"], writes=["ba_t"])
        P.add("act", lambda e: e.activation(ba_t[:C, 0:nch, :], ba_t[:C, 0:nch, :], AF.Exp),
              reads=["ba_t"], writes=["ba_t"])
        P.add("act", lambda e: e.activation(ba_t[:C, 0:nch, :], ba_t[:C, 0:nch, :], AF.Ln, bias=epsn[:C, 2:3]),
              reads=["ba_t", "epsc"], writes=["ba_t"])
        P.add("dve", lambda e: e.tensor_tensor(
            ba_g[:C, 0:nch, :], ba_t[:C, 0:nch, :], nA[:C, :].unsqueeze(1).to_broadcast([C, nch, 8]), ALU.mult),
            reads=["ba_t", "nA"], writes=["ba_g"])
        P.add("dve", lambda e: e.tensor_copy(ba_gh[:C, 0:nch, :], ba_g[:C, 0:nch, :]), reads=["ba_g"], writes=["ba_gh"])
        P.add("dve", lambda e: e.tensor_tensor(ba_gl[:C, 0:nch, :], ba_g[:C, 0:nch, :], ba_gh[:C, 0:nch, :],
                                               ALU.subtract), reads=["ba_g", "ba_gh"], writes=["ba_gh"])

        if full:
            def mk_consume(kind):
                def f(ch, ps, psr):
                    if kind == "z":
                        P.add("act", lambda e: e.activation(szg[:, ch, 0:N], ps, AF.Silu),
                              reads=[psr], writes=[f"szg/{ch}"])
                    elif kind == "sB":
                        P.add("dve", lambda e: e.tensor_tensor(cfl[:, ch, 0:N], ps, cfl[:, ch, 0:N], ALU.mult),
                              reads=[psr, f"r1/{ch}"], writes=[f"r1/{ch}"])
                    elif kind == "sC":
                        P.add("act", lambda e: e.activation(sCs[:, ch, 0:N], ps, AF.Copy),
                              reads=[psr], writes=[f"mixg/{ch}"])
                    elif kind == "sH":
                        P.add("dve", lambda e: e.tensor_tensor(
                            ppv[:, ch, :, 2:2 + L], v3(ps), v3(sCs[:, ch, 0:N]), ALU.mult),
                            reads=[psr, f"mixg/{ch}"], writes=[f"ppre/{ch}"])
                        c3 = v3(cfl[:, ch, 0:N])
                        P.add("pool", lambda e: e.tensor_tensor(
                            c3, ppv[:, ch, :, 0:L], cws[:, ch, 0:1].unsqueeze(1).to_broadcast([128, nseq, L]), ALU.mult),
                              reads=[f"ppre/{ch}", "prm"], writes=[f"r1/{ch}"])
                        for j in range(1, 3):
                            P.add("dve", lambda e, j=j: e.scalar_tensor_tensor(
                                c3, ppv[:, ch, :, j:j + L], cws[:, ch, j:j + 1], c3, ALU.mult, ALU.add),
                                reads=[f"ppre/{ch}", "prm", f"r1/{ch}"], writes=[f"r1/{ch}"])
                    elif kind == "gA":
                        i = rot("cacc", NCACC)
                        P.add("act", lambda e: e.activation(cacc[i][:, 0:N], ps, AF.Sigmoid),
                              reads=[psr], writes=[f"cacc{i}"])
                        P.add("pool", lambda e: e.tensor_tensor(
                            szg[:, ch, 0:N], szg[:, ch, 0:N], cacc[i][:, 0:N], ALU.mult),
                            reads=[f"szg/{ch}", f"cacc{i}"], writes=[f"szg/{ch}"])
                    elif kind == "gB":
                        i = rot("cacc", NCACC)
                        P.add("act", lambda e: e.activation(cacc[i][:, 0:N], ps, AF.Sigmoid),
                              reads=[psr], writes=[f"cacc{i}"])
                        P.add("pool", lambda e: e.tensor_tensor(
                            cfl[:, ch, 0:N], cfl[:, ch, 0:N], cacc[i][:, 0:N], ALU.mult),
                            reads=[f"r1/{ch}", f"cacc{i}"], writes=[f"r1/{ch}"])
                return f

            for kind, base in [("z", 3072), ("sC", 5120), ("sH", 6144), ("sB", 4096), ("gA", 7168), ("gB", 8192)]:
                for g in range(4):
                    i, wv_ = wload(None, 2048, "p (k c) -> p k c", k=8)
                    P.dma("sp", wst[i][:, 0:2048], win_g[base // 256 + g], reads=WIN_RES, writes=[f"wst{i}"])
                    proj(xT, "xT", wv_, f"wst{i}", 2, N, mk_consume(kind), cc0=g * 2)
                    if l2todo and kind in ("sC", "sH"):
                        l2norm2(l2todo.pop(0))
                    if kind == "sH" and g == 3:
                        while l2todo:
                            l2norm2(l2todo.pop(0))
                        l2_finish()
            while l2todo:
                l2norm2(l2todo.pop(0))
            l2_finish()

        yield "front_done"
        nst = {64: 5, 16: 3}[C]
        def chunk(c):
            b = c % NB
            cs = slice(c * C, (c + 1) * C)
            Sc, Scb, Scr = S, Sb, Sres
            g_c = ba_g[:C, c, :]
            be_c = ba_beta[:C, c, :]
            bk, bkr = bank()
            gh_c = ba_gh[:C, c, :]
            gl_c = ba_gl[:C, c, :]
            for (oap, lt) in ((bk[:C, 0:8], tri_bf[:C, :C]), (bk[:C, 8:16], sgt_bf[:C, :C]), (bk[:, 16:24], ones_bf[:C, :])):
                P.add("pe", lambda e, oap=oap, lt=lt: e.matmul(oap, lt, gh_c, start=True, stop=False),
                      reads=["ones_bf", "ba_gh"], writes=[bkr])
                P.add("pe", lambda e, oap=oap, lt=lt: e.matmul(oap, lt, gl_c, start=False, stop=True),
                      reads=["ones_bf", "ba_gh"], writes=[bkr])
            yield
            P.add("act", lambda e, bk=bk: e.activation(eGt[b][:C, :], bk[:C, 0:16], AF.Exp),
                  reads=[bkr], writes=[f"eGt{b}"])
            P.add("act", lambda e, bk=bk: e.activation(eGl[b][:, :], bk[:, 16:24], AF.Exp),
                  reads=[bkr], writes=[f"eGl{b}"])
            P.add("pool", lambda e: e.tensor_tensor(bg[b][:C, :], be_c, eGt[b][:C, 0:8], ALU.mult),
                  reads=["ba_beta", f"eGt{b}"], writes=[f"bg{b}"])
            P.add("dve", lambda e: e.tensor_tensor(
                gtri[b][:C, 0, :, :C], tri[:C, :C].unsqueeze(1).to_broadcast([C, 8, C]),
                gh_c.unsqueeze(2).to_broadcast([C, 8, C]), ALU.mult),
                reads=["cst", "ba_gh"], writes=[f"gtri{b}"])
            P.add("dve", lambda e: e.tensor_tensor(
                gtri[b][:C, 1, :, :C], tri[:C, :C].unsqueeze(1).to_broadcast([C, 8, C]),
                gl_c.unsqueeze(2).to_broadcast([C, 8, C]), ALU.mult),
                reads=["cst", "ba_gh"], writes=[f"gtri{b}"])
            yield
            bkD, bkDr = bank()
            Dv = bkD[:C, 0:8 * C].rearrange("p (h c) -> p h c", h=8)
            for h in range(8):
                for hl in range(2):
                    P.add("pe", lambda e, h=h, Dv=Dv, hl=hl: e.matmul(Dv[:, h, :], gtri[b][:C, hl, h, :C], sgt_bf[:C, :C],
                                                                     start=(hl == 0), stop=(hl == 1)),
                          reads=[f"gtri{b}", "ones_bf"], writes=[bkDr])
            yield
            P.add("act", lambda e, Dv=Dv: e.activation(Gam[b][:C, :, :C], Dv, AF.Exp),
                  reads=[bkDr], writes=[f"Gam{b}"])
            P.add("pool", lambda e: e.tensor_tensor(
                Gam[b][:C, :, :C], Gam[b][:C, :, :C], sgt[:C, :C].unsqueeze(1).to_broadcast([C, 8, C]), ALU.mult),
                reads=[f"Gam{b}", "cst"], writes=[f"Gam{b}"])
            P.add("pool", lambda e: e.tensor_tensor(
                Gam[b][:C, :, :C], Gam[b][:C, :, :C], be_c.unsqueeze(2).to_broadcast([C, 8, C]), ALU.mult),
                reads=[f"Gam{b}", "ba_beta"], writes=[f"Gam{b}"])
            if full:
                bkT, bkTr = bank()
                DTv = bkT[:C, 0:8 * C].rearrange("p (h c) -> p h c", h=8)
                for h in range(8):
                    for hl in range(2):
                        P.add("pe", lambda e, h=h, DTv=DTv, hl=hl: e.matmul(
                            DTv[:, h, :], sgt_bf[:C, :C], gtri[b][:C, hl, h, :C], start=(hl == 0), stop=(hl == 1)),
                            reads=[f"gtri{b}", "ones_bf"], writes=[bkTr])
                P.add("act", lambda e, DTv=DTv: e.activation(GamT[b][:C, :, :C], DTv, AF.Exp),
                      reads=[bkTr], writes=[f"GamT{b}"])
                P.add("pool", lambda e: e.tensor_tensor(
                    GamT[b][:C, :, :C], GamT[b][:C, :, :C], tri[:C, :C].unsqueeze(1).to_broadcast([C, 8, C]),
                    ALU.mult), reads=[f"GamT{b}", "cst"], writes=[f"GamT{b}"])
            if STOP2 == "c_a":
                return
            yield
            bkk, bkkr = bank()
            kt = bkk[:, :].bitcast(BF16)[:C, :].rearrange("p (h d) -> p h d", h=8)
            for h in range(8):
                P.add("pe", lambda e, h=h, kt=kt: e.transpose(kt[:, h, :], post[:, 8 + h, cs], ident_bf[:]),
                      reads=[f"post{8 + h}", "ident_bf"], writes=[bkkr])
            yield
            P.add("dve", lambda e, kt=kt: e.tensor_tensor(
                kbg[b][:C], kt, bg[b][:C, :].unsqueeze(2).to_broadcast([C, 8, 128]), ALU.mult),
                reads=[bkkr, f"bg{b}"], writes=[f"kbg{b}"])
            P.add("dve", lambda e, kt=kt: e.tensor_tensor(
                ktail[b][:C], kt, eGt[b][:C, 8:16].unsqueeze(2).to_broadcast([C, 8, 128]), ALU.mult),
                reads=[bkkr, f"eGt{b}"], writes=[f"ktail{b}"])
            yield
            bkv, bkvr = bank()
            vt = bkv[:, :].bitcast(BF16)[:C, :].rearrange("p (h d) -> p h d", h=8)
            for h in range(8):
                P.add("pe", lambda e, h=h, vt=vt: e.transpose(vt[:, h, :], post[:, 16 + h, cs], ident_bf[:]),
                      reads=[f"post{16 + h}", "ident_bf"], writes=[bkvr])
            yield
            P.add("dve", lambda e, vt=vt: e.tensor_tensor(
                vb[b][:C], vt, be_c.unsqueeze(2).to_broadcast([C, 8, 128]), ALU.mult),
                reads=[bkvr, "ba_beta"], writes=[f"vb{b}"])
            if STOP2 == "c_f":
                return
            yield
            bka, bkar = bank()
            kkv = bka[:C, 0:8 * C].rearrange("p (h c) -> p h c", h=8)
            for h in range(8):
                P.add("pe", lambda e, h=h, kkv=kkv: e.matmul(kkv[:, h, :], post[:, 8 + h, cs], post[:, 8 + h, cs],
                                                            start=True, stop=True),
                      reads=[f"post{8 + h}"], writes=[bkar])
            yield
            P.add("dve", lambda e, kkv=kkv: e.tensor_tensor(Am[b][:C, :, :C], kkv, Gam[b][:C, :, :C], ALU.mult),
                  reads=[bkar, f"Gam{b}"], writes=[f"Am{b}"])
            if STOP2 == "c_g1":
                return
            yield
            bkb, bkbr = bank()
            atv = bkb[:C, 0:8 * C].rearrange("p (h c) -> p h c", h=8)
            for h in range(8):
                P.add("pe", lambda e, h=h, atv=atv: e.matmul(atv[:, h, :], Am[b][:C, h, :C], ident_bf[:C, :C],
                                                            start=True, stop=True),
                      reads=[f"Am{b}", "ident_bf"], writes=[bkbr])
            yield
            X0, Y0, R0 = Xm[b][0], Am[b], Rm[b][0]
            P.add("act", lambda e, atv=atv: e.activation(X0[:C, :, :C], atv, AF.Copy),
                  reads=[bkbr], writes=[f"Xm{b}_0"])
            P.add("dve", lambda e, atv=atv: e.scalar_tensor_tensor(
                R0[:C, :, :C], atv, epsn[:C, 3:4], ident[:C, :C].unsqueeze(1).to_broadcast([C, 8, C]), ALU.mult, ALU.add),
                reads=[bkbr, "cst", "epsc"], writes=[f"Rm{b}_0"])
            Xc, Xr, Yc, Yr, Rc, Rr = X0, f"Xm{b}_0", Y0, f"Am{b}", R0, f"Rm{b}_0"
            yield "presolve_done"
            for n in range(1, nst + 1):
                yield
                k = n % 2
                Yn, Ynr = Ym[b][k], f"Ym{b}_{k}"
                bky, bkyr = bank()
                yv = bky[:C, 0:8 * C].rearrange("p (h c) -> p h c", h=8)
                for h in range(8):
                    P.add("pe", lambda e, h=h, yv=yv, Xc=Xc, Yc=Yc: e.matmul(
                        yv[:, h, :], Xc[:C, h, :C], Yc[:C, h, :C], start=True, stop=True),
                        reads=[Xr, Yr], writes=[bkyr])
                yield
                P.add("act", lambda e, yv=yv, Yn=Yn: e.activation(Yn[:C, :, :C], yv, AF.Copy),
                      reads=[bkyr], writes=[Ynr])
                if n < nst:
                    Xn, Xnr = Xm[b][k], f"Xm{b}_{k}"
                    bkx, bkxr = bank()
                    xv = bkx[:C, 0:8 * C].rearrange("p (h c) -> p h c", h=8)
                    for h in range(8):
                        P.add("pe", lambda e, h=h, xv=xv, Xc=Xc, Yc=Yc: e.matmul(
                            xv[:, h, :], Yc[:C, h, :C], Xc[:C, h, :C], start=True, stop=True),
                            reads=[Xr, Yr], writes=[bkxr])
                    yield
                    P.add("dve", lambda e, xv=xv, Xn=Xn: e.tensor_copy(Xn[:C, :, :C], xv),
                          reads=[bkxr], writes=[Xnr])
                yield
                Rn, Rnr = Rm[b][0], f"Rm{b}_0"
                bkq, bkqr = bank()
                rv = bkq[:C, 0:8 * C].rearrange("p (h c) -> p h c", h=8)
                for h in range(8):
                    P.add("pe", lambda e, h=h, rv=rv, Rc=Rc, Yn=Yn: e.matmul(
                        rv[:, h, :], Yn[:C, h, :C], Rc[:C, h, :C], start=True, stop=True),
                        reads=[Ynr, Rr], writes=[bkqr])
                yield
                P.add("dve", lambda e, rv=rv, Rn=Rn, Rc=Rc: e.tensor_tensor(Rn[:C, :, :C], rv, Rc[:C, :, :C], ALU.add),
                      reads=[bkqr, Rr], writes=[Rnr])
                if n < nst:
                    Xc, Xr = Xn, Xnr
                Yc, Yr, Rc, Rr = Yn, Ynr, Rn, Rnr
            TT, TTr = Rc, Rr
            if STOP2 == "c_h":
                return
            yield
            bkw, bkwr = bank()
            wv2 = bkw[:, 0:8 * C].rearrange("p (h c) -> p h c", h=8)
            for h in range(8):
                P.add("pe", lambda e, h=h, wv2=wv2, TT=TT: e.matmul(
                    wv2[:, h, :], kbg[b][:C, h, :], TT[:C, h, :C], start=True, stop=True),
                    reads=[f"kbg{b}", TTr], writes=[bkwr])
            yield
            P.add("act", lambda e, wv2=wv2: e.activation(nwT[b][:, :, :C], wv2, AF.Copy, scale=-1.0),
                  reads=[bkwr], writes=[f"nwT{b}"])
            yield "chain"
            if S_src is not None:
                P.dma("sp", Sc[:], S_src[c].rearrange("h k v -> k h v"), writes=[Scr])
                P.add("act", lambda e, Sc=Sc, Scb=Scb: e.activation(Scb[:], Sc[:], AF.Copy),
                      reads=[Scr], writes=[Scr + "b"])
            vbanks = []
            for hh in range(2):
                bkn, bknr = bank()
                vn = bkn[:C, :].rearrange("p (h d) -> p h d", h=4)
                for h4 in range(4):
                    h = hh * 4 + h4
                    P.add("pe", lambda e, h=h, h4=h4, vn=vn, TT=TT: e.matmul(
                        vn[:, h4, :], TT[:C, h, :C], vb[b][:C, h, :], start=True, stop=False),
                        reads=[TTr, f"vb{b}"], writes=[bknr])
                    P.add("pe", lambda e, h=h, h4=h4, vn=vn, Scb=Scb: e.matmul(
                        vn[:, h4, :], nwT[b][:, h, :C], Scb[:, h, :], start=False, stop=True),
                        reads=[f"nwT{b}", f"{Scr}b/{h // 4}"], writes=[bknr])
                eng = "act" if hh == 0 else "dve"
                if eng == "act":
                    P.add("act", lambda e, vn=vn, hh=hh: e.activation(vnew[b][:C, hh * 4:hh * 4 + 4, :], vn, AF.Copy),
                          reads=[bknr], writes=["vnew0"])
                else:
                    P.add("dve", lambda e, vn=vn, hh=hh: e.tensor_copy(vnew[b][:C, hh * 4:hh * 4 + 4, :], vn),
                          reads=[bknr], writes=["vnew0"])
            o1b = []
            if full:
                for hh in range(2):
                    bko, bkor = bank()
                    ov = bko[:C, :].rearrange("p (h d) -> p h d", h=4)
                    for h4 in range(4):
                        h = hh * 4 + h4
                        P.add("pe", lambda e, h=h, h4=h4, ov=ov, Scb=Scb: e.matmul(
                            ov[:, h4, :], post[:, h, cs], Scb[:, h, :], start=True, stop=True),
                            reads=[f"post{h}", f"{Scr}b/{h // 4}"], writes=[bkor])
                    o1b.append((ov, bkor))
            for hh in range(2):
                bkd, bkdr = bank()
                dv = bkd[:, :].rearrange("p (h d) -> p h d", h=4)
                for h4 in range(4):
                    h = hh * 4 + h4
                    P.add("pe", lambda e, h=h, h4=h4, dv=dv: e.matmul(
                        dv[:, h4, :], ktail[b][:C, h, :], vnew[b][:C, h, :], start=True, stop=True),
                        reads=[f"ktail{b}", "vnew0"], writes=[bkdr])
                for h4 in range(4):
                    h = hh * 4 + h4
                    P.add("dve", lambda e, h=h, h4=h4, dv=dv, Sc=Sc: e.scalar_tensor_tensor(
                        Sc[:, h, :], Sc[:, h, :], eGl[b][:, h:h + 1], dv[:, h4, :], ALU.mult, ALU.add),
                        reads=[bkdr, f"{Scr}/{h}", f"eGl{b}"], writes=[f"{Scr}/{h}"])
                P.add("act", lambda e, Sc=Sc, Scb=Scb, hh=hh: e.activation(
                    Scb[:, hh * 4:hh * 4 + 4, :], Sc[:, hh * 4:hh * 4 + 4, :], AF.Copy),
                    reads=[f"{Scr}/{h_}" for h_ in range(hh * 4, hh * 4 + 4)], writes=[f"{Scr}b/{hh}"])
            if nseq > 1 and state_dst is not None:
                P.dma("sp", state_dst[1][c].rearrange("h k v -> k h v"), Sc[:], reads=[Scr], writes=["dram_out"])
            if full:
                for hh, (ov, bkor) in enumerate(o1b):
                    P.add("dve", lambda e, ov=ov, hh=hh: e.tensor_tensor(
                        o1s[b][:C, hh * 4:hh * 4 + 4, :], ov,
                        eGt[b][:C, hh * 4:hh * 4 + 4].unsqueeze(2).to_broadcast([C, 4, 128]), ALU.mult),
                        reads=[bkor, f"eGt{b}"], writes=["o1s0"])
                bkq2, bkq2r = bank()
                qv = bkq2[:C, 0:8 * C].rearrange("p (h c) -> p h c", h=8)
                for h in range(8):
                    P.add("pe", lambda e, h=h, qv=qv: e.matmul(qv[:, h, :], post[:, 8 + h, cs], post[:, h, cs],
                                                              start=True, stop=True),
                          reads=[f"post{8 + h}", f"post{h}"], writes=[bkq2r])
                P.add("dve", lambda e, qv=qv: e.tensor_tensor(qkT[b][:C, :, :C], qv, GamT[b][:C, :, :C], ALU.mult),
                      reads=[bkq2r, f"GamT{b}"], writes=["qkT0"])
                for hh in range(2):
                    bko, bkor = bank()
                    ov = bko[:C, :].rearrange("p (h d) -> p h d", h=4)
                    for h4 in range(4):
                        h = hh * 4 + h4
                        P.add("pe", lambda e, h=h, h4=h4, ov=ov: e.matmul(
                            ov[:, h4, :], qkT[b][:C, h, :C], vnew[b][:C, h, :], start=True, stop=True),
                            reads=["qkT0", "vnew0"], writes=[bkor])
                    P.add("dve", lambda e, ov=ov, hh=hh: e.tensor_tensor(
                        o1s[b][:C, hh * 4:hh * 4 + 4, :], ov, o1s[b][:C, hh * 4:hh * 4 + 4, :], ALU.add),
                        reads=[bkor, "o1s0"], writes=["o1s0"])
            if full:
                P.add("act", lambda e: e.activation(osq[b][:C], o1s[b][:C], AF.Square),
                      reads=["o1s0"], writes=["osq0"])
                P.add("dve", lambda e: e.reduce_sum(oss[b][:C, :], osq[b][:C], axis=AX.X),
                      reads=["osq0"], writes=[f"oss{b}"])
                P.add("dve", lambda e: e.tensor_scalar(oss[b][:C, :], oss[b][:C, :], 1.0 / 128, NORM_EPS * 128,
                                                       ALU.mult, ALU.add),
                      reads=[f"oss{b}"], writes=[f"oss{b}"])
                P.add("act", lambda e: e.activation(oss[b][:C, :], oss[b][:C, :], AF.Sqrt),
                      reads=[f"oss{b}"], writes=[f"oss{b}"])
                P.add("dve", lambda e: e.reciprocal(oss[b][:C, :], oss[b][:C, :]),
                      reads=[f"oss{b}"], writes=[f"oss{b}"])
                P.add("dve", lambda e: e.tensor_tensor(
                    onb[b][:C], o1s[b][:C], oss[b][:C, :].unsqueeze(2).to_broadcast([C, 8, 128]), ALU.mult),
                    reads=["o1s0", f"oss{b}"], writes=[f"onb{b % 2}"])
                yield "tail"
                bkt, bktr = bank()
                tv = bkt[:, 0:8 * C].rearrange("p (h c) -> p h c", h=8)
                for h in range(8):
                    P.add("pe", lambda e, h=h, tv=tv: e.matmul(tv[:, h, :], onb[b][:C, h, :], ident_bf[:C, :C],
                                                              start=True, stop=True),
                          reads=[f"onb{b % 2}", "ident_bf"], writes=[bktr])
                P.add("dve", lambda e, tv=tv: e.scalar_tensor_tensor(
                    mixg[:, :, cs], tv, ong, szg[:, :, cs], ALU.mult, ALU.mult),
                    reads=[bktr, "prm", "szg"], writes=["mixg"])

        gens = [chunk(c) for c in range(nch)]
        live = list(gens)
        while live:
            for g_ in list(live):
                if next(g_) == "presolve_done":
                    live.remove(g_)
        yield "presolve_done"
        live = list(gens)
        while live:
            for g_ in list(live):
                if next(g_) == "chain":
                    live.remove(g_)
            yield "solve"
        yield "solve_done"
        prev_tail = None
        for g_ in gens:
            alive = next(g_, None) is not None
            yield "chain"
            if prev_tail is not None:
                for _ in prev_tail:
                    pass
            prev_tail = g_ if alive else None
        if prev_tail is not None:
            for _ in prev_tail:
                pass
        yield "chain_done"

        if not full:
            return

        P.add("dve", lambda e: e.tensor_tensor(mixed[:, :, 0:N], mixg[:, :, 0:N], cfl[:, :, 0:N], ALU.add),
              reads=["mixg"] + ["r1"], writes=["mixed"])

        def layer_norm(rbuf, rres, gcol, bcol, outf, outfres):
            for eng_, lo in (("pool", 0), ("dve", 4)):
                P.add(eng_, lambda e, lo=lo: e.tensor_tensor(
                    mixed[:, lo:lo + 4, 0:N], rbuf[:, lo:lo + 4, 0:N], rbuf[:, lo:lo + 4, 0:N], ALU.mult),
                    reads=[f"{rres}/{oc}" for oc in range(lo, lo + 4)], writes=[f"mixed/{oc}" for oc in range(lo, lo + 4)])
            P.add("act", lambda e: e.activation(mixg[:, :, 0:N], rbuf[:, :, 0:N], AF.Copy),
                  reads=[rres], writes=["mixg"])
            bk, bkr = bank()
            for oc in range(8):
                P.add("pe", lambda e, oc=oc, bk=bk: e.matmul(bk[:, 0:N], onesm_bf[:], mixg[:, oc, 0:N],
                                                            start=(oc == 0), stop=(oc == 7)),
                      reads=["mixg", "onesm_bf"], writes=[bkr])
            for oc in range(8):
                P.add("pe", lambda e, oc=oc, bk=bk: e.matmul(bk[:, 256:256 + N], onesm_bf[:], mixed[:, oc, 0:N],
                                                            start=(oc == 0), stop=(oc == 7)),
                      reads=[f"mixed/{oc}", "onesm_bf"], writes=[bkr])
            P.add("act", lambda e, bk=bk: e.activation(mean_sb[:, 0:N], bk[:, 0:N], AF.Copy),
                  reads=[bkr], writes=["mean_sb"])
            P.add("pool", lambda e: e.tensor_tensor(m2[:, 0:N], mean_sb[:, 0:N], mean_sb[:, 0:N], ALU.mult),
                  reads=["mean_sb"], writes=["m2"])
            P.add("dve", lambda e, bk=bk: e.tensor_tensor(rstd[:, 0:N], bk[:, 256:256 + N], m2[:, 0:N], ALU.subtract),
                  reads=[bkr, "m2"], writes=["rstd"])
            P.add("act", lambda e: e.activation(rstd[:, 0:N], rstd[:, 0:N], AF.Sqrt, bias=epsn[:, 1:2]),
                  reads=["rstd", "epsc"], writes=["rstd"])
            P.add("dve", lambda e: e.reciprocal(rstd[:, 0:N], rstd[:, 0:N]),
                  reads=["rstd"], writes=["rstd"])
            for stat_, statr, op_ in ((mean_sb, "mean_sb", ALU.subtract), (rstd, "rstd", ALU.mult)):
                for eng_, lo in (("pool", 0), ("dve", 4)):
                    names = [f"{rres}/{oc}" for oc in range(lo, lo + 4)]
                    P.add(eng_, lambda e, lo=lo, stat_=stat_, op_=op_: e.tensor_tensor(
                        rbuf[:, lo:lo + 4, 0:N], rbuf[:, lo:lo + 4, 0:N],
                        stat_[:, 0:N].unsqueeze(1).to_broadcast([128, 4, N]), op_),
                        reads=names + [statr], writes=names)
            for oc in range(8):
                P.add("dve", lambda e, oc=oc: e.tensor_scalar(
                    outf[:, oc, 0:N], rbuf[:, oc, 0:N], lnp[:, oc, gcol:gcol + 1], lnp[:, oc, bcol:bcol + 1],
                    ALU.mult, ALU.add), reads=[f"{rres}/{oc}", "prm"], writes=[f"{outfres}/{oc}"])

        def wo_consume(oc, ps, psr):
            P.add("dve", lambda e: e.scalar_tensor_tensor(r1[:, oc, 0:N], x_f[:, oc, 0:N], epsn[:, 4:5], ps, ALU.mult, ALU.add),
                  reads=[psr, "x_f"], writes=[f"r1/{oc}"])

        for g in range(4):
            i, wv_ = wload(None, 2048, "p (k c) -> p k c", k=8)
            P.dma("sp", wst[i][:, 0:2048], wo_g[g], reads=WO_RES, writes=[f"wst{i}"])
            proj(mixed, "mixed", wv_, f"wst{i}", 2, N, wo_consume, cc0=g * 2)
        layer_norm(r1, "r1", 0, 1, x1f, "r1")
        P.add("act", lambda e: e.activation(x1b[:, :, 0:N], x1f[:, :, 0:N], AF.Copy), reads=["r1"], writes=["mixed"])

        ffn_pend = []
        for c in range(NFC):
            i, wv_ = wload(None, 2048, "p (k c) -> p k c", k=8)
            P.dma("sp", wst[i][:, 0:2048], wup_g[c], reads=WUP_RES, writes=[f"wst{i}"])
            pbanks = []
            for half in range(2):
                bk, bkr = bank()
                for kc in range(8):
                    P.add("pe", lambda e, bk=bk, kc=kc, wv_=wv_, half=half: e.matmul(
                        bk[:, 0:N], wv_[:, kc, half * 128:(half + 1) * 128], x1b[:, kc, 0:N],
                        start=(kc == 0), stop=(kc == 7)), reads=["mixed", f"wst{i}"], writes=[bkr])
                pbanks.append((bk, bkr))
            (abk, abkr), (vbk, vbkr) = pbanks
            ai = rot("apre", 4)
            k = rot("cacc", NCACC)
            apv = apre[ai][:, 0:nseq * (L + 2)].rearrange("p (s l) -> p s l", s=nseq)
            a3 = v3(cacc[k][:, 0:N])
            P.add("pool", lambda e, c=c, apv=apv: e.tensor_copy(apv[:, :, 0:2], ahv[:, c, :, :]),
                  reads=[f"ahalo/{c}"], writes=[f"apre{ai}"])
            P.add("act", lambda e, apv=apv, abk=abk: e.activation(apv[:, :, 2:2 + L], v3(abk[:, 0:N]), AF.Copy),
                  reads=[abkr], writes=[f"apre{ai}"])
            P.add("pool", lambda e, c=c, apv=apv: e.tensor_copy(ahv[:, c, :, :], apv[:, :, L:L + 2]),
                  reads=[f"apre{ai}"], writes=[f"ahalo/{c}"])
            P.add("pool", lambda e, c=c, apv=apv, a3=a3: e.tensor_tensor(
                a3, apv[:, :, 0:L], cwf[:, c, 0:1].unsqueeze(1).to_broadcast([128, nseq, L]), ALU.mult),
                reads=[f"apre{ai}", "prm"], writes=[f"cacc{k}"])
            for j in range(1, 3):
                P.add("dve", lambda e, j=j, c=c, apv=apv, a3=a3: e.scalar_tensor_tensor(
                    a3, apv[:, :, j:j + L], cwf[:, c, j:j + 1], a3, ALU.mult, ALU.add),
                    reads=[f"apre{ai}", "prm", f"cacc{k}"], writes=[f"cacc{k}"])
            prev = list(ffn_pend)
            del ffn_pend[:]
            for th in prev:
                th()

            def stage2(c=c, ai=ai, k=k, vbk=vbk, vbkr=vbkr):
                P.add("act", lambda e: e.activation(gab[ai][:, 0:N], cacc[k][:, 0:N], AF.Gelu),
                      reads=[f"cacc{k}"], writes=[f"gab{ai}"])
                P.add("dve", lambda e: e.tensor_tensor(hb[:, c, 0:N], vbk[:, 0:N], gab[ai][:, 0:N], ALU.mult),
                      reads=[vbkr, f"gab{ai}"], writes=[f"hb/{c}"])
            ffn_pend.append(stage2)
        for th in ffn_pend:
            th()

        for cg in range(4):
            bk0, bkr0 = bank()
            bk1, bkr1 = bank()
            bks, bkrs = (bk0, bk1), (bkr0, bkr1)
            for kg in range(3):
                nk = 8 if kg < 2 else NFC - 16
                i, wv_ = wload(None, nk * 256, "p (k c) -> p k c", k=nk)
                P.dma("sp", wst[i][:, 0:nk * 256], wdn_g[cg * 3 + kg][:, 0:nk * 256], reads=WDN_RES, writes=[f"wst{i}"])
                for o2 in range(2):
                    for k8 in range(nk):
                        kc = kg * 8 + k8
                        P.add("pe", lambda e, bk=bks[o2], kc=kc, k8=k8, o2=o2, wv_=wv_: e.matmul(
                            bk[:, 0:N], wv_[:, k8, o2 * 128:(o2 + 1) * 128], hb[:, kc, 0:N],
                            start=(kc == 0), stop=(kc == NFC - 1)),
                            reads=[f"hb/{kc}", f"wst{i}"], writes=[bkrs[o2]])
            for o2 in range(2):
                oc = cg * 2 + o2
                P.add("dve", lambda e, bk=bks[o2], oc=oc, o2=o2: e.scalar_tensor_tensor(
                    r1[:, oc, 0:N], r1[:, oc, 0:N], epsn[:, 4:5], bk[:, 0:N], ALU.mult, ALU.add),
                    reads=[bkrs[o2], f"r1/{oc}"], writes=[f"r1/{oc}"])
        layer_norm(r1, "r1", 2, 3, x_f, "x_f")

        if write_y:
            for tb in range(ntb):
                for kq in range(2):
                    bk, bkr = bank()
                    for k4 in range(4):
                        oc = kq * 4 + k4
                        P.add("pe", lambda e, bk=bk, k4=k4, oc=oc, tb=tb: e.transpose(
                            bk[:TB, k4 * 128:(k4 + 1) * 128], x_f[:, oc, tb * 128:tb * 128 + TB], ident[:, :]),
                            reads=["x_f", "cst"], writes=[bkr])
                    P.add("act" if kq == 0 else "dve",
                          (lambda e, bk=bk, kq=kq, tb=tb: e.activation(yout[:TB, tb, kq * 512:(kq + 1) * 512], bk[:TB, :], AF.Copy))
                          if kq == 0 else
                          (lambda e, bk=bk, kq=kq, tb=tb: e.tensor_copy(yout[:TB, tb, kq * 512:(kq + 1) * 512], bk[:TB, :])),
                          reads=[bkr], writes=["hb"])
            P.dma("sp", y_dst.rearrange("(tb p) d -> p tb d", p=TB), yout[:TB, 0:ntb, :], reads=["hb"],
                  writes=["dram_out"])

        if state_dst is not None:
            d_gc, d_S, d_sc, d_ff = state_dst

            def store_state(dst_d, nrows, r, width, srcv, srcres):
                nchk = width // 128
                hv = hs[:, 0:nchk, 0:nrows].rearrange("p c (s r) -> p c s r", r=r)
                P.add("pool", lambda e: e.tensor_copy(hv, srcv), reads=[srcres], writes=["hs"])
                for c0 in range(0, width, 512):
                    wd = min(512, width - c0)
                    for c1 in range(0, wd, 512):
                        w2 = min(512, wd - c1)
                        bk, bkr = bank()
                        for k4 in range(w2 // 128):
                            ch = (c0 + c1) // 128 + k4
                            P.add("pe", lambda e, bk=bk, k4=k4, ch=ch: e.transpose(
                                bk[:nrows, k4 * 128:(k4 + 1) * 128], hs[:, ch, 0:nrows], ident[:, :]),
                                reads=["hs", "cst"], writes=[bkr])
                        P.add("dve", lambda e, bk=bk, c1=c1, w2=w2: e.tensor_copy(
                            hso[:nrows, c1:c1 + w2], bk[:nrows, 0:w2]), reads=[bkr], writes=["hso"])
                    P.dma("sp", dst_d[:, c0:c0 + wd], hso[:nrows, 0:wd], reads=["hso"], writes=["dram_out"])

            store_state(d_gc, nseq * 3, 3, 3072, pv[:, :, :, L:L + 3], "preqkv")
            store_state(d_sc, nseq * 2, 2, D, ppv[:, :, :, L:L + 2], "ppre")
            store_state(d_ff, nseq * 2, 2, DFF, ahv, "ahalo")
            if nseq == 1:
                P.dma("sp", d_S.rearrange("h k v -> k h v"), S[:], reads=[Sres], writes=["dram_out"])

    specs = []
    if STOP != "setup":
        for t in range(NPRE):
            specs.append(dict(full=False, x=xp[t * NT:(t + 1) * NT, :], tb=128, ntb=2, kw=dict(
                nseq=1, L=NT, C=64, full=False, S=Sp, Sb=Spb, Sres="Sp", y_dst=None)))
    if STOP not in ("setup", "pre"):
        for t in range(NFULL):
            r0 = (NPRE + t) * NT
            last = (t == NFULL - 1)
            specs.append(dict(full=True, x=xp[r0:r0 + NT, :], tb=128, ntb=2, kw=dict(
                nseq=1, L=NT, C=64, full=True, S=Sp, Sb=Spb, Sres="Sp",
                y_dst=yp[(t - 1) * NT:t * NT, :] if t > 0 else None,
                state_dst=(p_gc, p_S, p_sc, p_ff) if last else None, write_y=(t > 0))))
    if nsamp and STOP is None:
        specs.append(dict(full=True, x=xs, tb=nsamp * 16, ntb=1, kw=dict(
            nseq=nsamp, L=16, C=16, full=True, S=Ss, Sb=Ssb, Sres="Sp", y_dst=ys, hist_src=(st_gc, st_sc, st_ff),
            state_dst=(s_gc, s_S, s_sc, s_ff), write_y=True, S_src=st_S)))
    pend = None
    PEND_PER_FRONT = 1
    FRONT_PER_PEND = 1
    DEFER_AFTER = "solve_done"
    pend_last = [None]
    for ti, sp_ in enumerate(specs):
        P.epoch = ti
        nxt = specs[ti + 1] if ti + 1 < len(specs) else None
        x_next = (nxt["x"], nxt["tb"], nxt["ntb"]) if nxt is not None else None
        njobs = 3 if not sp_["full"] else len(cast_jobs)
        for _ in range(min(njobs, len(cast_jobs))):
            cast_jobs.pop(0)()
        g = tile(sp_["x"], x_next=x_next, **sp_["kw"])
        fstep = 0
        while True:
            if fstep % FRONT_PER_PEND == 0:
                for _ in range(PEND_PER_FRONT):
                    if pend is not None:
                        cur_pool[0] = "pend"
                        if next(pend, None) is None:
                            pend = None
            fstep += 1
            cur_pool[0] = "front" if pend is not None else "all"
            m = next(g)
            if m == "front_done":
                break
        if pend is not None:
            cur_pool[0] = "pend"
            for _ in pend:
                pass
            pend = None
        cur_pool[0] = "all"
        while next(g) != DEFER_AFTER:
            pass
        if sp_["full"]:
            for _ in g:
                pass
        else:
            pend = g
            pend_last[0] = "presolve_done"
    if pend is not None:
        cur_pool[0] = "pend"
        for _ in pend:
            pass

    P.emit(nc, es)
    es.close()
    return nc


def _consts():
    c = np.zeros((128, 384), np.float32)
    c[:, 0:128] = np.eye(128, dtype=np.float32)
    t = np.arange(64)
    c[:64, 128:192] = (t[:, None] <= t[None, :]).astype(np.float32)
    c[:64, 192:256] = (t[:, None] > t[None, :]).astype(np.float32)
    c[:, 256:384] = 1.0
    return c


def make_in_maps(inp, NPRE, NFULL, ncores=8):
    ROWS = (NPRE + NFULL) * NT
    NOWN = (NFULL - 1) * NT
    xp = np.asarray(inp["x_prompt"], np.float32)
    xs = np.asarray(inp["x_sample"], np.float32)
    nseg = xp.shape[1] // NOWN
    lnp = np.stack([np.asarray(inp[k], np.float32)[0] for k in ("ln1_g", "ln1_b", "ln2_g", "ln2_b")])
    shared = {
        "w_in": np.ascontiguousarray(inp["w_in"][0]), "gdn_conv_w": np.ascontiguousarray(inp["gdn_conv_w"][0]),
        "a_log": np.ascontiguousarray(inp["a_log"]), "dt_bias": np.ascontiguousarray(inp["dt_bias"]),
        "o_norm_g": np.ascontiguousarray(inp["o_norm_g"]), "sc_conv_w": np.ascontiguousarray(inp["sc_conv_w"][0]),
        "w_o": np.ascontiguousarray(inp["w_o"][0]), "lnp": lnp, "w_up": np.ascontiguousarray(inp["w_up"][0]),
        "ffn_conv_w": np.ascontiguousarray(inp["ffn_conv_w"][0]), "w_down": np.ascontiguousarray(inp["w_down"][0]),
        "consts": _consts(),
    }
    maps = []
    for core in range(ncores):
        b, j = core // nseg, core % nseg
        end = NOWN * (j + 1)
        x_ext = np.zeros((ROWS, D), np.float32)
        n = min(end, ROWS)
        x_ext[ROWS - n:] = xp[b, end - n:end]
        m = dict(shared)
        m["xp"] = x_ext
        m["xs"] = np.ascontiguousarray(xs[4 * core:4 * core + 4].reshape(64, D))
        m["st_gc"] = np.ascontiguousarray(inp["state_gdn_conv"][0, 4 * core:4 * core + 4].reshape(12, 3072))
        m["st_S"] = np.ascontiguousarray(inp["state_gdn_S"][0, 4 * core:4 * core + 4])
        m["st_sc"] = np.ascontiguousarray(inp["state_sc_conv"][0, 4 * core:4 * core + 4].reshape(8, D))
        m["st_ff"] = np.ascontiguousarray(inp["state_ffn_conv"][0, 4 * core:4 * core + 4].reshape(8, DFF))
        maps.append(m)
    return maps


_NC_CACHE = {}


def kernel(**inp):
    NPRE, NFULL = 47, 17
    key = (NPRE, NFULL)
    if key not in _NC_CACHE:
        _NC_CACHE[key] = build(NPRE, NFULL)
    nc = _NC_CACHE[key]
    maps = make_in_maps(inp, NPRE, NFULL)
    res = run_bass_kernel_spmd(nc, maps, core_ids=list(range(8))).results
    B, SEQ = 2, 16384
    yp = np.zeros((B, SEQ, D), np.float32)
    ys = np.zeros((32, 16, D), np.float32)
    p_gc = np.zeros((1, B, 3, 3072), np.float32)
    p_S = np.zeros((1, B, 8, 128, 128), np.float32)
    p_sc = np.zeros((1, B, 2, D), np.float32)
    p_ff = np.zeros((1, B, 2, DFF), np.float32)
    s_gc = np.zeros((1, 32, 3, 3072), np.float32)
    s_S = np.zeros((1, 32, 8, 128, 128), np.float32)
    s_sc = np.zeros((1, 32, 2, D), np.float32)
    s_ff = np.zeros((1, 32, 2, DFF), np.float32)
    for core in range(8):
        r = res[core]
        b, j = core // 4, core % 4
        yp[b, 4096 * j:4096 * (j + 1)] = r["yp"]
        ys[4 * core:4 * core + 4] = r["ys"].reshape(4, 16, D)
        s_gc[0, 4 * core:4 * core + 4] = r["s_gc"].reshape(4, 3, 3072)
        s_S[0, 4 * core:4 * core + 4] = r["s_S"]
        s_sc[0, 4 * core:4 * core + 4] = r["s_sc"].reshape(4, 2, D)
        s_ff[0, 4 * core:4 * core + 4] = r["s_ff"].reshape(4, 2, DFF)
        if j == 3:
            p_gc[0, b] = r["p_gc"]
            p_S[0, b] = r["p_S"]
            p_sc[0, b] = r["p_sc"]
            p_ff[0, b] = r["p_ff"]
    return (yp, ys, p_gc, p_S, p_sc, p_ff, s_gc, s_S, s_sc, s_ff)
```

```python
import numpy as np
from contextlib import ExitStack
import concourse.bass as bass
import concourse.mybir as mybir
from concourse.bass_utils import run_bass_kernel_spmd

F32 = mybir.dt.float32
BF16 = mybir.dt.bfloat16
ALU = mybir.AluOpType
AF = mybir.ActivationFunctionType
AX = mybir.AxisListType

D = 1024
NT = 256
DFF = 2816
NFC = 22
ALPHA = 2.0 ** 0.25
LN_EPS = 1e-5
NORM_EPS = 1e-6
ENGS = ("pe", "act", "dve", "pool", "sp")
NROT = 4
NDMASEM = 40
STOP = None
STOP2 = None
PEND_MODE = None


BANK_GEN = {}


class BankRes(str):
    def __new__(cls, s, gen):
        o = str.__new__(cls, s)
        o.gen = gen
        return o


class Op:
    __slots__ = ("eng", "fn", "waits", "inc", "epoch", "dma", "cnt")


class Prog:
    def __init__(self):
        self.ops = {e: [] for e in ENGS}
        self.lastw = {}
        self.rd = {}
        self.children = {}
        self.waited = {}
        self.dwaited = {}
        self.epoch = 0
        self.dma_vals = [0] * NDMASEM
        self.dma_rr = 0
        self.pool_dmas = 0

    @staticmethod
    def _norm(reads, writes):
        ps = [r for r in reads if r.startswith("ps")]
        if ps:
            reads = [r for r in reads if not r.startswith("ps")]
            writes = list(writes) + [p for p in ps if p not in writes]
        return reads, writes

    def _rel(self, r):
        if "/" in r:
            par = r.split("/")[0]
            self.children.setdefault(par, set()).add(r)
            return (r, par)
        ch = self.children.get(r)
        return (r,) + tuple(ch) if ch else (r,)

    def _mk(self, eng, fn, reads, writes, extra=()):
        deps = list(extra)
        for r in reads:
            for q in self._rel(r):
                t = self.lastw.get(q)
                if t is not None:
                    deps.append(t)
        for w in writes:
            for q in self._rel(w):
                t = self.lastw.get(q)
                if t is not None:
                    deps.append(t)
                deps.extend(self.rd.get(q, ()))
        op = Op()
        op.eng, op.fn, op.inc, op.epoch, op.dma, op.cnt = eng, fn, False, self.epoch, None, None
        waits = []
        best = {}
        for t in deps:
            if t[0] == "op":
                _, e2, i2 = t
                if e2 == eng and eng == "pe":
                    continue
                if i2 > best.get(e2, -1):
                    best[e2] = i2
            else:
                _, k, v = t
                if self.dwaited.get((eng, k), 0) < v:
                    self.dwaited[(eng, k)] = v
                    waits.append(("dma", k, v))
        for e2, i2 in best.items():
            if self.waited.get((eng, e2), -1) < i2:
                self.waited[(eng, e2)] = i2
                self.ops[e2][i2].inc = True
                waits.append(("op", e2, i2))
        op.waits = waits
        return op

    def _commit(self, tok, reads, writes):
        for r in reads:
            self.rd.setdefault(r, []).append(tok)
        for w in writes:
            self.lastw[w] = tok
            self.rd[w] = []

    def add(self, eng, fn, reads=(), writes=()):
        for r in list(reads) + list(writes):
            if isinstance(r, BankRes):
                assert BANK_GEN[str(r)] == r.gen, f"stale PSUM bank {r} gen {r.gen} != {BANK_GEN[str(r)]}"
        reads, writes = self._norm(reads, writes)
        op = self._mk(eng, fn, reads, writes)
        idx = len(self.ops[eng])
        self.ops[eng].append(op)
        self._commit(("op", eng, idx), reads, writes)

    def dma(self, eng, out, in_, reads=(), writes=()):
        if eng == "pool":
            op = self._mk(eng, lambda e: e.dma_start(out=out, in_=in_), reads, writes)
            self.pool_dmas += 1
            op.dma = ("p", 0)
            self.ops[eng].append(op)
            self._commit(("dma", "p", 1), reads, writes)
            return
        k = self.dma_rr
        self.dma_rr = (k + 1) % NDMASEM
        prev = self.dma_vals[k]
        extra = [("dma", k, prev)] if prev > 0 else []
        op = self._mk(eng, lambda e: e.dma_start(out=out, in_=in_), reads, writes, extra)
        self.dma_vals[k] = prev + 16
        op.dma = (k, prev + 16)
        self.ops[eng].append(op)
        self._commit(("dma", k, prev + 16), reads, writes)

    def emit(self, nc, es):
        sems = {e: [es.enter_context(nc.semaphore(f"s_{e}{r}")) for r in range(NROT)] for e in ENGS}
        dsems = [es.enter_context(nc.semaphore(f"d{k}")) for k in range(NDMASEM)]
        psem = es.enter_context(nc.semaphore("pdma"))
        ptotal = 16 * self.pool_dmas
        for e in ENGS:
            cnt = [0] * NROT
            for op in self.ops[e]:
                if op.inc:
                    r = op.epoch % NROT
                    cnt[r] += 1
                    op.cnt = (r, cnt[r])
        final_vals = list(self.dma_vals)

        def run(e, eo):
            for op in self.ops[e]:
                for w in op.waits:
                    if w[0] == "op":
                        d = self.ops[w[1]][w[2]]
                        eo.wait_ge(sems[w[1]][d.cnt[0]], d.cnt[1])
                    elif w[1] == "p":
                        eo.wait_ge(psem, ptotal)
                    else:
                        eo.wait_ge(dsems[w[1]], w[2])
                ins = op.fn(eo)
                if op.dma is not None and op.dma[0] == "p":
                    ins.then_inc(psem, 16)
                elif op.dma is not None:
                    ins.then_inc(dsems[op.dma[0]], 16)
                elif op.inc:
                    ins.then_inc(sems[e][op.cnt[0]], 1)
            if e == "sp":
                for k, v in enumerate(final_vals):
                    if v > 0:
                        eo.wait_ge(dsems[k], v)
                if ptotal:
                    eo.wait_ge(psem, ptotal)

        block = es.enter_context(nc.Block())

        @block.tensor
        def _(eo):
            run("pe", eo)

        @block.scalar
        def _(eo):
            run("act", eo)

        @block.vector
        def _(eo):
            run("dve", eo)

        @block.gpsimd
        def _(eo):
            run("pool", eo)

        @block.sync
        def _(eo):
            run("sp", eo)


def build(NPRE, NFULL, nsamp=4):
    BANK_GEN.clear()
    nc = bass.Bass("TRN2", target_bir_lowering=False)
    P = Prog()
    es = ExitStack()
    ROWS = (NPRE + NFULL) * NT
    NOWN = (NFULL - 1) * NT

    def din(name, shape):
        return nc.dram_tensor(name, list(shape), F32, kind="ExternalInput").ap()

    def dout(name, shape):
        return nc.dram_tensor(name, list(shape), F32, kind="ExternalOutput").ap()

    xp = din("xp", [ROWS, D])
    xs = din("xs", [nsamp * 16, D])
    st_gc = din("st_gc", [nsamp * 3, 3072])
    st_S = din("st_S", [nsamp, 8, 128, 128])
    st_sc = din("st_sc", [nsamp * 2, D])
    st_ff = din("st_ff", [nsamp * 2, DFF])
    w_in = din("w_in", [D, 9232])
    gdn_conv_w = din("gdn_conv_w", [4, 3072])
    a_log = din("a_log", [1, 8])
    dt_bias = din("dt_bias", [1, 8])
    o_norm_g = din("o_norm_g", [1, 128])
    sc_conv_w = din("sc_conv_w", [3, D])
    w_o = din("w_o", [D, D])
    lnp_d = din("lnp", [4, D])
    w_up = din("w_up", [D, 2 * DFF])
    ffn_conv_w = din("ffn_conv_w", [3, DFF])
    w_down = din("w_down", [DFF, D])
    consts = din("consts", [128, 384])

    yp = dout("yp", [NOWN, D])
    ys = dout("ys", [nsamp * 16, D])
    p_gc = dout("p_gc", [3, 3072])
    p_S = dout("p_S", [8, 128, 128])
    p_sc = dout("p_sc", [2, D])
    p_ff = dout("p_ff", [2, DFF])
    s_gc = dout("s_gc", [nsamp * 3, 3072])
    s_S = dout("s_S", [nsamp, 8, 128, 128])
    s_sc = dout("s_sc", [nsamp * 2, D])
    s_ff = dout("s_ff", [nsamp * 2, DFF])

    win_g = nc.dram_tensor("win_g", [36, 128, 2048], BF16).ap()
    wo_g = nc.dram_tensor("wo_g", [4, 128, 2048], BF16).ap()
    wup_g = nc.dram_tensor("wup_g", [NFC, 128, 2048], BF16).ap()
    wdn_g = nc.dram_tensor("wdn_g", [12, 128, 2048], BF16).ap()

    def sb(name, shape, dt=F32):
        return es.enter_context(nc.sbuf_tensor(name, list(shape), dt))

    banks = [es.enter_context(nc.psum_tensor(f"bank{i}", [128, 512], F32)) for i in range(8)]
    POOLS = {"all": list(range(8)), "front": [4, 5, 6, 7], "pend": [0, 1, 2, 3]}
    bank_rr = {"all": 0, "front": 0, "pend": 0}
    cur_pool = ["all"]

    def bank():
        pn = cur_pool[0]
        pool = POOLS[pn]
        i = pool[bank_rr[pn] % len(pool)]
        bank_rr[pn] += 1
        BANK_GEN[f"ps{i}"] = BANK_GEN.get(f"ps{i}", 0) + 1
        return banks[i], BankRes(f"ps{i}", BANK_GEN[f"ps{i}"])

    rot_state = {}

    def rot(name, n):
        i = rot_state.get(name, 0)
        rot_state[name] = (i + 1) % n
        return i

    cst = sb("cst", [128, 384])
    ident = cst[:, 0:128]
    tri = cst[:64, 128:192]
    sgt = cst[:64, 192:256]
    ones = cst[:, 256:384]
    ident_bf = sb("ident_bf", [128, 128], BF16)
    ones_bf = sb("ones_bf", [128, 128], BF16)
    trisg_bf = sb("trisg_bf", [64, 128], BF16)
    tri_bf = trisg_bf[:, 0:64]
    sgt_bf = trisg_bf[:, 64:128]
    onesm = sb("onesm", [128, 128])
    prm = sb("prm", [128, 224])
    epsn = sb("epsn", [128, 8])
    dtb = sb("dtb", [128, 8])
    nA = sb("nA", [128, 8])
    hso = sb("hso", [12, 512])

    P.dma("sp", cst[:], consts[:, :], writes=["cst"])
    P.dma("sp", hso[0:1, 0:8], dt_bias[:, :], writes=["hso"])
    P.dma("sp", hso[0:1, 8:16], a_log[:, :], writes=["hso"])
    bk, bkr = bank()
    P.add("pe", lambda e, bk=bk: e.matmul(bk[:, 0:16], ones[0:1, :], hso[0:1, 0:16], start=True, stop=True),
          reads=["hso", "cst"], writes=[bkr])
    P.add("dve", lambda e, bk=bk: e.tensor_copy(dtb[:], bk[:, 0:8]), reads=[bkr], writes=["dtb"])
    P.add("dve", lambda e, bk=bk: e.tensor_copy(nA[:], bk[:, 8:16]), reads=[bkr], writes=["nA"])

    WIN_RES = [f"win_sA{r}" for r in range(8)] + [f"win_sB{r}" for r in range(8)]
    WIN_KV = [f"win_kv{r}" for r in range(8)]
    WO_RES = [f"wo_s{r}" for r in range(8)]
    WUP_RES = [f"wup_sA{r}" for r in range(8)] + [f"wup_sB{r}" for r in range(8)]
    WDN_RES = [f"wdn_s{r}" for r in range(NFC)]
    wba = sb("wba", [128, 8, 16], BF16)

    P.add("pool", lambda e: e.memset(epsn[:, 0:1], NORM_EPS), writes=["epsc"])
    P.add("pool", lambda e: e.memset(epsn[:, 1:2], LN_EPS), writes=["epsc"])
    P.add("pool", lambda e: e.memset(epsn[:, 2:3], 1.0), writes=["epsc"])
    P.add("pool", lambda e: e.memset(epsn[:, 3:4], -1.0), writes=["epsc"])
    P.add("pool", lambda e: e.memset(epsn[:, 4:5], ALPHA), writes=["epsc"])
    P.add("dve", lambda e: e.tensor_copy(ident_bf[:], ident), reads=["cst"], writes=["ident_bf"])
    P.add("dve", lambda e: e.tensor_copy(ones_bf[:], ones), reads=["cst"], writes=["ones_bf"])
    P.add("dve", lambda e: e.tensor_copy(trisg_bf[:], cst[:64, 128:256]), reads=["cst"], writes=["ones_bf"])
    P.add("dve", lambda e: e.tensor_scalar(onesm[:], ones, 1.0 / D, None, ALU.mult), reads=["cst"], writes=["onesm"])
    P.add("dve", lambda e: e.tensor_copy(onesm_bf[:], onesm[:]), reads=["onesm"], writes=["onesm_bf"])
    P.add("act", lambda e: e.activation(nA[:], nA[:], AF.Exp), reads=["nA"], writes=["nA"])
    P.add("dve", lambda e: e.tensor_scalar(nA[:], nA[:], -1.0, None, ALU.mult), reads=["nA"], writes=["nA"])

    bk, bkr = bank()
    col = 0
    plist = [(gdn_conv_w, 4, 3072), (sc_conv_w, 3, D), (ffn_conv_w, 3, DFF), (lnp_d, 4, D), (o_norm_g, 1, 128)]
    for (src_d, r, width) in plist:
        for c0 in range(0, width, 512):
            wd = min(512, width - c0)
            P.dma("sp", hso[:r, 0:wd], src_d[:, c0:c0 + wd], writes=["hso"])
            for ch in range(wd // 128):
                o_ap = bk[:, col:col + r]
                i_ap = hso[:r, ch * 128:(ch + 1) * 128]
                P.add("pe", lambda e, o_ap=o_ap, i_ap=i_ap, r=r: e.transpose(o_ap, i_ap, ident[:r, :r]),
                      reads=["hso", "cst"], writes=[bkr])
                col += r
    P.add("dve", lambda e: e.tensor_copy(prm[:, 0:219], bk[:, 0:219]), reads=[bkr], writes=["prm"])
    cwg = prm[:, 0:96].rearrange("p (c k) -> p c k", k=4)
    cws = prm[:, 96:120].rearrange("p (c k) -> p c k", k=3)
    cwf = prm[:, 120:186].rearrange("p (c k) -> p c k", k=3)
    lnp = prm[:, 186:218].rearrange("p (c k) -> p c k", k=4)
    ong = prm[:, 218:219]

    xa = sb("xa", [128, 2, D])
    xT = sb("xT", [128, 8, NT], BF16)
    x_f = sb("x_f", [128, 8, NT])
    preqkv = sb("preqkv", [128, 24, NT + 3], BF16)
    post = sb("post", [128, 24, NT], BF16)
    NCACC = 4
    cacc = [sb(f"cacc{i}", [128, NT]) for i in range(NCACC)]
    sqb = [sb(f"sqb{i}", [128, NT], BF16) for i in range(2)]
    rsb = [sb(f"rsb{i}", [128, NT]) for i in range(2)]
    wst = [sb(f"wst{i}", [128, 2048], BF16) for i in range(4)]
    szg = sb("szg", [128, 8, NT], BF16)
    ppre = sb("ppre", [128, 8, NT + 2], BF16)
    mixg = sb("mixg", [128, 8, NT], BF16)
    sCs = mixg
    mixed = sb("mixed", [128, 8, NT], BF16)
    r1 = sb("r1", [128, 8, NT])
    cfl = r1
    onesm_bf = sb("onesm_bf", [128, 128], BF16)
    mean_sb = sb("mean_sb", [128, NT])
    m2 = sb("m2", [128, NT])
    rstd = sb("rstd", [128, NT])
    x1f = r1
    x1b = mixed
    apre = [sb(f"apre{i}", [128, NT + 2]) for i in range(4)]
    ahalo = sb("ahalo", [128, NFC, 4, 2])
    gab = [sb(f"gab{i}", [128, NT], BF16) for i in range(4)]
    hb = sb("hb", [128, NFC, NT], BF16)
    yout = hb[:].rearrange("p c t -> p (c t)").bitcast(F32)[:, 0:2 * D].rearrange("p (tb d) -> p tb d", tb=2)
    hs = sb("hs", [128, 24, 12])
    ba_beta = sb("ba_beta", [64, 4, 8])
    ba_t = sb("ba_t", [64, 4, 8])
    ba_g = sb("ba_g", [64, 4, 8])
    ba_gh = sb("ba_gh", [64, 4, 8], BF16)
    ba_gl = sb("ba_gl", [64, 4, 8], BF16)
    NB = 4
    eGt = [sb(f"eGt{i}", [64, 16]) for i in range(NB)]
    eGl = [sb(f"eGl{i}", [128, 8]) for i in range(NB)]
    bg = [sb(f"bg{i}", [64, 8]) for i in range(NB)]
    gtri = [sb(f"gtri{i}", [64, 2, 8, 64], BF16) for i in range(NB)]
    Gam = [sb(f"Gam{i}", [64, 8, 64], BF16) for i in range(NB)]
    GamT = [sb(f"GamT{i}", [64, 8, 64], BF16) for i in range(NB)]
    kbg = [sb(f"kbg{i}", [64, 8, 128], BF16) for i in range(NB)]
    ktail = [sb(f"ktail{i}", [64, 8, 128], BF16) for i in range(NB)]
    vb = [sb(f"vb{i}", [64, 8, 128], BF16) for i in range(NB)]
    Am = [sb(f"Am{i}", [64, 8, 64], BF16) for i in range(NB)]
    Xm = [[sb(f"Xm{i}_{k}", [64, 8, 64], BF16) for k in range(2)] for i in range(NB)]
    Ym = [[sb(f"Ym{i}_{k}", [64, 8, 64], BF16) for k in range(2)] for i in range(NB)]
    Rm = [[sb(f"Rm{i}_0", [64, 8, 64], BF16)] * 2 for i in range(NB)]
    nwT = [sb(f"nwT{i}", [128, 8, 64], BF16) for i in range(NB)]
    vnew = [sb("vnew0", [64, 8, 128], BF16)] * NB
    qkT = [sb("qkT0", [64, 8, 64], BF16)] * NB
    o1s = [sb("o1s0", [64, 8, 128])] * NB
    osq = [sb("osq0", [64, 8, 128], BF16)] * NB
    oss = [sb(f"oss{i}", [64, 8]) for i in range(NB)]
    onb = [sb(f"onb{i}", [64, 8, 128], BF16) for i in range(2)] * 2
    Sp = sb("Sp", [128, 8, 128])
    Spb = sb("Spb", [128, 8, 128], BF16)
    Ss, Ssb = Sp, Spb

    stg_f = [(r1, "r1"), (x_f, "x_f")]
    stg_b = [(mixed, "mixed"), (mixg, "mixg")]
    pp = [0]

    def cast_block(src_ap, dst_ap, ncol, res, nk=None):
        i = pp[0] % 2
        pp[0] += 1
        (sf, sfr), (sbf, sbr) = stg_f[i], stg_b[i]
        fv = sf[:].rearrange("p a b -> p (a b)")[:, 0:ncol]
        bv = sbf[:].rearrange("p a b -> p (a b)")[:, 0:ncol]
        if nk is not None:
            fv = fv.rearrange("p (c k) -> p c k", k=nk)
            bv = bv.rearrange("p (c k) -> p c k", k=nk)
        P.dma("sp", fv, src_ap, writes=[sfr])
        eng = ("act", "dve", "pool")[pp[0] % 3]
        if eng == "act":
            P.add("act", lambda e: e.activation(bv, fv, AF.Copy), reads=[sfr], writes=[sbr])
        else:
            P.add(eng, lambda e: e.tensor_copy(bv, fv), reads=[sfr], writes=[sbr])
        P.dma("sp", dst_ap, bv, reads=[sbr], writes=[res])

    cast_jobs = []
    for r in range(8):
        rs = slice(r * 128, (r + 1) * 128)
        cast_block(w_in[rs, 1024:3072].rearrange("r (g c) -> r g c", c=256), win_g[4:12, :, r * 256:(r + 1) * 256].rearrange("g p c -> p g c"), 2048, f"win_kv{r}", nk=256)
    for r in range(8):
        rs = slice(r * 128, (r + 1) * 128)
        cast_jobs.append(lambda rs=rs, r=r: cast_block(w_in[rs, 0:1024].rearrange("r (g c) -> r g c", c=256), win_g[0:4, :, r * 256:(r + 1) * 256].rearrange("g p c -> p g c"), 1024, f"win_sA{r}", nk=256))
        cast_jobs.append(lambda rs=rs, r=r: cast_block(w_in[rs, 3072:4096].rearrange("r (g c) -> r g c", c=256), win_g[12:16, :, r * 256:(r + 1) * 256].rearrange("g p c -> p g c"), 1024, f"win_sA{r}", nk=256))
        cast_jobs.append(lambda rs=rs, r=r: cast_block(w_in[rs, 4112:6160].rearrange("r (g c) -> r g c", c=256), win_g[16:24, :, r * 256:(r + 1) * 256].rearrange("g p c -> p g c"), 2048, f"win_sB{r}", nk=256))
        cast_jobs.append(lambda rs=rs, r=r: cast_block(w_in[rs, 6160:8208].rearrange("r (g c) -> r g c", c=256), win_g[24:32, :, r * 256:(r + 1) * 256].rearrange("g p c -> p g c"), 2048, f"win_sB{r}", nk=256))
        cast_jobs.append(lambda rs=rs, r=r: cast_block(w_in[rs, 8208:9232].rearrange("r (g c) -> r g c", c=256), win_g[32:36, :, r * 256:(r + 1) * 256].rearrange("g p c -> p g c"), 1024, f"win_sB{r}", nk=256))
        cast_jobs.append(lambda rs=rs, r=r: cast_block(w_o[rs, :].rearrange("r (g c) -> r g c", c=256), wo_g[:, :, r * 256:(r + 1) * 256].rearrange("g p c -> p g c"), 1024, f"wo_s{r}", nk=256))
        for (c0, c1) in ((0, 16), (16, NFC)):
            for half in range(2):
                cast_jobs.append(lambda rs=rs, r=r, c0=c0, c1=c1, half=half: cast_block(
                    w_up[rs, half * DFF + c0 * 128:half * DFF + c1 * 128].rearrange("r (c k) -> r c k", k=128),
                    wup_g[c0:c1, :, r * 256 + half * 128:r * 256 + half * 128 + 128].rearrange("g p c -> p g c"),
                    (c1 - c0) * 128, f"wup_s{'AB'[half]}{r}", nk=128))
    for r in range(NFC):
        rs = slice(r * 128, (r + 1) * 128)
        kg, k8 = r // 8, r % 8
        cast_jobs.append(lambda rs=rs, r=r, kg=kg, k8=k8: cast_block(
            w_down[rs, :].rearrange("r (g c) -> r g c", c=256),
            wdn_g.rearrange("(cg kg) p c -> cg kg p c", kg=3)[:, kg, :, k8 * 256:(k8 + 1) * 256].rearrange("g p c -> p g c"),
            1024, f"wdn_s{r}", nk=256))
    fv = r1[:].rearrange("p a b -> p (a b)")[:, 0:128].rearrange("p (k c) -> p k c", c=16)
    P.dma("sp", fv, w_in[:, 4096:4112].rearrange("(kc p) c -> p kc c", p=128), writes=["r1"])
    P.add("dve", lambda e: e.tensor_copy(wba[:], fv), reads=["r1"], writes=["wba"])

    P.add("pool", lambda e: e.memset(Sp[:], 0.0), writes=["Sp"])
    P.add("pool", lambda e: e.memset(Spb[:], 0.0), writes=["Spb"])
    P.add("pool", lambda e: e.memset(preqkv[:], 0.0), writes=["preqkv"])
    P.add("pool", lambda e: e.memset(ppre[:], 0.0), writes=["ppre"])
    P.add("pool", lambda e: e.memset(ahalo[:], 0.0), writes=["ahalo"])

    wst_rr = [0]
    xa_loaded = [False]

    def wload(src_ap, nel, shape_str, **kw):
        i = wst_rr[0]
        wst_rr[0] = (i + 1) % 4
        view = wst[i][:, 0:nel].rearrange(shape_str, **kw)
        return i, view

    def proj(xin, xres, wview, wres, ncc, N, consume, cc0=0):
        for cc in range(ncc):
            bk, bkr = bank()
            for kc in range(8):
                P.add("pe", lambda e, bk=bk, kc=kc, cc=cc: e.matmul(
                    bk[:, 0:N], wview[:, kc, cc * 128:(cc + 1) * 128], xin[:, kc, 0:N],
                    start=(kc == 0), stop=(kc == 7)), reads=[xres, wres], writes=[bkr])
            consume(cc0 + cc, bk[:, 0:N], bkr)

    def tile(x_src, nseq, L, C, full, S, Sb, Sres, y_dst, hist_src=None, state_dst=None, write_y=True, S_src=None, x_next=None):
        N = nseq * L
        nch = N // C
        ntb = (N + 127) // 128
        TB = min(N, 128)
        H = 3

        def v3(ap):
            return ap.rearrange("p (s l) -> p s l", s=nseq)

        if not xa_loaded[0]:
            P.dma("sp", xa[:TB, 0:ntb, :], x_src.rearrange("(tb p) d -> p tb d", p=TB), writes=["xa"])
        xa_loaded[0] = False
        for tb in range(ntb):
            for kq in range(2):
                bk, bkr = bank()
                for k4 in range(4):
                    kc = kq * 4 + k4
                    P.add("pe", lambda e, bk=bk, k4=k4, kc=kc, tb=tb: e.transpose(
                        bk[:, k4 * 128:k4 * 128 + TB], xa[:TB, tb, kc * 128:(kc + 1) * 128], ident[:TB, :TB]),
                        reads=["xa", "cst"], writes=[bkr])
                src = bk[:, :].rearrange("p (k t) -> p k t", k=4)[:, :, 0:TB]
                P.add("dve", lambda e, src=src, kq=kq, tb=tb: e.tensor_copy(
                    xT[:, kq * 4:kq * 4 + 4, tb * 128:tb * 128 + TB], src), reads=[bkr], writes=["xT"])
                if full:
                    P.add("act", lambda e, src=src, kq=kq, tb=tb: e.activation(
                        x_f[:, kq * 4:kq * 4 + 4, tb * 128:tb * 128 + TB], src, AF.Copy),
                        reads=[bkr], writes=["x_f"])

        if x_next is not None:
            xn, tbn, ntbn = x_next
            P.dma("sp", xa[:tbn, 0:ntbn, :], xn.rearrange("(tb p) d -> p tb d", p=tbn), reads=["xT"], writes=["xa"])
            xa_loaded[0] = True
        yield "front"
        pv = preqkv[:, :, 0:nseq * (L + H)].rearrange("p c (s l) -> p c s l", s=nseq)
        ppv = ppre[:, :, 0:nseq * (L + 2)].rearrange("p c (s l) -> p c s l", s=nseq)
        ahv = ahalo[:, :, 0:nseq, :]

        def load_hist(src_d, nrows, r, width, dst, dstres):
            for c0 in range(0, width, 512):
                wd = min(512, width - c0)
                ncc = wd // 128
                P.dma("sp", hso[:nrows, 0:wd], src_d[:, c0:c0 + wd], writes=["hso"])
                bk, bkr = bank()
                for k in range(ncc):
                    P.add("pe", lambda e, bk=bk, k=k: e.transpose(
                        bk[:, k * 12:k * 12 + nrows], hso[:nrows, k * 128:(k + 1) * 128],
                        ident[:nrows, :nrows]), reads=["hso", "cst"], writes=[bkr])
                srcv = bk[:, 0:ncc * 12].rearrange("p (c x) -> p c x", x=12)[:, :, 0:nrows].rearrange(
                    "p c (s r) -> p c s r", r=r)
                P.add("dve", lambda e, srcv=srcv, c0=c0, ncc=ncc: e.tensor_copy(
                    dst[:, c0 // 128:c0 // 128 + ncc, :, :], srcv), reads=[bkr], writes=[dstres])

        if hist_src is not None:
            (h_gc, h_sc, h_ff) = hist_src
            load_hist(h_gc, nseq * 3, 3, 3072, pv[:, :, :, 0:3], "preqkv")
            load_hist(h_sc, nseq * 2, 2, D, ppv[:, :, :, 0:2], "ppre")
            load_hist(h_ff, nseq * 2, 2, DFF, ahv, "ahalo")
        else:
            P.add("pool", lambda e: e.tensor_copy(pv[:, :, :, 0:3], pv[:, :, :, L:L + 3]),
                  reads=["preqkv"], writes=["preqkv"])
            if full:
                P.add("pool", lambda e: e.tensor_copy(ppv[:, :, :, 0:2], ppv[:, :, :, L:L + 2]),
                      reads=["ppre"], writes=["ppre"])

        qpend = []
        qstage2 = []

        def qkv_flush():
            grp = list(qpend)
            del qpend[:]
            accs = []
            for (ch, ps, psr) in grp:
                P.add("act", lambda e, ch=ch, ps=ps: e.activation(pv[:, ch, :, 3:3 + L], v3(ps), AF.Copy),
                      reads=[psr], writes=[f"preqkv/{ch}"])
                accs.append(rot("cacc", NCACC))
            for (ch, ps, psr), i in zip(grp, accs):
                a3 = v3(cacc[i][:, 0:N])
                P.add("pool", lambda e, ch=ch, a3=a3: e.tensor_tensor(
                    a3, pv[:, ch, :, 0:L], cwg[:, ch, 0:1].unsqueeze(1).to_broadcast([128, nseq, L]), ALU.mult),
                    reads=[f"preqkv/{ch}", "prm"], writes=[f"cacc{i}"])
            for j in range(1, 4):
                for (ch, ps, psr), i in zip(grp, accs):
                    a3 = v3(cacc[i][:, 0:N])
                    P.add("dve", lambda e, j=j, ch=ch, a3=a3: e.scalar_tensor_tensor(
                        a3, pv[:, ch, :, j:j + L], cwg[:, ch, j:j + 1], a3, ALU.mult, ALU.add),
                        reads=[f"preqkv/{ch}", "prm", f"cacc{i}"], writes=[f"cacc{i}"])
            prev = list(qstage2)
            del qstage2[:]
            for th in prev:
                th()
            for (ch, ps, psr), i in zip(grp, accs):
                qstage2.append(lambda ch=ch, i=i: P.add(
                    "act", lambda e: e.activation(post[:, ch, 0:N], cacc[i][:, 0:N], AF.Silu),
                    reads=[f"cacc{i}"], writes=[f"post{ch}"]))

        def qkv_finish():
            if qpend:
                qkv_flush()
            prev = list(qstage2)
            del qstage2[:]
            for th in prev:
                th()

        def qkv_consume(ch, ps, psr):
            qpend.append((ch, ps, psr))
            if len(qpend) == 2:
                qkv_flush()

        l2pendB = []

        def l2_finish(keep=0):
            while len(l2pendB) > keep:
                l2pendB.pop(0)()

        def l2norm2(chs):
            for ch in chs:
                i = rot("sqb", 2)
                k = rot("cacc", NCACC)
                rs_ = cacc[k]
                P.add("pool", lambda e, ch=ch, i=i: e.tensor_tensor(
                    sqb[i][:, 0:N], post[:, ch, 0:N], post[:, ch, 0:N], ALU.mult),
                    reads=[f"post{ch}"], writes=[f"sqb{i}"])
                bk, bkr = bank()
                P.add("pe", lambda e, bk=bk, i=i: e.matmul(bk[:, 0:N], ones_bf[:], sqb[i][:, 0:N], start=True, stop=True),
                      reads=[f"sqb{i}", "ones_bf"], writes=[bkr])
                P.add("act", lambda e, bk=bk, rs_=rs_: e.activation(rs_[:, 0:N], bk[:, 0:N], AF.Ln, bias=epsn[:, 0:1]),
                      reads=[bkr, "epsc"], writes=[f"cacc{k}"])
                l2_finish(keep=2)

                def stageB(ch=ch, k=k, rs_=rs_):
                    P.add("act", lambda e: e.activation(rs_[:, 0:N], rs_[:, 0:N], AF.Exp, scale=-0.5),
                          reads=[f"cacc{k}"], writes=[f"cacc{k}"])
                    P.add("pool", lambda e: e.tensor_tensor(
                        post[:, ch, 0:N], post[:, ch, 0:N], rs_[:, 0:N], ALU.mult),
                        reads=[f"post{ch}", f"cacc{k}"], writes=[f"post{ch}"])
                l2pendB.append(stageB)

        for g in range(8):
            i, wv_ = wload(None, 2048, "p (k c) -> p k c", k=8)
            P.dma("sp", wst[i][:, 0:2048], win_g[4 + g], reads=WIN_KV, writes=[f"wst{i}"])
            proj(xT, "xT", wv_, f"wst{i}", 2, N, qkv_consume, cc0=8 + g * 2)
            yield "front"
        if full:
            for g in range(4):
                i, wv_ = wload(None, 2048, "p (k c) -> p k c", k=8)
                P.dma("sp", wst[i][:, 0:2048], win_g[g], reads=WIN_RES, writes=[f"wst{i}"])
                proj(xT, "xT", wv_, f"wst{i}", 2, N, qkv_consume, cc0=g * 2)
        qkv_finish()
        l2todo = []
        if full:
            l2todo = [[ch, ch + 1] for ch in range(0, 16, 2)]
        else:
            for ch in range(8, 16, 2):
                l2norm2([ch, ch + 1])
                yield "front"
            l2_finish()

        bk, bkr = bank()
        bav = bk[:C, 0:nch * 16].rearrange("p (c k) -> p c k", k=16)
        for c in range(nch):
            for kc in range(8):
                P.add("pe", lambda e, c=c, kc=kc: e.matmul(
                    bav[:, c, :], xT[:, kc, c * C:(c + 1) * C], wba[:, kc, :], start=(kc == 0), stop=(kc == 7)),
                    reads=["xT", "wba"], writes=[bkr])
        P.add("act", lambda e: e.activation(ba_beta[:C, 0:nch, :], bav[:, :, 0:8], AF.Exp, scale=-1.0),
              reads=[bkr], writes=["ba_beta"])
        P.add("dve", lambda e: e.tensor_scalar(ba_beta[:C, 0:nch, :], ba_beta[:C, 0:nch, :], 1.0, None, ALU.add),
              reads=["ba_beta"], writes=["ba_beta"])
        P.add("dve", lambda e: e.reciprocal(ba_beta[:C, 0:nch, :], ba_beta[:C, 0:nch, :]),
              reads=["ba_beta"], writes=["ba_beta"])
        P.add("dve", lambda e: e.tensor_tensor(
            ba_t[:C, 0:nch, :], bav[:, :, 8:16], dtb[:C, :].unsqueeze(1).to_broadcast([C, nch, 8]), ALU.add),
            reads=[bkr, "dtb"], writes=["ba_t"])
        P.add("act", lambda e: e.activation(ba_t[:C, 0:nch, :], ba_t[:C, 0:nch, :], AF.Exp),
              reads=["ba_t"], writes=["ba_t"])
        P.add("act", lambda e: e.activation(ba_t[:C, 0:nch, :], ba_t[:C, 0:nch, :], AF.Ln, bias=epsn[:C, 2:3]),
              reads=["ba_t", "epsc"], writes=["ba_t"])
        P.add("dve", lambda e: e.tensor_tensor(
            ba_g[:C, 0:nch, :], ba_t[:C, 0:nch, :], nA[:C, :].unsqueeze(1).to_broadcast([C, nch, 8]), ALU.mult),
            reads=["ba_t", "nA"], writes=["ba_g"])
        P.add("dve", lambda e: e.tensor_copy(ba_gh[:C, 0:nch, :], ba_g[:C, 0:nch, :]), reads=["ba_g"], writes=["ba_gh"])
        P.add("dve", lambda e: e.tensor_tensor(ba_gl[:C, 0:nch, :], ba_g[:C, 0:nch, :], ba_gh[:C, 0:nch, :],
                                               ALU.subtract), reads=["ba_g", "ba_gh"], writes=["ba_gh"])

        if full:
            def mk_consume(kind):
                def f(ch, ps, psr):
                    if kind == "z":
                        P.add("act", lambda e: e.activation(szg[:, ch, 0:N], ps, AF.Silu),
                              reads=[psr], writes=[f"szg/{ch}"])
                    elif kind == "sB":
                        P.add("dve", lambda e: e.tensor_tensor(cfl[:, ch, 0:N], ps, cfl[:, ch, 0:N], ALU.mult),
                              reads=[psr, f"r1/{ch}"], writes=[f"r1/{ch}"])
                    elif kind == "sC":
                        P.add("act", lambda e: e.activation(sCs[:, ch, 0:N], ps, AF.Copy),
                              reads=[psr], writes=[f"mixg/{ch}"])
                    elif kind == "sH":
                        P.add("dve", lambda e: e.tensor_tensor(
                            ppv[:, ch, :, 2:2 + L], v3(ps), v3(sCs[:, ch, 0:N]), ALU.mult),
                            reads=[psr, f"mixg/{ch}"], writes=[f"ppre/{ch}"])
                        c3 = v3(cfl[:, ch, 0:N])
                        P.add("pool", lambda e: e.tensor_tensor(
                            c3, ppv[:, ch, :, 0:L], cws[:, ch, 0:1].unsqueeze(1).to_broadcast([128, nseq, L]), ALU.mult),
                              reads=[f"ppre/{ch}", "prm"], writes=[f"r1/{ch}"])
                        for j in range(1, 3):
                            P.add("dve", lambda e, j=j: e.scalar_tensor_tensor(
                                c3, ppv[:, ch, :, j:j + L], cws[:, ch, j:j + 1], c3, ALU.mult, ALU.add),
                                reads=[f"ppre/{ch}", "prm", f"r1/{ch}"], writes=[f"r1/{ch}"])
                    elif kind == "gA":
                        i = rot("cacc", NCACC)
                        P.add("act", lambda e: e.activation(cacc[i][:, 0:N], ps, AF.Sigmoid),
                              reads=[psr], writes=[f"cacc{i}"])
                        P.add("pool", lambda e: e.tensor_tensor(
                            szg[:, ch, 0:N], szg[:, ch, 0:N], cacc[i][:, 0:N], ALU.mult),
                            reads=[f"szg/{ch}", f"cacc{i}"], writes=[f"szg/{ch}"])
                    elif kind == "gB":
                        i = rot("cacc", NCACC)
                        P.add("act", lambda e: e.activation(cacc[i][:, 0:N], ps, AF.Sigmoid),
                              reads=[psr], writes=[f"cacc{i}"])
                        P.add("pool", lambda e: e.tensor_tensor(
                            cfl[:, ch, 0:N], cfl[:, ch, 0:N], cacc[i][:, 0:N], ALU.mult),
                            reads=[f"r1/{ch}", f"cacc{i}"], writes=[f"r1/{ch}"])
                return f

            for kind, base in [("z", 3072), ("sC", 5120), ("sH", 6144), ("sB", 4096), ("gA", 7168), ("gB", 8192)]:
                for g in range(4):
                    i, wv_ = wload(None, 2048, "p (k c) -> p k c", k=8)
                    P.dma("sp", wst[i][:, 0:2048], win_g[base // 256 + g], reads=WIN_RES, writes=[f"wst{i}"])
                    proj(xT, "xT", wv_, f"wst{i}", 2, N, mk_consume(kind), cc0=g * 2)
                    if l2todo and kind in ("sC", "sH"):
                        l2norm2(l2todo.pop(0))
                    if kind == "sH" and g == 3:
                        while l2todo:
                            l2norm2(l2todo.pop(0))
                        l2_finish()
            while l2todo:
                l2norm2(l2todo.pop(0))
            l2_finish()

        yield "front_done"
        nst = {64: 5, 16: 3}[C]
        def chunk(c):
            b = c % NB
            cs = slice(c * C, (c + 1) * C)
            Sc, Scb, Scr = S, Sb, Sres
            g_c = ba_g[:C, c, :]
            be_c = ba_beta[:C, c, :]
            bk, bkr = bank()
            gh_c = ba_gh[:C, c, :]
            gl_c = ba_gl[:C, c, :]
            for (oap, lt) in ((bk[:C, 0:8], tri_bf[:C, :C]), (bk[:C, 8:16], sgt_bf[:C, :C]), (bk[:, 16:24], ones_bf[:C, :])):
                P.add("pe", lambda e, oap=oap, lt=lt: e.matmul(oap, lt, gh_c, start=True, stop=False),
                      reads=["ones_bf", "ba_gh"], writes=[bkr])
                P.add("pe", lambda e, oap=oap, lt=lt: e.matmul(oap, lt, gl_c, start=False, stop=True),
                      reads=["ones_bf", "ba_gh"], writes=[bkr])
            yield
            P.add("act", lambda e, bk=bk: e.activation(eGt[b][:C, :], bk[:C, 0:16], AF.Exp),
                  reads=[bkr], writes=[f"eGt{b}"])
            P.add("act", lambda e, bk=bk: e.activation(eGl[b][:, :], bk[:, 16:24], AF.Exp),
                  reads=[bkr], writes=[f"eGl{b}"])
            P.add("pool", lambda e: e.tensor_tensor(bg[b][:C, :], be_c, eGt[b][:C, 0:8], ALU.mult),
                  reads=["ba_beta", f"eGt{b}"], writes=[f"bg{b}"])
            P.add("dve", lambda e: e.tensor_tensor(
                gtri[b][:C, 0, :, :C], tri[:C, :C].unsqueeze(1).to_broadcast([C, 8, C]),
                gh_c.unsqueeze(2).to_broadcast([C, 8, C]), ALU.mult),
                reads=["cst", "ba_gh"], writes=[f"gtri{b}"])
            P.add("dve", lambda e: e.tensor_tensor(
                gtri[b][:C, 1, :, :C], tri[:C, :C].unsqueeze(1).to_broadcast([C, 8, C]),
                gl_c.unsqueeze(2).to_broadcast([C, 8, C]), ALU.mult),
                reads=["cst", "ba_gh"], writes=[f"gtri{b}"])
            yield
            bkD, bkDr = bank()
            Dv = bkD[:C, 0:8 * C].rearrange("p (h c) -> p h c", h=8)
            for h in range(8):
                for hl in range(2):
                    P.add("pe", lambda e, h=h, Dv=Dv, hl=hl: e.matmul(Dv[:, h, :], gtri[b][:C, hl, h, :C], sgt_bf[:C, :C],
                                                                     start=(hl == 0), stop=(hl == 1)),
                          reads=[f"gtri{b}", "ones_bf"], writes=[bkDr])
            yield
            P.add("act", lambda e, Dv=Dv: e.activation(Gam[b][:C, :, :C], Dv, AF.Exp),
                  reads=[bkDr], writes=[f"Gam{b}"])
            P.add("pool", lambda e: e.tensor_tensor(
                Gam[b][:C, :, :C], Gam[b][:C, :, :C], sgt[:C, :C].unsqueeze(1).to_broadcast([C, 8, C]), ALU.mult),
                reads=[f"Gam{b}", "cst"], writes=[f"Gam{b}"])
            P.add("pool", lambda e: e.tensor_tensor(
                Gam[b][:C, :, :C], Gam[b][:C, :, :C], be_c.unsqueeze(2).to_broadcast([C, 8, C]), ALU.mult),
                reads=[f"Gam{b}", "ba_beta"], writes=[f"Gam{b}"])
            if full:
                bkT, bkTr = bank()
                DTv = bkT[:C, 0:8 * C].rearrange("p (h c) -> p h c", h=8)
                for h in range(8):
                    for hl in range(2):
                        P.add("pe", lambda e, h=h, DTv=DTv, hl=hl: e.matmul(
                            DTv[:, h, :], sgt_bf[:C, :C], gtri[b][:C, hl, h, :C], start=(hl == 0), stop=(hl == 1)),
                            reads=[f"gtri{b}", "ones_bf"], writes=[bkTr])
                P.add("act", lambda e, DTv=DTv: e.activation(GamT[b][:C, :, :C], DTv, AF.Exp),
                      reads=[bkTr], writes=[f"GamT{b}"])
                P.add("pool", lambda e: e.tensor_tensor(
                    GamT[b][:C, :, :C], GamT[b][:C, :, :C], tri[:C, :C].unsqueeze(1).to_broadcast([C, 8, C]),
                    ALU.mult), reads=[f"GamT{b}", "cst"], writes=[f"GamT{b}"])
            if STOP2 == "c_a":
                return
            yield
            bkk, bkkr = bank()
            kt = bkk[:, :].bitcast(BF16)[:C, :].rearrange("p (h d) -> p h d", h=8)
            for h in range(8):
                P.add("pe", lambda e, h=h, kt=kt: e.transpose(kt[:, h, :], post[:, 8 + h, cs], ident_bf[:]),
                      reads=[f"post{8 + h}", "ident_bf"], writes=[bkkr])
            yield
            P.add("dve", lambda e, kt=kt: e.tensor_tensor(
                kbg[b][:C], kt, bg[b][:C, :].unsqueeze(2).to_broadcast([C, 8, 128]), ALU.mult),
                reads=[bkkr, f"bg{b}"], writes=[f"kbg{b}"])
            P.add("dve", lambda e, kt=kt: e.tensor_tensor(
                ktail[b][:C], kt, eGt[b][:C, 8:16].unsqueeze(2).to_broadcast([C, 8, 128]), ALU.mult),
                reads=[bkkr, f"eGt{b}"], writes=[f"ktail{b}"])
            yield
            bkv, bkvr = bank()
            vt = bkv[:, :].bitcast(BF16)[:C, :].rearrange("p (h d) -> p h d", h=8)
            for h in range(8):
                P.add("pe", lambda e, h=h, vt=vt: e.transpose(vt[:, h, :], post[:, 16 + h, cs], ident_bf[:]),
                      reads=[f"post{16 + h}", "ident_bf"], writes=[bkvr])
            yield
            P.add("dve", lambda e, vt=vt: e.tensor_tensor(
                vb[b][:C], vt, be_c.unsqueeze(2).to_broadcast([C, 8, 128]), ALU.mult),
                reads=[bkvr, "ba_beta"], writes=[f"vb{b}"])
            if STOP2 == "c_f":
                return
            yield
            bka, bkar = bank()
            kkv = bka[:C, 0:8 * C].rearrange("p (h c) -> p h c", h=8)
            for h in range(8):
                P.add("pe", lambda e, h=h, kkv=kkv: e.matmul(kkv[:, h, :], post[:, 8 + h, cs], post[:, 8 + h, cs],
                                                            start=True, stop=True),
                      reads=[f"post{8 + h}"], writes=[bkar])
            yield
            P.add("dve", lambda e, kkv=kkv: e.tensor_tensor(Am[b][:C, :, :C], kkv, Gam[b][:C, :, :C], ALU.mult),
                  reads=[bkar, f"Gam{b}"], writes=[f"Am{b}"])
            if STOP2 == "c_g1":
                return
            yield
            bkb, bkbr = bank()
            atv = bkb[:C, 0:8 * C].rearrange("p (h c) -> p h c", h=8)
            for h in range(8):
                P.add("pe", lambda e, h=h, atv=atv: e.matmul(atv[:, h, :], Am[b][:C, h, :C], ident_bf[:C, :C],
                                                            start=True, stop=True),
                      reads=[f"Am{b}", "ident_bf"], writes=[bkbr])
            yield
            X0, Y0, R0 = Xm[b][0], Am[b], Rm[b][0]
            P.add("act", lambda e, atv=atv: e.activation(X0[:C, :, :C], atv, AF.Copy),
                  reads=[bkbr], writes=[f"Xm{b}_0"])
            P.add("dve", lambda e, atv=atv: e.scalar_tensor_tensor(
                R0[:C, :, :C], atv, epsn[:C, 3:4], ident[:C, :C].unsqueeze(1).to_broadcast([C, 8, C]), ALU.mult, ALU.add),
                reads=[bkbr, "cst", "epsc"], writes=[f"Rm{b}_0"])
            Xc, Xr, Yc, Yr, Rc, Rr = X0, f"Xm{b}_0", Y0, f"Am{b}", R0, f"Rm{b}_0"
            yield "presolve_done"
            for n in range(1, nst + 1):
                yield
                k = n % 2
                Yn, Ynr = Ym[b][k], f"Ym{b}_{k}"
                bky, bkyr = bank()
                yv = bky[:C, 0:8 * C].rearrange("p (h c) -> p h c", h=8)
                for h in range(8):
                    P.add("pe", lambda e, h=h, yv=yv, Xc=Xc, Yc=Yc: e.matmul(
                        yv[:, h, :], Xc[:C, h, :C], Yc[:C, h, :C], start=True, stop=True),
                        reads=[Xr, Yr], writes=[bkyr])
                yield
                P.add("act", lambda e, yv=yv, Yn=Yn: e.activation(Yn[:C, :, :C], yv, AF.Copy),
                      reads=[bkyr], writes=[Ynr])
                if n < nst:
                    Xn, Xnr = Xm[b][k], f"Xm{b}_{k}"
                    bkx, bkxr = bank()
                    xv = bkx[:C, 0:8 * C].rearrange("p (h c) -> p h c", h=8)
                    for h in range(8):
                        P.add("pe", lambda e, h=h, xv=xv, Xc=Xc, Yc=Yc: e.matmul(
                            xv[:, h, :], Yc[:C, h, :C], Xc[:C, h, :C], start=True, stop=True),
                            reads=[Xr, Yr], writes=[bkxr])
                    yield
                    P.add("dve", lambda e, xv=xv, Xn=Xn: e.tensor_copy(Xn[:C, :, :C], xv),
                          reads=[bkxr], writes=[Xnr])
                yield
                Rn, Rnr = Rm[b][0], f"Rm{b}_0"
                bkq, bkqr = bank()
                rv = bkq[:C, 0:8 * C].rearrange("p (h c) -> p h c", h=8)
                for h in range(8):
                    P.add("pe", lambda e, h=h, rv=rv, Rc=Rc, Yn=Yn: e.matmul(
                        rv[:, h, :], Yn[:C, h, :C], Rc[:C, h, :C], start=True, stop=True),
                        reads=[Ynr, Rr], writes=[bkqr])
                yield
                P.add("dve", lambda e, rv=rv, Rn=Rn, Rc=Rc: e.tensor_tensor(Rn[:C, :, :C], rv, Rc[:C, :, :C], ALU.add),
                      reads=[bkqr, Rr], writes=[Rnr])
                if n < nst:
                    Xc, Xr = Xn, Xnr
                Yc, Yr, Rc, Rr = Yn, Ynr, Rn, Rnr
            TT, TTr = Rc, Rr
            if STOP2 == "c_h":
                return
            yield
            bkw, bkwr = bank()
            wv2 = bkw[:, 0:8 * C].rearrange("p (h c) -> p h c", h=8)
            for h in range(8):
                P.add("pe", lambda e, h=h, wv2=wv2, TT=TT: e.matmul(
                    wv2[:, h, :], kbg[b][:C, h, :], TT[:C, h, :C], start=True, stop=True),
                    reads=[f"kbg{b}", TTr], writes=[bkwr])
            yield
            P.add("act", lambda e, wv2=wv2: e.activation(nwT[b][:, :, :C], wv2, AF.Copy, scale=-1.0),
                  reads=[bkwr], writes=[f"nwT{b}"])
            yield "chain"
            if S_src is not None:
                P.dma("sp", Sc[:], S_src[c].rearrange("h k v -> k h v"), writes=[Scr])
                P.add("act", lambda e, Sc=Sc, Scb=Scb: e.activation(Scb[:], Sc[:], AF.Copy),
                      reads=[Scr], writes=[Scr + "b"])
            vbanks = []
            for hh in range(2):
                bkn, bknr = bank()
                vn = bkn[:C, :].rearrange("p (h d) -> p h d", h=4)
                for h4 in range(4):
                    h = hh * 4 + h4
                    P.add("pe", lambda e, h=h, h4=h4, vn=vn, TT=TT: e.matmul(
                        vn[:, h4, :], TT[:C, h, :C], vb[b][:C, h, :], start=True, stop=False),
                        reads=[TTr, f"vb{b}"], writes=[bknr])
                    P.add("pe", lambda e, h=h, h4=h4, vn=vn, Scb=Scb: e.matmul(
                        vn[:, h4, :], nwT[b][:, h, :C], Scb[:, h, :], start=False, stop=True),
                        reads=[f"nwT{b}", f"{Scr}b/{h // 4}"], writes=[bknr])
                eng = "act" if hh == 0 else "dve"
                if eng == "act":
                    P.add("act", lambda e, vn=vn, hh=hh: e.activation(vnew[b][:C, hh * 4:hh * 4 + 4, :], vn, AF.Copy),
                          reads=[bknr], writes=["vnew0"])
                else:
                    P.add("dve", lambda e, vn=vn, hh=hh: e.tensor_copy(vnew[b][:C, hh * 4:hh * 4 + 4, :], vn),
                          reads=[bknr], writes=["vnew0"])
            o1b = []
            if full:
                for hh in range(2):
                    bko, bkor = bank()
                    ov = bko[:C, :].rearrange("p (h d) -> p h d", h=4)
                    for h4 in range(4):
                        h = hh * 4 + h4
                        P.add("pe", lambda e, h=h, h4=h4, ov=ov, Scb=Scb: e.matmul(
                            ov[:, h4, :], post[:, h, cs], Scb[:, h, :], start=True, stop=True),
                            reads=[f"post{h}", f"{Scr}b/{h // 4}"], writes=[bkor])
                    o1b.append((ov, bkor))
            for hh in range(2):
                bkd, bkdr = bank()
                dv = bkd[:, :].rearrange("p (h d) -> p h d", h=4)
                for h4 in range(4):
                    h = hh * 4 + h4
                    P.add("pe", lambda e, h=h, h4=h4, dv=dv: e.matmul(
                        dv[:, h4, :], ktail[b][:C, h, :], vnew[b][:C, h, :], start=True, stop=True),
                        reads=[f"ktail{b}", "vnew0"], writes=[bkdr])
                for h4 in range(4):
                    h = hh * 4 + h4
                    P.add("dve", lambda e, h=h, h4=h4, dv=dv, Sc=Sc: e.scalar_tensor_tensor(
                        Sc[:, h, :], Sc[:, h, :], eGl[b][:, h:h + 1], dv[:, h4, :], ALU.mult, ALU.add),
                        reads=[bkdr, f"{Scr}/{h}", f"eGl{b}"], writes=[f"{Scr}/{h}"])
                P.add("act", lambda e, Sc=Sc, Scb=Scb, hh=hh: e.activation(
                    Scb[:, hh * 4:hh * 4 + 4, :], Sc[:, hh * 4:hh * 4 + 4, :], AF.Copy),
                    reads=[f"{Scr}/{h_}" for h_ in range(hh * 4, hh * 4 + 4)], writes=[f"{Scr}b/{hh}"])
            if nseq > 1 and state_dst is not None:
                P.dma("sp", state_dst[1][c].rearrange("h k v -> k h v"), Sc[:], reads=[Scr], writes=["dram_out"])
            if full:
                for hh, (ov, bkor) in enumerate(o1b):
                    P.add("dve", lambda e, ov=ov, hh=hh: e.tensor_tensor(
                        o1s[b][:C, hh * 4:hh * 4 + 4, :], ov,
                        eGt[b][:C, hh * 4:hh * 4 + 4].unsqueeze(2).to_broadcast([C, 4, 128]), ALU.mult),
                        reads=[bkor, f"eGt{b}"], writes=["o1s0"])
                bkq2, bkq2r = bank()
                qv = bkq2[:C, 0:8 * C].rearrange("p (h c) -> p h c", h=8)
                for h in range(8):
                    P.add("pe", lambda e, h=h, qv=qv: e.matmul(qv[:, h, :], post[:, 8 + h, cs], post[:, h, cs],
                                                              start=True, stop=True),
                          reads=[f"post{8 + h}", f"post{h}"], writes=[bkq2r])
                P.add("dve", lambda e, qv=qv: e.tensor_tensor(qkT[b][:C, :, :C], qv, GamT[b][:C, :, :C], ALU.mult),
                      reads=[bkq2r, f"GamT{b}"], writes=["qkT0"])
                for hh in range(2):
                    bko, bkor = bank()
                    ov = bko[:C, :].rearrange("p (h d) -> p h d", h=4)
                    for h4 in range(4):
                        h = hh * 4 + h4
                        P.add("pe", lambda e, h=h, h4=h4, ov=ov: e.matmul(
                            ov[:, h4, :], qkT[b][:C, h, :C], vnew[b][:C, h, :], start=True, stop=True),
                            reads=["qkT0", "vnew0"], writes=[bkor])
                    P.add("dve", lambda e, ov=ov, hh=hh: e.tensor_tensor(
                        o1s[b][:C, hh * 4:hh * 4 + 4, :], ov, o1s[b][:C, hh * 4:hh * 4 + 4, :], ALU.add),
                        reads=[bkor, "o1s0"], writes=["o1s0"])
            if full:
                P.add("act", lambda e: e.activation(osq[b][:C], o1s[b][:C], AF.Square),
                      reads=["o1s0"], writes=["osq0"])
                P.add("dve", lambda e: e.reduce_sum(oss[b][:C, :], osq[b][:C], axis=AX.X),
                      reads=["osq0"], writes=[f"oss{b}"])
                P.add("dve", lambda e: e.tensor_scalar(oss[b][:C, :], oss[b][:C, :], 1.0 / 128, NORM_EPS * 128,
                                                       ALU.mult, ALU.add),
                      reads=[f"oss{b}"], writes=[f"oss{b}"])
                P.add("act", lambda e: e.activation(oss[b][:C, :], oss[b][:C, :], AF.Sqrt),
                      reads=[f"oss{b}"], writes=[f"oss{b}"])
                P.add("dve", lambda e: e.reciprocal(oss[b][:C, :], oss[b][:C, :]),
                      reads=[f"oss{b}"], writes=[f"oss{b}"])
                P.add("dve", lambda e: e.tensor_tensor(
                    onb[b][:C], o1s[b][:C], oss[b][:C, :].unsqueeze(2).to_broadcast([C, 8, 128]), ALU.mult),
                    reads=["o1s0", f"oss{b}"], writes=[f"onb{b % 2}"])
                yield "tail"
                bkt, bktr = bank()
                tv = bkt[:, 0:8 * C].rearrange("p (h c) -> p h c", h=8)
                for h in range(8):
                    P.add("pe", lambda e, h=h, tv=tv: e.matmul(tv[:, h, :], onb[b][:C, h, :], ident_bf[:C, :C],
                                                              start=True, stop=True),
                          reads=[f"onb{b % 2}", "ident_bf"], writes=[bktr])
                P.add("dve", lambda e, tv=tv: e.scalar_tensor_tensor(
                    mixg[:, :, cs], tv, ong, szg[:, :, cs], ALU.mult, ALU.mult),
                    reads=[bktr, "prm", "szg"], writes=["mixg"])

        gens = [chunk(c) for c in range(nch)]
        live = list(gens)
        while live:
            for g_ in list(live):
                if next(g_) == "presolve_done":
                    live.remove(g_)
        yield "presolve_done"
        live = list(gens)
        while live:
            for g_ in list(live):
                if next(g_) == "chain":
                    live.remove(g_)
            yield "solve"
        yield "solve_done"
        prev_tail = None
        for g_ in gens:
            alive = next(g_, None) is not None
            yield "chain"
            if prev_tail is not None:
                for _ in prev_tail:
                    pass
            prev_tail = g_ if alive else None
        if prev_tail is not None:
            for _ in prev_tail:
                pass
        yield "chain_done"

        if not full:
            return

        P.add("dve", lambda e: e.tensor_tensor(mixed[:, :, 0:N], mixg[:, :, 0:N], cfl[:, :, 0:N], ALU.add),
              reads=["mixg"] + ["r1"], writes=["mixed"])

        def layer_norm(rbuf, rres, gcol, bcol, outf, outfres):
            for eng_, lo in (("pool", 0), ("dve", 4)):
                P.add(eng_, lambda e, lo=lo: e.tensor_tensor(
                    mixed[:, lo:lo + 4, 0:N], rbuf[:, lo:lo + 4, 0:N], rbuf[:, lo:lo + 4, 0:N], ALU.mult),
                    reads=[f"{rres}/{oc}" for oc in range(lo, lo + 4)], writes=[f"mixed/{oc}" for oc in range(lo, lo + 4)])
            P.add("act", lambda e: e.activation(mixg[:, :, 0:N], rbuf[:, :, 0:N], AF.Copy),
                  reads=[rres], writes=["mixg"])
            bk, bkr = bank()
            for oc in range(8):
                P.add("pe", lambda e, oc=oc, bk=bk: e.matmul(bk[:, 0:N], onesm_bf[:], mixg[:, oc, 0:N],
                                                            start=(oc == 0), stop=(oc == 7)),
                      reads=["mixg", "onesm_bf"], writes=[bkr])
            for oc in range(8):
                P.add("pe", lambda e, oc=oc, bk=bk: e.matmul(bk[:, 256:256 + N], onesm_bf[:], mixed[:, oc, 0:N],
                                                            start=(oc == 0), stop=(oc == 7)),
                      reads=[f"mixed/{oc}", "onesm_bf"], writes=[bkr])
            P.add("act", lambda e, bk=bk: e.activation(mean_sb[:, 0:N], bk[:, 0:N], AF.Copy),
                  reads=[bkr], writes=["mean_sb"])
            P.add("pool", lambda e: e.tensor_tensor(m2[:, 0:N], mean_sb[:, 0:N], mean_sb[:, 0:N], ALU.mult),
                  reads=["mean_sb"], writes=["m2"])
            P.add("dve", lambda e, bk=bk: e.tensor_tensor(rstd[:, 0:N], bk[:, 256:256 + N], m2[:, 0:N], ALU.subtract),
                  reads=[bkr, "m2"], writes=["rstd"])
            P.add("act", lambda e: e.activation(rstd[:, 0:N], rstd[:, 0:N], AF.Sqrt, bias=epsn[:, 1:2]),
                  reads=["rstd", "epsc"], writes=["rstd"])
            P.add("dve", lambda e: e.reciprocal(rstd[:, 0:N], rstd[:, 0:N]),
                  reads=["rstd"], writes=["rstd"])
            for stat_, statr, op_ in ((mean_sb, "mean_sb", ALU.subtract), (rstd, "rstd", ALU.mult)):
                for eng_, lo in (("pool", 0), ("dve", 4)):
                    names = [f"{rres}/{oc}" for oc in range(lo, lo + 4)]
                    P.add(eng_, lambda e, lo=lo, stat_=stat_, op_=op_: e.tensor_tensor(
                        rbuf[:, lo:lo + 4, 0:N], rbuf[:, lo:lo + 4, 0:N],
                        stat_[:, 0:N].unsqueeze(1).to_broadcast([128, 4, N]), op_),
                        reads=names + [statr], writes=names)
            for oc in range(8):
                P.add("dve", lambda e, oc=oc: e.tensor_scalar(
                    outf[:, oc, 0:N], rbuf[:, oc, 0:N], lnp[:, oc, gcol:gcol + 1], lnp[:, oc, bcol:bcol + 1],
                    ALU.mult, ALU.add), reads=[f"{rres}/{oc}", "prm"], writes=[f"{outfres}/{oc}"])

        def wo_consume(oc, ps, psr):
            P.add("dve", lambda e: e.scalar_tensor_tensor(r1[:, oc, 0:N], x_f[:, oc, 0:N], epsn[:, 4:5], ps, ALU.mult, ALU.add),
                  reads=[psr, "x_f"], writes=[f"r1/{oc}"])

        for g in range(4):
            i, wv_ = wload(None, 2048, "p (k c) -> p k c", k=8)
            P.dma("sp", wst[i][:, 0:2048], wo_g[g], reads=WO_RES, writes=[f"wst{i}"])
            proj(mixed, "mixed", wv_, f"wst{i}", 2, N, wo_consume, cc0=g * 2)
        layer_norm(r1, "r1", 0, 1, x1f, "r1")
        P.add("act", lambda e: e.activation(x1b[:, :, 0:N], x1f[:, :, 0:N], AF.Copy), reads=["r1"], writes=["mixed"])

        ffn_pend = []
        for c in range(NFC):
            i, wv_ = wload(None, 2048, "p (k c) -> p k c", k=8)
            P.dma("sp", wst[i][:, 0:2048], wup_g[c], reads=WUP_RES, writes=[f"wst{i}"])
            pbanks = []
            for half in range(2):
                bk, bkr = bank()
                for kc in range(8):
                    P.add("pe", lambda e, bk=bk, kc=kc, wv_=wv_, half=half: e.matmul(
                        bk[:, 0:N], wv_[:, kc, half * 128:(half + 1) * 128], x1b[:, kc, 0:N],
                        start=(kc == 0), stop=(kc == 7)), reads=["mixed", f"wst{i}"], writes=[bkr])
                pbanks.append((bk, bkr))
            (abk, abkr), (vbk, vbkr) = pbanks
            ai = rot("apre", 4)
            k = rot("cacc", NCACC)
            apv = apre[ai][:, 0:nseq * (L + 2)].rearrange("p (s l) -> p s l", s=nseq)
            a3 = v3(cacc[k][:, 0:N])
            P.add("pool", lambda e, c=c, apv=apv: e.tensor_copy(apv[:, :, 0:2], ahv[:, c, :, :]),
                  reads=[f"ahalo/{c}"], writes=[f"apre{ai}"])
            P.add("act", lambda e, apv=apv, abk=abk: e.activation(apv[:, :, 2:2 + L], v3(abk[:, 0:N]), AF.Copy),
                  reads=[abkr], writes=[f"apre{ai}"])
            P.add("pool", lambda e, c=c, apv=apv: e.tensor_copy(ahv[:, c, :, :], apv[:, :, L:L + 2]),
                  reads=[f"apre{ai}"], writes=[f"ahalo/{c}"])
            P.add("pool", lambda e, c=c, apv=apv, a3=a3: e.tensor_tensor(
                a3, apv[:, :, 0:L], cwf[:, c, 0:1].unsqueeze(1).to_broadcast([128, nseq, L]), ALU.mult),
                reads=[f"apre{ai}", "prm"], writes=[f"cacc{k}"])
            for j in range(1, 3):
                P.add("dve", lambda e, j=j, c=c, apv=apv, a3=a3: e.scalar_tensor_tensor(
                    a3, apv[:, :, j:j + L], cwf[:, c, j:j + 1], a3, ALU.mult, ALU.add),
                    reads=[f"apre{ai}", "prm", f"cacc{k}"], writes=[f"cacc{k}"])
            prev = list(ffn_pend)
            del ffn_pend[:]
            for th in prev:
                th()

            def stage2(c=c, ai=ai, k=k, vbk=vbk, vbkr=vbkr):
                P.add("act", lambda e: e.activation(gab[ai][:, 0:N], cacc[k][:, 0:N], AF.Gelu),
                      reads=[f"cacc{k}"], writes=[f"gab{ai}"])
                P.add("dve", lambda e: e.tensor_tensor(hb[:, c, 0:N], vbk[:, 0:N], gab[ai][:, 0:N], ALU.mult),
                      reads=[vbkr, f"gab{ai}"], writes=[f"hb/{c}"])
            ffn_pend.append(stage2)
        for th in ffn_pend:
            th()

        for cg in range(4):
            bk0, bkr0 = bank()
            bk1, bkr1 = bank()
            bks, bkrs = (bk0, bk1), (bkr0, bkr1)
            for kg in range(3):
                nk = 8 if kg < 2 else NFC - 16
                i, wv_ = wload(None, nk * 256, "p (k c) -> p k c", k=nk)
                P.dma("sp", wst[i][:, 0:nk * 256], wdn_g[cg * 3 + kg][:, 0:nk * 256], reads=WDN_RES, writes=[f"wst{i}"])
                for o2 in range(2):
                    for k8 in range(nk):
                        kc = kg * 8 + k8
                        P.add("pe", lambda e, bk=bks[o2], kc=kc, k8=k8, o2=o2, wv_=wv_: e.matmul(
                            bk[:, 0:N], wv_[:, k8, o2 * 128:(o2 + 1) * 128], hb[:, kc, 0:N],
                            start=(kc == 0), stop=(kc == NFC - 1)),
                            reads=[f"hb/{kc}", f"wst{i}"], writes=[bkrs[o2]])
            for o2 in range(2):
                oc = cg * 2 + o2
                P.add("dve", lambda e, bk=bks[o2], oc=oc, o2=o2: e.scalar_tensor_tensor(
                    r1[:, oc, 0:N], r1[:, oc, 0:N], epsn[:, 4:5], bk[:, 0:N], ALU.mult, ALU.add),
                    reads=[bkrs[o2], f"r1/{oc}"], writes=[f"r1/{oc}"])
        layer_norm(r1, "r1", 2, 3, x_f, "x_f")

        if write_y:
            for tb in range(ntb):
                for kq in range(2):
                    bk, bkr = bank()
                    for k4 in range(4):
                        oc = kq * 4 + k4
                        P.add("pe", lambda e, bk=bk, k4=k4, oc=oc, tb=tb: e.transpose(
                            bk[:TB, k4 * 128:(k4 + 1) * 128], x_f[:, oc, tb * 128:tb * 128 + TB], ident[:, :]),
                            reads=["x_f", "cst"], writes=[bkr])
                    P.add("act" if kq == 0 else "dve",
                          (lambda e, bk=bk, kq=kq, tb=tb: e.activation(yout[:TB, tb, kq * 512:(kq + 1) * 512], bk[:TB, :], AF.Copy))
                          if kq == 0 else
                          (lambda e, bk=bk, kq=kq, tb=tb: e.tensor_copy(yout[:TB, tb, kq * 512:(kq + 1) * 512], bk[:TB, :])),
                          reads=[bkr], writes=["hb"])
            P.dma("sp", y_dst.rearrange("(tb p) d -> p tb d", p=TB), yout[:TB, 0:ntb, :], reads=["hb"],
                  writes=["dram_out"])

        if state_dst is not None:
            d_gc, d_S, d_sc, d_ff = state_dst

            def store_state(dst_d, nrows, r, width, srcv, srcres):
                nchk = width // 128
                hv = hs[:, 0:nchk, 0:nrows].rearrange("p c (s r) -> p c s r", r=r)
                P.add("pool", lambda e: e.tensor_copy(hv, srcv), reads=[srcres], writes=["hs"])
                for c0 in range(0, width, 512):
                    wd = min(512, width - c0)
                    for c1 in range(0, wd, 512):
                        w2 = min(512, wd - c1)
                        bk, bkr = bank()
                        for k4 in range(w2 // 128):
                            ch = (c0 + c1) // 128 + k4
                            P.add("pe", lambda e, bk=bk, k4=k4, ch=ch: e.transpose(
                                bk[:nrows, k4 * 128:(k4 + 1) * 128], hs[:, ch, 0:nrows], ident[:, :]),
                                reads=["hs", "cst"], writes=[bkr])
                        P.add("dve", lambda e, bk=bk, c1=c1, w2=w2: e.tensor_copy(
                            hso[:nrows, c1:c1 + w2], bk[:nrows, 0:w2]), reads=[bkr], writes=["hso"])
                    P.dma("sp", dst_d[:, c0:c0 + wd], hso[:nrows, 0:wd], reads=["hso"], writes=["dram_out"])

            store_state(d_gc, nseq * 3, 3, 3072, pv[:, :, :, L:L + 3], "preqkv")
            store_state(d_sc, nseq * 2, 2, D, ppv[:, :, :, L:L + 2], "ppre")
            store_state(d_ff, nseq * 2, 2, DFF, ahv, "ahalo")
            if nseq == 1:
                P.dma("sp", d_S.rearrange("h k v -> k h v"), S[:], reads=[Sres], writes=["dram_out"])

    specs = []
    if STOP != "setup":
        for t in range(NPRE):
            specs.append(dict(full=False, x=xp[t * NT:(t + 1) * NT, :], tb=128, ntb=2, kw=dict(
                nseq=1, L=NT, C=64, full=False, S=Sp, Sb=Spb, Sres="Sp", y_dst=None)))
    if STOP not in ("setup", "pre"):
        for t in range(NFULL):
            r0 = (NPRE + t) * NT
            last = (t == NFULL - 1)
            specs.append(dict(full=True, x=xp[r0:r0 + NT, :], tb=128, ntb=2, kw=dict(
                nseq=1, L=NT, C=64, full=True, S=Sp, Sb=Spb, Sres="Sp",
                y_dst=yp[(t - 1) * NT:t * NT, :] if t > 0 else None,
                state_dst=(p_gc, p_S, p_sc, p_ff) if last else None, write_y=(t > 0))))
    if nsamp and STOP is None:
        specs.append(dict(full=True, x=xs, tb=nsamp * 16, ntb=1, kw=dict(
            nseq=nsamp, L=16, C=16, full=True, S=Ss, Sb=Ssb, Sres="Sp", y_dst=ys, hist_src=(st_gc, st_sc, st_ff),
            state_dst=(s_gc, s_S, s_sc, s_ff), write_y=True, S_src=st_S)))
    pend = None
    PEND_PER_FRONT = 1
    FRONT_PER_PEND = 1
    DEFER_AFTER = "solve_done"
    pend_last = [None]
    for ti, sp_ in enumerate(specs):
        P.epoch = ti
        nxt = specs[ti + 1] if ti + 1 < len(specs) else None
        x_next = (nxt["x"], nxt["tb"], nxt["ntb"]) if nxt is not None else None
        njobs = 3 if not sp_["full"] else len(cast_jobs)
        for _ in range(min(njobs, len(cast_jobs))):
            cast_jobs.pop(0)()
        g = tile(sp_["x"], x_next=x_next, **sp_["kw"])
        fstep = 0
        while True:
            if fstep % FRONT_PER_PEND == 0:
                for _ in range(PEND_PER_FRONT):
                    if pend is not None:
                        cur_pool[0] = "pend"
                        if next(pend, None) is None:
                            pend = None
            fstep += 1
            cur_pool[0] = "front" if pend is not None else "all"
            m = next(g)
            if m == "front_done":
                break
        if pend is not None:
            cur_pool[0] = "pend"
            for _ in pend:
                pass
            pend = None
        cur_pool[0] = "all"
        while next(g) != DEFER_AFTER:
            pass
        if sp_["full"]:
            for _ in g:
                pass
        else:
            pend = g
            pend_last[0] = "presolve_done"
    if pend is not None:
        cur_pool[0] = "pend"
        for _ in pend:
            pass

    P.emit(nc, es)
    es.close()
    return nc


def _consts():
    c = np.zeros((128, 384), np.float32)
    c[:, 0:128] = np.eye(128, dtype=np.float32)
    t = np.arange(64)
    c[:64, 128:192] = (t[:, None] <= t[None, :]).astype(np.float32)
    c[:64, 192:256] = (t[:, None] > t[None, :]).astype(np.float32)
    c[:, 256:384] = 1.0
    return c


def make_in_maps(inp, NPRE, NFULL, ncores=8):
    ROWS = (NPRE + NFULL) * NT
    NOWN = (NFULL - 1) * NT
    xp = np.asarray(inp["x_prompt"], np.float32)
    xs = np.asarray(inp["x_sample"], np.float32)
    nseg = xp.shape[1] // NOWN
    lnp = np.stack([np.asarray(inp[k], np.float32)[0] for k in ("ln1_g", "ln1_b", "ln2_g", "ln2_b")])
    shared = {
        "w_in": np.ascontiguousarray(inp["w_in"][0]), "gdn_conv_w": np.ascontiguousarray(inp["gdn_conv_w"][0]),
        "a_log": np.ascontiguousarray(inp["a_log"]), "dt_bias": np.ascontiguousarray(inp["dt_bias"]),
        "o_norm_g": np.ascontiguousarray(inp["o_norm_g"]), "sc_conv_w": np.ascontiguousarray(inp["sc_conv_w"][0]),
        "w_o": np.ascontiguousarray(inp["w_o"][0]), "lnp": lnp, "w_up": np.ascontiguousarray(inp["w_up"][0]),
        "ffn_conv_w": np.ascontiguousarray(inp["ffn_conv_w"][0]), "w_down": np.ascontiguousarray(inp["w_down"][0]),
        "consts": _consts(),
    }
    maps = []
    for core in range(ncores):
        b, j = core // nseg, core % nseg
        end = NOWN * (j + 1)
        x_ext = np.zeros((ROWS, D), np.float32)
        n = min(end, ROWS)
        x_ext[ROWS - n:] = xp[b, end - n:end]
        m = dict(shared)
        m["xp"] = x_ext
        m["xs"] = np.ascontiguousarray(xs[4 * core:4 * core + 4].reshape(64, D))
        m["st_gc"] = np.ascontiguousarray(inp["state_gdn_conv"][0, 4 * core:4 * core + 4].reshape(12, 3072))
        m["st_S"] = np.ascontiguousarray(inp["state_gdn_S"][0, 4 * core:4 * core + 4])
        m["st_sc"] = np.ascontiguousarray(inp["state_sc_conv"][0, 4 * core:4 * core + 4].reshape(8, D))
        m["st_ff"] = np.ascontiguousarray(inp["state_ffn_conv"][0, 4 * core:4 * core + 4].reshape(8, DFF))
        maps.append(m)
    return maps


_NC_CACHE = {}


def kernel(**inp):
    NPRE, NFULL = 47, 17
    key = (NPRE, NFULL)
    if key not in _NC_CACHE:
        _NC_CACHE[key] = build(NPRE, NFULL)
    nc = _NC_CACHE[key]
    maps = make_in_maps(inp, NPRE, NFULL)
    res = run_bass_kernel_spmd(nc, maps, core_ids=list(range(8))).results
    B, SEQ = 2, 16384
    yp = np.zeros((B, SEQ, D), np.float32)
    ys = np.zeros((32, 16, D), np.float32)
    p_gc = np.zeros((1, B, 3, 3072), np.float32)
    p_S = np.zeros((1, B, 8, 128, 128), np.float32)
    p_sc = np.zeros((1, B, 2, D), np.float32)
    p_ff = np.zeros((1, B, 2, DFF), np.float32)
    s_gc = np.zeros((1, 32, 3, 3072), np.float32)
    s_S = np.zeros((1, 32, 8, 128, 128), np.float32)
    s_sc = np.zeros((1, 32, 2, D), np.float32)
    s_ff = np.zeros((1, 32, 2, DFF), np.float32)
    for core in range(8):
        r = res[core]
        b, j = core // 4, core % 4
        yp[b, 4096 * j:4096 * (j + 1)] = r["yp"]
        ys[4 * core:4 * core + 4] = r["ys"].reshape(4, 16, D)
        s_gc[0, 4 * core:4 * core + 4] = r["s_gc"].reshape(4, 3, 3072)
        s_S[0, 4 * core:4 * core + 4] = r["s_S"]
        s_sc[0, 4 * core:4 * core + 4] = r["s_sc"].reshape(4, 2, D)
        s_ff[0, 4 * core:4 * core + 4] = r["s_ff"].reshape(4, 2, DFF)
        if j == 3:
            p_gc[0, b] = r["p_gc"]
            p_S[0, b] = r["p_S"]
            p_sc[0, b] = r["p_sc"]
            p_ff[0, b] = r["p_ff"]
    return (yp, ys, p_gc, p_S, p_sc, p_ff, s_gc, s_S, s_sc, s_ff)
```

```python
import numpy as np
from contextlib import ExitStack
import concourse.bass as bass
import concourse.mybir as mybir
from concourse.bass_utils import run_bass_kernel_spmd

F32 = mybir.dt.float32
BF16 = mybir.dt.bfloat16
ALU = mybir.AluOpType
AF = mybir.ActivationFunctionType
AX = mybir.AxisListType

D = 1024
NT = 256
DFF = 2816
NFC = 22
ALPHA = 2.0 ** 0.25
LN_EPS = 1e-5
NORM_EPS = 1e-6
ENGS = ("pe", "act", "dve", "pool", "sp")
NROT = 4
NDMASEM = 40
STOP = None
STOP2 = None
PEND_MODE = None


BANK_GEN = {}


class BankRes(str):
    def __new__(cls, s, gen):
        o = str.__new__(cls, s)
        o.gen = gen
        return o


class Op:
    __slots__ = ("eng", "fn", "waits", "inc", "epoch", "dma", "cnt")


class Prog:
    def __init__(self):
        self.ops = {e: [] for e in ENGS}
        self.lastw = {}
        self.rd = {}
        self.children = {}
        self.waited = {}
        self.dwaited = {}
        self.epoch = 0
        self.dma_vals = [0] * NDMASEM
        self.dma_rr = 0
        self.pool_dmas = 0

    @staticmethod
    def _norm(reads, writes):
        ps = [r for r in reads if r.startswith("ps")]
        if ps:
            reads = [r for r in reads if not r.startswith("ps")]
            writes = list(writes) + [p for p in ps if p not in writes]
        return reads, writes

    def _rel(self, r):
        if "/" in r:
            par = r.split("/")[0]
            self.children.setdefault(par, set()).add(r)
            return (r, par)
        ch = self.children.get(r)
        return (r,) + tuple(ch) if ch else (r,)

    def _mk(self, eng, fn, reads, writes, extra=()):
        deps = list(extra)
        for r in reads:
            for q in self._rel(r):
                t = self.lastw.get(q)
                if t is not None:
                    deps.append(t)
        for w in writes:
            for q in self._rel(w):
                t = self.lastw.get(q)
                if t is not None:
                    deps.append(t)
                deps.extend(self.rd.get(q, ()))
        op = Op()
        op.eng, op.fn, op.inc, op.epoch, op.dma, op.cnt = eng, fn, False, self.epoch, None, None
        waits = []
        best = {}
        for t in deps:
            if t[0] == "op":
                _, e2, i2 = t
                if e2 == eng and eng == "pe":
                    continue
                if i2 > best.get(e2, -1):
                    best[e2] = i2
            else:
                _, k, v = t
                if self.dwaited.get((eng, k), 0) < v:
                    self.dwaited[(eng, k)] = v
                    waits.append(("dma", k, v))
        for e2, i2 in best.items():
            if self.waited.get((eng, e2), -1) < i2:
                self.waited[(eng, e2)] = i2
                self.ops[e2][i2].inc = True
                waits.append(("op", e2, i2))
        op.waits = waits
        return op

    def _commit(self, tok, reads, writes):
        for r in reads:
            self.rd.setdefault(r, []).append(tok)
        for w in writes:
            self.lastw[w] = tok
            self.rd[w] = []

    def add(self, eng, fn, reads=(), writes=()):
        for r in list(reads) + list(writes):
            if isinstance(r, BankRes):
                assert BANK_GEN[str(r)] == r.gen, f"stale PSUM bank {r} gen {r.gen} != {BANK_GEN[str(r)]}"
        ps_reads = [r for r in reads if r.startswith("ps") and r not in writes]
        reads = [r for r in reads if not r.startswith("ps")]
        extra = []
        for r in ps_reads:
            t = self.lastw.get(r)
            if t is not None:
                extra.append(t)
            for t in self.rd.get(r, ()):
                if t[0] == "op" and t[1] != eng:
                    extra.append(t)
        op = self._mk(eng, fn, reads, writes, extra)
        idx = len(self.ops[eng])
        self.ops[eng].append(op)
        tok = ("op", eng, idx)
        self._commit(tok, reads, writes)
        for r in ps_reads:
            self.rd.setdefault(r, []).append(tok)

    def dma(self, eng, out, in_, reads=(), writes=()):
        if eng == "pool":
            op = self._mk(eng, lambda e: e.dma_start(out=out, in_=in_), reads, writes)
            self.pool_dmas += 1
            op.dma = ("p", 0)
            self.ops[eng].append(op)
            self._commit(("dma", "p", 1), reads, writes)
            return
        k = self.dma_rr
        self.dma_rr = (k + 1) % NDMASEM
        prev = self.dma_vals[k]
        extra = [("dma", k, prev)] if prev > 0 else []
        op = self._mk(eng, lambda e: e.dma_start(out=out, in_=in_), reads, writes, extra)
        self.dma_vals[k] = prev + 16
        op.dma = (k, prev + 16)
        self.ops[eng].append(op)
        self._commit(("dma", k, prev + 16), reads, writes)

    def emit(self, nc, es):
        sems = {e: [es.enter_context(nc.semaphore(f"s_{e}{r}")) for r in range(NROT)] for e in ENGS}
        dsems = [es.enter_context(nc.semaphore(f"d{k}")) for k in range(NDMASEM)]
        psem = es.enter_context(nc.semaphore("pdma"))
        ptotal = 16 * self.pool_dmas
        for e in ENGS:
            cnt = [0] * NROT
            for op in self.ops[e]:
                if op.inc:
                    r = op.epoch % NROT
                    cnt[r] += 1
                    op.cnt = (r, cnt[r])
        final_vals = list(self.dma_vals)

        def run(e, eo):
            for op in self.ops[e]:
                for w in op.waits:
                    if w[0] == "op":
                        d = self.ops[w[1]][w[2]]
                        eo.wait_ge(sems[w[1]][d.cnt[0]], d.cnt[1])
                    elif w[1] == "p":
                        eo.wait_ge(psem, ptotal)
                    else:
                        eo.wait_ge(dsems[w[1]], w[2])
                ins = op.fn(eo)
                if op.dma is not None and op.dma[0] == "p":
                    ins.then_inc(psem, 16)
                elif op.dma is not None:
                    ins.then_inc(dsems[op.dma[0]], 16)
                elif op.inc:
                    ins.then_inc(sems[e][op.cnt[0]], 1)
            if e == "sp":
                for k, v in enumerate(final_vals):
                    if v > 0:
                        eo.wait_ge(dsems[k], v)
                if ptotal:
                    eo.wait_ge(psem, ptotal)

        block = es.enter_context(nc.Block())

        @block.tensor
        def _(eo):
            run("pe", eo)

        @block.scalar
        def _(eo):
            run("act", eo)

        @block.vector
        def _(eo):
            run("dve", eo)

        @block.gpsimd
        def _(eo):
            run("pool", eo)

        @block.sync
        def _(eo):
            run("sp", eo)


def build(NPRE, NFULL, nsamp=4):
    BANK_GEN.clear()
    nc = bass.Bass("TRN2", target_bir_lowering=False)
    P = Prog()
    es = ExitStack()
    ROWS = (NPRE + NFULL) * NT
    NOWN = (NFULL - 1) * NT

    def din(name, shape):
        return nc.dram_tensor(name, list(shape), F32, kind="ExternalInput").ap()

    def dout(name, shape):
        return nc.dram_tensor(name, list(shape), F32, kind="ExternalOutput").ap()

    xp = din("xp", [ROWS, D])
    xs = din("xs", [nsamp * 16, D])
    st_gc = din("st_gc", [nsamp * 3, 3072])
    st_S = din("st_S", [nsamp, 8, 128, 128])
    st_sc = din("st_sc", [nsamp * 2, D])
    st_ff = din("st_ff", [nsamp * 2, DFF])
    w_in = din("w_in", [D, 9232])
    gdn_conv_w = din("gdn_conv_w", [4, 3072])
    a_log = din("a_log", [1, 8])
    dt_bias = din("dt_bias", [1, 8])
    o_norm_g = din("o_norm_g", [1, 128])
    sc_conv_w = din("sc_conv_w", [3, D])
    w_o = din("w_o", [D, D])
    lnp_d = din("lnp", [4, D])
    w_up = din("w_up", [D, 2 * DFF])
    ffn_conv_w = din("ffn_conv_w", [3, DFF])
    w_down = din("w_down", [DFF, D])
    consts = din("consts", [128, 384])

    yp = dout("yp", [NOWN, D])
    ys = dout("ys", [nsamp * 16, D])
    p_gc = dout("p_gc", [3, 3072])
    p_S = dout("p_S", [8, 128, 128])
    p_sc = dout("p_sc", [2, D])
    p_ff = dout("p_ff", [2, DFF])
    s_gc = dout("s_gc", [nsamp * 3, 3072])
    s_S = dout("s_S", [nsamp, 8, 128, 128])
    s_sc = dout("s_sc", [nsamp * 2, D])
    s_ff = dout("s_ff", [nsamp * 2, DFF])

    win_g = nc.dram_tensor("win_g", [36, 128, 2048], BF16).ap()
    wo_g = nc.dram_tensor("wo_g", [4, 128, 2048], BF16).ap()
    wup_g = nc.dram_tensor("wup_g", [NFC, 128, 2048], BF16).ap()
    wdn_g = nc.dram_tensor("wdn_g", [12, 128, 2048], BF16).ap()

    def sb(name, shape, dt=F32):
        return es.enter_context(nc.sbuf_tensor(name, list(shape), dt))

    banks = [es.enter_context(nc.psum_tensor(f"bank{i}", [128, 512], F32)) for i in range(8)]
    POOLS = {"all": list(range(8)), "front": [4, 5, 6, 7], "pend": [0, 1, 2, 3]}
    bank_rr = {"all": 0, "front": 0, "pend": 0}
    cur_pool = ["all"]

    def bank():
        pn = cur_pool[0]
        pool = POOLS[pn]
        i = pool[bank_rr[pn] % len(pool)]
        bank_rr[pn] += 1
        BANK_GEN[f"ps{i}"] = BANK_GEN.get(f"ps{i}", 0) + 1
        return banks[i], BankRes(f"ps{i}", BANK_GEN[f"ps{i}"])

    rot_state = {}

    def rot(name, n):
        i = rot_state.get(name, 0)
        rot_state[name] = (i + 1) % n
        return i

    cst = sb("cst", [128, 384])
    ident = cst[:, 0:128]
    tri = cst[:64, 128:192]
    sgt = cst[:64, 192:256]
    ones = cst[:, 256:384]
    ident_bf = sb("ident_bf", [128, 128], BF16)
    ones_bf = sb("ones_bf", [128, 128], BF16)
    trisg_bf = sb("trisg_bf", [64, 128], BF16)
    tri_bf = trisg_bf[:, 0:64]
    sgt_bf = trisg_bf[:, 64:128]
    onesm = sb("onesm", [128, 128])
    prm = sb("prm", [128, 224])
    epsn = sb("epsn", [128, 8])
    dtb = sb("dtb", [128, 8])
    nA = sb("nA", [128, 8])
    hso = sb("hso", [12, 512])

    P.dma("sp", cst[:], consts[:, :], writes=["cst"])
    P.dma("sp", hso[0:1, 0:8], dt_bias[:, :], writes=["hso"])
    P.dma("sp", hso[0:1, 8:16], a_log[:, :], writes=["hso"])
    bk, bkr = bank()
    P.add("pe", lambda e, bk=bk: e.matmul(bk[:, 0:16], ones[0:1, :], hso[0:1, 0:16], start=True, stop=True),
          reads=["hso", "cst"], writes=[bkr])
    P.add("dve", lambda e, bk=bk: e.tensor_copy(dtb[:], bk[:, 0:8]), reads=[bkr], writes=["dtb"])
    P.add("dve", lambda e, bk=bk: e.tensor_copy(nA[:], bk[:, 8:16]), reads=[bkr], writes=["nA"])

    WIN_RES = [f"win_sA{r}" for r in range(8)] + [f"win_sB{r}" for r in range(8)]
    WIN_KV = [f"win_kv{r}" for r in range(8)]
    WO_RES = [f"wo_s{r}" for r in range(8)]
    WUP_RES = [f"wup_sA{r}" for r in range(8)] + [f"wup_sB{r}" for r in range(8)]
    WDN_RES = [f"wdn_s{r}" for r in range(NFC)]
    wba = sb("wba", [128, 8, 16], BF16)

    P.add("pool", lambda e: e.memset(epsn[:, 0:1], NORM_EPS), writes=["epsc"])
    P.add("pool", lambda e: e.memset(epsn[:, 1:2], LN_EPS), writes=["epsc"])
    P.add("pool", lambda e: e.memset(epsn[:, 2:3], 1.0), writes=["epsc"])
    P.add("pool", lambda e: e.memset(epsn[:, 3:4], -1.0), writes=["epsc"])
    P.add("pool", lambda e: e.memset(epsn[:, 4:5], ALPHA), writes=["epsc"])
    P.add("dve", lambda e: e.tensor_copy(ident_bf[:], ident), reads=["cst"], writes=["ident_bf"])
    P.add("dve", lambda e: e.tensor_copy(ones_bf[:], ones), reads=["cst"], writes=["ones_bf"])
    P.add("dve", lambda e: e.tensor_copy(trisg_bf[:], cst[:64, 128:256]), reads=["cst"], writes=["ones_bf"])
    P.add("dve", lambda e: e.tensor_scalar(onesm[:], ones, 1.0 / D, None, ALU.mult), reads=["cst"], writes=["onesm"])
    P.add("dve", lambda e: e.tensor_copy(onesm_bf[:], onesm[:]), reads=["onesm"], writes=["onesm_bf"])
    P.add("act", lambda e: e.activation(nA[:], nA[:], AF.Exp), reads=["nA"], writes=["nA"])
    P.add("dve", lambda e: e.tensor_scalar(nA[:], nA[:], -1.0, None, ALU.mult), reads=["nA"], writes=["nA"])

    bk, bkr = bank()
    col = 0
    plist = [(gdn_conv_w, 4, 3072), (sc_conv_w, 3, D), (ffn_conv_w, 3, DFF), (lnp_d, 4, D), (o_norm_g, 1, 128)]
    for (src_d, r, width) in plist:
        for c0 in range(0, width, 512):
            wd = min(512, width - c0)
            P.dma("sp", hso[:r, 0:wd], src_d[:, c0:c0 + wd], writes=["hso"])
            for ch in range(wd // 128):
                o_ap = bk[:, col:col + r]
                i_ap = hso[:r, ch * 128:(ch + 1) * 128]
                P.add("pe", lambda e, o_ap=o_ap, i_ap=i_ap, r=r: e.transpose(o_ap, i_ap, ident[:r, :r]),
                      reads=["hso", "cst"], writes=[bkr])
                col += r
    P.add("dve", lambda e: e.tensor_copy(prm[:, 0:219], bk[:, 0:219]), reads=[bkr], writes=["prm"])
    cwg = prm[:, 0:96].rearrange("p (c k) -> p c k", k=4)
    cws = prm[:, 96:120].rearrange("p (c k) -> p c k", k=3)
    cwf = prm[:, 120:186].rearrange("p (c k) -> p c k", k=3)
    lnp = prm[:, 186:218].rearrange("p (c k) -> p c k", k=4)
    ong = prm[:, 218:219]

    xa = sb("xa", [128, 2, D])
    xT = sb("xT", [128, 8, NT], BF16)
    x_f = sb("x_f", [128, 8, NT])
    preqkv = sb("preqkv", [128, 24, NT + 3], BF16)
    post = sb("post", [128, 24, NT], BF16)
    NCACC = 4
    cacc = [sb(f"cacc{i}", [128, NT]) for i in range(NCACC)]
    sqb = [sb(f"sqb{i}", [128, NT], BF16) for i in range(2)]
    rsb = [sb(f"rsb{i}", [128, NT]) for i in range(2)]
    wst = [sb(f"wst{i}", [128, 2048], BF16) for i in range(4)]
    szg = sb("szg", [128, 8, NT], BF16)
    ppre = sb("ppre", [128, 8, NT + 2], BF16)
    mixg = sb("mixg", [128, 8, NT], BF16)
    sCs = mixg
    mixed = sb("mixed", [128, 8, NT], BF16)
    r1 = sb("r1", [128, 8, NT])
    cfl = r1
    onesm_bf = sb("onesm_bf", [128, 128], BF16)
    mean_sb = sb("mean_sb", [128, NT])
    m2 = sb("m2", [128, NT])
    rstd = sb("rstd", [128, NT])
    x1f = r1
    x1b = mixed
    apre = [sb(f"apre{i}", [128, NT + 2]) for i in range(4)]
    ahalo = sb("ahalo", [128, NFC, 4, 2])
    gab = [sb(f"gab{i}", [128, NT], BF16) for i in range(4)]
    hb = sb("hb", [128, NFC, NT], BF16)
    yout = hb[:].rearrange("p c t -> p (c t)").bitcast(F32)[:, 0:2 * D].rearrange("p (tb d) -> p tb d", tb=2)
    hs = sb("hs", [128, 24, 12])
    ba_beta = sb("ba_beta", [64, 4, 8])
    ba_t = sb("ba_t", [64, 4, 8])
    ba_g = sb("ba_g", [64, 4, 8])
    ba_gh = sb("ba_gh", [64, 4, 8], BF16)
    ba_gl = sb("ba_gl", [64, 4, 8], BF16)
    NB = 4
    eGt = [sb(f"eGt{i}", [64, 16]) for i in range(NB)]
    eGl = [sb(f"eGl{i}", [128, 8]) for i in range(NB)]
    bg = [sb(f"bg{i}", [64, 8]) for i in range(NB)]
    gtri = [sb(f"gtri{i}", [64, 2, 8, 64], BF16) for i in range(NB)]
    Gam = [sb(f"Gam{i}", [64, 8, 64], BF16) for i in range(NB)]
    GamT = [sb(f"GamT{i}", [64, 8, 64], BF16) for i in range(NB)]
    kbg = [sb(f"kbg{i}", [64, 8, 128], BF16) for i in range(NB)]
    ktail = [sb(f"ktail{i}", [64, 8, 128], BF16) for i in range(NB)]
    vb = [sb(f"vb{i}", [64, 8, 128], BF16) for i in range(NB)]
    Am = [sb(f"Am{i}", [64, 8, 64], BF16) for i in range(NB)]
    Xm = [[sb(f"Xm{i}_{k}", [64, 8, 64], BF16) for k in range(2)] for i in range(NB)]
    Ym = [[sb(f"Ym{i}_{k}", [64, 8, 64], BF16) for k in range(2)] for i in range(NB)]
    Rm = [[sb(f"Rm{i}_0", [64, 8, 64], BF16)] * 2 for i in range(NB)]
    nwT = [sb(f"nwT{i}", [128, 8, 64], BF16) for i in range(NB)]
    vnew = [sb("vnew0", [64, 8, 128], BF16)] * NB
    qkT = [sb("qkT0", [64, 8, 64], BF16)] * NB
    o1s = [sb("o1s0", [64, 8, 128])] * NB
    osq = [sb("osq0", [64, 8, 128], BF16)] * NB
    oss = [sb(f"oss{i}", [64, 8]) for i in range(NB)]
    onb = [sb(f"onb{i}", [64, 8, 128], BF16) for i in range(2)] * 2
    Sp = sb("Sp", [128, 8, 128])
    Spb = sb("Spb", [128, 8, 128], BF16)
    Ss, Ssb = Sp, Spb

    stg_f = [(r1, "r1"), (x_f, "x_f")]
    stg_b = [(mixed, "mixed"), (mixg, "mixg")]
    pp = [0]

    def cast_block(src_ap, dst_ap, ncol, res, nk=None):
        i = pp[0] % 2
        pp[0] += 1
        (sf, sfr), (sbf, sbr) = stg_f[i], stg_b[i]
        fv = sf[:].rearrange("p a b -> p (a b)")[:, 0:ncol]
        bv = sbf[:].rearrange("p a b -> p (a b)")[:, 0:ncol]
        if nk is not None:
            fv = fv.rearrange("p (c k) -> p c k", k=nk)
            bv = bv.rearrange("p (c k) -> p c k", k=nk)
        P.dma("sp", fv, src_ap, writes=[sfr])
        eng = ("act", "dve", "pool")[pp[0] % 3]
        if eng == "act":
            P.add("act", lambda e: e.activation(bv, fv, AF.Copy), reads=[sfr], writes=[sbr])
        else:
            P.add(eng, lambda e: e.tensor_copy(bv, fv), reads=[sfr], writes=[sbr])
        P.dma("sp", dst_ap, bv, reads=[sbr], writes=[res])

    cast_jobs = []
    for r in range(8):
        rs = slice(r * 128, (r + 1) * 128)
        cast_block(w_in[rs, 1024:3072].rearrange("r (g c) -> r g c", c=256), win_g[4:12, :, r * 256:(r + 1) * 256].rearrange("g p c -> p g c"), 2048, f"win_kv{r}", nk=256)
    for r in range(8):
        rs = slice(r * 128, (r + 1) * 128)
        cast_jobs.append(lambda rs=rs, r=r: cast_block(w_in[rs, 0:1024].rearrange("r (g c) -> r g c", c=256), win_g[0:4, :, r * 256:(r + 1) * 256].rearrange("g p c -> p g c"), 1024, f"win_sA{r}", nk=256))
        cast_jobs.append(lambda rs=rs, r=r: cast_block(w_in[rs, 3072:4096].rearrange("r (g c) -> r g c", c=256), win_g[12:16, :, r * 256:(r + 1) * 256].rearrange("g p c -> p g c"), 1024, f"win_sA{r}", nk=256))
        cast_jobs.append(lambda rs=rs, r=r: cast_block(w_in[rs, 4112:6160].rearrange("r (g c) -> r g c", c=256), win_g[16:24, :, r * 256:(r + 1) * 256].rearrange("g p c -> p g c"), 2048, f"win_sB{r}", nk=256))
        cast_jobs.append(lambda rs=rs, r=r: cast_block(w_in[rs, 6160:8208].rearrange("r (g c) -> r g c", c=256), win_g[24:32, :, r * 256:(r + 1) * 256].rearrange("g p c -> p g c"), 2048, f"win_sB{r}", nk=256))
        cast_jobs.append(lambda rs=rs, r=r: cast_block(w_in[rs, 8208:9232].rearrange("r (g c) -> r g c", c=256), win_g[32:36, :, r * 256:(r + 1) * 256].rearrange("g p c -> p g c"), 1024, f"win_sB{r}", nk=256))
        cast_jobs.append(lambda rs=rs, r=r: cast_block(w_o[rs, :].rearrange("r (g c) -> r g c", c=256), wo_g[:, :, r * 256:(r + 1) * 256].rearrange("g p c -> p g c"), 1024, f"wo_s{r}", nk=256))
        for (c0, c1) in ((0, 16), (16, NFC)):
            for half in range(2):
                cast_jobs.append(lambda rs=rs, r=r, c0=c0, c1=c1, half=half: cast_block(
                    w_up[rs, half * DFF + c0 * 128:half * DFF + c1 * 128].rearrange("r (c k) -> r c k", k=128),
                    wup_g[c0:c1, :, r * 256 + half * 128:r * 256 + half * 128 + 128].rearrange("g p c -> p g c"),
                    (c1 - c0) * 128, f"wup_s{'AB'[half]}{r}", nk=128))
    for r in range(NFC):
        rs = slice(r * 128, (r + 1) * 128)
        kg, k8 = r // 8, r % 8
        cast_jobs.append(lambda rs=rs, r=r, kg=kg, k8=k8: cast_block(
            w_down[rs, :].rearrange("r (g c) -> r g c", c=256),
            wdn_g.rearrange("(cg kg) p c -> cg kg p c", kg=3)[:, kg, :, k8 * 256:(k8 + 1) * 256].rearrange("g p c -> p g c"),
            1024, f"wdn_s{r}", nk=256))
    fv = r1[:].rearrange("p a b -> p (a b)")[:, 0:128].rearrange("p (k c) -> p k c", c=16)
    P.dma("sp", fv, w_in[:, 4096:4112].rearrange("(kc p) c -> p kc c", p=128), writes=["r1"])
    P.add("dve", lambda e: e.tensor_copy(wba[:], fv), reads=["r1"], writes=["wba"])

    P.add("pool", lambda e: e.memset(Sp[:], 0.0), writes=["Sp"])
    P.add("pool", lambda e: e.memset(Spb[:], 0.0), writes=["Spb"])
    P.add("pool", lambda e: e.memset(preqkv[:], 0.0), writes=["preqkv"])
    P.add("pool", lambda e: e.memset(ppre[:], 0.0), writes=["ppre"])
    P.add("pool", lambda e: e.memset(ahalo[:], 0.0), writes=["ahalo"])

    wst_rr = [0]
    xa_loaded = [False]

    def wload(src_ap, nel, shape_str, **kw):
        i = wst_rr[0]
        wst_rr[0] = (i + 1) % 4
        view = wst[i][:, 0:nel].rearrange(shape_str, **kw)
        return i, view

    def proj(xin, xres, wview, wres, ncc, N, consume, cc0=0):
        for cc in range(ncc):
            bk, bkr = bank()
            for kc in range(8):
                P.add("pe", lambda e, bk=bk, kc=kc, cc=cc: e.matmul(
                    bk[:, 0:N], wview[:, kc, cc * 128:(cc + 1) * 128], xin[:, kc, 0:N],
                    start=(kc == 0), stop=(kc == 7)), reads=[xres, wres], writes=[bkr])
            consume(cc0 + cc, bk[:, 0:N], bkr)

    def tile(x_src, nseq, L, C, full, S, Sb, Sres, y_dst, hist_src=None, state_dst=None, write_y=True, S_src=None, x_next=None):
        N = nseq * L
        nch = N // C
        ntb = (N + 127) // 128
        TB = min(N, 128)
        H = 3

        def v3(ap):
            return ap.rearrange("p (s l) -> p s l", s=nseq)

        if not xa_loaded[0]:
            P.dma("sp", xa[:TB, 0:ntb, :], x_src.rearrange("(tb p) d -> p tb d", p=TB), writes=["xa"])
        xa_loaded[0] = False
        for tb in range(ntb):
            for kq in range(2):
                bk, bkr = bank()
                for k4 in range(4):
                    kc = kq * 4 + k4
                    P.add("pe", lambda e, bk=bk, k4=k4, kc=kc, tb=tb: e.transpose(
                        bk[:, k4 * 128:k4 * 128 + TB], xa[:TB, tb, kc * 128:(kc + 1) * 128], ident[:TB, :TB]),
                        reads=["xa", "cst"], writes=[bkr])
                src = bk[:, :].rearrange("p (k t) -> p k t", k=4)[:, :, 0:TB]
                P.add("dve", lambda e, src=src, kq=kq, tb=tb: e.tensor_copy(
                    xT[:, kq * 4:kq * 4 + 4, tb * 128:tb * 128 + TB], src), reads=[bkr], writes=["xT"])
                if full:
                    P.add("act", lambda e, src=src, kq=kq, tb=tb: e.activation(
                        x_f[:, kq * 4:kq * 4 + 4, tb * 128:tb * 128 + TB], src, AF.Copy),
                        reads=[bkr], writes=["x_f"])

        if x_next is not None:
            xn, tbn, ntbn = x_next
            P.dma("sp", xa[:tbn, 0:ntbn, :], xn.rearrange("(tb p) d -> p tb d", p=tbn), reads=["xT"], writes=["xa"])
            xa_loaded[0] = True
        yield "front"
        pv = preqkv[:, :, 0:nseq * (L + H)].rearrange("p c (s l) -> p c s l", s=nseq)
        ppv = ppre[:, :, 0:nseq * (L + 2)].rearrange("p c (s l) -> p c s l", s=nseq)
        ahv = ahalo[:, :, 0:nseq, :]

        def load_hist(src_d, nrows, r, width, dst, dstres):
            for c0 in range(0, width, 512):
                wd = min(512, width - c0)
                ncc = wd // 128
                P.dma("sp", hso[:nrows, 0:wd], src_d[:, c0:c0 + wd], writes=["hso"])
                bk, bkr = bank()
                for k in range(ncc):
                    P.add("pe", lambda e, bk=bk, k=k: e.transpose(
                        bk[:, k * 12:k * 12 + nrows], hso[:nrows, k * 128:(k + 1) * 128],
                        ident[:nrows, :nrows]), reads=["hso", "cst"], writes=[bkr])
                srcv = bk[:, 0:ncc * 12].rearrange("p (c x) -> p c x", x=12)[:, :, 0:nrows].rearrange(
                    "p c (s r) -> p c s r", r=r)
                P.add("dve", lambda e, srcv=srcv, c0=c0, ncc=ncc: e.tensor_copy(
                    dst[:, c0 // 128:c0 // 128 + ncc, :, :], srcv), reads=[bkr], writes=[dstres])

        if hist_src is not None:
            (h_gc, h_sc, h_ff) = hist_src
            load_hist(h_gc, nseq * 3, 3, 3072, pv[:, :, :, 0:3], "preqkv")
            load_hist(h_sc, nseq * 2, 2, D, ppv[:, :, :, 0:2], "ppre")
            load_hist(h_ff, nseq * 2, 2, DFF, ahv, "ahalo")
        else:
            P.add("pool", lambda e: e.tensor_copy(pv[:, :, :, 0:3], pv[:, :, :, L:L + 3]),
                  reads=["preqkv"], writes=["preqkv"])
            if full:
                P.add("pool", lambda e: e.tensor_copy(ppv[:, :, :, 0:2], ppv[:, :, :, L:L + 2]),
                      reads=["ppre"], writes=["ppre"])

        qpend = []
        qstage2 = []

        def qkv_flush():
            grp = list(qpend)
            del qpend[:]
            accs = []
            for (ch, ps, psr) in grp:
                P.add("act", lambda e, ch=ch, ps=ps: e.activation(pv[:, ch, :, 3:3 + L], v3(ps), AF.Copy),
                      reads=[psr], writes=[f"preqkv/{ch}"])
                accs.append(rot("cacc", NCACC))
            for (ch, ps, psr), i in zip(grp, accs):
                a3 = v3(cacc[i][:, 0:N])
                P.add("pool", lambda e, ch=ch, a3=a3: e.tensor_tensor(
                    a3, pv[:, ch, :, 0:L], cwg[:, ch, 0:1].unsqueeze(1).to_broadcast([128, nseq, L]), ALU.mult),
                    reads=[f"preqkv/{ch}", "prm"], writes=[f"cacc{i}"])
            for j in range(1, 4):
                for (ch, ps, psr), i in zip(grp, accs):
                    a3 = v3(cacc[i][:, 0:N])
                    P.add("dve", lambda e, j=j, ch=ch, a3=a3: e.scalar_tensor_tensor(
                        a3, pv[:, ch, :, j:j + L], cwg[:, ch, j:j + 1], a3, ALU.mult, ALU.add),
                        reads=[f"preqkv/{ch}", "prm", f"cacc{i}"], writes=[f"cacc{i}"])
            prev = list(qstage2)
            del qstage2[:]
            for th in prev:
                th()
            for (ch, ps, psr), i in zip(grp, accs):
                qstage2.append(lambda ch=ch, i=i: P.add(
                    "act", lambda e: e.activation(post[:, ch, 0:N], cacc[i][:, 0:N], AF.Silu),
                    reads=[f"cacc{i}"], writes=[f"post{ch}"]))

        def qkv_finish():
            if qpend:
                qkv_flush()
            prev = list(qstage2)
            del qstage2[:]
            for th in prev:
                th()

        def qkv_consume(ch, ps, psr):
            qpend.append((ch, ps, psr))
            if len(qpend) == 2:
                qkv_flush()

        l2pendB = []

        def l2_finish(keep=0):
            while len(l2pendB) > keep:
                l2pendB.pop(0)()

        def l2norm2(chs):
            for ch in chs:
                i = rot("sqb", 2)
                k = rot("cacc", NCACC)
                rs_ = cacc[k]
                P.add("pool", lambda e, ch=ch, i=i: e.tensor_tensor(
                    sqb[i][:, 0:N], post[:, ch, 0:N], post[:, ch, 0:N], ALU.mult),
                    reads=[f"post{ch}"], writes=[f"sqb{i}"])
                bk, bkr = bank()
                P.add("pe", lambda e, bk=bk, i=i: e.matmul(bk[:, 0:N], ones_bf[:], sqb[i][:, 0:N], start=True, stop=True),
                      reads=[f"sqb{i}", "ones_bf"], writes=[bkr])
                P.add("act", lambda e, bk=bk, rs_=rs_: e.activation(rs_[:, 0:N], bk[:, 0:N], AF.Ln, bias=epsn[:, 0:1]),
                      reads=[bkr, "epsc"], writes=[f"cacc{k}"])
                l2_finish(keep=2)

                def stageB(ch=ch, k=k, rs_=rs_):
                    P.add("act", lambda e: e.activation(rs_[:, 0:N], rs_[:, 0:N], AF.Exp, scale=-0.5),
                          reads=[f"cacc{k}"], writes=[f"cacc{k}"])
                    P.add("pool", lambda e: e.tensor_tensor(
                        post[:, ch, 0:N], post[:, ch, 0:N], rs_[:, 0:N], ALU.mult),
                        reads=[f"post{ch}", f"cacc{k}"], writes=[f"post{ch}"])
                l2pendB.append(stageB)

        for g in range(8):
            i, wv_ = wload(None, 2048, "p (k c) -> p k c", k=8)
            P.dma("sp", wst[i][:, 0:2048], win_g[4 + g], reads=WIN_KV, writes=[f"wst{i}"])
            proj(xT, "xT", wv_, f"wst{i}", 2, N, qkv_consume, cc0=8 + g * 2)
            yield "front"
        if full:
            for g in range(4):
                i, wv_ = wload(None, 2048, "p (k c) -> p k c", k=8)
                P.dma("sp", wst[i][:, 0:2048], win_g[g], reads=WIN_RES, writes=[f"wst{i}"])
                proj(xT, "xT", wv_, f"wst{i}", 2, N, qkv_consume, cc0=g * 2)
        qkv_finish()
        l2todo = []
        if full:
            l2todo = [[ch, ch + 1] for ch in range(0, 16, 2)]
        else:
            for ch in range(8, 16, 2):
                l2norm2([ch, ch + 1])
                yield "front"
            l2_finish()

        def emit_gates():
            bk, bkr = bank()
            bav = bk[:C, 0:nch * 16].rearrange("p (c k) -> p c k", k=16)
            for c in range(nch):
                for kc in range(8):
                    P.add("pe", lambda e, c=c, kc=kc: e.matmul(
                        bav[:, c, :], xT[:, kc, c * C:(c + 1) * C], wba[:, kc, :], start=(kc == 0), stop=(kc == 7)),
                        reads=["xT", "wba"], writes=[bkr])
            P.add("act", lambda e: e.activation(ba_beta[:C, 0:nch, :], bav[:, :, 0:8], AF.Exp, scale=-1.0),
                  reads=[bkr], writes=["ba_beta"])
            P.add("dve", lambda e: e.tensor_scalar(ba_beta[:C, 0:nch, :], ba_beta[:C, 0:nch, :], 1.0, None, ALU.add),
                  reads=["ba_beta"], writes=["ba_beta"])
            P.add("dve", lambda e: e.reciprocal(ba_beta[:C, 0:nch, :], ba_beta[:C, 0:nch, :]),
                  reads=["ba_beta"], writes=["ba_beta"])
            P.add("dve", lambda e: e.tensor_tensor(
                ba_t[:C, 0:nch, :], bav[:, :, 8:16], dtb[:C, :].unsqueeze(1).to_broadcast([C, nch, 8]), ALU.add),
                reads=[bkr, "dtb"], writes=["ba_t"])
            P.add("act", lambda e: e.activation(ba_t[:C, 0:nch, :], ba_t[:C, 0:nch, :], AF.Exp),
                  reads=["ba_t"], writes=["ba_t"])
            P.add("act", lambda e: e.activation(ba_t[:C, 0:nch, :], ba_t[:C, 0:nch, :], AF.Ln, bias=epsn[:C, 2:3]),
                  reads=["ba_t", "epsc"], writes=["ba_t"])
            P.add("dve", lambda e: e.tensor_tensor(
                ba_g[:C, 0:nch, :], ba_t[:C, 0:nch, :], nA[:C, :].unsqueeze(1).to_broadcast([C, nch, 8]), ALU.mult),
                reads=["ba_t", "nA"], writes=["ba_g"])
            P.add("dve", lambda e: e.tensor_copy(ba_gh[:C, 0:nch, :], ba_g[:C, 0:nch, :]), reads=["ba_g"], writes=["ba_gh"])
            P.add("dve", lambda e: e.tensor_tensor(ba_gl[:C, 0:nch, :], ba_g[:C, 0:nch, :], ba_gh[:C, 0:nch, :],
                                                   ALU.subtract), reads=["ba_g", "ba_gh"], writes=["ba_gh"])


        if not full:
            emit_gates()

        if full:
            def mk_consume(kind):
                def f(ch, ps, psr):
                    if kind == "z":
                        P.add("act", lambda e: e.activation(szg[:, ch, 0:N], ps, AF.Silu),
                              reads=[psr], writes=[f"szg/{ch}"])
                    elif kind == "sB":
                        P.add("dve", lambda e: e.tensor_tensor(cfl[:, ch, 0:N], ps, cfl[:, ch, 0:N], ALU.mult),
                              reads=[psr, f"r1/{ch}"], writes=[f"r1/{ch}"])
                    elif kind == "sC":
                        P.add("act", lambda e: e.activation(sCs[:, ch, 0:N], ps, AF.Copy),
                              reads=[psr], writes=[f"mixg/{ch}"])
                    elif kind == "sH":
                        P.add("dve", lambda e: e.tensor_tensor(
                            ppv[:, ch, :, 2:2 + L], v3(ps), v3(sCs[:, ch, 0:N]), ALU.mult),
                            reads=[psr, f"mixg/{ch}"], writes=[f"ppre/{ch}"])
                        c3 = v3(cfl[:, ch, 0:N])
                        P.add("pool", lambda e: e.tensor_tensor(
                            c3, ppv[:, ch, :, 0:L], cws[:, ch, 0:1].unsqueeze(1).to_broadcast([128, nseq, L]), ALU.mult),
                              reads=[f"ppre/{ch}", "prm"], writes=[f"r1/{ch}"])
                        for j in range(1, 3):
                            P.add("dve", lambda e, j=j: e.scalar_tensor_tensor(
                                c3, ppv[:, ch, :, j:j + L], cws[:, ch, j:j + 1], c3, ALU.mult, ALU.add),
                                reads=[f"ppre/{ch}", "prm", f"r1/{ch}"], writes=[f"r1/{ch}"])
                    elif kind == "gA":
                        i = rot("cacc", NCACC)
                        P.add("act", lambda e: e.activation(cacc[i][:, 0:N], ps, AF.Sigmoid),
                              reads=[psr], writes=[f"cacc{i}"])
                        P.add("pool", lambda e: e.tensor_tensor(
                            szg[:, ch, 0:N], szg[:, ch, 0:N], cacc[i][:, 0:N], ALU.mult),
                            reads=[f"szg/{ch}", f"cacc{i}"], writes=[f"szg/{ch}"])
                    elif kind == "gB":
                        i = rot("cacc", NCACC)
                        P.add("act", lambda e: e.activation(cacc[i][:, 0:N], ps, AF.Sigmoid),
                              reads=[psr], writes=[f"cacc{i}"])
                        P.add("pool", lambda e: e.tensor_tensor(
                            cfl[:, ch, 0:N], cfl[:, ch, 0:N], cacc[i][:, 0:N], ALU.mult),
                            reads=[f"r1/{ch}", f"cacc{i}"], writes=[f"r1/{ch}"])
                return f

            for kind, base in [("z", 3072), ("sC", 5120), ("sH", 6144), ("sB", 4096), ("gA", 7168), ("gB", 8192)]:
                for g in range(4):
                    i, wv_ = wload(None, 2048, "p (k c) -> p k c", k=8)
                    P.dma("sp", wst[i][:, 0:2048], win_g[base // 256 + g], reads=WIN_RES, writes=[f"wst{i}"])
                    proj(xT, "xT", wv_, f"wst{i}", 2, N, mk_consume(kind), cc0=g * 2)
                    if l2todo and kind in ("sC", "sH"):
                        l2norm2(l2todo.pop(0))
                    if kind == "sH" and g == 3:
                        while l2todo:
                            l2norm2(l2todo.pop(0))
                        l2_finish()
                        emit_gates()
            while l2todo:
                l2norm2(l2todo.pop(0))
            l2_finish()

        yield "front_done"
        nst = {64: 5, 16: 3}[C]
        def chunk(c):
            b = c % NB
            cs = slice(c * C, (c + 1) * C)
            Sc, Scb, Scr = S, Sb, Sres
            g_c = ba_g[:C, c, :]
            be_c = ba_beta[:C, c, :]
            bk, bkr = bank()
            gh_c = ba_gh[:C, c, :]
            gl_c = ba_gl[:C, c, :]
            for (oap, lt) in ((bk[:C, 0:8], tri_bf[:C, :C]), (bk[:C, 8:16], sgt_bf[:C, :C]), (bk[:, 16:24], ones_bf[:C, :])):
                P.add("pe", lambda e, oap=oap, lt=lt: e.matmul(oap, lt, gh_c, start=True, stop=False),
                      reads=["ones_bf", "ba_gh"], writes=[bkr])
                P.add("pe", lambda e, oap=oap, lt=lt: e.matmul(oap, lt, gl_c, start=False, stop=True),
                      reads=["ones_bf", "ba_gh"], writes=[bkr])
            yield
            P.add("act", lambda e, bk=bk: e.activation(eGt[b][:C, :], bk[:C, 0:16], AF.Exp),
                  reads=[bkr], writes=[f"eGt{b}"])
            P.add("act", lambda e, bk=bk: e.activation(eGl[b][:, :], bk[:, 16:24], AF.Exp),
                  reads=[bkr], writes=[f"eGl{b}"])
            P.add("pool", lambda e: e.tensor_tensor(bg[b][:C, :], be_c, eGt[b][:C, 0:8], ALU.mult),
                  reads=["ba_beta", f"eGt{b}"], writes=[f"bg{b}"])
            P.add("dve", lambda e: e.tensor_tensor(
                gtri[b][:C, 0, :, :C], tri[:C, :C].unsqueeze(1).to_broadcast([C, 8, C]),
                gh_c.unsqueeze(2).to_broadcast([C, 8, C]), ALU.mult),
                reads=["cst", "ba_gh"], writes=[f"gtri{b}"])
            P.add("dve", lambda e: e.tensor_tensor(
                gtri[b][:C, 1, :, :C], tri[:C, :C].unsqueeze(1).to_broadcast([C, 8, C]),
                gl_c.unsqueeze(2).to_broadcast([C, 8, C]), ALU.mult),
                reads=["cst", "ba_gh"], writes=[f"gtri{b}"])
            yield
            bkD, bkDr = bank()
            Dv = bkD[:C, 0:8 * C].rearrange("p (h c) -> p h c", h=8)
            for h in range(8):
                for hl in range(2):
                    P.add("pe", lambda e, h=h, Dv=Dv, hl=hl: e.matmul(Dv[:, h, :], gtri[b][:C, hl, h, :C], sgt_bf[:C, :C],
                                                                     start=(hl == 0), stop=(hl == 1)),
                          reads=[f"gtri{b}", "ones_bf"], writes=[bkDr])
            yield
            P.add("act", lambda e, Dv=Dv: e.activation(Gam[b][:C, :, :C], Dv, AF.Exp),
                  reads=[bkDr], writes=[f"Gam{b}"])
            P.add("pool", lambda e: e.tensor_tensor(
                Gam[b][:C, :, :C], Gam[b][:C, :, :C], sgt[:C, :C].unsqueeze(1).to_broadcast([C, 8, C]), ALU.mult),
                reads=[f"Gam{b}", "cst"], writes=[f"Gam{b}"])
            P.add("pool", lambda e: e.tensor_tensor(
                Gam[b][:C, :, :C], Gam[b][:C, :, :C], be_c.unsqueeze(2).to_broadcast([C, 8, C]), ALU.mult),
                reads=[f"Gam{b}", "ba_beta"], writes=[f"Gam{b}"])
            if full:
                bkT, bkTr = bank()
                DTv = bkT[:C, 0:8 * C].rearrange("p (h c) -> p h c", h=8)
                for h in range(8):
                    for hl in range(2):
                        P.add("pe", lambda e, h=h, DTv=DTv, hl=hl: e.matmul(
                            DTv[:, h, :], sgt_bf[:C, :C], gtri[b][:C, hl, h, :C], start=(hl == 0), stop=(hl == 1)),
                            reads=[f"gtri{b}", "ones_bf"], writes=[bkTr])
                P.add("act", lambda e, DTv=DTv: e.activation(GamT[b][:C, :, :C], DTv, AF.Exp),
                      reads=[bkTr], writes=[f"GamT{b}"])
                P.add("pool", lambda e: e.tensor_tensor(
                    GamT[b][:C, :, :C], GamT[b][:C, :, :C], tri[:C, :C].unsqueeze(1).to_broadcast([C, 8, C]),
                    ALU.mult), reads=[f"GamT{b}", "cst"], writes=[f"GamT{b}"])
            if STOP2 == "c_a":
                return
            yield
            bkk, bkkr = bank()
            kt = bkk[:, :].bitcast(BF16)[:C, :].rearrange("p (h d) -> p h d", h=8)
            for h in range(8):
                P.add("pe", lambda e, h=h, kt=kt: e.transpose(kt[:, h, :], post[:, 8 + h, cs], ident_bf[:]),
                      reads=[f"post{8 + h}", "ident_bf"], writes=[bkkr])
            yield
            P.add("dve", lambda e, kt=kt: e.tensor_tensor(
                kbg[b][:C], kt, bg[b][:C, :].unsqueeze(2).to_broadcast([C, 8, 128]), ALU.mult),
                reads=[bkkr, f"bg{b}"], writes=[f"kbg{b}"])
            P.add("dve", lambda e, kt=kt: e.tensor_tensor(
                ktail[b][:C], kt, eGt[b][:C, 8:16].unsqueeze(2).to_broadcast([C, 8, 128]), ALU.mult),
                reads=[bkkr, f"eGt{b}"], writes=[f"ktail{b}"])
            yield
            bkv, bkvr = bank()
            vt = bkv[:, :].bitcast(BF16)[:C, :].rearrange("p (h d) -> p h d", h=8)
            for h in range(8):
                P.add("pe", lambda e, h=h, vt=vt: e.transpose(vt[:, h, :], post[:, 16 + h, cs], ident_bf[:]),
                      reads=[f"post{16 + h}", "ident_bf"], writes=[bkvr])
            yield
            P.add("dve", lambda e, vt=vt: e.tensor_tensor(
                vb[b][:C], vt, be_c.unsqueeze(2).to_broadcast([C, 8, 128]), ALU.mult),
                reads=[bkvr, "ba_beta"], writes=[f"vb{b}"])
            if STOP2 == "c_f":
                return
            yield
            bka, bkar = bank()
            kkv = bka[:C, 0:8 * C].rearrange("p (h c) -> p h c", h=8)
            for h in range(8):
                P.add("pe", lambda e, h=h, kkv=kkv: e.matmul(kkv[:, h, :], post[:, 8 + h, cs], post[:, 8 + h, cs],
                                                            start=True, stop=True),
                      reads=[f"post{8 + h}"], writes=[bkar])
            yield
            P.add("dve", lambda e, kkv=kkv: e.tensor_tensor(Am[b][:C, :, :C], kkv, Gam[b][:C, :, :C], ALU.mult),
                  reads=[bkar, f"Gam{b}"], writes=[f"Am{b}"])
            if STOP2 == "c_g1":
                return
            yield
            bkb, bkbr = bank()
            atv = bkb[:C, 0:8 * C].rearrange("p (h c) -> p h c", h=8)
            for h in range(8):
                P.add("pe", lambda e, h=h, atv=atv: e.matmul(atv[:, h, :], Am[b][:C, h, :C], ident_bf[:C, :C],
                                                            start=True, stop=True),
                      reads=[f"Am{b}", "ident_bf"], writes=[bkbr])
            yield
            X0, Y0, R0 = Xm[b][0], Am[b], Rm[b][0]
            P.add("act", lambda e, atv=atv: e.activation(X0[:C, :, :C], atv, AF.Copy),
                  reads=[bkbr], writes=[f"Xm{b}_0"])
            P.add("dve", lambda e, atv=atv: e.scalar_tensor_tensor(
                R0[:C, :, :C], atv, epsn[:C, 3:4], ident[:C, :C].unsqueeze(1).to_broadcast([C, 8, C]), ALU.mult, ALU.add),
                reads=[bkbr, "cst", "epsc"], writes=[f"Rm{b}_0"])
            Xc, Xr, Yc, Yr, Rc, Rr = X0, f"Xm{b}_0", Y0, f"Am{b}", R0, f"Rm{b}_0"
            yield "presolve_done"
            for n in range(1, nst + 1):
                yield
                k = n % 2
                Yn, Ynr = Ym[b][k], f"Ym{b}_{k}"
                bky, bkyr = bank()
                yv = bky[:C, 0:8 * C].rearrange("p (h c) -> p h c", h=8)
                for h in range(8):
                    P.add("pe", lambda e, h=h, yv=yv, Xc=Xc, Yc=Yc: e.matmul(
                        yv[:, h, :], Xc[:C, h, :C], Yc[:C, h, :C], start=True, stop=True),
                        reads=[Xr, Yr], writes=[bkyr])
                yield
                P.add("act", lambda e, yv=yv, Yn=Yn: e.activation(Yn[:C, :, :C], yv, AF.Copy),
                      reads=[bkyr], writes=[Ynr])
                if n < nst:
                    Xn, Xnr = Xm[b][k], f"Xm{b}_{k}"
                    bkx, bkxr = bank()
                    xv = bkx[:C, 0:8 * C].rearrange("p (h c) -> p h c", h=8)
                    for h in range(8):
                        P.add("pe", lambda e, h=h, xv=xv, Xc=Xc, Yc=Yc: e.matmul(
                            xv[:, h, :], Yc[:C, h, :C], Xc[:C, h, :C], start=True, stop=True),
                            reads=[Xr, Yr], writes=[bkxr])
                    yield
                    P.add("dve", lambda e, xv=xv, Xn=Xn: e.tensor_copy(Xn[:C, :, :C], xv),
                          reads=[bkxr], writes=[Xnr])
                yield
                Rn, Rnr = Rm[b][0], f"Rm{b}_0"
                bkq, bkqr = bank()
                rv = bkq[:C, 0:8 * C].rearrange("p (h c) -> p h c", h=8)
                for h in range(8):
                    P.add("pe", lambda e, h=h, rv=rv, Rc=Rc, Yn=Yn: e.matmul(
                        rv[:, h, :], Yn[:C, h, :C], Rc[:C, h, :C], start=True, stop=True),
                        reads=[Ynr, Rr], writes=[bkqr])
                yield
                P.add("dve", lambda e, rv=rv, Rn=Rn, Rc=Rc: e.tensor_tensor(Rn[:C, :, :C], rv, Rc[:C, :, :C], ALU.add),
                      reads=[bkqr, Rr], writes=[Rnr])
                if n < nst:
                    Xc, Xr = Xn, Xnr
                Yc, Yr, Rc, Rr = Yn, Ynr, Rn, Rnr
            TT, TTr = Rc, Rr
            if STOP2 == "c_h":
                return
            yield
            bkw, bkwr = bank()
            wv2 = bkw[:, 0:8 * C].rearrange("p (h c) -> p h c", h=8)
            for h in range(8):
                P.add("pe", lambda e, h=h, wv2=wv2, TT=TT: e.matmul(
                    wv2[:, h, :], kbg[b][:C, h, :], TT[:C, h, :C], start=True, stop=True),
                    reads=[f"kbg{b}", TTr], writes=[bkwr])
            yield
            P.add("act", lambda e, wv2=wv2: e.activation(nwT[b][:, :, :C], wv2, AF.Copy, scale=-1.0),
                  reads=[bkwr], writes=[f"nwT{b}"])
            yield "chain"
            if S_src is not None:
                P.dma("sp", Sc[:], S_src[c].rearrange("h k v -> k h v"), writes=[Scr])
                P.add("act", lambda e, Sc=Sc, Scb=Scb: e.activation(Scb[:], Sc[:], AF.Copy),
                      reads=[Scr], writes=[Scr + "b"])
            vbanks = []
            for hh in range(2):
                bkn, bknr = bank()
                vn = bkn[:C, :].rearrange("p (h d) -> p h d", h=4)
                for h4 in range(4):
                    h = hh * 4 + h4
                    P.add("pe", lambda e, h=h, h4=h4, vn=vn, TT=TT: e.matmul(
                        vn[:, h4, :], TT[:C, h, :C], vb[b][:C, h, :], start=True, stop=False),
                        reads=[TTr, f"vb{b}"], writes=[bknr])
                    P.add("pe", lambda e, h=h, h4=h4, vn=vn, Scb=Scb: e.matmul(
                        vn[:, h4, :], nwT[b][:, h, :C], Scb[:, h, :], start=False, stop=True),
                        reads=[f"nwT{b}", f"{Scr}b/{h // 4}"], writes=[bknr])
                eng = "act" if hh == 0 else "dve"
                if eng == "act":
                    P.add("act", lambda e, vn=vn, hh=hh: e.activation(vnew[b][:C, hh * 4:hh * 4 + 4, :], vn, AF.Copy),
                          reads=[bknr], writes=["vnew0"])
                else:
                    P.add("dve", lambda e, vn=vn, hh=hh: e.tensor_copy(vnew[b][:C, hh * 4:hh * 4 + 4, :], vn),
                          reads=[bknr], writes=["vnew0"])
            o1b = []
            if full:
                for hh in range(2):
                    bko, bkor = bank()
                    ov = bko[:C, :].rearrange("p (h d) -> p h d", h=4)
                    for h4 in range(4):
                        h = hh * 4 + h4
                        P.add("pe", lambda e, h=h, h4=h4, ov=ov, Scb=Scb: e.matmul(
                            ov[:, h4, :], post[:, h, cs], Scb[:, h, :], start=True, stop=True),
                            reads=[f"post{h}", f"{Scr}b/{h // 4}"], writes=[bkor])
                    o1b.append((ov, bkor))
            for hh in range(2):
                bkd, bkdr = bank()
                dv = bkd[:, :].rearrange("p (h d) -> p h d", h=4)
                for h4 in range(4):
                    h = hh * 4 + h4
                    P.add("pe", lambda e, h=h, h4=h4, dv=dv: e.matmul(
                        dv[:, h4, :], ktail[b][:C, h, :], vnew[b][:C, h, :], start=True, stop=True),
                        reads=[f"ktail{b}", "vnew0"], writes=[bkdr])
                for h4 in range(4):
                    h = hh * 4 + h4
                    P.add("dve", lambda e, h=h, h4=h4, dv=dv, Sc=Sc: e.scalar_tensor_tensor(
                        Sc[:, h, :], Sc[:, h, :], eGl[b][:, h:h + 1], dv[:, h4, :], ALU.mult, ALU.add),
                        reads=[bkdr, f"{Scr}/{h}", f"eGl{b}"], writes=[f"{Scr}/{h}"])
                P.add("act", lambda e, Sc=Sc, Scb=Scb, hh=hh: e.activation(
                    Scb[:, hh * 4:hh * 4 + 4, :], Sc[:, hh * 4:hh * 4 + 4, :], AF.Copy),
                    reads=[f"{Scr}/{h_}" for h_ in range(hh * 4, hh * 4 + 4)], writes=[f"{Scr}b/{hh}"])
            if nseq > 1 and state_dst is not None:
                P.dma("sp", state_dst[1][c].rearrange("h k v -> k h v"), Sc[:], reads=[Scr], writes=["dram_out"])
            if full:
                for hh, (ov, bkor) in enumerate(o1b):
                    P.add("dve", lambda e, ov=ov, hh=hh: e.tensor_tensor(
                        o1s[b][:C, hh * 4:hh * 4 + 4, :], ov,
                        eGt[b][:C, hh * 4:hh * 4 + 4].unsqueeze(2).to_broadcast([C, 4, 128]), ALU.mult),
                        reads=[bkor, f"eGt{b}"], writes=["o1s0"])
                bkq2, bkq2r = bank()
                qv = bkq2[:C, 0:8 * C].rearrange("p (h c) -> p h c", h=8)
                for h in range(8):
                    P.add("pe", lambda e, h=h, qv=qv: e.matmul(qv[:, h, :], post[:, 8 + h, cs], post[:, h, cs],
                                                              start=True, stop=True),
                          reads=[f"post{8 + h}", f"post{h}"], writes=[bkq2r])
                P.add("dve", lambda e, qv=qv: e.tensor_tensor(qkT[b][:C, :, :C], qv, GamT[b][:C, :, :C], ALU.mult),
                      reads=[bkq2r, f"GamT{b}"], writes=["qkT0"])
                for hh in range(2):
                    bko, bkor = bank()
                    ov = bko[:C, :].rearrange("p (h d) -> p h d", h=4)
                    for h4 in range(4):
                        h = hh * 4 + h4
                        P.add("pe", lambda e, h=h, h4=h4, ov=ov: e.matmul(
                            ov[:, h4, :], qkT[b][:C, h, :C], vnew[b][:C, h, :], start=True, stop=True),
                            reads=["qkT0", "vnew0"], writes=[bkor])
                    P.add("dve", lambda e, ov=ov, hh=hh: e.tensor_tensor(
                        o1s[b][:C, hh * 4:hh * 4 + 4, :], ov, o1s[b][:C, hh * 4:hh * 4 + 4, :], ALU.add),
                        reads=[bkor, "o1s0"], writes=["o1s0"])
            if full:
                P.add("act", lambda e: e.activation(osq[b][:C], o1s[b][:C], AF.Square),
                      reads=["o1s0"], writes=["osq0"])
                P.add("dve", lambda e: e.reduce_sum(oss[b][:C, :], osq[b][:C], axis=AX.X),
                      reads=["osq0"], writes=[f"oss{b}"])
                P.add("dve", lambda e: e.tensor_scalar(oss[b][:C, :], oss[b][:C, :], 1.0 / 128, NORM_EPS * 128,
                                                       ALU.mult, ALU.add),
                      reads=[f"oss{b}"], writes=[f"oss{b}"])
                P.add("act", lambda e: e.activation(oss[b][:C, :], oss[b][:C, :], AF.Ln),
                      reads=[f"oss{b}"], writes=[f"oss{b}"])
                P.add("act", lambda e: e.activation(oss[b][:C, :], oss[b][:C, :], AF.Exp, scale=-0.5),
                      reads=[f"oss{b}"], writes=[f"oss{b}"])
                P.add("dve", lambda e: e.tensor_tensor(
                    onb[b][:C], o1s[b][:C], oss[b][:C, :].unsqueeze(2).to_broadcast([C, 8, 128]), ALU.mult),
                    reads=["o1s0", f"oss{b}"], writes=[f"onb{b % 2}"])
                yield "tail"
                bkt, bktr = bank()
                tv = bkt[:, 0:8 * C].rearrange("p (h c) -> p h c", h=8)
                for h in range(8):
                    P.add("pe", lambda e, h=h, tv=tv: e.matmul(tv[:, h, :], onb[b][:C, h, :], ident_bf[:C, :C],
                                                              start=True, stop=True),
                          reads=[f"onb{b % 2}", "ident_bf"], writes=[bktr])
                P.add("dve", lambda e, tv=tv: e.scalar_tensor_tensor(
                    mixg[:, :, cs], tv, ong, szg[:, :, cs], ALU.mult, ALU.mult),
                    reads=[bktr, "prm", "szg"], writes=["mixg"])

        gens = [chunk(c) for c in range(nch)]
        live = list(gens)
        while live:
            for g_ in list(live):
                if next(g_) == "presolve_done":
                    live.remove(g_)
        yield "presolve_done"
        live = list(gens)
        while live:
            for g_ in list(live):
                if next(g_) == "chain":
                    live.remove(g_)
            yield "solve"
        yield "solve_done"
        prev_tail = None
        for g_ in gens:
            alive = next(g_, None) is not None
            yield "chain"
            if prev_tail is not None:
                for _ in prev_tail:
                    pass
            prev_tail = g_ if alive else None
        if prev_tail is not None:
            for _ in prev_tail:
                pass
        yield "chain_done"

        if not full:
            return

        P.add("dve", lambda e: e.tensor_tensor(mixed[:, :, 0:N], mixg[:, :, 0:N], cfl[:, :, 0:N], ALU.add),
              reads=["mixg"] + ["r1"], writes=["mixed"])

        def layer_norm(rbuf, rres, gcol, bcol, outf, outfres):
            for eng_, lo in (("pool", 0), ("dve", 4)):
                P.add(eng_, lambda e, lo=lo: e.tensor_tensor(
                    mixed[:, lo:lo + 4, 0:N], rbuf[:, lo:lo + 4, 0:N], rbuf[:, lo:lo + 4, 0:N], ALU.mult),
                    reads=[f"{rres}/{oc}" for oc in range(lo, lo + 4)], writes=[f"mixed/{oc}" for oc in range(lo, lo + 4)])
            P.add("act", lambda e: e.activation(mixg[:, :, 0:N], rbuf[:, :, 0:N], AF.Copy),
                  reads=[rres], writes=["mixg"])
            bk, bkr = bank()
            for oc in range(8):
                P.add("pe", lambda e, oc=oc, bk=bk: e.matmul(bk[:, 0:N], onesm_bf[:], mixg[:, oc, 0:N],
                                                            start=(oc == 0), stop=(oc == 7)),
                      reads=["mixg", "onesm_bf"], writes=[bkr])
            for oc in range(8):
                P.add("pe", lambda e, oc=oc, bk=bk: e.matmul(bk[:, 256:256 + N], onesm_bf[:], mixed[:, oc, 0:N],
                                                            start=(oc == 0), stop=(oc == 7)),
                      reads=[f"mixed/{oc}", "onesm_bf"], writes=[bkr])
            P.add("act", lambda e, bk=bk: e.activation(mean_sb[:, 0:N], bk[:, 0:N], AF.Copy),
                  reads=[bkr], writes=["mean_sb"])
            P.add("pool", lambda e: e.tensor_tensor(m2[:, 0:N], mean_sb[:, 0:N], mean_sb[:, 0:N], ALU.mult),
                  reads=["mean_sb"], writes=["m2"])
            P.add("dve", lambda e, bk=bk: e.tensor_tensor(rstd[:, 0:N], bk[:, 256:256 + N], m2[:, 0:N], ALU.subtract),
                  reads=[bkr, "m2"], writes=["rstd"])
            P.add("act", lambda e: e.activation(rstd[:, 0:N], rstd[:, 0:N], AF.Ln, bias=epsn[:, 1:2]),
                  reads=["rstd", "epsc"], writes=["rstd"])
            P.add("act", lambda e: e.activation(rstd[:, 0:N], rstd[:, 0:N], AF.Exp, scale=-0.5),
                  reads=["rstd"], writes=["rstd"])
            for stat_, statr, op_ in ((mean_sb, "mean_sb", ALU.subtract), (rstd, "rstd", ALU.mult)):
                for eng_, lo in (("pool", 0), ("dve", 4)):
                    names = [f"{rres}/{oc}" for oc in range(lo, lo + 4)]
                    P.add(eng_, lambda e, lo=lo, stat_=stat_, op_=op_: e.tensor_tensor(
                        rbuf[:, lo:lo + 4, 0:N], rbuf[:, lo:lo + 4, 0:N],
                        stat_[:, 0:N].unsqueeze(1).to_broadcast([128, 4, N]), op_),
                        reads=names + [statr], writes=names)
            for oc in range(8):
                P.add("dve", lambda e, oc=oc: e.tensor_scalar(
                    outf[:, oc, 0:N], rbuf[:, oc, 0:N], lnp[:, oc, gcol:gcol + 1], lnp[:, oc, bcol:bcol + 1],
                    ALU.mult, ALU.add), reads=[f"{rres}/{oc}", "prm"], writes=[f"{outfres}/{oc}"])

        def wo_consume(oc, ps, psr):
            P.add("dve", lambda e: e.scalar_tensor_tensor(r1[:, oc, 0:N], x_f[:, oc, 0:N], epsn[:, 4:5], ps, ALU.mult, ALU.add),
                  reads=[psr, "x_f"], writes=[f"r1/{oc}"])

        for g in range(4):
            i, wv_ = wload(None, 2048, "p (k c) -> p k c", k=8)
            P.dma("sp", wst[i][:, 0:2048], wo_g[g], reads=WO_RES, writes=[f"wst{i}"])
            proj(mixed, "mixed", wv_, f"wst{i}", 2, N, wo_consume, cc0=g * 2)
        layer_norm(r1, "r1", 0, 1, x1f, "r1")
        P.add("act", lambda e: e.activation(x1b[:, :, 0:N], x1f[:, :, 0:N], AF.Copy), reads=["r1"], writes=["mixed"])

        ffn_pend = []
        for c in range(NFC):
            i, wv_ = wload(None, 2048, "p (k c) -> p k c", k=8)
            P.dma("sp", wst[i][:, 0:2048], wup_g[c], reads=WUP_RES, writes=[f"wst{i}"])
            pbanks = []
            for half in range(2):
                bk, bkr = bank()
                for kc in range(8):
                    P.add("pe", lambda e, bk=bk, kc=kc, wv_=wv_, half=half: e.matmul(
                        bk[:, 0:N], wv_[:, kc, half * 128:(half + 1) * 128], x1b[:, kc, 0:N],
                        start=(kc == 0), stop=(kc == 7)), reads=["mixed", f"wst{i}"], writes=[bkr])
                pbanks.append((bk, bkr))
            (abk, abkr), (vbk, vbkr) = pbanks
            ai = rot("apre", 4)
            k = rot("cacc", NCACC)
            apv = apre[ai][:, 0:nseq * (L + 2)].rearrange("p (s l) -> p s l", s=nseq)
            a3 = v3(cacc[k][:, 0:N])
            P.add("pool", lambda e, c=c, apv=apv: e.tensor_copy(apv[:, :, 0:2], ahv[:, c, :, :]),
                  reads=[f"ahalo/{c}"], writes=[f"apre{ai}"])
            P.add("act", lambda e, apv=apv, abk=abk: e.activation(apv[:, :, 2:2 + L], v3(abk[:, 0:N]), AF.Copy),
                  reads=[abkr], writes=[f"apre{ai}"])
            P.add("pool", lambda e, c=c, apv=apv: e.tensor_copy(ahv[:, c, :, :], apv[:, :, L:L + 2]),
                  reads=[f"apre{ai}"], writes=[f"ahalo/{c}"])
            P.add("pool", lambda e, c=c, apv=apv, a3=a3: e.tensor_tensor(
                a3, apv[:, :, 0:L], cwf[:, c, 0:1].unsqueeze(1).to_broadcast([128, nseq, L]), ALU.mult),
                reads=[f"apre{ai}", "prm"], writes=[f"cacc{k}"])
            for j in range(1, 3):
                P.add("dve", lambda e, j=j, c=c, apv=apv, a3=a3: e.scalar_tensor_tensor(
                    a3, apv[:, :, j:j + L], cwf[:, c, j:j + 1], a3, ALU.mult, ALU.add),
                    reads=[f"apre{ai}", "prm", f"cacc{k}"], writes=[f"cacc{k}"])
            prev = list(ffn_pend)
            del ffn_pend[:]
            for th in prev:
                th()

            def stage2(c=c, ai=ai, k=k, vbk=vbk, vbkr=vbkr):
                P.add("act", lambda e: e.activation(gab[ai][:, 0:N], cacc[k][:, 0:N], AF.Gelu),
                      reads=[f"cacc{k}"], writes=[f"gab{ai}"])
                P.add("dve", lambda e: e.tensor_tensor(hb[:, c, 0:N], vbk[:, 0:N], gab[ai][:, 0:N], ALU.mult),
                      reads=[vbkr, f"gab{ai}"], writes=[f"hb/{c}"])
            ffn_pend.append(stage2)
        for th in ffn_pend:
            th()

        for cg in range(4):
            bk0, bkr0 = bank()
            bk1, bkr1 = bank()
            bks, bkrs = (bk0, bk1), (bkr0, bkr1)
            for kg in range(3):
                nk = 8 if kg < 2 else NFC - 16
                i, wv_ = wload(None, nk * 256, "p (k c) -> p k c", k=nk)
                P.dma("sp", wst[i][:, 0:nk * 256], wdn_g[cg * 3 + kg][:, 0:nk * 256], reads=WDN_RES, writes=[f"wst{i}"])
                for o2 in range(2):
                    for k8 in range(nk):
                        kc = kg * 8 + k8
                        P.add("pe", lambda e, bk=bks[o2], kc=kc, k8=k8, o2=o2, wv_=wv_: e.matmul(
                            bk[:, 0:N], wv_[:, k8, o2 * 128:(o2 + 1) * 128], hb[:, kc, 0:N],
                            start=(kc == 0), stop=(kc == NFC - 1)),
                            reads=[f"hb/{kc}", f"wst{i}"], writes=[bkrs[o2]])
            for o2 in range(2):
                oc = cg * 2 + o2
                P.add("dve", lambda e, bk=bks[o2], oc=oc, o2=o2: e.scalar_tensor_tensor(
                    r1[:, oc, 0:N], r1[:, oc, 0:N], epsn[:, 4:5], bk[:, 0:N], ALU.mult, ALU.add),
                    reads=[bkrs[o2], f"r1/{oc}"], writes=[f"r1/{oc}"])
        layer_norm(r1, "r1", 2, 3, x_f, "x_f")

        if write_y:
            for tb in range(ntb):
                for kq in range(2):
                    bk, bkr = bank()
                    for k4 in range(4):
                        oc = kq * 4 + k4
                        P.add("pe", lambda e, bk=bk, k4=k4, oc=oc, tb=tb: e.transpose(
                            bk[:TB, k4 * 128:(k4 + 1) * 128], x_f[:, oc, tb * 128:tb * 128 + TB], ident[:, :]),
                            reads=["x_f", "cst"], writes=[bkr])
                    P.add("act" if kq == 0 else "dve",
                          (lambda e, bk=bk, kq=kq, tb=tb: e.activation(yout[:TB, tb, kq * 512:(kq + 1) * 512], bk[:TB, :], AF.Copy))
                          if kq == 0 else
                          (lambda e, bk=bk, kq=kq, tb=tb: e.tensor_copy(yout[:TB, tb, kq * 512:(kq + 1) * 512], bk[:TB, :])),
                          reads=[bkr], writes=["hb"])
            P.dma("sp", y_dst.rearrange("(tb p) d -> p tb d", p=TB), yout[:TB, 0:ntb, :], reads=["hb"],
                  writes=["dram_out"])

        if state_dst is not None:
            d_gc, d_S, d_sc, d_ff = state_dst

            def store_state(dst_d, nrows, r, width, srcv, srcres):
                nchk = width // 128
                hv = hs[:, 0:nchk, 0:nrows].rearrange("p c (s r) -> p c s r", r=r)
                P.add("pool", lambda e: e.tensor_copy(hv, srcv), reads=[srcres], writes=["hs"])
                for c0 in range(0, width, 512):
                    wd = min(512, width - c0)
                    for c1 in range(0, wd, 512):
                        w2 = min(512, wd - c1)
                        bk, bkr = bank()
                        for k4 in range(w2 // 128):
                            ch = (c0 + c1) // 128 + k4
                            P.add("pe", lambda e, bk=bk, k4=k4, ch=ch: e.transpose(
                                bk[:nrows, k4 * 128:(k4 + 1) * 128], hs[:, ch, 0:nrows], ident[:, :]),
                                reads=["hs", "cst"], writes=[bkr])
                        P.add("dve", lambda e, bk=bk, c1=c1, w2=w2: e.tensor_copy(
                            hso[:nrows, c1:c1 + w2], bk[:nrows, 0:w2]), reads=[bkr], writes=["hso"])
                    P.dma("sp", dst_d[:, c0:c0 + wd], hso[:nrows, 0:wd], reads=["hso"], writes=["dram_out"])

            store_state(d_gc, nseq * 3, 3, 3072, pv[:, :, :, L:L + 3], "preqkv")
            store_state(d_sc, nseq * 2, 2, D, ppv[:, :, :, L:L + 2], "ppre")
            store_state(d_ff, nseq * 2, 2, DFF, ahv, "ahalo")
            if nseq == 1:
                P.dma("sp", d_S.rearrange("h k v -> k h v"), S[:], reads=[Sres], writes=["dram_out"])

    specs = []
    if STOP != "setup":
        for t in range(NPRE):
            specs.append(dict(full=False, x=xp[t * NT:(t + 1) * NT, :], tb=128, ntb=2, kw=dict(
                nseq=1, L=NT, C=64, full=False, S=Sp, Sb=Spb, Sres="Sp", y_dst=None)))
    if STOP not in ("setup", "pre"):
        for t in range(NFULL):
            r0 = (NPRE + t) * NT
            last = (t == NFULL - 1)
            specs.append(dict(full=True, x=xp[r0:r0 + NT, :], tb=128, ntb=2, kw=dict(
                nseq=1, L=NT, C=64, full=True, S=Sp, Sb=Spb, Sres="Sp",
                y_dst=yp[(t - 1) * NT:t * NT, :] if t > 0 else None,
                state_dst=(p_gc, p_S, p_sc, p_ff) if last else None, write_y=(t > 0))))
    if nsamp and STOP is None:
        specs.append(dict(full=True, x=xs, tb=nsamp * 16, ntb=1, kw=dict(
            nseq=nsamp, L=16, C=16, full=True, S=Ss, Sb=Ssb, Sres="Sp", y_dst=ys, hist_src=(st_gc, st_sc, st_ff),
            state_dst=(s_gc, s_S, s_sc, s_ff), write_y=True, S_src=st_S)))
    pend = None
    PEND_PER_FRONT = 1
    FRONT_PER_PEND = 1
    DEFER_AFTER = "solve_done"
    pend_last = [None]
    for ti, sp_ in enumerate(specs):
        P.epoch = ti
        nxt = specs[ti + 1] if ti + 1 < len(specs) else None
        x_next = (nxt["x"], nxt["tb"], nxt["ntb"]) if nxt is not None else None
        njobs = 3 if not sp_["full"] else len(cast_jobs)
        for _ in range(min(njobs, len(cast_jobs))):
            cast_jobs.pop(0)()
        g = tile(sp_["x"], x_next=x_next, **sp_["kw"])
        fstep = 0
        while True:
            if fstep % FRONT_PER_PEND == 0:
                for _ in range(PEND_PER_FRONT):
                    if pend is not None:
                        cur_pool[0] = "pend"
                        if next(pend, None) is None:
                            pend = None
            fstep += 1
            cur_pool[0] = "front" if pend is not None else "all"
            m = next(g)
            if m == "front_done":
                break
        if pend is not None:
            cur_pool[0] = "pend"
            for _ in pend:
                pass
            pend = None
        cur_pool[0] = "all"
        while next(g) != DEFER_AFTER:
            pass
        if sp_["full"]:
            for _ in g:
                pass
        else:
            pend = g
            pend_last[0] = "presolve_done"
    if pend is not None:
        cur_pool[0] = "pend"
        for _ in pend:
            pass

    P.emit(nc, es)
    es.close()
    return nc


def _consts():
    c = np.zeros((128, 384), np.float32)
    c[:, 0:128] = np.eye(128, dtype=np.float32)
    t = np.arange(64)
    c[:64, 128:192] = (t[:, None] <= t[None, :]).astype(np.float32)
    c[:64, 192:256] = (t[:, None] > t[None, :]).astype(np.float32)
    c[:, 256:384] = 1.0
    return c


def make_in_maps(inp, NPRE, NFULL, ncores=8):
    ROWS = (NPRE + NFULL) * NT
    NOWN = (NFULL - 1) * NT
    xp = np.asarray(inp["x_prompt"], np.float32)
    xs = np.asarray(inp["x_sample"], np.float32)
    nseg = xp.shape[1] // NOWN
    lnp = np.stack([np.asarray(inp[k], np.float32)[0] for k in ("ln1_g", "ln1_b", "ln2_g", "ln2_b")])
    shared = {
        "w_in": np.ascontiguousarray(inp["w_in"][0]), "gdn_conv_w": np.ascontiguousarray(inp["gdn_conv_w"][0]),
        "a_log": np.ascontiguousarray(inp["a_log"]), "dt_bias": np.ascontiguousarray(inp["dt_bias"]),
        "o_norm_g": np.ascontiguousarray(inp["o_norm_g"]), "sc_conv_w": np.ascontiguousarray(inp["sc_conv_w"][0]),
        "w_o": np.ascontiguousarray(inp["w_o"][0]), "lnp": lnp, "w_up": np.ascontiguousarray(inp["w_up"][0]),
        "ffn_conv_w": np.ascontiguousarray(inp["ffn_conv_w"][0]), "w_down": np.ascontiguousarray(inp["w_down"][0]),
        "consts": _consts(),
    }
    maps = []
    for core in range(ncores):
        b, j = core // nseg, core % nseg
        end = NOWN * (j + 1)
        x_ext = np.zeros((ROWS, D), np.float32)
        n = min(end, ROWS)
        x_ext[ROWS - n:] = xp[b, end - n:end]
        m = dict(shared)
        m["xp"] = x_ext
        m["xs"] = np.ascontiguousarray(xs[4 * core:4 * core + 4].reshape(64, D))
        m["st_gc"] = np.ascontiguousarray(inp["state_gdn_conv"][0, 4 * core:4 * core + 4].reshape(12, 3072))
        m["st_S"] = np.ascontiguousarray(inp["state_gdn_S"][0, 4 * core:4 * core + 4])
        m["st_sc"] = np.ascontiguousarray(inp["state_sc_conv"][0, 4 * core:4 * core + 4].reshape(8, D))
        m["st_ff"] = np.ascontiguousarray(inp["state_ffn_conv"][0, 4 * core:4 * core + 4].reshape(8, DFF))
        maps.append(m)
    return maps


_NC_CACHE = {}


def kernel(**inp):
    NPRE, NFULL = 47, 17
    key = (NPRE, NFULL)
    if key not in _NC_CACHE:
        _NC_CACHE[key] = build(NPRE, NFULL)
    nc = _NC_CACHE[key]
    maps = make_in_maps(inp, NPRE, NFULL)
    res = run_bass_kernel_spmd(nc, maps, core_ids=list(range(8))).results
    B, SEQ = 2, 16384
    yp = np.zeros((B, SEQ, D), np.float32)
    ys = np.zeros((32, 16, D), np.float32)
    p_gc = np.zeros((1, B, 3, 3072), np.float32)
    p_S = np.zeros((1, B, 8, 128, 128), np.float32)
    p_sc = np.zeros((1, B, 2, D), np.float32)
    p_ff = np.zeros((1, B, 2, DFF), np.float32)
    s_gc = np.zeros((1, 32, 3, 3072), np.float32)
    s_S = np.zeros((1, 32, 8, 128, 128), np.float32)
    s_sc = np.zeros((1, 32, 2, D), np.float32)
    s_ff = np.zeros((1, 32, 2, DFF), np.float32)
    for core in range(8):
        r = res[core]
        b, j = core // 4, core % 4
        yp[b, 4096 * j:4096 * (j + 1)] = r["yp"]
        ys[4 * core:4 * core + 4] = r["ys"].reshape(4, 16, D)
        s_gc[0, 4 * core:4 * core + 4] = r["s_gc"].reshape(4, 3, 3072)
        s_S[0, 4 * core:4 * core + 4] = r["s_S"]
        s_sc[0, 4 * core:4 * core + 4] = r["s_sc"].reshape(4, 2, D)
        s_ff[0, 4 * core:4 * core + 4] = r["s_ff"].reshape(4, 2, DFF)
        if j == 3:
            p_gc[0, b] = r["p_gc"]
            p_S[0, b] = r["p_S"]
            p_sc[0, b] = r["p_sc"]
            p_ff[0, b] = r["p_ff"]
    return (yp, ys, p_gc, p_S, p_sc, p_ff, s_gc, s_S, s_sc, s_ff)
```

```python
import numpy as np
from contextlib import ExitStack
import concourse.bass as bass
import concourse.mybir as mybir
from concourse.bass_utils import run_bass_kernel_spmd

F32 = mybir.dt.float32
BF16 = mybir.dt.bfloat16
ALU = mybir.AluOpType
AF = mybir.ActivationFunctionType
AX = mybir.AxisListType

D = 1024
NT = 256
DFF = 2816
NFC = 22
ALPHA = 2.0 ** 0.25
LN_EPS = 1e-5
NORM_EPS = 1e-6
ENGS = ("pe", "act", "dve", "pool", "sp")
NROT = 4
NDMASEM = 40
STOP = None
STOP2 = None
PEND_MODE = None


BANK_GEN = {}


class BankRes(str):
    def __new__(cls, s, gen):
        o = str.__new__(cls, s)
        o.gen = gen
        return o


class Op:
    __slots__ = ("eng", "fn", "waits", "inc", "epoch", "dma", "cnt")


class Prog:
    def __init__(self):
        self.ops = {e: [] for e in ENGS}
        self.lastw = {}
        self.rd = {}
        self.children = {}
        self.waited = {}
        self.dwaited = {}
        self.epoch = 0
        self.dma_vals = [0] * NDMASEM
        self.dma_rr = 0
        self.pool_dmas = 0

    @staticmethod
    def _norm(reads, writes):
        ps = [r for r in reads if r.startswith("ps")]
        if ps:
            reads = [r for r in reads if not r.startswith("ps")]
            writes = list(writes) + [p for p in ps if p not in writes]
        return reads, writes

    def _rel(self, r):
        if "/" in r:
            par = r.split("/")[0]
            self.children.setdefault(par, set()).add(r)
            return (r, par)
        ch = self.children.get(r)
        return (r,) + tuple(ch) if ch else (r,)

    def _mk(self, eng, fn, reads, writes, extra=()):
        deps = list(extra)
        for r in reads:
            for q in self._rel(r):
                t = self.lastw.get(q)
                if t is not None:
                    deps.append(t)
        for w in writes:
            for q in self._rel(w):
                t = self.lastw.get(q)
                if t is not None:
                    deps.append(t)
                deps.extend(self.rd.get(q, ()))
        op = Op()
        op.eng, op.fn, op.inc, op.epoch, op.dma, op.cnt = eng, fn, False, self.epoch, None, None
        waits = []
        best = {}
        for t in deps:
            if t[0] == "op":
                _, e2, i2 = t
                if e2 == eng and eng == "pe":
                    continue
                if i2 > best.get(e2, -1):
                    best[e2] = i2
            else:
                _, k, v = t
                if self.dwaited.get((eng, k), 0) < v:
                    self.dwaited[(eng, k)] = v
                    waits.append(("dma", k, v))
        for e2, i2 in best.items():
            if self.waited.get((eng, e2), -1) < i2:
                self.waited[(eng, e2)] = i2
                self.ops[e2][i2].inc = True
                waits.append(("op", e2, i2))
        op.waits = waits
        return op

    def _commit(self, tok, reads, writes):
        for r in reads:
            self.rd.setdefault(r, []).append(tok)
        for w in writes:
            self.lastw[w] = tok
            self.rd[w] = []

    def add(self, eng, fn, reads=(), writes=()):
        for r in list(reads) + list(writes):
            if isinstance(r, BankRes):
                assert BANK_GEN[str(r)] == r.gen, f"stale PSUM bank {r} gen {r.gen} != {BANK_GEN[str(r)]}"
        ps_reads = [r for r in reads if r.startswith("ps") and r not in writes]
        reads = [r for r in reads if not r.startswith("ps")]
        extra = []
        for r in ps_reads:
            t = self.lastw.get(r)
            if t is not None:
                extra.append(t)
            for t in self.rd.get(r, ()):
                if t[0] == "op" and t[1] != eng:
                    extra.append(t)
        op = self._mk(eng, fn, reads, writes, extra)
        idx = len(self.ops[eng])
        self.ops[eng].append(op)
        tok = ("op", eng, idx)
        self._commit(tok, reads, writes)
        for r in ps_reads:
            self.rd.setdefault(r, []).append(tok)

    def dma(self, eng, out, in_, reads=(), writes=()):
        if eng == "pool":
            op = self._mk(eng, lambda e: e.dma_start(out=out, in_=in_), reads, writes)
            self.pool_dmas += 1
            op.dma = ("p", 0)
            self.ops[eng].append(op)
            self._commit(("dma", "p", 1), reads, writes)
            return
        k = self.dma_rr
        self.dma_rr = (k + 1) % NDMASEM
        prev = self.dma_vals[k]
        extra = [("dma", k, prev)] if prev > 0 else []
        op = self._mk(eng, lambda e: e.dma_start(out=out, in_=in_), reads, writes, extra)
        self.dma_vals[k] = prev + 16
        op.dma = (k, prev + 16)
        self.ops[eng].append(op)
        self._commit(("dma", k, prev + 16), reads, writes)

    def emit(self, nc, es):
        sems = {e: [es.enter_context(nc.semaphore(f"s_{e}{r}")) for r in range(NROT)] for e in ENGS}
        dsems = [es.enter_context(nc.semaphore(f"d{k}")) for k in range(NDMASEM)]
        psem = es.enter_context(nc.semaphore("pdma"))
        ptotal = 16 * self.pool_dmas
        for e in ENGS:
            cnt = [0] * NROT
            for op in self.ops[e]:
                if op.inc:
                    r = op.epoch % NROT
                    cnt[r] += 1
                    op.cnt = (r, cnt[r])
        final_vals = list(self.dma_vals)

        def run(e, eo):
            for op in self.ops[e]:
                for w in op.waits:
                    if w[0] == "op":
                        d = self.ops[w[1]][w[2]]
                        eo.wait_ge(sems[w[1]][d.cnt[0]], d.cnt[1])
                    elif w[1] == "p":
                        eo.wait_ge(psem, ptotal)
                    else:
                        eo.wait_ge(dsems[w[1]], w[2])
                ins = op.fn(eo)
                if op.dma is not None and op.dma[0] == "p":
                    ins.then_inc(psem, 16)
                elif op.dma is not None:
                    ins.then_inc(dsems[op.dma[0]], 16)
                elif op.inc:
                    ins.then_inc(sems[e][op.cnt[0]], 1)
            if e == "sp":
                for k, v in enumerate(final_vals):
                    if v > 0:
                        eo.wait_ge(dsems[k], v)
                if ptotal:
                    eo.wait_ge(psem, ptotal)

        block = es.enter_context(nc.Block())

        @block.tensor
        def _(eo):
            run("pe", eo)

        @block.scalar
        def _(eo):
            run("act", eo)

        @block.vector
        def _(eo):
            run("dve", eo)

        @block.gpsimd
        def _(eo):
            run("pool", eo)

        @block.sync
        def _(eo):
            run("sp", eo)


def build(NPRE, NFULL, nsamp=4):
    BANK_GEN.clear()
    nc = bass.Bass("TRN2", target_bir_lowering=False)
    P = Prog()
    es = ExitStack()
    ROWS = (NPRE + NFULL) * NT
    NOWN = (NFULL - 1) * NT

    def din(name, shape):
        return nc.dram_tensor(name, list(shape), F32, kind="ExternalInput").ap()

    def dout(name, shape):
        return nc.dram_tensor(name, list(shape), F32, kind="ExternalOutput").ap()

    xp = din("xp", [ROWS, D])
    xs = din("xs", [nsamp * 16, D])
    st_gc = din("st_gc", [nsamp * 3, 3072])
    st_S = din("st_S", [nsamp, 8, 128, 128])
    st_sc = din("st_sc", [nsamp * 2, D])
    st_ff = din("st_ff", [nsamp * 2, DFF])
    w_in = din("w_in", [D, 9232])
    gdn_conv_w = din("gdn_conv_w", [4, 3072])
    a_log = din("a_log", [1, 8])
    dt_bias = din("dt_bias", [1, 8])
    o_norm_g = din("o_norm_g", [1, 128])
    sc_conv_w = din("sc_conv_w", [3, D])
    w_o = din("w_o", [D, D])
    lnp_d = din("lnp", [4, D])
    w_up = din("w_up", [D, 2 * DFF])
    ffn_conv_w = din("ffn_conv_w", [3, DFF])
    w_down = din("w_down", [DFF, D])
    consts = din("consts", [128, 384])

    yp = dout("yp", [NOWN, D])
    ys = dout("ys", [nsamp * 16, D])
    p_gc = dout("p_gc", [3, 3072])
    p_S = dout("p_S", [8, 128, 128])
    p_sc = dout("p_sc", [2, D])
    p_ff = dout("p_ff", [2, DFF])
    s_gc = dout("s_gc", [nsamp * 3, 3072])
    s_S = dout("s_S", [nsamp, 8, 128, 128])
    s_sc = dout("s_sc", [nsamp * 2, D])
    s_ff = dout("s_ff", [nsamp * 2, DFF])

    win_g = nc.dram_tensor("win_g", [36, 128, 2048], BF16).ap()
    wo_g = nc.dram_tensor("wo_g", [4, 128, 2048], BF16).ap()
    wup_g = nc.dram_tensor("wup_g", [NFC, 128, 2048], BF16).ap()
    wdn_g = nc.dram_tensor("wdn_g", [12, 128, 2048], BF16).ap()

    def sb(name, shape, dt=F32):
        return es.enter_context(nc.sbuf_tensor(name, list(shape), dt))

    banks = [es.enter_context(nc.psum_tensor(f"bank{i}", [128, 512], F32)) for i in range(8)]
    POOLS = {"all": list(range(8)), "front": [4, 5, 6, 7], "pend": [0, 1, 2, 3]}
    bank_rr = {"all": 0, "front": 0, "pend": 0}
    cur_pool = ["all"]

    def bank():
        pn = cur_pool[0]
        pool = POOLS[pn]
        i = pool[bank_rr[pn] % len(pool)]
        bank_rr[pn] += 1
        BANK_GEN[f"ps{i}"] = BANK_GEN.get(f"ps{i}", 0) + 1
        return banks[i], BankRes(f"ps{i}", BANK_GEN[f"ps{i}"])

    rot_state = {}

    def rot(name, n):
        i = rot_state.get(name, 0)
        rot_state[name] = (i + 1) % n
        return i

    cst = sb("cst", [128, 384])
    ident = cst[:, 0:128]
    tri = cst[:64, 128:192]
    sgt = cst[:64, 192:256]
    ones = cst[:, 256:384]
    ident_bf = sb("ident_bf", [128, 128], BF16)
    ones_bf = sb("ones_bf", [128, 128], BF16)
    trisg_bf = sb("trisg_bf", [64, 128], BF16)
    tri_bf = trisg_bf[:, 0:64]
    sgt_bf = trisg_bf[:, 64:128]
    onesm = sb("onesm", [128, 128])
    prm = sb("prm", [128, 224])
    epsn = sb("epsn", [128, 8])
    dtb = sb("dtb", [128, 8])
    nA = sb("nA", [128, 8])
    hso = sb("hso", [12, 512])

    P.dma("sp", cst[:], consts[:, :], writes=["cst"])
    P.dma("sp", hso[0:1, 0:8], dt_bias[:, :], writes=["hso"])
    P.dma("sp", hso[0:1, 8:16], a_log[:, :], writes=["hso"])
    bk, bkr = bank()
    P.add("pe", lambda e, bk=bk: e.matmul(bk[:, 0:16], ones[0:1, :], hso[0:1, 0:16], start=True, stop=True),
          reads=["hso", "cst"], writes=[bkr])
    P.add("dve", lambda e, bk=bk: e.tensor_copy(dtb[:], bk[:, 0:8]), reads=[bkr], writes=["dtb"])
    P.add("dve", lambda e, bk=bk: e.tensor_copy(nA[:], bk[:, 8:16]), reads=[bkr], writes=["nA"])

    WIN_RES = [f"win_sA{r}" for r in range(8)] + [f"win_sB{r}" for r in range(8)]
    WIN_KV = [f"win_kv{r}" for r in range(8)]
    WO_RES = [f"wo_s{r}" for r in range(8)]
    WUP_RES = [f"wup_sA{r}" for r in range(8)] + [f"wup_sB{r}" for r in range(8)]
    WDN_RES = [f"wdn_s{r}" for r in range(NFC)]
    wba = sb("wba", [128, 8, 16], BF16)

    P.add("pool", lambda e: e.memset(epsn[:, 0:1], NORM_EPS), writes=["epsc"])
    P.add("pool", lambda e: e.memset(epsn[:, 1:2], LN_EPS), writes=["epsc"])
    P.add("pool", lambda e: e.memset(epsn[:, 2:3], 1.0), writes=["epsc"])
    P.add("pool", lambda e: e.memset(epsn[:, 3:4], -1.0), writes=["epsc"])
    P.add("pool", lambda e: e.memset(epsn[:, 4:5], ALPHA), writes=["epsc"])
    P.add("dve", lambda e: e.tensor_copy(ident_bf[:], ident), reads=["cst"], writes=["ident_bf"])
    P.add("dve", lambda e: e.tensor_copy(ones_bf[:], ones), reads=["cst"], writes=["ones_bf"])
    P.add("dve", lambda e: e.tensor_copy(trisg_bf[:], cst[:64, 128:256]), reads=["cst"], writes=["ones_bf"])
    P.add("dve", lambda e: e.tensor_scalar(onesm[:], ones, 1.0 / D, None, ALU.mult), reads=["cst"], writes=["onesm"])
    P.add("dve", lambda e: e.tensor_copy(onesm_bf[:], onesm[:]), reads=["onesm"], writes=["onesm_bf"])
    P.add("act", lambda e: e.activation(nA[:], nA[:], AF.Exp), reads=["nA"], writes=["nA"])
    P.add("dve", lambda e: e.tensor_scalar(nA[:], nA[:], -1.0, None, ALU.mult), reads=["nA"], writes=["nA"])

    bk, bkr = bank()
    col = 0
    plist = [(gdn_conv_w, 4, 3072), (sc_conv_w, 3, D), (ffn_conv_w, 3, DFF), (lnp_d, 4, D), (o_norm_g, 1, 128)]
    for (src_d, r, width) in plist:
        for c0 in range(0, width, 512):
            wd = min(512, width - c0)
            P.dma("sp", hso[:r, 0:wd], src_d[:, c0:c0 + wd], writes=["hso"])
            for ch in range(wd // 128):
                o_ap = bk[:, col:col + r]
                i_ap = hso[:r, ch * 128:(ch + 1) * 128]
                P.add("pe", lambda e, o_ap=o_ap, i_ap=i_ap, r=r: e.transpose(o_ap, i_ap, ident[:r, :r]),
                      reads=["hso", "cst"], writes=[bkr])
                col += r
    P.add("dve", lambda e: e.tensor_copy(prm[:, 0:219], bk[:, 0:219]), reads=[bkr], writes=["prm"])
    cwg = prm[:, 0:96].rearrange("p (c k) -> p c k", k=4)
    cws = prm[:, 96:120].rearrange("p (c k) -> p c k", k=3)
    cwf = prm[:, 120:186].rearrange("p (c k) -> p c k", k=3)
    lnp = prm[:, 186:218].rearrange("p (c k) -> p c k", k=4)
    ong = prm[:, 218:219]

    xa = sb("xa", [128, 2, D])
    xT = sb("xT", [128, 8, NT], BF16)
    x_f = sb("x_f", [128, 8, NT])
    preqkv = sb("preqkv", [128, 24, NT + 3], BF16)
    post = sb("post", [128, 24, NT], BF16)
    NCACC = 4
    cacc = [sb(f"cacc{i}", [128, NT]) for i in range(NCACC)]
    sqb = [sb(f"sqb{i}", [128, NT], BF16) for i in range(2)]
    rsb = [sb(f"rsb{i}", [128, NT]) for i in range(2)]
    wst = [sb(f"wst{i}", [128, 2048], BF16) for i in range(4)]
    szg = sb("szg", [128, 8, NT], BF16)
    ppre = sb("ppre", [128, 8, NT + 2], BF16)
    mixg = sb("mixg", [128, 8, NT], BF16)
    sCs = mixg
    mixed = sb("mixed", [128, 8, NT], BF16)
    r1 = sb("r1", [128, 8, NT])
    cfl = r1
    onesm_bf = sb("onesm_bf", [128, 128], BF16)
    mean_sb = sb("mean_sb", [128, NT])
    m2 = sb("m2", [128, NT])
    rstd = sb("rstd", [128, NT])
    x1f = r1
    x1b = mixed
    apre = [sb(f"apre{i}", [128, NT + 2]) for i in range(4)]
    ahalo = sb("ahalo", [128, NFC, 4, 2])
    gab = [sb(f"gab{i}", [128, NT], BF16) for i in range(4)]
    hb = sb("hb", [128, NFC, NT], BF16)
    yout = hb[:].rearrange("p c t -> p (c t)").bitcast(F32)[:, 0:2 * D].rearrange("p (tb d) -> p tb d", tb=2)
    hs = sb("hs", [128, 24, 12])
    ba_beta = sb("ba_beta", [64, 4, 8])
    ba_t = sb("ba_t", [64, 4, 8])
    ba_g = sb("ba_g", [64, 4, 8])
    ba_gh = sb("ba_gh", [64, 4, 8], BF16)
    ba_gl = sb("ba_gl", [64, 4, 8], BF16)
    NB = 4
    eGt = [sb(f"eGt{i}", [64, 16]) for i in range(NB)]
    eGl = [sb(f"eGl{i}", [128, 8]) for i in range(NB)]
    bg = [sb(f"bg{i}", [64, 8]) for i in range(NB)]
    gtri = [sb(f"gtri{i}", [64, 2, 8, 64], BF16) for i in range(NB)]
    Gam = [sb(f"Gam{i}", [64, 8, 64], BF16) for i in range(NB)]
    GamT = [sb(f"GamT{i}", [64, 8, 64], BF16) for i in range(NB)]
    kbg = [sb(f"kbg{i}", [64, 8, 128], BF16) for i in range(NB)]
    ktail = [sb(f"ktail{i}", [64, 8, 128], BF16) for i in range(NB)]
    vb = [sb(f"vb{i}", [64, 8, 128], BF16) for i in range(NB)]
    Am = [sb(f"Am{i}", [64, 8, 64], BF16) for i in range(NB)]
    Xm = [[sb(f"Xm{i}_{k}", [64, 8, 64], BF16) for k in range(2)] for i in range(NB)]
    Ym = [[sb(f"Ym{i}_{k}", [64, 8, 64], BF16) for k in range(2)] for i in range(NB)]
    Rm = [[sb(f"Rm{i}_0", [64, 8, 64], BF16)] * 2 for i in range(NB)]
    nwT = [sb(f"nwT{i}", [128, 8, 64], BF16) for i in range(NB)]
    vnew = [sb("vnew0", [64, 8, 128], BF16)] * NB
    qkT = [sb("qkT0", [64, 8, 64], BF16)] * NB
    o1s = [sb("o1s0", [64, 8, 128])] * NB
    osq = [sb("osq0", [64, 8, 128], BF16)] * NB
    oss = [sb(f"oss{i}", [64, 8]) for i in range(NB)]
    onb = [sb(f"onb{i}", [64, 8, 128], BF16) for i in range(2)] * 2
    Sp = sb("Sp", [128, 8, 128])
    Spb = sb("Spb", [128, 8, 128], BF16)
    Ss, Ssb = Sp, Spb

    stg_f = [(r1, "r1"), (x_f, "x_f")]
    stg_b = [(mixed, "mixed"), (mixg, "mixg")]
    pp = [0]

    def cast_block(src_ap, dst_ap, ncol, res, nk=None):
        i = pp[0] % 2
        pp[0] += 1
        (sf, sfr), (sbf, sbr) = stg_f[i], stg_b[i]
        fv = sf[:].rearrange("p a b -> p (a b)")[:, 0:ncol]
        bv = sbf[:].rearrange("p a b -> p (a b)")[:, 0:ncol]
        if nk is not None:
            fv = fv.rearrange("p (c k) -> p c k", k=nk)
            bv = bv.rearrange("p (c k) -> p c k", k=nk)
        P.dma("sp", fv, src_ap, writes=[sfr])
        eng = ("act", "dve", "pool")[pp[0] % 3]
        if eng == "act":
            P.add("act", lambda e: e.activation(bv, fv, AF.Copy), reads=[sfr], writes=[sbr])
        else:
            P.add(eng, lambda e: e.tensor_copy(bv, fv), reads=[sfr], writes=[sbr])
        P.dma("sp", dst_ap, bv, reads=[sbr], writes=[res])

    cast_jobs = []
    for r in range(8):
        rs = slice(r * 128, (r + 1) * 128)
        cast_block(w_in[rs, 1024:3072].rearrange("r (g c) -> r g c", c=256), win_g[4:12, :, r * 256:(r + 1) * 256].rearrange("g p c -> p g c"), 2048, f"win_kv{r}", nk=256)
    for r in range(8):
        rs = slice(r * 128, (r + 1) * 128)
        cast_jobs.append(lambda rs=rs, r=r: cast_block(w_in[rs, 0:1024].rearrange("r (g c) -> r g c", c=256), win_g[0:4, :, r * 256:(r + 1) * 256].rearrange("g p c -> p g c"), 1024, f"win_sA{r}", nk=256))
        cast_jobs.append(lambda rs=rs, r=r: cast_block(w_in[rs, 3072:4096].rearrange("r (g c) -> r g c", c=256), win_g[12:16, :, r * 256:(r + 1) * 256].rearrange("g p c -> p g c"), 1024, f"win_sA{r}", nk=256))
        cast_jobs.append(lambda rs=rs, r=r: cast_block(w_in[rs, 4112:6160].rearrange("r (g c) -> r g c", c=256), win_g[16:24, :, r * 256:(r + 1) * 256].rearrange("g p c -> p g c"), 2048, f"win_sB{r}", nk=256))
        cast_jobs.append(lambda rs=rs, r=r: cast_block(w_in[rs, 6160:8208].rearrange("r (g c) -> r g c", c=256), win_g[24:32, :, r * 256:(r + 1) * 256].rearrange("g p c -> p g c"), 2048, f"win_sB{r}", nk=256))
        cast_jobs.append(lambda rs=rs, r=r: cast_block(w_in[rs, 8208:9232].rearrange("r (g c) -> r g c", c=256), win_g[32:36, :, r * 256:(r + 1) * 256].rearrange("g p c -> p g c"), 1024, f"win_sB{r}", nk=256))
        cast_jobs.append(lambda rs=rs, r=r: cast_block(w_o[rs, :].rearrange("r (g c) -> r g c", c=256), wo_g[:, :, r * 256:(r + 1) * 256].rearrange("g p c -> p g c"), 1024, f"wo_s{r}", nk=256))
        for (c0, c1) in ((0, 16), (16, NFC)):
            for half in range(2):
                cast_jobs.append(lambda rs=rs, r=r, c0=c0, c1=c1, half=half: cast_block(
                    w_up[rs, half * DFF + c0 * 128:half * DFF + c1 * 128].rearrange("r (c k) -> r c k", k=128),
                    wup_g[c0:c1, :, r * 256 + half * 128:r * 256 + half * 128 + 128].rearrange("g p c -> p g c"),
                    (c1 - c0) * 128, f"wup_s{'AB'[half]}{r}", nk=128))
    for r in range(NFC):
        rs = slice(r * 128, (r + 1) * 128)
        kg, k8 = r // 8, r % 8
        cast_jobs.append(lambda rs=rs, r=r, kg=kg, k8=k8: cast_block(
            w_down[rs, :].rearrange("r (g c) -> r g c", c=256),
            wdn_g.rearrange("(cg kg) p c -> cg kg p c", kg=3)[:, kg, :, k8 * 256:(k8 + 1) * 256].rearrange("g p c -> p g c"),
            1024, f"wdn_s{r}", nk=256))
    fv = r1[:].rearrange("p a b -> p (a b)")[:, 0:128].rearrange("p (k c) -> p k c", c=16)
    P.dma("sp", fv, w_in[:, 4096:4112].rearrange("(kc p) c -> p kc c", p=128), writes=["r1"])
    P.add("dve", lambda e: e.tensor_copy(wba[:], fv), reads=["r1"], writes=["wba"])

    P.add("pool", lambda e: e.memset(Sp[:], 0.0), writes=["Sp"])
    P.add("pool", lambda e: e.memset(Spb[:], 0.0), writes=["Spb"])
    P.add("pool", lambda e: e.memset(preqkv[:], 0.0), writes=["preqkv"])
    P.add("pool", lambda e: e.memset(ppre[:], 0.0), writes=["ppre"])
    P.add("pool", lambda e: e.memset(ahalo[:], 0.0), writes=["ahalo"])

    wst_rr = [0]
    xa_loaded = [False]

    def wload(src_ap, nel, shape_str, **kw):
        i = wst_rr[0]
        wst_rr[0] = (i + 1) % 4
        view = wst[i][:, 0:nel].rearrange(shape_str, **kw)
        return i, view

    def proj(xin, xres, wview, wres, ncc, N, consume, cc0=0):
        for cc in range(ncc):
            bk, bkr = bank()
            for kc in range(8):
                P.add("pe", lambda e, bk=bk, kc=kc, cc=cc: e.matmul(
                    bk[:, 0:N], wview[:, kc, cc * 128:(cc + 1) * 128], xin[:, kc, 0:N],
                    start=(kc == 0), stop=(kc == 7)), reads=[xres, wres], writes=[bkr])
            consume(cc0 + cc, bk[:, 0:N], bkr)

    def tile(x_src, nseq, L, C, full, S, Sb, Sres, y_dst, hist_src=None, state_dst=None, write_y=True, S_src=None, x_next=None):
        N = nseq * L
        nch = N // C
        ntb = (N + 127) // 128
        TB = min(N, 128)
        H = 3

        def v3(ap):
            return ap.rearrange("p (s l) -> p s l", s=nseq)

        if not xa_loaded[0]:
            P.dma("sp", xa[:TB, 0:ntb, :], x_src.rearrange("(tb p) d -> p tb d", p=TB), writes=["xa"])
        xa_loaded[0] = False
        for tb in range(ntb):
            for kq in range(2):
                bk, bkr = bank()
                for k4 in range(4):
                    kc = kq * 4 + k4
                    P.add("pe", lambda e, bk=bk, k4=k4, kc=kc, tb=tb: e.transpose(
                        bk[:, k4 * 128:k4 * 128 + TB], xa[:TB, tb, kc * 128:(kc + 1) * 128], ident[:TB, :TB]),
                        reads=["xa", "cst"], writes=[bkr])
                src = bk[:, :].rearrange("p (k t) -> p k t", k=4)[:, :, 0:TB]
                P.add("dve", lambda e, src=src, kq=kq, tb=tb: e.tensor_copy(
                    xT[:, kq * 4:kq * 4 + 4, tb * 128:tb * 128 + TB], src), reads=[bkr], writes=["xT"])
                if full:
                    P.add("act", lambda e, src=src, kq=kq, tb=tb: e.activation(
                        x_f[:, kq * 4:kq * 4 + 4, tb * 128:tb * 128 + TB], src, AF.Copy),
                        reads=[bkr], writes=["x_f"])

        if x_next is not None:
            xn, tbn, ntbn = x_next
            P.dma("sp", xa[:tbn, 0:ntbn, :], xn.rearrange("(tb p) d -> p tb d", p=tbn), reads=["xT"], writes=["xa"])
            xa_loaded[0] = True
        yield "front"
        pv = preqkv[:, :, 0:nseq * (L + H)].rearrange("p c (s l) -> p c s l", s=nseq)
        ppv = ppre[:, :, 0:nseq * (L + 2)].rearrange("p c (s l) -> p c s l", s=nseq)
        ahv = ahalo[:, :, 0:nseq, :]

        def load_hist(src_d, nrows, r, width, dst, dstres):
            for c0 in range(0, width, 512):
                wd = min(512, width - c0)
                ncc = wd // 128
                P.dma("sp", hso[:nrows, 0:wd], src_d[:, c0:c0 + wd], writes=["hso"])
                bk, bkr = bank()
                for k in range(ncc):
                    P.add("pe", lambda e, bk=bk, k=k: e.transpose(
                        bk[:, k * 12:k * 12 + nrows], hso[:nrows, k * 128:(k + 1) * 128],
                        ident[:nrows, :nrows]), reads=["hso", "cst"], writes=[bkr])
                srcv = bk[:, 0:ncc * 12].rearrange("p (c x) -> p c x", x=12)[:, :, 0:nrows].rearrange(
                    "p c (s r) -> p c s r", r=r)
                P.add("dve", lambda e, srcv=srcv, c0=c0, ncc=ncc: e.tensor_copy(
                    dst[:, c0 // 128:c0 // 128 + ncc, :, :], srcv), reads=[bkr], writes=[dstres])

        if hist_src is not None:
            (h_gc, h_sc, h_ff) = hist_src
            load_hist(h_gc, nseq * 3, 3, 3072, pv[:, :, :, 0:3], "preqkv")
            load_hist(h_sc, nseq * 2, 2, D, ppv[:, :, :, 0:2], "ppre")
            load_hist(h_ff, nseq * 2, 2, DFF, ahv, "ahalo")
        else:
            P.add("pool", lambda e: e.tensor_copy(pv[:, :, :, 0:3], pv[:, :, :, L:L + 3]),
                  reads=["preqkv"], writes=["preqkv"])
            if full:
                P.add("pool", lambda e: e.tensor_copy(ppv[:, :, :, 0:2], ppv[:, :, :, L:L + 2]),
                      reads=["ppre"], writes=["ppre"])

        qpend = []
        qstage2 = []

        def qkv_flush():
            grp = list(qpend)
            del qpend[:]
            accs = []
            for (ch, ps, psr) in grp:
                P.add("act", lambda e, ch=ch, ps=ps: e.activation(pv[:, ch, :, 3:3 + L], v3(ps), AF.Copy),
                      reads=[psr], writes=[f"preqkv/{ch}"])
                accs.append(rot("cacc", NCACC))
            for (ch, ps, psr), i in zip(grp, accs):
                a3 = v3(cacc[i][:, 0:N])
                P.add("pool", lambda e, ch=ch, a3=a3: e.tensor_tensor(
                    a3, pv[:, ch, :, 0:L], cwg[:, ch, 0:1].unsqueeze(1).to_broadcast([128, nseq, L]), ALU.mult),
                    reads=[f"preqkv/{ch}", "prm"], writes=[f"cacc{i}"])
            for j in range(1, 4):
                for (ch, ps, psr), i in zip(grp, accs):
                    a3 = v3(cacc[i][:, 0:N])
                    P.add("dve", lambda e, j=j, ch=ch, a3=a3: e.scalar_tensor_tensor(
                        a3, pv[:, ch, :, j:j + L], cwg[:, ch, j:j + 1], a3, ALU.mult, ALU.add),
                        reads=[f"preqkv/{ch}", "prm", f"cacc{i}"], writes=[f"cacc{i}"])
            prev = list(qstage2)
            del qstage2[:]
            for th in prev:
                th()
            for (ch, ps, psr), i in zip(grp, accs):
                qstage2.append(lambda ch=ch, i=i: P.add(
                    "act", lambda e: e.activation(post[:, ch, 0:N], cacc[i][:, 0:N], AF.Silu),
                    reads=[f"cacc{i}"], writes=[f"post{ch}"]))

        def qkv_finish():
            if qpend:
                qkv_flush()
            prev = list(qstage2)
            del qstage2[:]
            for th in prev:
                th()

        def qkv_consume(ch, ps, psr):
            qpend.append((ch, ps, psr))
            if len(qpend) == 2:
                qkv_flush()

        l2pendB = []

        def l2_finish(keep=0):
            while len(l2pendB) > keep:
                l2pendB.pop(0)()

        def l2norm2(chs):
            for ch in chs:
                i = rot("sqb", 2)
                k = rot("cacc", NCACC)
                rs_ = cacc[k]
                P.add("pool", lambda e, ch=ch, i=i: e.tensor_tensor(
                    sqb[i][:, 0:N], post[:, ch, 0:N], post[:, ch, 0:N], ALU.mult),
                    reads=[f"post{ch}"], writes=[f"sqb{i}"])
                bk, bkr = bank()
                P.add("pe", lambda e, bk=bk, i=i: e.matmul(bk[:, 0:N], ones_bf[:], sqb[i][:, 0:N], start=True, stop=True),
                      reads=[f"sqb{i}", "ones_bf"], writes=[bkr])
                P.add("act", lambda e, bk=bk, rs_=rs_: e.activation(rs_[:, 0:N], bk[:, 0:N], AF.Ln, bias=epsn[:, 0:1]),
                      reads=[bkr, "epsc"], writes=[f"cacc{k}"])
                l2_finish(keep=2)

                def stageB(ch=ch, k=k, rs_=rs_):
                    P.add("act", lambda e: e.activation(rs_[:, 0:N], rs_[:, 0:N], AF.Exp, scale=-0.5),
                          reads=[f"cacc{k}"], writes=[f"cacc{k}"])
                    P.add("pool", lambda e: e.tensor_tensor(
                        post[:, ch, 0:N], post[:, ch, 0:N], rs_[:, 0:N], ALU.mult),
                        reads=[f"post{ch}", f"cacc{k}"], writes=[f"post{ch}"])
                l2pendB.append(stageB)

        for g in range(8):
            i, wv_ = wload(None, 2048, "p (k c) -> p k c", k=8)
            P.dma("sp", wst[i][:, 0:2048], win_g[4 + g], reads=WIN_KV, writes=[f"wst{i}"])
            proj(xT, "xT", wv_, f"wst{i}", 2, N, qkv_consume, cc0=8 + g * 2)
            yield "front"
        if full:
            for g in range(4):
                i, wv_ = wload(None, 2048, "p (k c) -> p k c", k=8)
                P.dma("sp", wst[i][:, 0:2048], win_g[g], reads=WIN_RES, writes=[f"wst{i}"])
                proj(xT, "xT", wv_, f"wst{i}", 2, N, qkv_consume, cc0=g * 2)
        qkv_finish()
        l2todo = []
        if full:
            l2todo = [[ch, ch + 1] for ch in range(0, 16, 2)]
        else:
            for ch in range(8, 16, 2):
                l2norm2([ch, ch + 1])
                yield "front"
            l2_finish()

        def emit_gates():
            bk, bkr = bank()
            bav = bk[:C, 0:nch * 16].rearrange("p (c k) -> p c k", k=16)
            for c in range(nch):
                for kc in range(8):
                    P.add("pe", lambda e, c=c, kc=kc: e.matmul(
                        bav[:, c, :], xT[:, kc, c * C:(c + 1) * C], wba[:, kc, :], start=(kc == 0), stop=(kc == 7)),
                        reads=["xT", "wba"], writes=[bkr])
            P.add("act", lambda e: e.activation(ba_beta[:C, 0:nch, :], bav[:, :, 0:8], AF.Exp, scale=-1.0),
                  reads=[bkr], writes=["ba_beta"])
            P.add("dve", lambda e: e.tensor_scalar(ba_beta[:C, 0:nch, :], ba_beta[:C, 0:nch, :], 1.0, None, ALU.add),
                  reads=["ba_beta"], writes=["ba_beta"])
            P.add("dve", lambda e: e.reciprocal(ba_beta[:C, 0:nch, :], ba_beta[:C, 0:nch, :]),
                  reads=["ba_beta"], writes=["ba_beta"])
            P.add("dve", lambda e: e.tensor_tensor(
                ba_t[:C, 0:nch, :], bav[:, :, 8:16], dtb[:C, :].unsqueeze(1).to_broadcast([C, nch, 8]), ALU.add),
                reads=[bkr, "dtb"], writes=["ba_t"])
            P.add("act", lambda e: e.activation(ba_t[:C, 0:nch, :], ba_t[:C, 0:nch, :], AF.Exp),
                  reads=["ba_t"], writes=["ba_t"])
            P.add("act", lambda e: e.activation(ba_t[:C, 0:nch, :], ba_t[:C, 0:nch, :], AF.Ln, bias=epsn[:C, 2:3]),
                  reads=["ba_t", "epsc"], writes=["ba_t"])
            P.add("dve", lambda e: e.tensor_tensor(
                ba_g[:C, 0:nch, :], ba_t[:C, 0:nch, :], nA[:C, :].unsqueeze(1).to_broadcast([C, nch, 8]), ALU.mult),
                reads=["ba_t", "nA"], writes=["ba_g"])
            P.add("dve", lambda e: e.tensor_copy(ba_gh[:C, 0:nch, :], ba_g[:C, 0:nch, :]), reads=["ba_g"], writes=["ba_gh"])
            P.add("dve", lambda e: e.tensor_tensor(ba_gl[:C, 0:nch, :], ba_g[:C, 0:nch, :], ba_gh[:C, 0:nch, :],
                                                   ALU.subtract), reads=["ba_g", "ba_gh"], writes=["ba_gh"])


        if not full:
            emit_gates()

        if full:
            def mk_consume(kind):
                def f(ch, ps, psr):
                    if kind == "z":
                        P.add("act", lambda e: e.activation(szg[:, ch, 0:N], ps, AF.Silu),
                              reads=[psr], writes=[f"szg/{ch}"])
                    elif kind == "sB":
                        P.add("dve", lambda e: e.tensor_tensor(cfl[:, ch, 0:N], ps, cfl[:, ch, 0:N], ALU.mult),
                              reads=[psr, f"r1/{ch}"], writes=[f"r1/{ch}"])
                    elif kind == "sC":
                        P.add("act", lambda e: e.activation(sCs[:, ch, 0:N], ps, AF.Copy),
                              reads=[psr], writes=[f"mixg/{ch}"])
                    elif kind == "sH":
                        P.add("dve", lambda e: e.tensor_tensor(
                            ppv[:, ch, :, 2:2 + L], v3(ps), v3(sCs[:, ch, 0:N]), ALU.mult),
                            reads=[psr, f"mixg/{ch}"], writes=[f"ppre/{ch}"])
                        c3 = v3(cfl[:, ch, 0:N])
                        P.add("pool", lambda e: e.tensor_tensor(
                            c3, ppv[:, ch, :, 0:L], cws[:, ch, 0:1].unsqueeze(1).to_broadcast([128, nseq, L]), ALU.mult),
                              reads=[f"ppre/{ch}", "prm"], writes=[f"r1/{ch}"])
                        for j in range(1, 3):
                            P.add("dve", lambda e, j=j: e.scalar_tensor_tensor(
                                c3, ppv[:, ch, :, j:j + L], cws[:, ch, j:j + 1], c3, ALU.mult, ALU.add),
                                reads=[f"ppre/{ch}", "prm", f"r1/{ch}"], writes=[f"r1/{ch}"])
                    elif kind == "gA":
                        i = rot("cacc", NCACC)
                        P.add("act", lambda e: e.activation(cacc[i][:, 0:N], ps, AF.Sigmoid),
                              reads=[psr], writes=[f"cacc{i}"])
                        P.add("pool", lambda e: e.tensor_tensor(
                            szg[:, ch, 0:N], szg[:, ch, 0:N], cacc[i][:, 0:N], ALU.mult),
                            reads=[f"szg/{ch}", f"cacc{i}"], writes=[f"szg/{ch}"])
                    elif kind == "gB":
                        i = rot("cacc", NCACC)
                        P.add("act", lambda e: e.activation(cacc[i][:, 0:N], ps, AF.Sigmoid),
                              reads=[psr], writes=[f"cacc{i}"])
                        P.add("pool", lambda e: e.tensor_tensor(
                            cfl[:, ch, 0:N], cfl[:, ch, 0:N], cacc[i][:, 0:N], ALU.mult),
                            reads=[f"r1/{ch}", f"cacc{i}"], writes=[f"r1/{ch}"])
                return f

            for kind, base in [("z", 3072), ("sC", 5120), ("sH", 6144), ("sB", 4096), ("gA", 7168), ("gB", 8192)]:
                for g in range(4):
                    i, wv_ = wload(None, 2048, "p (k c) -> p k c", k=8)
                    P.dma("sp", wst[i][:, 0:2048], win_g[base // 256 + g], reads=WIN_RES, writes=[f"wst{i}"])
                    proj(xT, "xT", wv_, f"wst{i}", 2, N, mk_consume(kind), cc0=g * 2)
                    if l2todo and kind in ("sC", "sH"):
                        l2norm2(l2todo.pop(0))
                    if kind == "sH" and g == 3:
                        while l2todo:
                            l2norm2(l2todo.pop(0))
                        l2_finish()
                        emit_gates()
            while l2todo:
                l2norm2(l2todo.pop(0))
            l2_finish()

        yield "front_done"
        nst = {64: 5, 16: 3}[C]
        def chunk(c):
            b = c % NB
            cs = slice(c * C, (c + 1) * C)
            Sc, Scb, Scr = S, Sb, Sres
            g_c = ba_g[:C, c, :]
            be_c = ba_beta[:C, c, :]
            bk, bkr = bank()
            gh_c = ba_gh[:C, c, :]
            gl_c = ba_gl[:C, c, :]
            for (oap, lt) in ((bk[:C, 0:8], tri_bf[:C, :C]), (bk[:C, 8:16], sgt_bf[:C, :C]), (bk[:, 16:24], ones_bf[:C, :])):
                P.add("pe", lambda e, oap=oap, lt=lt: e.matmul(oap, lt, gh_c, start=True, stop=False),
                      reads=["ones_bf", "ba_gh"], writes=[bkr])
                P.add("pe", lambda e, oap=oap, lt=lt: e.matmul(oap, lt, gl_c, start=False, stop=True),
                      reads=["ones_bf", "ba_gh"], writes=[bkr])
            yield
            P.add("act", lambda e, bk=bk: e.activation(eGt[b][:C, :], bk[:C, 0:16], AF.Exp),
                  reads=[bkr], writes=[f"eGt{b}"])
            P.add("act", lambda e, bk=bk: e.activation(eGl[b][:, :], bk[:, 16:24], AF.Exp),
                  reads=[bkr], writes=[f"eGl{b}"])
            P.add("pool", lambda e: e.tensor_tensor(bg[b][:C, :], be_c, eGt[b][:C, 0:8], ALU.mult),
                  reads=["ba_beta", f"eGt{b}"], writes=[f"bg{b}"])
            P.add("dve", lambda e: e.tensor_tensor(
                gtri[b][:C, 0, :, :C], tri[:C, :C].unsqueeze(1).to_broadcast([C, 8, C]),
                gh_c.unsqueeze(2).to_broadcast([C, 8, C]), ALU.mult),
                reads=["cst", "ba_gh"], writes=[f"gtri{b}"])
            P.add("dve", lambda e: e.tensor_tensor(
                gtri[b][:C, 1, :, :C], tri[:C, :C].unsqueeze(1).to_broadcast([C, 8, C]),
                gl_c.unsqueeze(2).to_broadcast([C, 8, C]), ALU.mult),
                reads=["cst", "ba_gh"], writes=[f"gtri{b}"])
            yield
            bkD, bkDr = bank()
            Dv = bkD[:C, 0:8 * C].rearrange("p (h c) -> p h c", h=8)
            for h in range(8):
                for hl in range(2):
                    P.add("pe", lambda e, h=h, Dv=Dv, hl=hl: e.matmul(Dv[:, h, :], gtri[b][:C, hl, h, :C], sgt_bf[:C, :C],
                                                                     start=(hl == 0), stop=(hl == 1)),
                          reads=[f"gtri{b}", "ones_bf"], writes=[bkDr])
            yield
            P.add("act", lambda e, Dv=Dv: e.activation(Gam[b][:C, :, :C], Dv, AF.Exp),
                  reads=[bkDr], writes=[f"Gam{b}"])
            P.add("pool", lambda e: e.tensor_tensor(
                Gam[b][:C, :, :C], Gam[b][:C, :, :C], sgt[:C, :C].unsqueeze(1).to_broadcast([C, 8, C]), ALU.mult),
                reads=[f"Gam{b}", "cst"], writes=[f"Gam{b}"])
            P.add("pool", lambda e: e.tensor_tensor(
                Gam[b][:C, :, :C], Gam[b][:C, :, :C], be_c.unsqueeze(2).to_broadcast([C, 8, C]), ALU.mult),
                reads=[f"Gam{b}", "ba_beta"], writes=[f"Gam{b}"])
            if full:
                bkT, bkTr = bank()
                DTv = bkT[:C, 0:8 * C].rearrange("p (h c) -> p h c", h=8)
                for h in range(8):
                    for hl in range(2):
                        P.add("pe", lambda e, h=h, DTv=DTv, hl=hl: e.matmul(
                            DTv[:, h, :], sgt_bf[:C, :C], gtri[b][:C, hl, h, :C], start=(hl == 0), stop=(hl == 1)),
                            reads=[f"gtri{b}", "ones_bf"], writes=[bkTr])
                P.add("act", lambda e, DTv=DTv: e.activation(GamT[b][:C, :, :C], DTv, AF.Exp),
                      reads=[bkTr], writes=[f"GamT{b}"])
                P.add("pool", lambda e: e.tensor_tensor(
                    GamT[b][:C, :, :C], GamT[b][:C, :, :C], tri[:C, :C].unsqueeze(1).to_broadcast([C, 8, C]),
                    ALU.mult), reads=[f"GamT{b}", "cst"], writes=[f"GamT{b}"])
            if STOP2 == "c_a":
                return
            yield
            bkk, bkkr = bank()
            kt = bkk[:, :].bitcast(BF16)[:C, :].rearrange("p (h d) -> p h d", h=8)
            for h in range(8):
                P.add("pe", lambda e, h=h, kt=kt: e.transpose(kt[:, h, :], post[:, 8 + h, cs], ident_bf[:]),
                      reads=[f"post{8 + h}", "ident_bf"], writes=[bkkr])
            yield
            P.add("dve", lambda e, kt=kt: e.tensor_tensor(
                kbg[b][:C], kt, bg[b][:C, :].unsqueeze(2).to_broadcast([C, 8, 128]), ALU.mult),
                reads=[bkkr, f"bg{b}"], writes=[f"kbg{b}"])
            P.add("dve", lambda e, kt=kt: e.tensor_tensor(
                ktail[b][:C], kt, eGt[b][:C, 8:16].unsqueeze(2).to_broadcast([C, 8, 128]), ALU.mult),
                reads=[bkkr, f"eGt{b}"], writes=[f"ktail{b}"])
            yield
            bkv, bkvr = bank()
            vt = bkv[:, :].bitcast(BF16)[:C, :].rearrange("p (h d) -> p h d", h=8)
            for h in range(8):
                P.add("pe", lambda e, h=h, vt=vt: e.transpose(vt[:, h, :], post[:, 16 + h, cs], ident_bf[:]),
                      reads=[f"post{16 + h}", "ident_bf"], writes=[bkvr])
            yield
            P.add("dve", lambda e, vt=vt: e.tensor_tensor(
                vb[b][:C], vt, be_c.unsqueeze(2).to_broadcast([C, 8, 128]), ALU.mult),
                reads=[bkvr, "ba_beta"], writes=[f"vb{b}"])
            if STOP2 == "c_f":
                return
            yield
            bka, bkar = bank()
            kkv = bka[:C, 0:8 * C].rearrange("p (h c) -> p h c", h=8)
            for h in range(8):
                P.add("pe", lambda e, h=h, kkv=kkv: e.matmul(kkv[:, h, :], post[:, 8 + h, cs], post[:, 8 + h, cs],
                                                            start=True, stop=True),
                      reads=[f"post{8 + h}"], writes=[bkar])
            yield
            P.add("dve", lambda e, kkv=kkv: e.tensor_tensor(Am[b][:C, :, :C], kkv, Gam[b][:C, :, :C], ALU.mult),
                  reads=[bkar, f"Gam{b}"], writes=[f"Am{b}"])
            if STOP2 == "c_g1":
                return
            yield
            bkb, bkbr = bank()
            atv = bkb[:C, 0:8 * C].rearrange("p (h c) -> p h c", h=8)
            for h in range(8):
                P.add("pe", lambda e, h=h, atv=atv: e.matmul(atv[:, h, :], Am[b][:C, h, :C], ident_bf[:C, :C],
                                                            start=True, stop=True),
                      reads=[f"Am{b}", "ident_bf"], writes=[bkbr])
            yield
            X0, Y0, R0 = Xm[b][0], Am[b], Rm[b][0]
            P.add("act", lambda e, atv=atv: e.activation(X0[:C, :, :C], atv, AF.Copy),
                  reads=[bkbr], writes=[f"Xm{b}_0"])
            P.add("dve", lambda e, atv=atv: e.scalar_tensor_tensor(
                R0[:C, :, :C], atv, epsn[:C, 3:4], ident[:C, :C].unsqueeze(1).to_broadcast([C, 8, C]), ALU.mult, ALU.add),
                reads=[bkbr, "cst", "epsc"], writes=[f"Rm{b}_0"])
            Xc, Xr, Yc, Yr, Rc, Rr = X0, f"Xm{b}_0", Y0, f"Am{b}", R0, f"Rm{b}_0"
            yield "presolve_done"
            for n in range(1, nst + 1):
                yield
                k = n % 2
                Yn, Ynr = Ym[b][k], f"Ym{b}_{k}"
                bky, bkyr = bank()
                yv = bky[:C, 0:8 * C].rearrange("p (h c) -> p h c", h=8)
                for h in range(8):
                    P.add("pe", lambda e, h=h, yv=yv, Xc=Xc, Yc=Yc: e.matmul(
                        yv[:, h, :], Xc[:C, h, :C], Yc[:C, h, :C], start=True, stop=True),
                        reads=[Xr, Yr], writes=[bkyr])
                yield
                P.add("act", lambda e, yv=yv, Yn=Yn: e.activation(Yn[:C, :, :C], yv, AF.Copy),
                      reads=[bkyr], writes=[Ynr])
                if n < nst:
                    Xn, Xnr = Xm[b][k], f"Xm{b}_{k}"
                    bkx, bkxr = bank()
                    xv = bkx[:C, 0:8 * C].rearrange("p (h c) -> p h c", h=8)
                    for h in range(8):
                        P.add("pe", lambda e, h=h, xv=xv, Xc=Xc, Yc=Yc: e.matmul(
                            xv[:, h, :], Yc[:C, h, :C], Xc[:C, h, :C], start=True, stop=True),
                            reads=[Xr, Yr], writes=[bkxr])
                    yield
                    P.add("dve", lambda e, xv=xv, Xn=Xn: e.tensor_copy(Xn[:C, :, :C], xv),
                          reads=[bkxr], writes=[Xnr])
                yield
                Rn, Rnr = Rm[b][0], f"Rm{b}_0"
                bkq, bkqr = bank()
                rv = bkq[:C, 0:8 * C].rearrange("p (h c) -> p h c", h=8)
                for h in range(8):
                    P.add("pe", lambda e, h=h, rv=rv, Rc=Rc, Yn=Yn: e.matmul(
                        rv[:, h, :], Yn[:C, h, :C], Rc[:C, h, :C], start=True, stop=True),
                        reads=[Ynr, Rr], writes=[bkqr])
                yield
                P.add("dve", lambda e, rv=rv, Rn=Rn, Rc=Rc: e.tensor_tensor(Rn[:C, :, :C], rv, Rc[:C, :, :C], ALU.add),
                      reads=[bkqr, Rr], writes=[Rnr])
                if n < nst:
                    Xc, Xr = Xn, Xnr
                Yc, Yr, Rc, Rr = Yn, Ynr, Rn, Rnr
            TT, TTr = Rc, Rr
            if STOP2 == "c_h":
                return
            yield
            bkw, bkwr = bank()
            wv2 = bkw[:, 0:8 * C].rearrange("p (h c) -> p h c", h=8)
            for h in range(8):
                P.add("pe", lambda e, h=h, wv2=wv2, TT=TT: e.matmul(
                    wv2[:, h, :], kbg[b][:C, h, :], TT[:C, h, :C], start=True, stop=True),
                    reads=[f"kbg{b}", TTr], writes=[bkwr])
            yield
            P.add("act", lambda e, wv2=wv2: e.activation(nwT[b][:, :, :C], wv2, AF.Copy, scale=-1.0),
                  reads=[bkwr], writes=[f"nwT{b}"])
            yield "chain"
            if S_src is not None:
                P.dma("sp", Sc[:], S_src[c].rearrange("h k v -> k h v"), writes=[Scr])
                P.add("act", lambda e, Sc=Sc, Scb=Scb: e.activation(Scb[:], Sc[:], AF.Copy),
                      reads=[Scr], writes=[Scr + "b"])
            vbanks = []
            for hh in range(2):
                bkn, bknr = bank()
                vn = bkn[:C, :].rearrange("p (h d) -> p h d", h=4)
                for h4 in range(4):
                    h = hh * 4 + h4
                    P.add("pe", lambda e, h=h, h4=h4, vn=vn, TT=TT: e.matmul(
                        vn[:, h4, :], TT[:C, h, :C], vb[b][:C, h, :], start=True, stop=False),
                        reads=[TTr, f"vb{b}"], writes=[bknr])
                    P.add("pe", lambda e, h=h, h4=h4, vn=vn, Scb=Scb: e.matmul(
                        vn[:, h4, :], nwT[b][:, h, :C], Scb[:, h, :], start=False, stop=True),
                        reads=[f"nwT{b}", f"{Scr}b/{h // 4}"], writes=[bknr])
                eng = "act" if hh == 0 else "dve"
                if eng == "act":
                    P.add("act", lambda e, vn=vn, hh=hh: e.activation(vnew[b][:C, hh * 4:hh * 4 + 4, :], vn, AF.Copy),
                          reads=[bknr], writes=[f"vnew0/{hh}"])
                else:
                    P.add("dve", lambda e, vn=vn, hh=hh: e.tensor_copy(vnew[b][:C, hh * 4:hh * 4 + 4, :], vn),
                          reads=[bknr], writes=[f"vnew0/{hh}"])
            o1b = []
            if full:
                for hh in range(2):
                    bko, bkor = bank()
                    ov = bko[:C, :].rearrange("p (h d) -> p h d", h=4)
                    for h4 in range(4):
                        h = hh * 4 + h4
                        P.add("pe", lambda e, h=h, h4=h4, ov=ov, Scb=Scb: e.matmul(
                            ov[:, h4, :], post[:, h, cs], Scb[:, h, :], start=True, stop=True),
                            reads=[f"post{h}", f"{Scr}b/{h // 4}"], writes=[bkor])
                    o1b.append((ov, bkor))
            for hh in range(2):
                bkd, bkdr = bank()
                dv = bkd[:, :].rearrange("p (h d) -> p h d", h=4)
                for h4 in range(4):
                    h = hh * 4 + h4
                    P.add("pe", lambda e, h=h, h4=h4, dv=dv: e.matmul(
                        dv[:, h4, :], ktail[b][:C, h, :], vnew[b][:C, h, :], start=True, stop=True),
                        reads=[f"ktail{b}", f"vnew0/{h // 4}"], writes=[bkdr])
                for h4 in range(4):
                    h = hh * 4 + h4
                    P.add("dve", lambda e, h=h, h4=h4, dv=dv, Sc=Sc: e.scalar_tensor_tensor(
                        Sc[:, h, :], Sc[:, h, :], eGl[b][:, h:h + 1], dv[:, h4, :], ALU.mult, ALU.add),
                        reads=[bkdr, f"{Scr}/{h}", f"eGl{b}"], writes=[f"{Scr}/{h}"])
                P.add("act", lambda e, Sc=Sc, Scb=Scb, hh=hh: e.activation(
                    Scb[:, hh * 4:hh * 4 + 4, :], Sc[:, hh * 4:hh * 4 + 4, :], AF.Copy),
                    reads=[f"{Scr}/{h_}" for h_ in range(hh * 4, hh * 4 + 4)], writes=[f"{Scr}b/{hh}"])
            if nseq > 1 and state_dst is not None:
                P.dma("sp", state_dst[1][c].rearrange("h k v -> k h v"), Sc[:], reads=[Scr], writes=["dram_out"])
            if full:
                for hh, (ov, bkor) in enumerate(o1b):
                    P.add("dve", lambda e, ov=ov, hh=hh: e.tensor_tensor(
                        o1s[b][:C, hh * 4:hh * 4 + 4, :], ov,
                        eGt[b][:C, hh * 4:hh * 4 + 4].unsqueeze(2).to_broadcast([C, 4, 128]), ALU.mult),
                        reads=[bkor, f"eGt{b}"], writes=[f"o1s0/{hh}"])
                bkq2, bkq2r = bank()
                qv = bkq2[:C, 0:8 * C].rearrange("p (h c) -> p h c", h=8)
                for h in range(8):
                    P.add("pe", lambda e, h=h, qv=qv: e.matmul(qv[:, h, :], post[:, 8 + h, cs], post[:, h, cs],
                                                              start=True, stop=True),
                          reads=[f"post{8 + h}", f"post{h}"], writes=[bkq2r])
                P.add("dve", lambda e, qv=qv: e.tensor_tensor(qkT[b][:C, :, :C], qv, GamT[b][:C, :, :C], ALU.mult),
                      reads=[bkq2r, f"GamT{b}"], writes=["qkT0"])
                for hh in range(2):
                    bko, bkor = bank()
                    ov = bko[:C, :].rearrange("p (h d) -> p h d", h=4)
                    for h4 in range(4):
                        h = hh * 4 + h4
                        P.add("pe", lambda e, h=h, h4=h4, ov=ov: e.matmul(
                            ov[:, h4, :], qkT[b][:C, h, :C], vnew[b][:C, h, :], start=True, stop=True),
                            reads=["qkT0", f"vnew0/{h // 4}"], writes=[bkor])
                    P.add("dve", lambda e, ov=ov, hh=hh: e.tensor_tensor(
                        o1s[b][:C, hh * 4:hh * 4 + 4, :], ov, o1s[b][:C, hh * 4:hh * 4 + 4, :], ALU.add),
                        reads=[bkor, f"o1s0/{hh}"], writes=[f"o1s0/{hh}"])
            if full:
                P.add("act", lambda e: e.activation(osq[b][:C], o1s[b][:C], AF.Square),
                      reads=["o1s0"], writes=["osq0"])
                P.add("dve", lambda e: e.reduce_sum(oss[b][:C, :], osq[b][:C], axis=AX.X),
                      reads=["osq0"], writes=[f"oss{b}"])
                P.add("dve", lambda e: e.tensor_scalar(oss[b][:C, :], oss[b][:C, :], 1.0 / 128, NORM_EPS * 128,
                                                       ALU.mult, ALU.add),
                      reads=[f"oss{b}"], writes=[f"oss{b}"])
                P.add("act", lambda e: e.activation(oss[b][:C, :], oss[b][:C, :], AF.Ln),
                      reads=[f"oss{b}"], writes=[f"oss{b}"])
                P.add("act", lambda e: e.activation(oss[b][:C, :], oss[b][:C, :], AF.Exp, scale=-0.5),
                      reads=[f"oss{b}"], writes=[f"oss{b}"])
                P.add("dve", lambda e: e.tensor_tensor(
                    onb[b][:C], o1s[b][:C], oss[b][:C, :].unsqueeze(2).to_broadcast([C, 8, 128]), ALU.mult),
                    reads=["o1s0", f"oss{b}"], writes=[f"onb{b % 2}"])
                yield "tail"
                bkt, bktr = bank()
                tv = bkt[:, 0:8 * C].rearrange("p (h c) -> p h c", h=8)
                for h in range(8):
                    P.add("pe", lambda e, h=h, tv=tv: e.matmul(tv[:, h, :], onb[b][:C, h, :], ident_bf[:C, :C],
                                                              start=True, stop=True),
                          reads=[f"onb{b % 2}", "ident_bf"], writes=[bktr])
                P.add("dve", lambda e, tv=tv: e.scalar_tensor_tensor(
                    mixg[:, :, cs], tv, ong, szg[:, :, cs], ALU.mult, ALU.mult),
                    reads=[bktr, "prm", "szg"], writes=["mixg"])

        gens = [chunk(c) for c in range(nch)]
        live = list(gens)
        while live:
            for g_ in list(live):
                if next(g_) == "presolve_done":
                    live.remove(g_)
        yield "presolve_done"
        live = list(gens)
        while live:
            for g_ in list(live):
                if next(g_) == "chain":
                    live.remove(g_)
            yield "solve"
        yield "solve_done"
        prev_tail = None
        for g_ in gens:
            alive = next(g_, None) is not None
            yield "chain"
            if prev_tail is not None:
                for _ in prev_tail:
                    pass
            prev_tail = g_ if alive else None
        if prev_tail is not None:
            for _ in prev_tail:
                pass
        yield "chain_done"

        if not full:
            return

        P.add("dve", lambda e: e.tensor_tensor(mixed[:, :, 0:N], mixg[:, :, 0:N], cfl[:, :, 0:N], ALU.add),
              reads=["mixg"] + ["r1"], writes=["mixed"])

        def layer_norm(rbuf, rres, gcol, bcol, outf, outfres):
            for eng_, lo in (("pool", 0), ("dve", 4)):
                P.add(eng_, lambda e, lo=lo: e.tensor_tensor(
                    mixed[:, lo:lo + 4, 0:N], rbuf[:, lo:lo + 4, 0:N], rbuf[:, lo:lo + 4, 0:N], ALU.mult),
                    reads=[f"{rres}/{oc}" for oc in range(lo, lo + 4)], writes=[f"mixed/{oc}" for oc in range(lo, lo + 4)])
            P.add("act", lambda e: e.activation(mixg[:, :, 0:N], rbuf[:, :, 0:N], AF.Copy),
                  reads=[rres], writes=["mixg"])
            bk, bkr = bank()
            for oc in range(8):
                P.add("pe", lambda e, oc=oc, bk=bk: e.matmul(bk[:, 0:N], onesm_bf[:], mixg[:, oc, 0:N],
                                                            start=(oc == 0), stop=(oc == 7)),
                      reads=["mixg", "onesm_bf"], writes=[bkr])
            for oc in range(8):
                P.add("pe", lambda e, oc=oc, bk=bk: e.matmul(bk[:, 256:256 + N], onesm_bf[:], mixed[:, oc, 0:N],
                                                            start=(oc == 0), stop=(oc == 7)),
                      reads=[f"mixed/{oc}", "onesm_bf"], writes=[bkr])
            P.add("act", lambda e, bk=bk: e.activation(mean_sb[:, 0:N], bk[:, 0:N], AF.Copy),
                  reads=[bkr], writes=["mean_sb"])
            P.add("pool", lambda e: e.tensor_tensor(m2[:, 0:N], mean_sb[:, 0:N], mean_sb[:, 0:N], ALU.mult),
                  reads=["mean_sb"], writes=["m2"])
            P.add("dve", lambda e, bk=bk: e.tensor_tensor(rstd[:, 0:N], bk[:, 256:256 + N], m2[:, 0:N], ALU.subtract),
                  reads=[bkr, "m2"], writes=["rstd"])
            P.add("act", lambda e: e.activation(rstd[:, 0:N], rstd[:, 0:N], AF.Ln, bias=epsn[:, 1:2]),
                  reads=["rstd", "epsc"], writes=["rstd"])
            P.add("act", lambda e: e.activation(rstd[:, 0:N], rstd[:, 0:N], AF.Exp, scale=-0.5),
                  reads=["rstd"], writes=["rstd"])
            for stat_, statr, op_ in ((mean_sb, "mean_sb", ALU.subtract), (rstd, "rstd", ALU.mult)):
                for eng_, lo in (("pool", 0), ("dve", 4)):
                    names = [f"{rres}/{oc}" for oc in range(lo, lo + 4)]
                    P.add(eng_, lambda e, lo=lo, stat_=stat_, op_=op_: e.tensor_tensor(
                        rbuf[:, lo:lo + 4, 0:N], rbuf[:, lo:lo + 4, 0:N],
                        stat_[:, 0:N].unsqueeze(1).to_broadcast([128, 4, N]), op_),
                        reads=names + [statr], writes=names)
            for oc in range(8):
                P.add("dve", lambda e, oc=oc: e.tensor_scalar(
                    outf[:, oc, 0:N], rbuf[:, oc, 0:N], lnp[:, oc, gcol:gcol + 1], lnp[:, oc, bcol:bcol + 1],
                    ALU.mult, ALU.add), reads=[f"{rres}/{oc}", "prm"], writes=[f"{outfres}/{oc}"])

        def wo_consume(oc, ps, psr):
            P.add("dve", lambda e: e.scalar_tensor_tensor(r1[:, oc, 0:N], x_f[:, oc, 0:N], epsn[:, 4:5], ps, ALU.mult, ALU.add),
                  reads=[psr, "x_f"], writes=[f"r1/{oc}"])

        for g in range(4):
            i, wv_ = wload(None, 2048, "p (k c) -> p k c", k=8)
            P.dma("sp", wst[i][:, 0:2048], wo_g[g], reads=WO_RES, writes=[f"wst{i}"])
            proj(mixed, "mixed", wv_, f"wst{i}", 2, N, wo_consume, cc0=g * 2)
        layer_norm(r1, "r1", 0, 1, x1f, "r1")
        P.add("act", lambda e: e.activation(x1b[:, :, 0:N], x1f[:, :, 0:N], AF.Copy), reads=["r1"], writes=["mixed"])

        ffn_pend = []
        for c in range(NFC):
            i, wv_ = wload(None, 2048, "p (k c) -> p k c", k=8)
            P.dma("sp", wst[i][:, 0:2048], wup_g[c], reads=WUP_RES, writes=[f"wst{i}"])
            pbanks = []
            for half in range(2):
                bk, bkr = bank()
                for kc in range(8):
                    P.add("pe", lambda e, bk=bk, kc=kc, wv_=wv_, half=half: e.matmul(
                        bk[:, 0:N], wv_[:, kc, half * 128:(half + 1) * 128], x1b[:, kc, 0:N],
                        start=(kc == 0), stop=(kc == 7)), reads=["mixed", f"wst{i}"], writes=[bkr])
                pbanks.append((bk, bkr))
            (abk, abkr), (vbk, vbkr) = pbanks
            ai = rot("apre", 4)
            k = rot("cacc", NCACC)
            apv = apre[ai][:, 0:nseq * (L + 2)].rearrange("p (s l) -> p s l", s=nseq)
            a3 = v3(cacc[k][:, 0:N])
            P.add("pool", lambda e, c=c, apv=apv: e.tensor_copy(apv[:, :, 0:2], ahv[:, c, :, :]),
                  reads=[f"ahalo/{c}"], writes=[f"apre{ai}"])
            P.add("act", lambda e, apv=apv, abk=abk: e.activation(apv[:, :, 2:2 + L], v3(abk[:, 0:N]), AF.Copy),
                  reads=[abkr], writes=[f"apre{ai}"])
            P.add("pool", lambda e, c=c, apv=apv: e.tensor_copy(ahv[:, c, :, :], apv[:, :, L:L + 2]),
                  reads=[f"apre{ai}"], writes=[f"ahalo/{c}"])
            P.add("pool", lambda e, c=c, apv=apv, a3=a3: e.tensor_tensor(
                a3, apv[:, :, 0:L], cwf[:, c, 0:1].unsqueeze(1).to_broadcast([128, nseq, L]), ALU.mult),
                reads=[f"apre{ai}", "prm"], writes=[f"cacc{k}"])
            for j in range(1, 3):
                P.add("dve", lambda e, j=j, c=c, apv=apv, a3=a3: e.scalar_tensor_tensor(
                    a3, apv[:, :, j:j + L], cwf[:, c, j:j + 1], a3, ALU.mult, ALU.add),
                    reads=[f"apre{ai}", "prm", f"cacc{k}"], writes=[f"cacc{k}"])
            prev = list(ffn_pend)
            del ffn_pend[:]
            for th in prev:
                th()

            def stage2(c=c, ai=ai, k=k, vbk=vbk, vbkr=vbkr):
                P.add("act", lambda e: e.activation(gab[ai][:, 0:N], cacc[k][:, 0:N], AF.Gelu),
                      reads=[f"cacc{k}"], writes=[f"gab{ai}"])
                P.add("dve", lambda e: e.tensor_tensor(hb[:, c, 0:N], vbk[:, 0:N], gab[ai][:, 0:N], ALU.mult),
                      reads=[vbkr, f"gab{ai}"], writes=[f"hb/{c}"])
            ffn_pend.append(stage2)
        for th in ffn_pend:
            th()

        for cg in range(4):
            bk0, bkr0 = bank()
            bk1, bkr1 = bank()
            bks, bkrs = (bk0, bk1), (bkr0, bkr1)
            for kg in range(3):
                nk = 8 if kg < 2 else NFC - 16
                i, wv_ = wload(None, nk * 256, "p (k c) -> p k c", k=nk)
                P.dma("sp", wst[i][:, 0:nk * 256], wdn_g[cg * 3 + kg][:, 0:nk * 256], reads=WDN_RES, writes=[f"wst{i}"])
                for o2 in range(2):
                    for k8 in range(nk):
                        kc = kg * 8 + k8
                        P.add("pe", lambda e, bk=bks[o2], kc=kc, k8=k8, o2=o2, wv_=wv_: e.matmul(
                            bk[:, 0:N], wv_[:, k8, o2 * 128:(o2 + 1) * 128], hb[:, kc, 0:N],
                            start=(kc == 0), stop=(kc == NFC - 1)),
                            reads=[f"hb/{kc}", f"wst{i}"], writes=[bkrs[o2]])
            for o2 in range(2):
                oc = cg * 2 + o2
                P.add("dve", lambda e, bk=bks[o2], oc=oc, o2=o2: e.scalar_tensor_tensor(
                    r1[:, oc, 0:N], r1[:, oc, 0:N], epsn[:, 4:5], bk[:, 0:N], ALU.mult, ALU.add),
                    reads=[bkrs[o2], f"r1/{oc}"], writes=[f"r1/{oc}"])
        layer_norm(r1, "r1", 2, 3, x_f, "x_f")

        if write_y:
            for tb in range(ntb):
                for kq in range(2):
                    bk, bkr = bank()
                    for k4 in range(4):
                        oc = kq * 4 + k4
                        P.add("pe", lambda e, bk=bk, k4=k4, oc=oc, tb=tb: e.transpose(
                            bk[:TB, k4 * 128:(k4 + 1) * 128], x_f[:, oc, tb * 128:tb * 128 + TB], ident[:, :]),
                            reads=["x_f", "cst"], writes=[bkr])
                    P.add("act" if kq == 0 else "dve",
                          (lambda e, bk=bk, kq=kq, tb=tb: e.activation(yout[:TB, tb, kq * 512:(kq + 1) * 512], bk[:TB, :], AF.Copy))
                          if kq == 0 else
                          (lambda e, bk=bk, kq=kq, tb=tb: e.tensor_copy(yout[:TB, tb, kq * 512:(kq + 1) * 512], bk[:TB, :])),
                          reads=[bkr], writes=["hb"])
            P.dma("sp", y_dst.rearrange("(tb p) d -> p tb d", p=TB), yout[:TB, 0:ntb, :], reads=["hb"],
                  writes=["dram_out"])

        if state_dst is not None:
            d_gc, d_S, d_sc, d_ff = state_dst

            def store_state(dst_d, nrows, r, width, srcv, srcres):
                nchk = width // 128
                hv = hs[:, 0:nchk, 0:nrows].rearrange("p c (s r) -> p c s r", r=r)
                P.add("pool", lambda e: e.tensor_copy(hv, srcv), reads=[srcres], writes=["hs"])
                for c0 in range(0, width, 512):
                    wd = min(512, width - c0)
                    for c1 in range(0, wd, 512):
                        w2 = min(512, wd - c1)
                        bk, bkr = bank()
                        for k4 in range(w2 // 128):
                            ch = (c0 + c1) // 128 + k4
                            P.add("pe", lambda e, bk=bk, k4=k4, ch=ch: e.transpose(
                                bk[:nrows, k4 * 128:(k4 + 1) * 128], hs[:, ch, 0:nrows], ident[:, :]),
                                reads=["hs", "cst"], writes=[bkr])
                        P.add("dve", lambda e, bk=bk, c1=c1, w2=w2: e.tensor_copy(
                            hso[:nrows, c1:c1 + w2], bk[:nrows, 0:w2]), reads=[bkr], writes=["hso"])
                    P.dma("sp", dst_d[:, c0:c0 + wd], hso[:nrows, 0:wd], reads=["hso"], writes=["dram_out"])

            store_state(d_gc, nseq * 3, 3, 3072, pv[:, :, :, L:L + 3], "preqkv")
            store_state(d_sc, nseq * 2, 2, D, ppv[:, :, :, L:L + 2], "ppre")
            store_state(d_ff, nseq * 2, 2, DFF, ahv, "ahalo")
            if nseq == 1:
                P.dma("sp", d_S.rearrange("h k v -> k h v"), S[:], reads=[Sres], writes=["dram_out"])

    specs = []
    if STOP != "setup":
        for t in range(NPRE):
            specs.append(dict(full=False, x=xp[t * NT:(t + 1) * NT, :], tb=128, ntb=2, kw=dict(
                nseq=1, L=NT, C=64, full=False, S=Sp, Sb=Spb, Sres="Sp", y_dst=None)))
    if STOP not in ("setup", "pre"):
        for t in range(NFULL):
            r0 = (NPRE + t) * NT
            last = (t == NFULL - 1)
            specs.append(dict(full=True, x=xp[r0:r0 + NT, :], tb=128, ntb=2, kw=dict(
                nseq=1, L=NT, C=64, full=True, S=Sp, Sb=Spb, Sres="Sp",
                y_dst=yp[(t - 1) * NT:t * NT, :] if t > 0 else None,
                state_dst=(p_gc, p_S, p_sc, p_ff) if last else None, write_y=(t > 0))))
    if nsamp and STOP is None:
        specs.append(dict(full=True, x=xs, tb=nsamp * 16, ntb=1, kw=dict(
            nseq=nsamp, L=16, C=16, full=True, S=Ss, Sb=Ssb, Sres="Sp", y_dst=ys, hist_src=(st_gc, st_sc, st_ff),
            state_dst=(s_gc, s_S, s_sc, s_ff), write_y=True, S_src=st_S)))
    pend = None
    PEND_PER_FRONT = 1
    FRONT_PER_PEND = 1
    DEFER_AFTER = "solve_done"
    pend_last = [None]
    for ti, sp_ in enumerate(specs):
        P.epoch = ti
        nxt = specs[ti + 1] if ti + 1 < len(specs) else None
        x_next = (nxt["x"], nxt["tb"], nxt["ntb"]) if nxt is not None else None
        njobs = 3 if not sp_["full"] else len(cast_jobs)
        for _ in range(min(njobs, len(cast_jobs))):
            cast_jobs.pop(0)()
        g = tile(sp_["x"], x_next=x_next, **sp_["kw"])
        fstep = 0
        while True:
            if fstep % FRONT_PER_PEND == 0:
                for _ in range(PEND_PER_FRONT):
                    if pend is not None:
                        cur_pool[0] = "pend"
                        if next(pend, None) is None:
                            pend = None
            fstep += 1
            cur_pool[0] = "front" if pend is not None else "all"
            m = next(g)
            if m == "front_done":
                break
        if pend is not None:
            cur_pool[0] = "pend"
            for _ in pend:
                pass
            pend = None
        cur_pool[0] = "all"
        while next(g) != DEFER_AFTER:
            pass
        if sp_["full"]:
            for _ in g:
                pass
        else:
            pend = g
            pend_last[0] = "presolve_done"
    if pend is not None:
        cur_pool[0] = "pend"
        for _ in pend:
            pass

    P.emit(nc, es)
    es.close()
    return nc


def _consts():
    c = np.zeros((128, 384), np.float32)
    c[:, 0:128] = np.eye(128, dtype=np.float32)
    t = np.arange(64)
    c[:64, 128:192] = (t[:, None] <= t[None, :]).astype(np.float32)
    c[:64, 192:256] = (t[:, None] > t[None, :]).astype(np.float32)
    c[:, 256:384] = 1.0
    return c


def make_in_maps(inp, NPRE, NFULL, ncores=8):
    ROWS = (NPRE + NFULL) * NT
    NOWN = (NFULL - 1) * NT
    xp = np.asarray(inp["x_prompt"], np.float32)
    xs = np.asarray(inp["x_sample"], np.float32)
    nseg = xp.shape[1] // NOWN
    lnp = np.stack([np.asarray(inp[k], np.float32)[0] for k in ("ln1_g", "ln1_b", "ln2_g", "ln2_b")])
    shared = {
        "w_in": np.ascontiguousarray(inp["w_in"][0]), "gdn_conv_w": np.ascontiguousarray(inp["gdn_conv_w"][0]),
        "a_log": np.ascontiguousarray(inp["a_log"]), "dt_bias": np.ascontiguousarray(inp["dt_bias"]),
        "o_norm_g": np.ascontiguousarray(inp["o_norm_g"]), "sc_conv_w": np.ascontiguousarray(inp["sc_conv_w"][0]),
        "w_o": np.ascontiguousarray(inp["w_o"][0]), "lnp": lnp, "w_up": np.ascontiguousarray(inp["w_up"][0]),
        "ffn_conv_w": np.ascontiguousarray(inp["ffn_conv_w"][0]), "w_down": np.ascontiguousarray(inp["w_down"][0]),
        "consts": _consts(),
    }
    maps = []
    for core in range(ncores):
        b, j = core // nseg, core % nseg
        end = NOWN * (j + 1)
        x_ext = np.zeros((ROWS, D), np.float32)
        n = min(end, ROWS)
        x_ext[ROWS - n:] = xp[b, end - n:end]
        m = dict(shared)
        m["xp"] = x_ext
        m["xs"] = np.ascontiguousarray(xs[4 * core:4 * core + 4].reshape(64, D))
        m["st_gc"] = np.ascontiguousarray(inp["state_gdn_conv"][0, 4 * core:4 * core + 4].reshape(12, 3072))
        m["st_S"] = np.ascontiguousarray(inp["state_gdn_S"][0, 4 * core:4 * core + 4])
        m["st_sc"] = np.ascontiguousarray(inp["state_sc_conv"][0, 4 * core:4 * core + 4].reshape(8, D))
        m["st_ff"] = np.ascontiguousarray(inp["state_ffn_conv"][0, 4 * core:4 * core + 4].reshape(8, DFF))
        maps.append(m)
    return maps


_NC_CACHE = {}


def kernel(**inp):
    NPRE, NFULL = 47, 17
    key = (NPRE, NFULL)
    if key not in _NC_CACHE:
        _NC_CACHE[key] = build(NPRE, NFULL)
    nc = _NC_CACHE[key]
    maps = make_in_maps(inp, NPRE, NFULL)
    res = run_bass_kernel_spmd(nc, maps, core_ids=list(range(8))).results
    B, SEQ = 2, 16384
    yp = np.zeros((B, SEQ, D), np.float32)
    ys = np.zeros((32, 16, D), np.float32)
    p_gc = np.zeros((1, B, 3, 3072), np.float32)
    p_S = np.zeros((1, B, 8, 128, 128), np.float32)
    p_sc = np.zeros((1, B, 2, D), np.float32)
    p_ff = np.zeros((1, B, 2, DFF), np.float32)
    s_gc = np.zeros((1, 32, 3, 3072), np.float32)
    s_S = np.zeros((1, 32, 8, 128, 128), np.float32)
    s_sc = np.zeros((1, 32, 2, D), np.float32)
    s_ff = np.zeros((1, 32, 2, DFF), np.float32)
    for core in range(8):
        r = res[core]
        b, j = core // 4, core % 4
        yp[b, 4096 * j:4096 * (j + 1)] = r["yp"]
        ys[4 * core:4 * core + 4] = r["ys"].reshape(4, 16, D)
        s_gc[0, 4 * core:4 * core + 4] = r["s_gc"].reshape(4, 3, 3072)
        s_S[0, 4 * core:4 * core + 4] = r["s_S"]
        s_sc[0, 4 * core:4 * core + 4] = r["s_sc"].reshape(4, 2, D)
        s_ff[0, 4 * core:4 * core + 4] = r["s_ff"].reshape(4, 2, DFF)
        if j == 3:
            p_gc[0, b] = r["p_gc"]
            p_S[0, b] = r["p_S"]
            p_sc[0, b] = r["p_sc"]
            p_ff[0, b] = r["p_ff"]
    return (yp, ys, p_gc, p_S, p_sc, p_ff, s_gc, s_S, s_sc, s_ff)
```
